# Optimizing a Trainium2 kernel written in Bass

```python
import math
import jax
import jax.numpy as jnp
from jax import lax
import numpy as np

D_MODEL = 1024
BATCH = 16
SEQ = 2048
DEPTH = 4

N_EVEN = (DEPTH + 1) // 2
N_ODD = DEPTH // 2

GLA_HEADS = 4
GLA_DV = D_MODEL // (2 * GLA_HEADS)
GLA_DK = GLA_DV // 2
GLA_RANK = 16
GLA_GATE_NORM = 16.0
GLA_CHUNK = 16

HGRN_HEADS = 4
HGRN_DV = D_MODEL // (2 * HGRN_HEADS)
HGRN_DK = HGRN_DV // 2
HGRN_CHUNK = 16
HGRN_MIN_F = 1e-20

RET_HEADS = 4
RET_DK = D_MODEL // (2 * RET_HEADS)
RET_DV = (3 * D_MODEL) // (4 * RET_HEADS)
RET_CHUNK = 64
ROPE_BASE = 10000.0

S5_WIDTH = D_MODEL // 4
S5_GROUP_CH = 16
S5_GROUPS = S5_WIDTH // S5_GROUP_CH
S5_STATE = 64

FFN_DIM = ((8 * D_MODEL // 3 + 255) // 256) * 256
CONV_WIDTH = 3
EPS = 1e-6

EVEN_COLS = (GLA_HEADS * GLA_DK, GLA_HEADS * GLA_DK, GLA_HEADS * GLA_DV, GLA_HEADS * GLA_DV, GLA_RANK, GLA_RANK,
             HGRN_HEADS * HGRN_DK, HGRN_HEADS * HGRN_DK, HGRN_HEADS * HGRN_DK, HGRN_HEADS * HGRN_DV, HGRN_HEADS * HGRN_DV)
ODD_COLS = (RET_HEADS * RET_DK, RET_HEADS * RET_DK, RET_HEADS * RET_DV, RET_HEADS * RET_DV, S5_WIDTH)
EVEN_IN = sum(EVEN_COLS)
ODD_IN = sum(ODD_COLS)
EVEN_MIX = GLA_HEADS * GLA_DV + HGRN_HEADS * HGRN_DV
ODD_MIX = RET_HEADS * RET_DV + S5_WIDTH

kernel_name = 'bidir_hybrid_gla_hgrn2_retnet_s5_convffn'

F32 = jnp.float32


def _split(p, sizes):
    return jnp.split(p, np.cumsum(sizes)[:-1].tolist(), axis=-1)


def rmsnorm(x, g):
    xf = x.astype(F32)
    y = xf * lax.rsqrt(jnp.mean(xf * xf, axis=-1, keepdims=True) + EPS)
    return (y * g.astype(F32)).astype(x.dtype)


def to_heads(t, h):
    b, s, _ = t.shape
    return t.reshape(b, s, h, -1).transpose(0, 2, 1, 3).astype(F32)


def from_heads(t):
    return t.transpose(0, 2, 1, 3)


def head_rmsnorm(o, g):
    b, s, h, d = o.shape
    y = o * lax.rsqrt(jnp.mean(o * o, axis=-1, keepdims=True) + EPS)
    return (y * g.astype(F32).reshape(h, d)).reshape(b, s, h * d)


def head_groupnorm(o, g):
    b, s, h, d = o.shape
    mu = jnp.mean(o, axis=-1, keepdims=True)
    c = o - mu
    y = c * lax.rsqrt(jnp.mean(c * c, axis=-1, keepdims=True) + EPS)
    return (y * g.astype(F32).reshape(h, d)).reshape(b, s, h * d)


def chunk_gated_scan(q, k, v, log_a, chunk):
    b, h, s, dk = q.shape
    dv = v.shape[-1]
    dg = log_a.shape[-1]
    n = s // chunk
    q, k, v, log_a = (t.reshape(b, h, n, chunk, t.shape[-1]) for t in (q, k, v, log_a))
    cum = jnp.cumsum(log_a, axis=3)
    last = cum[:, :, :, -1:, :]
    pos = jnp.arange(chunk)
    lower = (pos[:, None] >= pos[None, :])[:, :, None]
    rel = cum[:, :, :, :, None, :] - cum[:, :, :, None, :, :]
    decay = jnp.where(lower, jnp.exp(jnp.where(lower, rel, 0.0)), 0.0)
    if dg == 1:
        scores = jnp.einsum('bhnid,bhnjd->bhnij', q, k) * decay[..., 0]
    else:
        scores = jnp.einsum('bhnid,bhnjd,bhnijd->bhnij', q, k, decay)
    o_intra = jnp.einsum('bhnij,bhnjv->bhniv', scores, v)
    q_in = q * jnp.exp(cum)
    k_out = k * jnp.exp(last - cum)
    d_state = jnp.einsum('bhnid,bhniv->bhndv', k_out, v)
    chunk_decay = jnp.exp(last[:, :, :, 0, :])

    def step(state, inp):
        g_c, ds_c = inp
        return g_c[..., None] * state + ds_c, state

    init = jnp.zeros((b, h, dk, dv), q.dtype)
    _, prev = lax.scan(step, init, (jnp.moveaxis(chunk_decay, 2, 0), jnp.moveaxis(d_state, 2, 0)))
    prev = jnp.moveaxis(prev, 0, 2)
    o_inter = jnp.einsum('bhnid,bhndv->bhniv', q_in, prev)
    return (o_intra + o_inter).reshape(b, h, s, dv)


def bidir_scan(q, k_f, k_b, v, la_f, la_b, chunk):
    flip = lambda t: jnp.flip(t, axis=2)
    fwd = chunk_gated_scan(q, k_f, v, la_f, chunk)
    bwd = chunk_gated_scan(flip(q), flip(k_b), flip(v), flip(la_b), chunk)
    return fwd + flip(bwd)


def gla_mixer(q, k, v, r, lr_f, lr_b, wa2, ba, norm_g):
    q = to_heads(q, GLA_HEADS)
    k = to_heads(k, GLA_HEADS) * (GLA_DK ** -0.5)
    v = to_heads(v, GLA_HEADS)

    def log_gate(lr, d):
        z = jnp.einsum('bsr,rk->bsk', lr, wa2[d]) + ba[d]
        return to_heads(jax.nn.log_sigmoid(z.astype(F32)) / GLA_GATE_NORM, GLA_HEADS)

    o = bidir_scan(q, k, k, v, log_gate(lr_f, 0), log_gate(lr_b, 1), GLA_CHUNK)
    o = head_rmsnorm(from_heads(o), norm_g)
    return o * jax.nn.silu(r.astype(F32))


def hgrn_lower_bounds(lb_logits):
    p = jax.nn.softmax(lb_logits.astype(F32), axis=1)
    return jnp.cumsum(p, axis=1) - p[:, :1]


def hgrn2_mixer(q, z_f, z_b, i, g, lb_f, lb_b, norm_g):
    q = jax.nn.silu(to_heads(q, HGRN_HEADS))
    v = to_heads(i, HGRN_HEADS)

    def gate(z, lb):
        z = z.astype(F32)
        f = lb + (1.0 - lb) * jax.nn.sigmoid(z)
        log_f = jnp.log(jnp.maximum(f, HGRN_MIN_F))
        key = (1.0 - lb) * jax.nn.sigmoid(-z)
        return to_heads(log_f, HGRN_HEADS), to_heads(key, HGRN_HEADS)

    la_f, k_f = gate(z_f, lb_f)
    la_b, k_b = gate(z_b, lb_b)
    o = bidir_scan(q, k_f, k_b, v, la_f, la_b, HGRN_CHUNK)
    o = head_rmsnorm(from_heads(o), norm_g)
    return o * jax.nn.silu(g.astype(F32))


def rotary(t):
    s, d = t.shape[2], t.shape[3]
    half = d // 2
    inv = ROPE_BASE ** (-jnp.arange(half, dtype=F32) / half)
    ang = jnp.arange(s, dtype=F32)[:, None] * inv[None, :]
    cos, sin = jnp.cos(ang), jnp.sin(ang)
    t1, t2 = t[..., :half], t[..., half:]
    return jnp.concatenate([t1 * cos - t2 * sin, t1 * sin + t2 * cos], axis=-1)


def retention_mixer(q, k, v, g, norm_g):
    q = rotary(to_heads(q, RET_HEADS))
    k = rotary(to_heads(k, RET_HEADS)) * (RET_DK ** -0.5)
    v = to_heads(v, RET_HEADS)
    b, h, s, _ = q.shape
    hidx = jnp.arange(RET_HEADS, dtype=F32)
    log_gamma_f = jnp.log1p(-jnp.exp2(-5.0 - hidx))
    log_gamma_b = jnp.log1p(-jnp.exp2(-5.5 - hidx))
    la_f = jnp.broadcast_to(log_gamma_f[None, :, None, None], (b, h, s, 1))
    la_b = jnp.broadcast_to(log_gamma_b[None, :, None, None], (b, h, s, 1))
    o = bidir_scan(q, k, k, v, la_f, la_b, RET_CHUNK)
    o = head_groupnorm(from_heads(o), norm_g)
    return o * jax.nn.silu(g.astype(F32))


def _complex_affine_combine(e1, e2):
    a1r, a1i, b1r, b1i = e1
    a2r, a2i, b2r, b2i = e2
    return (a1r * a2r - a1i * a2i,
            a1r * a2i + a1i * a2r,
            a2r * b1r - a2i * b1i + b2r,
            a2r * b1i + a2i * b1r + b2i)


def _s5_direction(ug, lam_re, lam_im, log_dt, b_re, b_im, reverse):
    s = ug.shape[1]
    lr = jnp.minimum(lam_re.astype(F32), -1e-4)
    li = lam_im.astype(F32)
    dt = jnp.exp(log_dt.astype(F32))[:, None]
    mag = jnp.exp(lr * dt)
    ar, ai = mag * jnp.cos(li * dt), mag * jnp.sin(li * dt)
    den = lr * lr + li * li
    nr = ar - 1.0
    cr = (nr * lr + ai * li) / den
    ci = (ai * lr - nr * li) / den
    br, bi = b_re.astype(F32), b_im.astype(F32)
    bbr = cr[..., None] * br - ci[..., None] * bi
    bbi = cr[..., None] * bi + ci[..., None] * br
    xr = jnp.einsum('bsgp,gnp->bsgn', ug, bbr)
    xi = jnp.einsum('bsgp,gnp->bsgn', ug, bbi)
    g_, n_ = ar.shape
    a_r = jnp.broadcast_to(ar[None, None], (1, s, g_, n_))
    a_i = jnp.broadcast_to(ai[None, None], (1, s, g_, n_))
    _, _, sr, si = lax.associative_scan(_complex_affine_combine, (a_r, a_i, xr, xi), reverse=reverse, axis=1)
    return sr, si


def s5_mixer(u, lam_re, lam_im, log_dt, b_re, b_im, c_re, c_im, d_skip, glu_w, glu_b):
    bsz, s, _ = u.shape
    uf = u.astype(F32)
    ug = uf.reshape(bsz, s, S5_GROUPS, S5_GROUP_CH)
    fr, fi = _s5_direction(ug, lam_re[0], lam_im[0], log_dt[0], b_re, b_im, False)
    rr, ri = _s5_direction(ug, lam_re[1], lam_im[1], log_dt[1], b_re, b_im, True)
    hr, hi = fr + rr, fi + ri
    y = (jnp.einsum('bsgn,gpn->bsgp', hr, c_re.astype(F32))
         - jnp.einsum('bsgn,gpn->bsgp', hi, c_im.astype(F32)))
    y = y.reshape(bsz, s, S5_WIDTH) + d_skip.astype(F32) * uf
    g = jax.nn.gelu(y)
    return g * jax.nn.sigmoid(jnp.einsum('bsc,ce->bse', g, glu_w.astype(F32)) + glu_b.astype(F32))


def even_mixer(h, w_in, w_out, wa2, ba, gla_g, lb_f, lb_b, hgrn_g):
    p = jnp.einsum('bsd,de->bse', h, w_in)
    gq, gk, gv, gr, glf, glb, hq, hzf, hzb, hi, hg = _split(p, EVEN_COLS)
    a = gla_mixer(gq, gk, gv, gr, glf, glb, wa2, ba, gla_g)
    bm = hgrn2_mixer(hq, hzf, hzb, hi, hg, lb_f, lb_b, hgrn_g)
    y = jnp.concatenate([a, bm], axis=-1).astype(h.dtype)
    return jnp.einsum('bse,ed->bsd', y, w_out)


def odd_mixer(h, w_in, w_out, ret_g, lam_re, lam_im, log_dt, b_re, b_im, c_re, c_im, d_skip, glu_w, glu_b):
    p = jnp.einsum('bsd,de->bse', h, w_in)
    rq, rk, rv, rg, su = _split(p, ODD_COLS)
    c = retention_mixer(rq, rk, rv, rg, ret_g)
    dm = s5_mixer(su, lam_re, lam_im, log_dt, b_re, b_im, c_re, c_im, d_skip, glu_w, glu_b)
    y = jnp.concatenate([c, dm], axis=-1).astype(h.dtype)
    return jnp.einsum('bse,ed->bsd', y, w_out)


def conv_ffn(h, w_up, conv_w, conv_b, w_down):
    u = jnp.einsum('bsd,df->bsf', h, w_up)
    s = u.shape[1]
    pad = CONV_WIDTH // 2
    up = jnp.pad(u, ((0, 0), (pad, pad), (0, 0)))
    c = conv_b + up[:, 0:s] * conv_w[0]
    for t in range(1, CONV_WIDTH):
        c = c + up[:, t:t + s] * conv_w[t]
    a, v = jnp.split(c, 2, axis=-1)
    return jnp.einsum('bsf,fd->bsd', jax.nn.silu(a) * v, w_down)


def setup_inputs(seed: int = 0) -> dict:
    key = jax.random.key(seed)
    keys = iter(jax.random.split(key, 40))

    def nrm(shape, scale=1.0):
        return scale * jax.random.normal(next(keys), shape, F32)

    def gain(shape):
        return 1.0 + 0.01 * nrm(shape)

    gla_hk = GLA_HEADS * GLA_DK
    hgrn_hk = HGRN_HEADS * HGRN_DK
    x = nrm((BATCH, SEQ, D_MODEL))
    mix_norm_g = gain((DEPTH, D_MODEL))
    ffn_norm_g = gain((DEPTH, D_MODEL))
    final_norm_g = gain((D_MODEL,))
    w_in_even = nrm((N_EVEN, D_MODEL, EVEN_IN), D_MODEL ** -0.5)
    w_out_even = nrm((N_EVEN, EVEN_MIX, D_MODEL), EVEN_MIX ** -0.5)
    gla_wa2 = nrm((N_EVEN, 2, GLA_RANK, gla_hk), GLA_RANK ** -0.5)
    gla_ba = nrm((N_EVEN, 2, gla_hk), 0.1)
    gla_norm_g = gain((N_EVEN, GLA_HEADS * GLA_DV))
    hgrn_lb_logits = nrm((2, N_EVEN, hgrn_hk), 0.1)
    hgrn_norm_g = gain((N_EVEN, HGRN_HEADS * HGRN_DV))
    w_in_odd = nrm((N_ODD, D_MODEL, ODD_IN), D_MODEL ** -0.5)
    w_out_odd = nrm((N_ODD, ODD_MIX, D_MODEL), ODD_MIX ** -0.5)
    ret_norm_g = gain((N_ODD, RET_HEADS * RET_DV))
    s5_lam_re = -0.5 + 0.01 * nrm((N_ODD, 2, S5_GROUPS, S5_STATE))
    s5_lam_im = jnp.pi * jnp.arange(S5_STATE, dtype=F32) + 0.01 * nrm((N_ODD, 2, S5_GROUPS, S5_STATE))
    s5_log_dt = jax.random.uniform(next(keys), (N_ODD, 2, S5_GROUPS), F32, math.log(1e-3), math.log(1e-1))
    s5_b_re = nrm((N_ODD, S5_GROUPS, S5_STATE, S5_GROUP_CH), (2.0 * S5_GROUP_CH) ** -0.5)
    s5_b_im = nrm((N_ODD, S5_GROUPS, S5_STATE, S5_GROUP_CH), (2.0 * S5_GROUP_CH) ** -0.5)
    s5_c_re = nrm((N_ODD, S5_GROUPS, S5_GROUP_CH, S5_STATE), S5_STATE ** -0.5)
    s5_c_im = nrm((N_ODD, S5_GROUPS, S5_GROUP_CH, S5_STATE), S5_STATE ** -0.5)
    s5_d = nrm((N_ODD, S5_WIDTH))
    s5_glu_w = nrm((N_ODD, S5_WIDTH, S5_WIDTH), S5_WIDTH ** -0.5)
    s5_glu_b = nrm((N_ODD, S5_WIDTH), 0.01)
    ffn_w_up = nrm((DEPTH, D_MODEL, 2 * FFN_DIM), D_MODEL ** -0.5)
    ffn_conv_w = nrm((DEPTH, CONV_WIDTH, 2 * FFN_DIM), CONV_WIDTH ** -0.5)
    ffn_conv_b = nrm((DEPTH, 2 * FFN_DIM), 0.01)
    ffn_w_down = nrm((DEPTH, FFN_DIM, D_MODEL), FFN_DIM ** -0.5)
    return {'x': x, 'mix_norm_g': mix_norm_g, 'ffn_norm_g': ffn_norm_g, 'final_norm_g': final_norm_g,
            'w_in_even': w_in_even, 'w_out_even': w_out_even, 'gla_wa2': gla_wa2, 'gla_ba': gla_ba,
            'gla_norm_g': gla_norm_g, 'hgrn_lb_logits': hgrn_lb_logits, 'hgrn_norm_g': hgrn_norm_g,
            'w_in_odd': w_in_odd, 'w_out_odd': w_out_odd, 'ret_norm_g': ret_norm_g,
            's5_lam_re': s5_lam_re, 's5_lam_im': s5_lam_im, 's5_log_dt': s5_log_dt,
            's5_b_re': s5_b_re, 's5_b_im': s5_b_im, 's5_c_re': s5_c_re, 's5_c_im': s5_c_im,
            's5_d': s5_d, 's5_glu_w': s5_glu_w, 's5_glu_b': s5_glu_b,
            'ffn_w_up': ffn_w_up, 'ffn_conv_w': ffn_conv_w, 'ffn_conv_b': ffn_conv_b, 'ffn_w_down': ffn_w_down}


def reference(x, mix_norm_g, ffn_norm_g, final_norm_g,
              w_in_even, w_out_even, gla_wa2, gla_ba, gla_norm_g, hgrn_lb_logits, hgrn_norm_g,
              w_in_odd, w_out_odd, ret_norm_g, s5_lam_re, s5_lam_im, s5_log_dt,
              s5_b_re, s5_b_im, s5_c_re, s5_c_im, s5_d, s5_glu_w, s5_glu_b,
              ffn_w_up, ffn_conv_w, ffn_conv_b, ffn_w_down):
    lbs = hgrn_lower_bounds(hgrn_lb_logits)
    for layer in range(DEPTH):
        j = layer // 2
        h = rmsnorm(x, mix_norm_g[layer])
        if layer % 2 == 0:
            mix = even_mixer(h, w_in_even[j], w_out_even[j], gla_wa2[j], gla_ba[j], gla_norm_g[j],
                             lbs[0, j], lbs[1, j], hgrn_norm_g[j])
        else:
            mix = odd_mixer(h, w_in_odd[j], w_out_odd[j], ret_norm_g[j], s5_lam_re[j], s5_lam_im[j],
                            s5_log_dt[j], s5_b_re[j], s5_b_im[j], s5_c_re[j], s5_c_im[j], s5_d[j],
                            s5_glu_w[j], s5_glu_b[j])
        x = x + mix.astype(x.dtype)
        hf = rmsnorm(x, ffn_norm_g[layer])
        x = x + conv_ffn(hf, ffn_w_up[layer], ffn_conv_w[layer], ffn_conv_b[layer], ffn_w_down[layer]).astype(x.dtype)
    return rmsnorm(x, final_norm_g)
```

```python
import contextlib
import math
import numpy as np
import concourse.bass as bass
import concourse.mybir as mybir
from concourse.bass_utils import run_bass_kernel_spmd

F32 = mybir.dt.float32
BF16 = mybir.dt.bfloat16
AF = mybir.ActivationFunctionType
ALU = mybir.AluOpType

D = 1024
T = 2048
KT = D // 128
NBLK = T // 512
NCH = T // 128
DEPTH = 4
FF = 2816
FT = FF // 128
NSEQ = 2
NCORES = 8
EPS = 1e-6
SAME_ENG_SYNC = True


class Builder:
    def __init__(self):
        self.nc = bass.Bass("TRN2", target_bir_lowering=False, dynamic_dma_scratch_size=4096)
        nc = self.nc
        self.es = contextlib.ExitStack()
        self.eng = dict(pe=nc.tensor, act=nc.scalar, dve=nc.vector, pool=nc.gpsimd, sp=nc.sync)
        self.sem = {e: self.es.enter_context(nc.semaphore("s_" + e)) for e in self.eng}
        self.cnt = {e: 0 for e in self.eng}
        self.seen = {e: {} for e in self.eng}
        self.parts = {}
        self.streams = {}
        self.uid = 0
        self.muted = False

    def sb(self, name, shape, dt, es=None):
        self.uid += 1
        return (es or self.es).enter_context(self.nc.sbuf_tensor(f"{name}_{self.uid}", list(shape), dt))

    def dram_in(self, name, shape, dt=F32):
        return self.nc.dram_tensor(name, list(shape), dt, kind="ExternalInput").ap()

    def dram_out(self, name, shape, dt=F32):
        return self.nc.dram_tensor(name, list(shape), dt, kind="ExternalOutput").ap()

    def _part(self, k):
        p = self.parts.get(k)
        if p is None:
            p = [[], []]
            self.parts[k] = p
        return p

    def _wait(self, e, tickets):
        need = {}
        for (key, h, v, src) in tickets:
            if src == e and (e == "pe" or not SAME_ENG_SYNC):
                continue
            if src is None:
                v = 16 * self.streams[key[2:]][1]
            if self.seen[e].get(key, 0) >= v:
                continue
            if key not in need or need[key][1] < v:
                need[key] = (h, v)
        for key, (h, v) in need.items():
            self.eng[e].wait_ge(h, v)
            self.seen[e][key] = v

    def _deps(self, r, w):
        deps = []
        for k in r:
            deps += self._part(k)[0]
        for k in w:
            p = self._part(k)
            deps += p[0] + p[1]
        return deps

    def _record(self, t, r, w):
        for k in r:
            p = self._part(k)
            p[1] = [x for x in p[1] if x[0] != t[0]] + [t]
        for k in w:
            p = self._part(k)
            p[0] = [t]
            p[1] = []

    def op(self, e, fn, r=(), w=()):
        if self.muted:
            return None
        pr = [k for k in r if isinstance(k, tuple) and k[0] in ("ps", "bank")]
        if pr:
            r = [k for k in r if k not in pr]
            w = list(w) + [k for k in pr if k not in w]
        self._wait(e, self._deps(r, w))
        ins = fn()
        self.cnt[e] += 1
        ins.then_inc(self.sem[e], 1)
        self._record(("e_" + e, self.sem[e], self.cnt[e], e), r, w)
        return ins

    def dma(self, q, out, in_, r=(), w=(), stream="d0"):
        if self.muted:
            return
        st = self.streams.get(stream)
        if st is None:
            st = [self.es.enter_context(self.nc.semaphore("d_" + stream)), 0]
            self.streams[stream] = st
        self._wait(q, self._deps(r, w))
        ins = self.eng[q].dma_start(out=out, in_=in_)
        st[1] += 1
        ins.then_inc(st[0], 16)
        self._record(("d_" + stream, st[0], 16 * st[1], None), r, w)

    def barrier(self):
        ts = [("e_" + e, self.sem[e], self.cnt[e], e) for e in self.eng if self.cnt[e] > 0]
        ts += [("d_" + s, st[0], 16 * st[1], None) for s, st in self.streams.items() if st[1] > 0]
        for e in self.eng:
            self._wait(e, [t for t in ts if t[3] != e])
        self.parts = {}

    def finish(self):
        self.barrier()
        self.es.close()


class StopBuild(Exception):
    pass


class Prog:
    def check(self, tag):
        if self.cfg.get("stop") == tag:
            self.b.muted = True

    def __init__(self, cfg):
        self.cfg = cfg
        self.b = Builder()
        b = self.b
        nc = b.nc
        self.nc = nc
        self.x_in = b.dram_in("x_in", [NSEQ, 128, KT, T])
        self.y_out = b.dram_out("y_out", [NSEQ, 128, KT, T])
        self.gains_d = b.dram_in("gains", [128, 9 * KT])
        self.w_up_d = b.dram_in("w_up_t", [DEPTH, 2 * FT, 128, KT * 128])
        self.w_dn_d = b.dram_in("w_dn_t", [DEPTH, 2, KT, 128, 11 * 128])
        self.conv_d = b.dram_in("conv_p", [128, DEPTH * 2 * FT * 4])
        self.w_out_d = b.dram_in("w_out_t", [DEPTH, KT, 128, KT * 128])
        self.w_tm_e_d = b.dram_in("w_tm_e", [2, 8, 128, KT * 320])
        self.w_fm_e_d = b.dram_in("w_fm_e", [2, 9, 128, KT * 128])
        self.wa2p_d = b.dram_in("wa2p", [2, 4, 32, 128])
        self.bah_d = b.dram_in("bah", [2, 4, 1, 128])
        self.lbl_d = b.dram_in("lbl", [2 * 2 * 256])
        self.hgain_d = b.dram_in("hgain", [128, 16])
        self.tri6_d = b.dram_in("tri6", [128, 6 * 128])
        self.w_tm_o_d = b.dram_in("w_tm_o", [2, 4, 128, KT * 448])
        self.w_fm_o_d = b.dram_in("w_fm_o", [2, 8, 128, KT * 128])
        self.w_u_o_d = b.dram_in("w_u_o", [2, 128, KT * 256])
        self.hgain_o_d = b.dram_in("hgain_o", [128, 16])
        self.rope_d = b.dram_in("rope", [128, 2 * NCH * 64])
        self.rdec_d = b.dram_in("rdec", [128, 16])
        self.dmask_d = b.dram_in("dmask", [128, 4 * 128])
        self.s5_gw_d = b.dram_in("s5_gw", [2, 128, 2 * 256])
        self.s5_prm_d = b.dram_in("s5_prm", [2, 128, 3 * 2 * 8])
        self.s5_bc_d = b.dram_in("s5_bc", [2, 128, 4 * 8 * 16])
        self.s5_dsk_d = b.dram_in("s5_dsk", [2, 128, 16])
        self.s5_glb_d = b.dram_in("s5_glb", [2, 128, 2])
        self.s5_bm_d = b.dram_in("s5_bm", [128, 2 * 128])
        self.sumcols_d = b.dram_in("sumcols", [128, 8])
        self.ident_d = b.dram_in("ident", [128, 128])
        self.xT = b.sb("xT", [128, KT, T], F32)
        self.hT = b.sb("hT", [128, KT, T], BF16)
        self.gains = b.sb("gains", [128, 9 * KT], F32)
        self.ones_bf = b.sb("ones", [128, 128], BF16)
        self.hgain = b.sb("hgain", [128, 16], F32)
        self.hgain_o = b.sb("hgain_o", [128, 16], F32)
        self.tri6 = b.sb("tri6", [128, 6, 128], F32)
        self.sumcols = b.sb("sumcols", [128, 8], F32)
        self.ident = b.sb("ident", [128, 128], BF16)
        self.one_t = b.sb("one", [128, 1], F32)
        self.ps = b.es.enter_context(nc.psum_tensor("ps_all", [128, 4096], F32))
        self.bankrr = 0

    def bank(self, i):
        return self.ps[:, i * 512:(i + 1) * 512]


    def pk(self, bank, c0, c1):
        return [("ps", bank)]

    def pv(self, bank, c0, c1):
        return self.ps[:, bank * 512 + c0: bank * 512 + c1]

    def next_bank(self):
        i = self.bankrr
        self.bankrr = (self.bankrr + 1) % 8
        return i

    def setup(self):
        b = self.b
        nc = self.nc
        b.dma("sp", self.gains[:], self.gains_d[:, :], w=["gains"], stream="c0")
        b.op("pool", lambda: nc.gpsimd.memset(self.ones_bf[:], 1.0), w=["ones"])
        b.op("pool", lambda: nc.gpsimd.memset(self.one_t[:], 1.0), w=["one"])
        b.dma("sp", self.hgain[:], self.hgain_d[:, :], w=["hgain"], stream="c0")
        b.dma("sp", self.hgain_o[:], self.hgain_o_d[:, :], w=["hgain_o"], stream="c0")
        b.dma("sp", self.tri6[:], self.tri6_d[:, :].rearrange("p (a c) -> p a c", a=6), w=["tri6"], stream="c0")
        b.dma("sp", self.sumcols[:], self.sumcols_d[:, :], w=["sumcols"], stream="c0")
        b.dma("pool", self.ident[:], self.ident_d[:, :], w=["ident"], stream="c1")

    def load_x(self, s):
        b = self.b
        for k in range(KT):
            b.dma("sp", self.xT[:, k, :], self.x_in[s, :, k, :], w=[("x", k, blk) for blk in range(NBLK)],
                  stream="xin")

    def store_x(self, s):
        b = self.b
        for k in range(KT):
            b.dma("sp", self.y_out[s, :, k, :], self.xT[:, k, :], r=[("x", k, blk) for blk in range(NBLK)],
                  stream="xout")

    def rmsnorm(self, gidx, out_f32_inplace=False):
        b = self.b
        nc = self.nc
        with contextlib.ExitStack() as es:
            sq = b.sb("sq", [128, 2, KT, 512], BF16, es)
            rs = b.sb("rstd", [128, 2, 512], F32, es)
            for blk in range(NBLK):
                par = blk % 2
                sl = slice(blk * 512, (blk + 1) * 512)
                xk = [("x", k, blk) for k in range(KT)]
                b.op("act", lambda: nc.scalar.activation(out=sq[:, par, :, :], in_=self.xT[:, :, sl], func=AF.Square),
                     r=xk, w=[("sq", par)])
                bi = self.next_bank()
                for k in range(KT):
                    b.op("pe", lambda: nc.tensor.matmul(self.bank(bi), lhsT=self.ones_bf[:], rhs=sq[:, par, k, :],
                                                        start=(k == 0), stop=(k == KT - 1)),
                         r=[("sq", par), "ones"], w=[("bank", bi)])
                b.op("act", lambda: nc.scalar.activation(out=rs[:, par, :], in_=self.bank(bi), func=AF.Sqrt,
                                                         scale=1.0 / D, bias=self.eps_t[:, 0:1]),
                     r=[("bank", bi), "eps"], w=[("rs", par)])
                b.op("dve", lambda: nc.vector.reciprocal(out=rs[:, par, :], in_=rs[:, par, :]),
                     r=[("rs", par)], w=[("rs", par)])
                for k in range(KT):
                    g = self.gains[:, gidx * KT + k: gidx * KT + k + 1]
                    if out_f32_inplace:
                        b.op("dve", lambda: nc.vector.scalar_tensor_tensor(
                            out=self.xT[:, k, sl], in0=self.xT[:, k, sl], scalar=g, in1=rs[:, par, :],
                            op0=ALU.mult, op1=ALU.mult), r=[("x", k, blk), ("rs", par), "gains"], w=[("x", k, blk)])
                    else:
                        b.op("dve", lambda: nc.vector.scalar_tensor_tensor(
                            out=self.hT[:, k, sl], in0=self.xT[:, k, sl], scalar=g, in1=rs[:, par, :],
                            op0=ALU.mult, op1=ALU.mult), r=[("x", k, blk), ("rs", par), "gains"], w=[("h", k, blk)])
            b.barrier()

    def ffn(self, layer):
        b = self.b
        nc = self.nc
        self.rmsnorm(DEPTH + layer)
        hall = [("h", k, blk) for k in range(KT) for blk in range(NBLK)]
        with contextlib.ExitStack() as es:
            g = b.sb("ffg", [128, 11, T], BF16, es)
            wup = b.sb("wup", [128, 3, KT, 128], BF16, es)
            wdn = b.sb("wdn", [128, 2, 11, 128], BF16, es)
            cbuf = b.sb("cbuf", [128, 2, T], F32, es)
            sbuf = b.sb("sbuf", [128, T], BF16, es)
            cp = b.sb("convp", [128, 2 * FT * 4], F32, es)
            b.dma("sp", cp[:], self.conv_d[:, layer * 2 * FT * 4:(layer + 1) * 2 * FT * 4], w=["convp"], stream="c0")
            ucount = 0
            dcount = 0
            for half in range(2):
                for mm in range(11):
                    m = half * 11 + mm
                    for kind in range(2):
                        unit = 2 * m + kind
                        slot = ucount % 3
                        par = ucount % 2
                        ucount += 1
                        b.dma("pool", wup[:, slot, :, :], self.w_up_d[layer, unit, :, :].rearrange("p (k c) -> p k c", k=KT),
                              w=[("wup", slot)], stream=f"wup{slot}")
                        banks = [4 * par + i for i in range(4)]
                        for blk in range(NBLK):
                            for k in range(KT):
                                b.op("pe", lambda: nc.tensor.matmul(
                                    self.bank(banks[blk]), lhsT=wup[:, slot, k, :],
                                    rhs=self.hT[:, k, blk * 512:(blk + 1) * 512],
                                    start=(k == 0), stop=(k == KT - 1)),
                                    r=[("wup", slot), ("h", k, blk)], w=[("bank", banks[blk])])
                        u = self.ps[:, 2048 * par: 2048 * (par + 1)]
                        base = unit * 4
                        w0, w1, w2, cb = (cp[:, base + i: base + i + 1] for i in range(4))
                        bk = [("bank", x) for x in banks]
                        c = cbuf[:, par, :]
                        b.op("act", lambda: nc.scalar.activation(out=c, in_=u, func=AF.Identity, scale=w1, bias=cb),
                             r=bk + ["convp"], w=[("c", par)])
                        b.op("dve", lambda: nc.vector.scalar_tensor_tensor(
                            out=c[:, 1:], in0=u[:, :T - 1], scalar=w0, in1=c[:, 1:], op0=ALU.mult, op1=ALU.add),
                            r=bk + ["convp", ("c", par)], w=[("c", par)])
                        b.op("dve", lambda: nc.vector.scalar_tensor_tensor(
                            out=c[:, :T - 1], in0=u[:, 1:], scalar=w2, in1=c[:, :T - 1], op0=ALU.mult, op1=ALU.add),
                            r=bk + ["convp", ("c", par)], w=[("c", par)])
                        if kind == 0:
                            b.op("act", lambda: nc.scalar.activation(out=sbuf[:], in_=c, func=AF.Silu),
                                 r=[("c", par)], w=["s"])
                        else:
                            b.op("pool", lambda: nc.gpsimd.tensor_tensor(out=g[:, mm, :], in0=sbuf[:], in1=c,
                                                                           op=ALU.mult),
                                 r=[("c", par), "s"], w=[("g", mm)])
                for dt in range(KT):
                    slot = dcount % 2
                    dcount += 1
                    b.dma("pool", wdn[:, slot, :, :], self.w_dn_d[layer, half, dt, :, :].rearrange("p (k c) -> p k c", k=11),
                          w=[("wdn", slot)], stream=f"wdn{slot}")
                    for blk in range(NBLK):
                        bi = self.next_bank()
                        for kk in range(11):
                            b.op("pe", lambda: nc.tensor.matmul(
                                self.bank(bi), lhsT=wdn[:, slot, kk, :], rhs=g[:, kk, blk * 512:(blk + 1) * 512],
                                start=(kk == 0), stop=(kk == 10)),
                                r=[("wdn", slot), ("g", kk)], w=[("bank", bi)])
                        sl = slice(blk * 512, (blk + 1) * 512)
                        b.op("dve", lambda: nc.vector.tensor_tensor(out=self.xT[:, dt, sl], in0=self.bank(bi),
                                                                     in1=self.xT[:, dt, sl], op=ALU.add),
                             r=[("bank", bi), ("x", dt, blk)], w=[("x", dt, blk)])
            b.barrier()

    def build(self):
        b = self.b
        nc = self.nc
        cfg = self.cfg
        self.eps_t = b.sb("eps", [128, 1], F32)
        b.op("pool", lambda: nc.gpsimd.memset(self.eps_t[:], EPS), w=["eps"])
        self.setup()
        for s in range(cfg.get("nseq", NSEQ)):
            self.load_x(s)
            try:
                for layer in cfg.get("layers", range(DEPTH)):
                    if "mix" in cfg.get("phases", ("mix", "ffn")):
                        self.mixer(layer)
                    if "ffn" in cfg.get("phases", ("mix", "ffn")):
                        self.ffn(layer)
            except StopBuild:
                pass
            b.muted = False
            b.barrier()
            if cfg.get("final", True):
                self.rmsnorm(2 * DEPTH, out_f32_inplace=True)
            self.store_x(s)
            b.barrier()
        b.finish()
        return nc


    def mixer(self, layer):
        self.rmsnorm(layer)
        with contextlib.ExitStack() as es:
            self.yT = self.b.sb("yT", [128, KT, T], BF16, es)
            if layer % 2 == 0:
                self.even_mixer(layer // 2, es)
            else:
                self.odd_mixer(layer // 2, es)
            if self.cfg.get("dbg_y"):
                for k in self.cfg.get("dbg_tiles", range(KT)):
                    self.b.op("act", lambda: self.nc.scalar.activation(out=self.xT[:, k, :], in_=self.yT[:, k, :], func=AF.Identity),
                              r=[("y", k, blk) for blk in range(NBLK)], w=[("x", k, blk) for blk in range(NBLK)])
            else:
                self.out_proj(layer, es)
            self.b.barrier()

    def out_proj(self, layer, es):
        b = self.b
        nc = self.nc
        wo = b.sb("wo", [128, 2, KT, 128], BF16, es)
        for dt in range(KT):
            slot = dt % 2
            b.dma("pool", wo[:, slot, :, :], self.w_out_d[layer, dt, :, :].rearrange("p (k c) -> p k c", k=KT),
                  w=[("wo", slot)], stream=f"wo{slot}")
            for blk in range(NBLK):
                bi = dt % 2
                for k in range(KT):
                    b.op("pe", lambda: nc.tensor.matmul(self.pv(bi, 0, 512), lhsT=wo[:, slot, k, :],
                                                        rhs=self.yT[:, k, blk * 512:(blk + 1) * 512],
                                                        start=(k == 0), stop=(k == KT - 1)),
                         r=[("wo", slot), ("y", k, blk)], w=self.pk(bi, 0, 512))
                sl = slice(blk * 512, (blk + 1) * 512)
                b.op("dve", lambda: nc.vector.tensor_tensor(out=self.xT[:, dt, sl], in0=self.pv(bi, 0, 512),
                                                             in1=self.xT[:, dt, sl], op=ALU.add),
                     r=self.pk(bi, 0, 512) + [("x", dt, blk)], w=[("x", dt, blk)])

    def head_phase23(self, es, tmp, qT, kT, kt_all, v_all, v3, S_all, Gall, dk, nsub, gate_w, gain_ap, etile,
                     groupnorm=False):
        b = self.b
        nc = self.nc
        Pst = tmp["Pst"]
        sub = 128 // nsub
        nsc = NCH * nsub

        def kv_ops(g):
            c, s_ = divmod(g, nsub)
            if nsub == 1:
                return c, slice(0, 128), v_all, ("v", c)
            if s_ < 3:
                return c, slice(32 * s_, 32 * s_ + 32), v_all, ("v", c)
            return c, slice(64, 128), v3, ("v3", c)

        for step in range(nsc - 1):
            for d, lo in ((0, 0), (1, dk)):
                g = step if d == 0 else nsc - 1 - step
                nxt = 1 if d == 0 else -1
                c, rows, vv, vkey = kv_ops(g)
                bank = (4 if d == 0 else 6) + step % 2
                kv = self.pv(bank, 0, 128)[lo:lo + dk, :]
                kvk = self.pk(bank, 0, 128)
                prow = slice(lo, lo + dk)
                b.op("pe", lambda: nc.tensor.matmul(kv, lhsT=kt_all[rows, c, lo:lo + dk], rhs=vv[rows, c, :],
                                                    start=True, stop=True), r=[("kt", c), vkey], w=kvk)
                pkey = ("Pst", lo)
                if step == 0:
                    b.op("dve", lambda: nc.vector.tensor_copy(out=Pst[prow, :], in_=kv), r=kvk, w=[pkey])
                else:
                    gprev = Gall[prow, g - nxt: g - nxt + 1]
                    b.op("dve", lambda: nc.vector.scalar_tensor_tensor(
                        out=Pst[prow, :], in0=Pst[prow, :], scalar=gprev, in1=kv, op0=ALU.mult, op1=ALU.add),
                        r=kvk + [pkey, "G"], w=[pkey])
                b.op("act", lambda: nc.scalar.activation(out=S_all[prow, g + nxt, :], in_=Pst[prow, :], func=AF.Identity,
                                                         scale=Gall[prow, g: g + 1]),
                     r=[pkey, "G"], w=[("S", g + nxt, lo)])
        self.check("e_p2")
        mF = 4 if nsub == 1 else 2
        maskF = self.tri6[:, mF, :]
        maskB = self.tri6[:, mF + 1, :]
        K2 = 2 * dk
        for blk in range(NBLK):
            sl = slice(blk * 512, (blk + 1) * 512)
            for k in range(KT):
                b.op("pe", lambda: nc.tensor.matmul(self.pv(2, 0, 512), lhsT=gate_w[:, k, :], rhs=self.hT[:, k, sl],
                                                    start=(k == 0), stop=(k == KT - 1)),
                     r=["wfm", ("h", k, blk)], w=self.pk(2, 0, 512))
            for cc in range(4):
                c = blk * 4 + cc
                ch = slice(c * 128, (c + 1) * 128)
                par = c % 2
                sbs = (0, 4) if par == 0 else (3, 6)
                for d, lo in ((0, 0), (1, dk)):
                    b.op("pe", lambda: nc.tensor.matmul(self.pv(sbs[d], 0, 128),
                                                        lhsT=kT[lo:lo + dk, ch], rhs=qT[lo:lo + dk, ch],
                                                        start=True, stop=True),
                         r=[("kT", c), ("qT", c)], w=self.pk(sbs[d], 0, 128))
                for d, lo in ((0, 0), (1, dk)):
                    b.op("dve", lambda: nc.vector.tensor_tensor(
                        out=tmp["sT"][:, par, d, :], in0=self.pv(sbs[d], 0, 128),
                        in1=(maskF if d == 0 else maskB), op=ALU.mult),
                        r=self.pk(sbs[d], 0, 128) + ["tri6"], w=[("sT", par, d)])
                ov = self.pv(1, cc * 128, cc * 128 + 128)
                ok = self.pk(1, cc * 128, cc * 128 + 128)
                b.op("pe", lambda: nc.tensor.matmul(ov, lhsT=v_all[:, c, :], rhs=tmp["sT"][:, par, 0, :],
                                                    start=True, stop=False), r=[("v", c), ("sT", par, 0)], w=ok)
                b.op("pe", lambda: nc.tensor.matmul(ov, lhsT=v_all[:, c, :], rhs=tmp["sT"][:, par, 1, :],
                                                    start=False, stop=False), r=[("v", c), ("sT", par, 1)], w=ok)
                for s_ in range(nsub):
                    g = c * nsub + s_
                    b.op("pe", lambda: nc.tensor.matmul(ov[:, s_ * sub:(s_ + 1) * sub], lhsT=S_all[0:K2, g, :],
                                                        rhs=qT[0:K2, c * 128 + s_ * sub: c * 128 + (s_ + 1) * sub],
                                                        start=False, stop=(s_ == nsub - 1)),
                         r=[("S", g, 0), ("S", g, dk), ("qT", c)], w=ok)
            self.head_norm_gate(tmp, blk, gain_ap, etile, groupnorm, 128)

    def head_norm_gate(self, tmp, blk, gain_ap, etile, groupnorm, dv):
        b = self.b
        nc = self.nc
        sl = slice(blk * 512, (blk + 1) * 512)
        o = self.pv(1, 0, 512)
        ok = self.pk(1, 0, 512)
        osq, rstd, e1, t1 = tmp["osq"], tmp["rstd"], tmp["e1"], tmp["t1"]
        src = o
        srck = ok
        if groupnorm:
            b.op("act", lambda: nc.scalar.activation(out=tmp["o32b"][:dv, :], in_=o[:dv, :], func=AF.Identity),
                 r=ok, w=["o32b"])
            b.op("pe", lambda: nc.tensor.matmul(self.pv(5, 0, 512)[:dv, :], lhsT=self.ones_bf[:dv, :dv],
                                                rhs=tmp["o32b"][:dv, :], start=True, stop=True),
                 r=["o32b", "ones"], w=self.pk(5, 0, 512))
            b.op("dve", lambda: nc.vector.scalar_tensor_tensor(
                out=tmp["oc"][:dv, :], in0=self.pv(5, 0, 512)[:dv, :], scalar=-1.0 / dv, in1=o[:dv, :],
                op0=ALU.mult, op1=ALU.add), r=self.pk(5, 0, 512) + ok, w=["oc"])
            src = tmp["oc"]
            srck = ["oc"]
        b.op("act", lambda: nc.scalar.activation(out=osq[:dv, :], in_=src[:dv, :], func=AF.Square), r=srck, w=["osq"])
        b.op("pe", lambda: nc.tensor.matmul(self.pv(5, 0, 512)[:dv, :], lhsT=self.ones_bf[:dv, :dv], rhs=osq[:dv, :],
                                            start=True, stop=True), r=["osq", "ones"], w=self.pk(5, 0, 512))
        b.op("act", lambda: nc.scalar.activation(out=rstd[:dv, :], in_=self.pv(5, 0, 512)[:dv, :], func=AF.Sqrt,
                                                 scale=1.0 / dv, bias=self.eps_t[:dv, 0:1]),
             r=self.pk(5, 0, 512) + ["eps"], w=["rstd"])
        b.op("dve", lambda: nc.vector.reciprocal(out=rstd[:dv, :], in_=rstd[:dv, :]), r=["rstd"], w=["rstd"])
        gp = self.pv(2, 0, 512)
        gk = self.pk(2, 0, 512)
        b.op("act", lambda: nc.scalar.activation(out=e1[:dv, :], in_=gp[:dv, :], func=AF.Exp, scale=-1.0),
             r=gk, w=["e1"])
        b.op("dve", lambda: nc.vector.tensor_scalar_add(out=e1[:dv, :], in0=e1[:dv, :], scalar1=1.0),
             r=["e1"], w=["e1"])
        b.op("dve", lambda: nc.vector.reciprocal(out=e1[:dv, :], in_=e1[:dv, :]), r=["e1"], w=["e1"])
        b.op("dve", lambda: nc.vector.tensor_tensor(out=e1[:dv, :], in0=gp[:dv, :], in1=e1[:dv, :], op=ALU.mult),
             r=gk + ["e1"], w=["e1"])
        b.op("dve", lambda: nc.vector.scalar_tensor_tensor(out=t1[:dv, :], in0=src[:dv, :], scalar=gain_ap,
                                                           in1=rstd[:dv, :], op0=ALU.mult, op1=ALU.mult),
             r=srck + ["rstd", "hgain"], w=["t1"])
        b.op("dve", lambda: nc.vector.tensor_tensor(out=self.yT[:dv, etile, sl], in0=t1[:dv, :], in1=e1[:dv, :],
                                                     op=ALU.mult), r=["t1", "e1"], w=[("y", etile, blk)])


    RET_PIECES = {0: ((0, 128, 0, 0), (128, 64, 1, 0)), 1: ((0, 64, 1, 64), (64, 128, 2, 0)),
                  2: ((0, 128, 3, 0), (128, 64, 4, 0)), 3: ((0, 64, 4, 64), (64, 128, 5, 0))}

    def odd_mixer(self, j, es):
        with contextlib.ExitStack() as es_r:
            self.retnet(j, es_r)
            self.b.barrier()
        if self.cfg.get("s5", True):
            with contextlib.ExitStack() as es_s:
                self.s5(j, es_s)
                self.b.barrier()

    def retnet(self, j, es):
        b = self.b
        nc = self.nc
        tmp = self.alloc_head_tmps(es, groupnorm=True)
        wtm = b.sb("wtmo", [128, KT, 448], BF16, es)
        wfm = b.sb("wfmo", [128, 2, KT, 128], BF16, es)
        qT3 = b.sb("qT3", [128, 3, T], BF16, es)
        kT = b.sb("kTo", [128, T], BF16, es)
        kt_all = b.sb("kt_allo", [128, NCH, 2, 128], BF16, es)
        v_all = b.sb("v_allo", [128, NCH, 192], BF16, es)
        S_all = b.sb("S_allo", [128, 2, NCH, 192], BF16, es)
        Pst = b.sb("Psto", [128, 2, 192], F32, es)
        rope = b.sb("rope", [128, 2, NCH, 64], BF16, es)
        rdec = b.sb("rdec", [128, 16], F32, es)
        dmask = b.sb("dmask", [128, 4, 128], F32, es)
        AB = b.sb("AB", [128, 2, 2, 4, 64], F32, es)
        rot = b.sb("rot", [128, 2, 4, 64], F32, es)
        qt3 = b.sb("qt3", [128, 2, 4, 128], BF16, es)
        mean_sb = b.sb("mean_sb", [128, 512], F32, es)
        oc2 = b.sb("oc2", [128, 512], F32, es)
        b.dma("pool", rope[:], self.rope_d[:, :].rearrange("p (a c k) -> p a c k", a=2, c=NCH), w=["rope"], stream="c1")
        b.dma("sp", rdec[:], self.rdec_d[:, :], w=["rdec"], stream="c0")
        b.dma("sp", dmask[:], self.dmask_d[:, :].rearrange("p (h i) -> p h i", h=4), w=["dmask"], stream="c0")
        b.op("pool", lambda: nc.gpsimd.memset(S_all[:], 0.0), w=[("S", d, c) for d in range(2) for c in range(NCH)])
        for h in range(4):
            gf128 = float((1.0 - 2.0 ** (-5.0 - h)) ** 128)
            gb128 = float((1.0 - 2.0 ** (-5.5 - h)) ** 128)
            pieces = self.RET_PIECES[h]
            b.dma("pool", wtm[:], self.w_tm_o_d[j, h, :, :].rearrange("p (k c) -> p k c", k=KT), w=["wtm"], stream="wtm0")
            for pi in range(2):
                b.dma("pool", wfm[:, pi, :, :], self.w_fm_o_d[j, 2 * h + pi, :, :].rearrange("p (k c) -> p k c", k=KT),
                      w=["wfm"], stream="wfm")
            for c in range(NCH):
                ch = slice(c * 128, (c + 1) * 128)
                par = c % 2
                P = self.pv(par, 0, 512)
                Pk = self.pk(par, 0, 448)
                for k in range(KT):
                    b.op("pe", lambda: nc.tensor.matmul(P[:, 0:448], lhsT=self.hT[:, k, ch], rhs=wtm[:, k, :],
                                                        start=(k == 0), stop=(k == KT - 1)),
                         r=["wtm", ("h", k, c // 4)], w=Pk)
                P4 = P[:, 0:256].rearrange("p (a k) -> p a k", a=4)
                cosb = rope[:, 0, c, :].unsqueeze(1).to_broadcast([128, 4, 64])
                sinb = rope[:, 1, c, :].unsqueeze(1).to_broadcast([128, 4, 64])
                A = AB[:, par, 0, :, :]
                Bm = AB[:, par, 1, :, :]
                R = rot[:, par, :, :]
                b.op("dve", lambda: nc.vector.tensor_tensor(out=A, in0=P4, in1=cosb, op=ALU.mult), r=Pk + ["rope"],
                     w=[("A", par)])
                b.op("dve", lambda: nc.vector.tensor_tensor(out=Bm, in0=P4, in1=sinb, op=ALU.mult), r=Pk + ["rope"],
                     w=[("B", par)])
                b.op("pool", lambda: nc.gpsimd.tensor_tensor(out=R[:, 0::2, :], in0=A[:, 0::2, :], in1=Bm[:, 1::2, :],
                                                              op=ALU.subtract), r=[("A", par), ("B", par)], w=[("rot", par)])
                b.op("pool", lambda: nc.gpsimd.tensor_tensor(out=R[:, 1::2, :], in0=Bm[:, 0::2, :], in1=A[:, 1::2, :],
                                                              op=ALU.add), r=[("A", par), ("B", par)], w=[("rot", par)])
                rq = rot[:, par, 0:2, :].rearrange("p a k -> p (a k)")
                rk = rot[:, par, 2:4, :].rearrange("p a k -> p (a k)")
                QT = qt3[:, par, :, :]
                sc_k = 128.0 ** -0.5
                b.op("act", lambda: nc.scalar.activation(out=QT[:, 0, :], in_=rq, func=AF.Identity), r=[("rot", par)],
                     w=[("qt3", par)])
                b.op("act", lambda: nc.scalar.activation(out=QT[:, 1, :], in_=rq, func=AF.Identity,
                                                         scale=rdec[:, 4 * h + 0: 4 * h + 1]),
                     r=[("rot", par), "rdec"], w=[("qt3", par)])
                b.op("act", lambda: nc.scalar.activation(out=QT[:, 2, :], in_=rq, func=AF.Identity,
                                                         scale=rdec[:, 4 * h + 1: 4 * h + 2]),
                     r=[("rot", par), "rdec"], w=[("qt3", par)])
                b.op("act", lambda: nc.scalar.activation(out=QT[:, 3, :], in_=rk, func=AF.Identity, scale=sc_k),
                     r=[("rot", par)], w=[("qt3", par)])
                b.op("pool", lambda: nc.gpsimd.tensor_scalar(out=kt_all[:, c, 0, :], in0=rk,
                                                              scalar1=rdec[:, 4 * h + 2: 4 * h + 3], scalar2=None,
                                                              op0=ALU.mult), r=[("rot", par), "rdec"], w=[("kt", c)])
                b.op("pool", lambda: nc.gpsimd.tensor_scalar(out=kt_all[:, c, 1, :], in0=rk,
                                                              scalar1=rdec[:, 4 * h + 3: 4 * h + 4], scalar2=None,
                                                              op0=ALU.mult), r=[("rot", par), "rdec"], w=[("kt", c)])
                b.op("act", lambda: nc.scalar.activation(out=v_all[:, c, :], in_=P[:, 256:448], func=AF.Identity), r=Pk,
                     w=[("v", c)])
                zb = 2 + par
                tp = self.pv(zb, 0, 256).bitcast(BF16)
                tkey = self.pk(zb, 0, 256)
                for a_ in range(4):
                    b.op("pe", lambda: nc.tensor.transpose(tp[:, a_ * 128:(a_ + 1) * 128], QT[:, a_, :], self.ident[:]),
                         r=[("qt3", par), "ident"], w=tkey)
                b.op("dve", lambda: nc.vector.tensor_copy(out=qT3[:, :, ch],
                                                           in_=tp[:, 0:384].rearrange("p (a t) -> p a t", a=3)),
                     r=tkey, w=[("qT", c)])
                b.op("act", lambda: nc.scalar.activation(out=kT[:, ch], in_=tp[:, 384:512], func=AF.Identity), r=tkey,
                     w=[("kT", c)])
            for step in range(NCH - 1):
                for d in range(2):
                    c = step if d == 0 else NCH - 1 - step
                    nxt = 1 if d == 0 else -1
                    gdec = gf128 if d == 0 else gb128
                    bank = (4 if d == 0 else 6) + step % 2
                    kv = self.pv(bank, 0, 192)
                    kvk = self.pk(bank, 0, 192)
                    b.op("pe", lambda: nc.tensor.matmul(kv, lhsT=kt_all[:, c, d, :], rhs=v_all[:, c, :], start=True,
                                                        stop=True), r=[("kt", c), ("v", c)], w=kvk)
                    pkey = ("Pst", d)
                    if step == 0:
                        b.op("dve", lambda: nc.vector.tensor_copy(out=Pst[:, d, :], in_=kv), r=kvk, w=[pkey])
                    else:
                        b.op("dve", lambda: nc.vector.scalar_tensor_tensor(out=Pst[:, d, :], in0=Pst[:, d, :], scalar=gdec,
                                                                           in1=kv, op0=ALU.mult, op1=ALU.add),
                             r=kvk + [pkey], w=[pkey])
                    b.op("act", lambda: nc.scalar.activation(out=S_all[:, d, c + nxt, :], in_=Pst[:, d, :], func=AF.Identity),
                         r=[pkey], w=[("S", d, c + nxt)])
            obank = (1, 6)
            gbank = (2, 7)
            for blk in range(NBLK):
                sl = slice(blk * 512, (blk + 1) * 512)
                for pi, (f0, nf, etile, p0) in enumerate(pieces):
                    for k in range(KT):
                        b.op("pe", lambda: nc.tensor.matmul(self.pv(gbank[pi], 0, 512)[p0:p0 + nf, :],
                                                            lhsT=wfm[:, pi, k, 0:nf], rhs=self.hT[:, k, sl],
                                                            start=(k == 0), stop=(k == KT - 1)),
                             r=["wfm", ("h", k, blk)], w=self.pk(gbank[pi], 0, 512))
                for cc in range(4):
                    c = blk * 4 + cc
                    ch = slice(c * 128, (c + 1) * 128)
                    par = c % 2
                    sb_ = 0 if par == 0 else 3
                    b.op("pe", lambda: nc.tensor.matmul(self.pv(sb_, 0, 128), lhsT=kT[:, ch], rhs=qT3[:, 0, ch],
                                                        start=True, stop=True), r=[("kT", c), ("qT", c)],
                         w=self.pk(sb_, 0, 128))
                    b.op("dve", lambda: nc.vector.tensor_tensor(out=tmp["sT"][:, par, 0, :], in0=self.pv(sb_, 0, 128),
                                                                 in1=dmask[:, h, :], op=ALU.mult),
                         r=self.pk(sb_, 0, 128) + ["dmask"], w=[("sT", par)])
                    for pi, (f0, nf, etile, p0) in enumerate(pieces):
                        ov = self.pv(obank[pi], cc * 128, cc * 128 + 128)[p0:p0 + nf, :]
                        ok = self.pk(obank[pi], 0, 512)
                        b.op("pe", lambda: nc.tensor.matmul(ov, lhsT=v_all[:, c, f0:f0 + nf], rhs=tmp["sT"][:, par, 0, :],
                                                            start=True, stop=False), r=[("v", c), ("sT", par)], w=ok)
                        b.op("pe", lambda: nc.tensor.matmul(ov, lhsT=S_all[:, 0, c, f0:f0 + nf], rhs=qT3[:, 1, ch],
                                                            start=False, stop=False), r=[("S", 0, c), ("qT", c)], w=ok)
                        b.op("pe", lambda: nc.tensor.matmul(ov, lhsT=S_all[:, 1, c, f0:f0 + nf], rhs=qT3[:, 2, ch],
                                                            start=False, stop=True), r=[("S", 1, c), ("qT", c)], w=ok)
                stat = self.pv(5, 0, 512)
                statk = self.pk(5, 0, 512)
                osq, rstd, e1, t1, o32b, oc = (tmp[n] for n in ("osq", "rstd", "e1", "t1", "o32b", "oc"))
                for pi, (f0, nf, etile, p0) in enumerate(pieces):
                    pr = slice(p0, p0 + nf)
                    b.op("act", lambda: nc.scalar.activation(out=o32b[pr, :] if pi == 0 else osq[pr, :],
                                                             in_=self.pv(obank[pi], 0, 512)[pr, :], func=AF.Identity),
                         r=self.pk(obank[pi], 0, 512), w=[("o32b", pi)])
                for pi, (f0, nf, etile, p0) in enumerate(pieces):
                    pr = slice(p0, p0 + nf)
                    src = o32b if pi == 0 else osq
                    b.op("pe", lambda: nc.tensor.matmul(stat, lhsT=self.ones_bf[pr, :], rhs=src[pr, :], start=(pi == 0),
                                                        stop=(pi == 1)), r=[("o32b", pi), "ones"], w=statk)
                b.op("act", lambda: nc.scalar.activation(out=mean_sb[:], in_=stat, func=AF.Identity, scale=-1.0 / 192),
                     r=statk, w=["mean"])
                ocs = (oc, oc2)
                for pi, (f0, nf, etile, p0) in enumerate(pieces):
                    pr = slice(p0, p0 + nf)
                    b.op("dve", lambda: nc.vector.tensor_tensor(out=ocs[pi][pr, :], in0=self.pv(obank[pi], 0, 512)[pr, :],
                                                                 in1=mean_sb[pr, :], op=ALU.add),
                         r=self.pk(obank[pi], 0, 512) + ["mean"], w=[("oc", pi)])
                    b.op("act", lambda: nc.scalar.activation(out=o32b[pr, :] if pi == 0 else osq[pr, :], in_=ocs[pi][pr, :],
                                                             func=AF.Square), r=[("oc", pi)], w=[("o32b", pi)])
                for pi, (f0, nf, etile, p0) in enumerate(pieces):
                    pr = slice(p0, p0 + nf)
                    src = o32b if pi == 0 else osq
                    b.op("pe", lambda: nc.tensor.matmul(stat, lhsT=self.ones_bf[pr, :], rhs=src[pr, :], start=(pi == 0),
                                                        stop=(pi == 1)), r=[("o32b", pi), "ones"], w=statk)
                b.op("act", lambda: nc.scalar.activation(out=rstd[:], in_=stat, func=AF.Sqrt, scale=1.0 / 192,
                                                         bias=self.eps_t[:, 0:1]), r=statk + ["eps"], w=["rstd"])
                b.op("dve", lambda: nc.vector.reciprocal(out=rstd[:], in_=rstd[:]), r=["rstd"], w=["rstd"])
                for pi, (f0, nf, etile, p0) in enumerate(pieces):
                    pr = slice(p0, p0 + nf)
                    gp = self.pv(gbank[pi], 0, 512)[pr, :]
                    gk = self.pk(gbank[pi], 0, 512)
                    gain_ap = self.hgain_o[pr, j * 8 + 2 * h + pi: j * 8 + 2 * h + pi + 1]
                    b.op("act", lambda: nc.scalar.activation(out=e1[pr, :], in_=gp, func=AF.Exp, scale=-1.0), r=gk,
                         w=["e1"])
                    b.op("dve", lambda: nc.vector.tensor_scalar_add(out=e1[pr, :], in0=e1[pr, :], scalar1=1.0),
                         r=["e1"], w=["e1"])
                    b.op("dve", lambda: nc.vector.reciprocal(out=e1[pr, :], in_=e1[pr, :]), r=["e1"], w=["e1"])
                    b.op("dve", lambda: nc.vector.tensor_tensor(out=e1[pr, :], in0=gp, in1=e1[pr, :], op=ALU.mult),
                         r=gk + ["e1"], w=["e1"])
                    b.op("dve", lambda: nc.vector.scalar_tensor_tensor(out=t1[pr, :], in0=ocs[pi][pr, :], scalar=gain_ap,
                                                                       in1=rstd[pr, :], op0=ALU.mult, op1=ALU.mult),
                         r=[("oc", pi), "rstd", "hgain_o"], w=["t1"])
                    b.op("dve", lambda: nc.vector.tensor_tensor(out=self.yT[pr, etile, sl], in0=t1[pr, :], in1=e1[pr, :],
                                                                 op=ALU.mult), r=["t1", "e1"],
                         w=[("y", etile, blk)])


    def cmul(self, out, x, y, t, shape_keys):
        b = self.b
        nc = self.nc
        (o_r, o_i), ko = out
        (xr, xi), kx = x
        (yr, yi), ky = y
        (t1, t2), kt = t
        b.op("dve", lambda: nc.vector.tensor_tensor(out=t1, in0=xr, in1=yr, op=ALU.mult), r=[kx, ky], w=[kt])
        b.op("dve", lambda: nc.vector.tensor_tensor(out=t2, in0=xi, in1=yi, op=ALU.mult), r=[kx, ky], w=[kt])
        b.op("dve", lambda: nc.vector.tensor_tensor(out=o_r, in0=t1, in1=t2, op=ALU.subtract), r=[kt], w=[ko])
        b.op("dve", lambda: nc.vector.tensor_tensor(out=t1, in0=xr, in1=yi, op=ALU.mult), r=[kx, ky, ko], w=[kt])
        b.op("dve", lambda: nc.vector.tensor_tensor(out=t2, in0=xi, in1=yr, op=ALU.mult), r=[kx, ky], w=[kt])
        b.op("dve", lambda: nc.vector.tensor_tensor(out=o_i, in0=t1, in1=t2, op=ALU.add), r=[kt], w=[ko])

    def s5(self, j, es):
        b = self.b
        nc = self.nc
        V = nc.vector
        NC8 = T // 8

        def tt(out, in0, in1, op, r, w):
            b.op("dve", lambda: V.tensor_tensor(out=out, in0=in0, in1=in1, op=op), r=r, w=w)

        def ts(out, in0, s1, s2, op0, op1, r, w):
            b.op("dve", lambda: V.tensor_scalar(out=out, in0=in0, scalar1=s1, scalar2=s2, op0=op0, op1=op1), r=r, w=w)

        Uflat = b.sb("Uflat", [128, 16, NC8], BF16, es)
        Tg = b.sb("Tg", [128, 16, 128], BF16, es)
        KX = b.sb("KX", [128, 2, 2, 8, 128], BF16, es)
        QY = b.sb("QY", [128, 2, 2, 8, 128], BF16, es)
        Hs = b.sb("Hs", [128, 2, 2, 8, NC8], BF16, es)
        gw = b.sb("gw", [128, 2, 256], BF16, es)
        glb = b.sb("s5glb", [128, 2], F32, es)
        A8 = b.sb("A8", [128, 2, 2, 8], F32, es)
        b.dma("pool", gw[:], self.s5_gw_d[j, :, :].rearrange("p (k c) -> p k c", k=2), w=["gw"], stream="wfm")
        b.dma("sp", glb[:], self.s5_glb_d[j, :, :], w=["glb"], stream="c0")

        with contextlib.ExitStack() as e1:
            u2 = b.sb("u2", [128, 2, 16, 8, 16], BF16, e1)
            wu = b.sb("wu", [128, KT, 256], BF16, e1)
            b.dma("pool", wu[:], self.w_u_o_d[j, :, :].rearrange("p (k c) -> p k c", k=KT), w=["wu"], stream="wtm0")
            cnt = 0
            for half in range(2):
                for jj in range(8):
                    bi = cnt % 2
                    cnt += 1
                    for k in range(KT):
                        b.op("pe", lambda: nc.tensor.matmul(
                            self.pv(bi, 0, 256), lhsT=self.hT[:, k, half * 1024 + jj: half * 1024 + 1024: 8],
                            rhs=wu[:, k, :], start=(k == 0), stop=(k == KT - 1)),
                            r=["wu"] + [("h", k, blk) for blk in (2 * half, 2 * half + 1)], w=self.pk(bi, 0, 256))
                    b.op("act", lambda: nc.scalar.activation(
                        out=u2[:, half, :, jj, :], in_=self.pv(bi, 0, 256).rearrange("p (g k) -> p g k", g=16),
                        func=AF.Identity), r=self.pk(bi, 0, 256), w=[("u2", half)])
            cnt = 0
            for half in range(2):
                for g4 in range(4):
                    bi = 2 + cnt % 2
                    cnt += 1
                    tp = self.pv(bi, 0, 256).bitcast(BF16)
                    for gg in range(4):
                        g = g4 * 4 + gg
                        b.op("pe", lambda: nc.tensor.transpose(tp[:, gg * 128:(gg + 1) * 128],
                                                               u2[:, half, g, :, :].rearrange("p a k -> p (a k)"),
                                                               self.ident[:]),
                             r=[("u2", half), "ident"], w=self.pk(bi, 0, 256))
                    b.op("dve", lambda: V.tensor_copy(out=Uflat[:, g4 * 4:(g4 + 1) * 4, half * 128:(half + 1) * 128],
                                                      in_=tp.rearrange("p (a c) -> p a c", a=4)),
                         r=self.pk(bi, 0, 256), w=["Uflat"])
            b.barrier()

        with contextlib.ExitStack() as e2:
            prm = b.sb("s5prm", [128, 3, 2, 8], F32, e2)
            BC = b.sb("s5BC", [128, 4, 8, 16], F32, e2)
            dsk = b.sb("s5dsk", [128, 16], F32, e2)
            bmask = b.sb("s5bm", [128, 2, 128], F32, e2)
            identf = b.sb("identf", [128, 128], F32, e2)
            b.dma("sp", prm[:], self.s5_prm_d[j, :, :].rearrange("p (a d g) -> p a d g", a=3, d=2), w=["prm"], stream="c0")
            b.dma("sp", BC[:], self.s5_bc_d[j, :, :].rearrange("p (a g k) -> p a g k", a=4, g=8), w=["BC"], stream="c0")
            b.dma("sp", dsk[:], self.s5_dsk_d[j, :, :], w=["dsk"], stream="c0")
            b.dma("sp", bmask[:], self.s5_bm_d[:, :].rearrange("p (a c) -> p a c", a=2), w=["bmask"], stream="c0")
            b.dma("sp", identf[:], self.ident_d[:, :], w=["identf"], stream="c0")
            sc = {}
            for nm in ("lr", "dt", "th", "s8", "c8", "m8", "ar", "ai", "t1", "t2", "t3", "den", "cr", "ci", "nar", "nai"):
                sc[nm] = b.sb("s5_" + nm, [128, 2, 8], F32, e2)
            PW = b.sb("s5PW", [128, 2, 2, 2, 8, 9], F32, e2)
            BB = b.sb("s5BB", [128, 2, 2, 8, 16], F32, e2)
            TB = b.sb("s5TB", [128, 4, 8, 8, 16], F32, e2)
            TX = b.sb("s5TX", [128, 2, 8, 8, 16], F32, e2)
            tw = b.sb("s5tw", [128, 2, 8, 8, 16], F32, e2)
            Tacc = b.sb("s5Tacc", [128, 128], F32, e2)
            lr, dt, th = sc["lr"][:], sc["dt"][:], sc["th"][:]
            ts(lr, prm[:, 0, :, :], -1e-4, None, ALU.min, ALU.bypass, ["prm"], ["lr"])
            b.op("act", lambda: nc.scalar.activation(out=dt, in_=prm[:, 2, :, :], func=AF.Exp), r=["prm"], w=["dt"])
            tt(th, prm[:, 1, :, :], dt, ALU.mult, ["prm", "dt"], ["th"])
            b.op("act", lambda: nc.scalar.activation(out=sc["s8"][:], in_=th, func=AF.Sin, scale=1.0 / 8), r=["th"],
                 w=["s8"])
            b.op("act", lambda: nc.scalar.activation(out=sc["t1"][:], in_=th, func=AF.Sin, scale=1.0 / 16), r=["th"],
                 w=["t1"])
            tt(sc["t1"][:], sc["t1"][:], sc["t1"][:], ALU.mult, ["t1"], ["t1"])
            ts(sc["c8"][:], sc["t1"][:], -2.0, 1.0, ALU.mult, ALU.add, ["t1"], ["c8"])
            tt(sc["t2"][:], lr, dt, ALU.mult, ["lr", "dt"], ["t2"])
            b.op("act", lambda: nc.scalar.activation(out=sc["m8"][:], in_=sc["t2"][:], func=AF.Exp, scale=1.0 / 8),
                 r=["t2"], w=["m8"])
            ar, ai = sc["ar"][:], sc["ai"][:]
            tt(ar, sc["m8"][:], sc["c8"][:], ALU.mult, ["m8", "c8"], ["a"])
            tt(ai, sc["m8"][:], sc["s8"][:], ALU.mult, ["m8", "s8"], ["a"])
            for _ in range(3):
                tt(sc["t1"][:], ar, ar, ALU.mult, ["a"], ["t1"])
                tt(sc["t2"][:], ai, ai, ALU.mult, ["a"], ["t2"])
                tt(sc["t3"][:], ar, ai, ALU.mult, ["a"], ["t3"])
                tt(ar, sc["t1"][:], sc["t2"][:], ALU.subtract, ["t1", "t2"], ["a"])
                ts(ai, sc["t3"][:], 2.0, None, ALU.mult, ALU.bypass, ["t3"], ["a"])
            tt(sc["t1"][:], ar, ar, ALU.mult, ["a"], ["t1"])
            tt(sc["t2"][:], ai, ai, ALU.mult, ["a"], ["t2"])
            tt(sc["t1"][:], sc["t1"][:], sc["t2"][:], ALU.add, ["t1", "t2"], ["t1"])
            b.op("dve", lambda: V.reciprocal(out=sc["t1"][:], in_=sc["t1"][:]), r=["t1"], w=["t1"])
            tt(sc["nar"][:], ar, sc["t1"][:], ALU.mult, ["a", "t1"], ["na"])
            tt(sc["nai"][:], ai, sc["t1"][:], ALU.mult, ["a", "t1"], ["na"])
            ts(sc["nai"][:], sc["nai"][:], -1.0, None, ALU.mult, ALU.bypass, ["na"], ["na"])
            b.op("dve", lambda: V.memset(PW[:, :, 0, :, :, 0], 1.0), w=["PW"])
            b.op("dve", lambda: V.memset(PW[:, :, 1, :, :, 0], 0.0), r=["PW"], w=["PW"])
            for pi_, (br_, bi_, kb) in enumerate(((ar, ai, "a"), (sc["nar"][:], sc["nai"][:], "na"))):
                for k in range(8):
                    self.cmul(((PW[:, pi_, 0, :, :, k + 1], PW[:, pi_, 1, :, :, k + 1]), "PW"),
                              ((PW[:, pi_, 0, :, :, k], PW[:, pi_, 1, :, :, k]), "PW"),
                              ((br_, bi_), kb), ((sc["t1"][:], sc["t2"][:]), "t12"), None)
            b.op("dve", lambda: V.tensor_copy(out=A8[:, 0, :, :], in_=PW[:, 0, 0, :, :, 8]), r=["PW"], w=["A8"])
            b.op("dve", lambda: V.tensor_copy(out=A8[:, 1, :, :], in_=PW[:, 0, 1, :, :, 8]), r=["PW"], w=["A8"])
            den, cr, ci = sc["den"][:], sc["cr"][:], sc["ci"][:]
            li = prm[:, 1, :, :]
            tt(sc["t1"][:], lr, lr, ALU.mult, ["lr"], ["t1"])
            tt(sc["t2"][:], li, li, ALU.mult, ["prm"], ["t2"])
            tt(den, sc["t1"][:], sc["t2"][:], ALU.add, ["t1", "t2"], ["den"])
            b.op("dve", lambda: V.reciprocal(out=den, in_=den), r=["den"], w=["den"])
            ts(sc["t3"][:], ar, -1.0, None, ALU.add, ALU.bypass, ["a"], ["t3"])
            tt(sc["t1"][:], sc["t3"][:], lr, ALU.mult, ["t3", "lr"], ["t1"])
            tt(sc["t2"][:], ai, li, ALU.mult, ["a", "prm"], ["t2"])
            tt(cr, sc["t1"][:], sc["t2"][:], ALU.add, ["t1", "t2"], ["c"])
            tt(cr, cr, den, ALU.mult, ["c", "den"], ["c"])
            tt(sc["t1"][:], ai, lr, ALU.mult, ["a", "lr"], ["t1"])
            tt(sc["t2"][:], sc["t3"][:], li, ALU.mult, ["t3", "prm"], ["t2"])
            tt(ci, sc["t1"][:], sc["t2"][:], ALU.subtract, ["t1", "t2"], ["c"])
            tt(ci, ci, den, ALU.mult, ["c", "den"], ["c"])
            sh4 = [128, 2, 8, 16]
            crb = cr.unsqueeze(3).to_broadcast(sh4)
            cib = ci.unsqueeze(3).to_broadcast(sh4)
            brb = BC[:, 0, :, :].unsqueeze(1).to_broadcast(sh4)
            bib = BC[:, 1, :, :].unsqueeze(1).to_broadcast(sh4)
            w4 = tw[:, :, 0:2, 0, :].rearrange("p a d k -> p a d k")
            t4a = tw[:, 0, :, 0:2, :]
            tA = TX[:, 0, 0:2, :, :]
            tB = TX[:, 1, 0:2, :, :]
            self.cmul(((BB[:, 0, :, :, :], BB[:, 1, :, :, :]), "BB"), ((crb, cib), "c"), ((brb, bib), "BC"),
                      ((tA, tB), "TX"), None)
            sh5 = [128, 8, 8, 16]
            for d in range(2):
                kp = 1 if d == 0 else 0
                qp = 0 if d == 0 else 1
                pkr = PW[:, kp, 0, d, :, 0:8].unsqueeze(3).to_broadcast(sh5)
                pki = PW[:, kp, 1, d, :, 0:8].unsqueeze(3).to_broadcast(sh5)
                pqr = PW[:, qp, 0, d, :, 0:8].unsqueeze(3).to_broadcast(sh5)
                pqi = PW[:, qp, 1, d, :, 0:8].unsqueeze(3).to_broadcast(sh5)
                bbr = BB[:, 0, d, :, :].unsqueeze(2).to_broadcast(sh5)
                bbi = BB[:, 1, d, :, :].unsqueeze(2).to_broadcast(sh5)
                ccr = BC[:, 2, :, :].unsqueeze(2).to_broadcast(sh5)
                cci = BC[:, 3, :, :].unsqueeze(2).to_broadcast(sh5)
                self.cmul(((TB[:, 0], TB[:, 1]), "TB"), ((pkr, pki), "PW"), ((bbr, bbi), "BB"), ((tw[:, 0], tw[:, 1]), "tw"),
                          None)
                self.cmul(((TB[:, 2], TB[:, 3]), "TB"), ((pqr, pqi), "PW"), ((ccr, cci), "BC"), ((tw[:, 0], tw[:, 1]), "tw"),
                          None)
                ts(TB[:, 3], TB[:, 3], -1.0, None, ALU.mult, ALU.bypass, ["TB"], ["TB"])
                for g in range(16):
                    pair, g2 = divmod(g, 2)
                    rows = slice(g2 * 64, g2 * 64 + 64)
                    bi = g % 2
                    tps = self.pv(bi, 0, 128)
                    fl = lambda a_: TB[rows, a_, pair, :, :].rearrange("p a k -> p (a k)")
                    b.op("pe", lambda: nc.tensor.matmul(tps, lhsT=fl(0), rhs=fl(2), start=True, stop=False), r=["TB"],
                         w=self.pk(bi, 0, 128))
                    b.op("pe", lambda: nc.tensor.matmul(tps, lhsT=fl(1), rhs=fl(3), start=False, stop=True), r=["TB"],
                         w=self.pk(bi, 0, 128))
                    if d == 0:
                        tt(Tacc[:], tps, bmask[:, 0, :], ALU.mult, self.pk(bi, 0, 128) + ["bmask"], ["Tacc"])
                        b.op("dve", lambda: V.scalar_tensor_tensor(out=Tacc[:], in0=identf[:], scalar=dsk[:, g:g + 1],
                                                                   in1=Tacc[:], op0=ALU.mult, op1=ALU.add),
                             r=["identf", "dsk", "Tacc"], w=["Tacc"])
                        b.op("dve", lambda: V.tensor_copy(out=Tg[:, g, :], in_=Tacc[:]), r=["Tacc"], w=[("Tg", g)])
                    else:
                        tt(Tacc[:], tps, bmask[:, 1, :], ALU.mult, self.pk(bi, 0, 128) + ["bmask"], ["Tacc"])
                        tt(Tg[:, g, :], Tacc[:], Tg[:, g, :], ALU.add, ["Tacc", ("Tg", g)], [("Tg", g)])
                if d == 0:
                    p7r = PW[:, 0, 0, d, :, 7].unsqueeze(2).unsqueeze(3).to_broadcast(sh5)
                    p7i = PW[:, 0, 1, d, :, 7].unsqueeze(2).unsqueeze(3).to_broadcast(sh5)
                    self.cmul(((TX[:, 0], TX[:, 1]), "TX"), ((TB[:, 0], TB[:, 1]), "TB"), ((p7r, p7i), "PW"),
                              ((tw[:, 0], tw[:, 1]), "tw"), None)
                    kxr, kxi, kxk = TX[:, 0], TX[:, 1], "TX"
                else:
                    kxr, kxi, kxk = TB[:, 0], TB[:, 1], "TB"
                for ri, src in enumerate((kxr, kxi)):
                    for p4 in range(2):
                        bi = 2 + (ri * 2 + p4) % 2
                        for pp in range(4):
                            pair = p4 * 4 + pp
                            b.op("pe", lambda: nc.tensor.transpose(self.pv(bi, pp * 128, pp * 128 + 128),
                                                                   src[:, pair, :, :].rearrange("p a k -> p (a k)"),
                                                                   identf[:]), r=[kxk, "identf"], w=self.pk(bi, 0, 512))
                        b.op("act", lambda: nc.scalar.activation(
                            out=KX[:, d, ri, p4 * 4:(p4 + 1) * 4, :],
                            in_=self.pv(bi, 0, 512).rearrange("p (a q) -> p a q", a=4), func=AF.Identity),
                            r=self.pk(bi, 0, 512), w=["KX"])
                pw_idx = 1 if d == 0 else 8
                pyr = PW[:, 0, 0, d, :, pw_idx].unsqueeze(2).unsqueeze(3).to_broadcast(sh5)
                pyi = PW[:, 0, 1, d, :, pw_idx].unsqueeze(2).unsqueeze(3).to_broadcast(sh5)
                tt(tw[:, 0], TB[:, 2], pyr, ALU.mult, ["TB", "PW"], ["tw"])
                tt(tw[:, 1], TB[:, 3], pyi, ALU.mult, ["TB", "PW"], ["tw"])
                tt(TX[:, 0], tw[:, 0], tw[:, 1], ALU.add, ["tw"], ["TX"])
                b.op("act", lambda: nc.scalar.activation(out=QY[:, d, 0, :, :], in_=TX[:, 0].rearrange("p g a k -> p g (a k)"),
                                                         func=AF.Identity), r=["TX"], w=["QY"])
                tt(tw[:, 0], TB[:, 3], pyr, ALU.mult, ["TB", "PW", "TX"], ["tw"])
                tt(tw[:, 1], TB[:, 2], pyi, ALU.mult, ["TB", "PW"], ["tw"])
                tt(TX[:, 1], tw[:, 0], tw[:, 1], ALU.subtract, ["tw"], ["TX"])
                b.op("act", lambda: nc.scalar.activation(out=QY[:, d, 1, :, :], in_=TX[:, 1].rearrange("p g a k -> p g (a k)"),
                                                         func=AF.Identity), r=["TX"], w=["QY"])
            b.barrier()

        with contextlib.ExitStack() as e3:
            RC = b.sb("s5RC", [128, 2, 8, NC8], F32, e3)
            XH = b.sb("s5XH", [128, 2, 8, NC8], F32, e3)
            W1 = b.sb("s5W1", [128, 8, 128], F32, e3)
            W2 = b.sb("s5W2", [128, 8, 128], F32, e3)
            mm = b.sb("s5mm", [128, 3, 8], F32, e3)
            for d in range(2):
                a8r, a8i = A8[:, 0, d, :], A8[:, 1, d, :]
                tt(mm[:, 1, :], a8r, a8r, ALU.mult, ["A8"], ["mm1"])
                tt(mm[:, 2, :], a8i, a8i, ALU.mult, ["A8"], ["mm2"])
                tt(mm[:, 0, :], mm[:, 1, :], mm[:, 2, :], ALU.add, ["mm1", "mm2"], ["mm0"])
                b.op("act", lambda: nc.scalar.activation(out=mm[:, 0, :], in_=mm[:, 0, :], func=AF.Sqrt), r=["mm0"],
                     w=["mm0"])
                b.op("dve", lambda: V.reciprocal(out=mm[:, 1, :], in_=mm[:, 0, :]), r=["mm0", "mm1"], w=["mm1"])
                tt(mm[:, 2, :], a8i, mm[:, 1, :], ALU.mult, ["A8", "mm1", "mm2"], ["mm2"])
                tt(mm[:, 1, :], a8r, mm[:, 1, :], ALU.mult, ["A8", "mm1"], ["mm1"])
                b.op("dve", lambda: V.tensor_copy(out=RC[:, 0, :, 0], in_=mm[:, 1, :]), r=["mm1"], w=["RC"])
                b.op("dve", lambda: V.tensor_copy(out=RC[:, 1, :, 0], in_=mm[:, 2, :]), r=["mm2"], w=["RC"])
                wdt = 1
                while wdt < NC8:
                    shw = [128, 8, wdt]
                    sr = RC[:, 0, :, wdt - 1].unsqueeze(2).to_broadcast(shw)
                    si = RC[:, 1, :, wdt - 1].unsqueeze(2).to_broadcast(shw)
                    self.cmul(((RC[:, 0, :, wdt:2 * wdt], RC[:, 1, :, wdt:2 * wdt]), "RC"),
                              ((RC[:, 0, :, 0:wdt], RC[:, 1, :, 0:wdt]), "RC"), ((sr, si), "RC"),
                              ((W1[:, :, 0:wdt], W2[:, :, 0:wdt]), "W12"), None)
                    wdt *= 2
                for g in range(16):
                    pair, g2 = divmod(g, 2)
                    rows = slice(g2 * 64, g2 * 64 + 64)
                    for ri in range(2):
                        bi = ri * 4 + pair // 2
                        c0 = (pair % 2) * 256
                        b.op("pe", lambda: nc.tensor.matmul(self.pv(bi, c0, c0 + 256)[rows, :], lhsT=KX[:, d, ri, pair, rows],
                                                            rhs=Uflat[:, g, :], start=True, stop=True),
                             r=["KX", "Uflat"], w=self.pk(bi, 0, 512))
                XR = self.ps[:, 0:2048].rearrange("p (g c) -> p g c", g=8)
                XI = self.ps[:, 2048:4096].rearrange("p (g c) -> p g c", g=8)
                kR = [("ps", i) for i in range(4)]
                kI = [("ps", i) for i in range(4, 8)]
                if d == 0:
                    rcr, rci = RC[:, 0, :, :], RC[:, 1, :, :]
                else:
                    rcr, rci = RC[:, 0, :, ::-1], RC[:, 1, :, ::-1]
                for hc in range(2):
                    cs = slice(hc * 128, (hc + 1) * 128)
                    tt(W1[:], XR[:, :, cs], rcr[:, :, cs], ALU.mult, kR + ["RC"], ["W1"])
                    tt(W2[:], XI[:, :, cs], rci[:, :, cs], ALU.mult, kI + ["RC"], ["W2"])
                    tt(XH[:, 0, :, cs], W1[:], W2[:], ALU.add, ["W1", "W2"], ["XHr"])
                    tt(W1[:], XI[:, :, cs], rcr[:, :, cs], ALU.mult, kI + ["RC", "XHr"], ["W1"])
                    tt(W2[:], XR[:, :, cs], rci[:, :, cs], ALU.mult, kR + ["RC", "XHr"], ["W2"])
                    tt(XH[:, 1, :, cs], W1[:], W2[:], ALU.subtract, ["W1", "W2"], ["XHi"])
                for ri, kk in ((0, "XHr"), (1, "XHi")):
                    for pair in range(8):
                        mb = mm[:, 0, pair:pair + 1].to_broadcast([128, NC8])
                        if d == 0:
                            dat = XH[:, ri, pair, :]
                        else:
                            dat = XH[:, ri, pair, ::-1]
                        b.op("dve", lambda: V.tensor_tensor_scan(out=dat, data0=mb, data1=dat, initial=0.0, op0=ALU.mult,
                                                                 op1=ALU.add), r=[kk, "mm0"], w=[kk])
                sh_ = 1 if d == 0 else -1
                zsl = slice(0, 1) if d == 0 else slice(NC8 - 1, NC8)
                for hc in range(2):
                    c0, c1 = hc * 128, (hc + 1) * 128
                    s0, s1 = c0, c1
                    if d == 0 and hc == 1:
                        s1 = NC8 - 1
                    if d == 1 and hc == 0:
                        s0 = 1
                    cs = slice(c0, c1)
                    wsl = slice(s0 - c0, s1 - c0)
                    dsl = slice(s0 + sh_, s1 + sh_)
                    tt(W1[:], XH[:, 0, :, cs], rcr[:, :, cs], ALU.mult, ["XHr", "RC", ("Hs", d)], ["W1"])
                    tt(W2[:], XH[:, 1, :, cs], rci[:, :, cs], ALU.mult, ["XHi", "RC", ("Hs", d)], ["W2"])
                    tt(Hs[:, d, 0, :, dsl], W1[:, :, wsl], W2[:, :, wsl], ALU.subtract, ["W1", "W2"], [("Hs", d)])
                    tt(W1[:], XH[:, 0, :, cs], rci[:, :, cs], ALU.mult, ["XHr", "RC", ("Hs", d)], ["W1"])
                    tt(W2[:], XH[:, 1, :, cs], rcr[:, :, cs], ALU.mult, ["XHi", "RC", ("Hs", d)], ["W2"])
                    tt(Hs[:, d, 1, :, dsl], W1[:, :, wsl], W2[:, :, wsl], ALU.add, ["W1", "W2"], [("Hs", d)])
                b.op("dve", lambda: V.memset(Hs[:, d, :, :, zsl], 0.0), r=[("Hs", d)], w=[("Hs", d)])
            b.barrier()

        with contextlib.ExitStack() as e4:
            yf = b.sb("s5yf", [128, 2, T], F32, e4)
            nglb = b.sb("s5nglb", [128, 2], F32, e4)
            e4a = contextlib.ExitStack()
            Ysb = b.sb("s5Ysb", [128, 16, NC8], BF16, e4a)
            y2 = b.sb("s5y2", [128, 2, 8, 16, 16], BF16, e4a)
            for g in range(16):
                pair, g2 = divmod(g, 2)
                rows = slice(g2 * 64, g2 * 64 + 64)
                bi = g % 4
                yp = self.pv(bi, 0, 256)
                yk = self.pk(bi, 0, 256)
                b.op("pe", lambda: nc.tensor.matmul(yp, lhsT=Tg[:, g, :], rhs=Uflat[:, g, :], start=True, stop=False),
                     r=[("Tg", g), "Uflat"], w=yk)
                for d in range(2):
                    for ri in range(2):
                        b.op("pe", lambda: nc.tensor.matmul(yp, lhsT=QY[rows, d, ri, pair, :], rhs=Hs[rows, d, ri, pair, :],
                                                            start=False, stop=(d == 1 and ri == 1)),
                             r=["QY", ("Hs", d)], w=yk)
                b.op("act", lambda: nc.scalar.activation(out=Ysb[:, g, :], in_=yp, func=AF.Identity), r=yk, w=[("Ysb", g)])
            cnt = 0
            for half in range(2):
                for g4 in range(4):
                    bi = 4 + cnt % 2
                    cnt += 1
                    tp = self.pv(bi, 0, 256).bitcast(BF16)
                    for gg in range(4):
                        g = g4 * 4 + gg
                        b.op("pe", lambda: nc.tensor.transpose(tp[:, gg * 128:(gg + 1) * 128],
                                                               Ysb[:, g, half * 128:(half + 1) * 128], self.ident[:]),
                             r=[("Ysb", g), "ident"], w=self.pk(bi, 0, 256))
                    b.op("dve", lambda: V.tensor_copy(
                        out=y2[:, half, :, g4 * 4:(g4 + 1) * 4, :].rearrange("p i g k -> p g i k"),
                        in_=tp.rearrange("p (g i k) -> p g i k", g=4, i=8)), r=self.pk(bi, 0, 256), w=[("y2", half)])
            cnt = 0
            for half in range(2):
                for i2 in range(2):
                    bi = 6 + cnt % 2
                    cnt += 1
                    tp = self.pv(bi, 0, 512).bitcast(BF16)
                    for i4 in range(4):
                        ii_ = i2 * 4 + i4
                        for kk in range(2):
                            b.op("pe", lambda: nc.tensor.transpose(
                                tp[:, (i4 * 2 + kk) * 128:(i4 * 2 + kk + 1) * 128],
                                y2[:, half, ii_, kk * 8:(kk + 1) * 8, :].rearrange("p g k -> p (g k)"), self.ident[:]),
                                r=[("y2", half), "ident"], w=self.pk(bi, 0, 512))
                    for kk in range(2):
                        src = tp.rearrange("p (i k c) -> p i k c", i=4, k=2)[:, :, kk, :]
                        base = half * 1024 + i2 * 4
                        dst = yf[:, kk, half * 1024:(half + 1) * 1024].rearrange("p (c i) -> p i c", i=8)[:, i2 * 4:(i2 + 1) * 4, :]
                        b.op("act", lambda: nc.scalar.activation(out=dst, in_=src, func=AF.Identity), r=self.pk(bi, 0, 512),
                             w=["yf"])
            b.barrier()
            e4a.close()
            gt = b.sb("s5gt", [128, 2, T], F32, e4)
            geb = b.sb("s5geb", [128, 2, T], BF16, e4)
            c2 = 2.0 * math.sqrt(2.0 / math.pi)
            tt(gt[:], yf[:], yf[:], ALU.mult, ["yf"], ["gt"])
            ts(gt[:], gt[:], 0.044715, 1.0, ALU.mult, ALU.add, ["gt"], ["gt"])
            tt(gt[:], gt[:], yf[:], ALU.mult, ["gt", "yf"], ["gt"])
            ts(gt[:], gt[:], -30.0, None, ALU.max, ALU.bypass, ["gt"], ["gt"])
            b.op("act", lambda: nc.scalar.activation(out=gt[:], in_=gt[:], func=AF.Exp, scale=-c2), r=["gt"], w=["gt"])
            ts(gt[:], gt[:], 1.0, None, ALU.add, ALU.bypass, ["gt"], ["gt"])
            b.op("dve", lambda: V.reciprocal(out=gt[:], in_=gt[:]), r=["gt"], w=["gt"])
            tt(yf[:], yf[:], gt[:], ALU.mult, ["gt", "yf"], ["yf"])
            b.op("act", lambda: nc.scalar.activation(out=geb[:], in_=yf[:], func=AF.Identity), r=["yf"], w=["geb"])
            ts(nglb[:], glb[:], -1.0, None, ALU.mult, ALU.bypass, ["glb"], ["nglb"])
            for et in range(2):
                for blk in range(NBLK):
                    sl = slice(blk * 512, (blk + 1) * 512)
                    bi = (et * NBLK + blk) % 4
                    zp = self.pv(bi, 0, 512)
                    zk = self.pk(bi, 0, 512)
                    for kk in range(2):
                        b.op("pe", lambda: nc.tensor.matmul(zp, lhsT=gw[:, kk, et * 128:(et + 1) * 128], rhs=geb[:, kk, sl],
                                                            start=(kk == 0), stop=(kk == 1)), r=["gw", "geb"], w=zk)
                    g1 = gt[:, et, sl]
                    ts(g1, zp, glb[:, et:et + 1], None, ALU.add, ALU.bypass, zk + ["glb"], [("g1", et, blk)])
                    b.op("act", lambda: nc.scalar.activation(out=g1, in_=g1, func=AF.Exp, scale=-1.0),
                         r=[("g1", et, blk)], w=[("g1", et, blk)])
                    ts(g1, g1, 1.0, None, ALU.add, ALU.bypass, [("g1", et, blk)], [("g1", et, blk)])
                    b.op("dve", lambda: V.reciprocal(out=g1, in_=g1), r=[("g1", et, blk)], w=[("g1", et, blk)])
                    tt(self.yT[:, 6 + et, sl], yf[:, et, sl], g1, ALU.mult, ["yf", ("g1", et, blk)], [("y", 6 + et, blk)])
            b.barrier()

    def alloc_head_tmps(self, es, groupnorm=False):
        b = self.b
        tmp = {}
        lst = [("Pst", [128, 128], F32), ("sT", [128, 2, 2, 128], BF16), ("osq", [128, 512], BF16),
               ("rstd", [128, 512], F32), ("e1", [128, 512], F32), ("t1", [128, 512], F32)]
        if groupnorm:
            lst += [("o32b", [128, 512], BF16), ("oc", [128, 512], F32)]
        for nm, shape, dt in lst:
            tmp[nm] = b.sb(nm, shape, dt, es)
        return tmp

    def even_mixer(self, j, es):
        b = self.b
        nc = self.nc
        tmp = self.alloc_head_tmps(es)
        wtm = b.sb("wtm", [128, 1, KT, 320], BF16, es)
        wfm = b.sb("wfm", [128, 2, KT, 128], BF16, es)
        wa2p = b.sb("wa2p", [32, 128], BF16, es)
        bah = b.sb("bah", [1, 128], BF16, es)
        lrT = b.sb("lrT", [32, T], BF16, es)
        qT = b.sb("qT", [128, T], BF16, es)
        kT = b.sb("kT", [128, T], BF16, es)
        kt_all = b.sb("kt_all", [128, NCH, 128], BF16, es)
        v_all = b.sb("v_all", [128, NCH, 128], BF16, es)
        v3 = b.sb("v3", [128, NCH, 128], BF16, es)
        S_all = b.sb("S_all", [128, NCH * 4, 128], BF16, es)
        Gall = b.sb("Gall", [128, NCH * 4], F32, es)
        LB = b.sb("LB", [128, 2, 256], F32, es)
        OML = b.sb("OML", [128, 2, 256], F32, es)
        es_lg = contextlib.ExitStack()
        lg = b.sb("lg", [128, 2, 2, 256], F32, es_lg)
        b.op("pool", lambda: nc.gpsimd.memset(S_all[:], 0.0), w=[("S", c, lo) for c in range(NCH * 4) for lo in (0, 64)])
        b.op("pool", lambda: nc.gpsimd.memset(v3[:], 0.0), w=[("v3", c) for c in range(NCH)])
        b.dma("sp", lg[:], self.lbl_d.partition_broadcast(128).rearrange("p (d l k) -> p d l k", d=2, l=2), w=["lg"], stream="c0")
        if j == 0:
            b.op("pool", lambda: nc.gpsimd.memset(LB[:], 0.0), w=["LB"])
            b.op("pool", lambda: nc.gpsimd.memset(OML[:], 1.0), w=["OML"])
        else:
            b.op("dve", lambda: nc.vector.tensor_tensor(out=OML[:], in0=lg[:, :, 0, :], in1=lg[:, :, 1, :], op=ALU.max),
                 r=["lg"], w=["OML"])
            for li in range(2):
                b.op("dve", lambda: nc.vector.tensor_tensor(out=lg[:, :, li, :], in0=lg[:, :, li, :], in1=OML[:],
                                                             op=ALU.subtract), r=["lg", "OML"], w=["lg"])
            b.op("act", lambda: nc.scalar.activation(out=lg[:], in_=lg[:], func=AF.Exp), r=["lg"], w=["lg"])
            b.op("dve", lambda: nc.vector.tensor_tensor(out=OML[:], in0=lg[:, :, 0, :], in1=lg[:, :, 1, :], op=ALU.add),
                 r=["lg"], w=["OML"])
            b.op("dve", lambda: nc.vector.reciprocal(out=OML[:], in_=OML[:]), r=["OML"], w=["OML"])
            b.op("dve", lambda: nc.vector.tensor_tensor(out=LB[:], in0=lg[:, :, 1, :], in1=OML[:], op=ALU.mult),
                 r=["lg", "OML"], w=["LB"])
            b.op("dve", lambda: nc.vector.tensor_scalar(out=OML[:], in0=LB[:], scalar1=-1.0, scalar2=1.0,
                                                        op0=ALU.mult, op1=ALU.add), r=["LB"], w=["OML"])
        b.barrier()
        es_lg.close()
        self.check("e_setup")
        ck = {}
        for nm, shape, dt in (("e", [128, 2, 128], F32), ("l", [128, 2, 128], F32), ("E", [128, 2, 128], F32),
                              ("Ei", [128, 2, 128], F32), ("qt", [128, 2, 128], BF16), ("key", [128, 2, 128], F32),
                              ("qs", [128, 2, 64], F32)):
            ck[nm] = b.sb(nm, shape, dt, es)
        b.dma("pool", wfm[:, 0, :, :], self.w_fm_e_d[j, 8, :, :].rearrange("p (k c) -> p k c", k=KT), w=["wfm"],
              stream="wfm")
        for blk in range(NBLK):
            sl = slice(blk * 512, (blk + 1) * 512)
            for k in range(KT):
                b.op("pe", lambda: nc.tensor.matmul(self.pv(0, 0, 512), lhsT=wfm[:, 0, k, :], rhs=self.hT[:, k, sl],
                                                    start=(k == 0), stop=(k == KT - 1)),
                     r=["wfm", ("h", k, blk)], w=self.pk(0, 0, 512))
            b.op("act", lambda: nc.scalar.activation(out=lrT[:, sl], in_=self.pv(0, 0, 512)[0:32, :], func=AF.Identity),
                 r=self.pk(0, 0, 512), w=["lrT"])
        self.check("e_lrT")
        for h in range(8):
            gla = h < 4
            hh = h % 4
            ncols = 256 if gla else 320
            slot = 0
            b.dma("pool", wtm[:, slot, :, :], self.w_tm_e_d[j, h, :, :].rearrange("p (k c) -> p k c", k=KT),
                  w=[("wtm", slot)], stream=f"wtm{slot}")
            b.dma("pool", wfm[:, 1, :, :], self.w_fm_e_d[j, h, :, :].rearrange("p (k c) -> p k c", k=KT), w=["wfm"],
                  stream="wfm")
            if gla:
                b.dma("pool", wa2p[:], self.wa2p_d[j, hh, :, :], w=["wa2p"], stream="wa2")
                b.dma("pool", bah[:], self.bah_d[j, hh, :, :], w=["bah"], stream="wa2")
            for c in range(NCH):
                ch = slice(c * 128, (c + 1) * 128)
                par = c % 2
                P = self.pv(par, 0, 512)
                Pk = self.pk(par, 0, ncols)
                for k in range(KT):
                    b.op("pe", lambda: nc.tensor.matmul(P[:, 0:ncols], lhsT=self.hT[:, k, ch], rhs=wtm[:, slot, k, 0:ncols],
                                                        start=(k == 0), stop=(k == KT - 1)),
                         r=[("wtm", slot), ("h", k, c // 4)], w=Pk)
                self.check("e_inproj")
                zb = 2 + par
                e, l, E, Ei, qt = (ck[n][:, par, :] for n in ("e", "l", "E", "Ei", "qt"))
                if gla:
                    zv = self.pv(zb, 0, 128)
                    zk = self.pk(zb, 0, 128)
                    b.op("pe", lambda: nc.tensor.matmul(zv, lhsT=lrT[0:32, ch], rhs=wa2p[0:32, :], start=True, stop=False),
                         r=["lrT", "wa2p"], w=zk)
                    self.check("e_z1")
                    b.op("pe", lambda: nc.tensor.matmul(zv, lhsT=self.ones_bf[0:1, :], rhs=bah[0:1, :], start=False,
                                                        stop=True), r=["ones", "bah"], w=zk)
                    self.check("e_z2")
                    b.op("act", lambda: nc.scalar.activation(out=e, in_=zv, func=AF.Exp, scale=-1.0), r=zk, w=[("e", par)])
                    self.check("e_z3")
                    b.op("dve", lambda: nc.vector.tensor_scalar_add(out=e, in0=e, scalar1=1.0), r=[("e", par)], w=[("e", par)])
                    b.op("act", lambda: nc.scalar.activation(out=l, in_=e, func=AF.Ln), r=[("e", par)], w=[("l", par)])
                    mi, s0, ns = 0, 0, 1
                    q_src, q_keys = P[:, 0:64], Pk
                    v_src = P[:, 128:256]
                else:
                    key = ck["key"][:, par, :]
                    qs = ck["qs"][:, par, :]
                    zz = P[:, 64:192]
                    b.op("act", lambda: nc.scalar.activation(out=e, in_=zz, func=AF.Exp, scale=-1.0), r=Pk, w=[("e", par)])
                    b.op("dve", lambda: nc.vector.tensor_scalar_add(out=e, in0=e, scalar1=1.0), r=[("e", par)], w=[("e", par)])
                    b.op("dve", lambda: nc.vector.reciprocal(out=e, in_=e), r=[("e", par)], w=[("e", par)])
                    e3 = e.rearrange("p (d k) -> p d k", d=2)
                    lbv = LB[:, :, hh * 64:(hh + 1) * 64]
                    omv = OML[:, :, hh * 64:(hh + 1) * 64]
                    b.op("dve", lambda: nc.vector.tensor_tensor(out=e3, in0=e3, in1=omv, op=ALU.mult),
                         r=[("e", par), "OML"], w=[("e", par)])
                    b.op("dve", lambda: nc.vector.tensor_tensor(out=e3, in0=e3, in1=lbv, op=ALU.add),
                         r=[("e", par), "LB"], w=[("e", par)])
                    b.op("dve", lambda: nc.vector.tensor_scalar(out=key, in0=e, scalar1=-1.0, scalar2=1.0, op0=ALU.mult,
                                                                op1=ALU.add), r=[("e", par)], w=[("key", par)])
                    b.op("dve", lambda: nc.vector.tensor_scalar_max(out=e, in0=e, scalar1=1e-20), r=[("e", par)],
                         w=[("e", par)])
                    b.op("act", lambda: nc.scalar.activation(out=l, in_=e, func=AF.Ln), r=[("e", par)], w=[("l", par)])
                    b.op("act", lambda: nc.scalar.activation(out=qs, in_=P[:, 0:64], func=AF.Exp, scale=-1.0), r=Pk,
                         w=[("qs", par)])
                    b.op("dve", lambda: nc.vector.tensor_scalar_add(out=qs, in0=qs, scalar1=1.0), r=[("qs", par)],
                         w=[("qs", par)])
                    b.op("dve", lambda: nc.vector.reciprocal(out=qs, in_=qs), r=[("qs", par)], w=[("qs", par)])
                    b.op("dve", lambda: nc.vector.tensor_tensor(out=qs, in0=P[:, 0:64], in1=qs, op=ALU.mult),
                         r=Pk + [("qs", par)], w=[("qs", par)])
                    mi, s0, ns = 2, 4, 4
                    q_src, q_keys = qs, [("qs", par)]
                    v_src = P[:, 192:320]
                self.check("e_z")
                for d in range(2):
                    b.op("pe", lambda: nc.tensor.matmul(self.pv(zb, 128 + d * 64, 192 + d * 64), lhsT=self.tri6[:, mi + d, :],
                                                        rhs=l[:, d * 64:(d + 1) * 64], start=True, stop=True),
                         r=["tri6", ("l", par)], w=self.pk(zb, 128, 256))
                b.op("pe", lambda: nc.tensor.matmul(self.pv(zb, 256, 260), lhsT=l, rhs=self.sumcols[:, s0:s0 + 4],
                                                    start=True, stop=True), r=["sumcols", ("l", par)], w=self.pk(zb, 256, 260))
                b.op("act", lambda: nc.scalar.activation(out=Gall[:, c * ns:(c + 1) * ns], in_=self.pv(zb, 256, 256 + ns),
                                                         func=AF.Exp), r=self.pk(zb, 256, 260), w=["G"])
                self.check("e_cum")
                Cv = self.pv(zb, 128, 256)
                Ck = self.pk(zb, 128, 256)
                b.op("act", lambda: nc.scalar.activation(out=E, in_=Cv, func=AF.Exp), r=Ck, w=[("E", par)])
                self.check("e_E1")
                b.op("act", lambda: nc.scalar.activation(out=Ei, in_=Cv, func=AF.Exp, scale=-1.0), r=Ck, w=[("Ei", par)])
                self.check("e_E2")
                for d in range(2):
                    cs = slice(d * 64, (d + 1) * 64)
                    b.op("dve", lambda: nc.vector.tensor_tensor(out=qt[:, cs], in0=q_src, in1=E[:, cs], op=ALU.mult),
                         r=q_keys + [("E", par)], w=[("qt", par)])
                    self.check("e_E3")
                    if gla:
                        b.op("dve", lambda: nc.vector.scalar_tensor_tensor(
                            out=kt_all[:, c, cs], in0=P[:, 64:128], scalar=0.125, in1=Ei[:, cs], op0=ALU.mult, op1=ALU.mult),
                            r=Pk + [("Ei", par)], w=[("kt", c)])
                if not gla:
                    b.op("dve", lambda: nc.vector.tensor_tensor(out=kt_all[:, c, :], in0=ck["key"][:, par, :], in1=Ei,
                                                                 op=ALU.mult), r=[("key", par), ("Ei", par)], w=[("kt", c)])
                self.check("e_E4")
                b.op("act", lambda: nc.scalar.activation(out=v_all[:, c, :], in_=v_src, func=AF.Identity), r=Pk, w=[("v", c)])
                if not gla:
                    b.op("act", lambda: nc.scalar.activation(out=v3[96:128, c, :], in_=v_src[96:128, :], func=AF.Identity),
                         r=Pk, w=[("v3", c)])
                self.check("e_E")
                tq = self.pv(zb, 384, 448).bitcast(BF16)
                tk = self.pv(zb, 448, 512).bitcast(BF16)
                tkey = self.pk(zb, 384, 512)
                b.op("pe", lambda: nc.tensor.transpose(tq, qt, self.ident[:]), r=[("qt", par), "ident"], w=tkey)
                b.op("pe", lambda: nc.tensor.transpose(tk, kt_all[:, c, :], self.ident[:]), r=[("kt", c), "ident"], w=tkey)
                b.op("act", lambda: nc.scalar.activation(out=qT[:, ch], in_=tq, func=AF.Identity), r=tkey, w=[("qT", c)])
                b.op("dve", lambda: nc.vector.tensor_copy(out=kT[:, ch], in_=tk), r=tkey, w=[("kT", c)])
            self.check("e_p1")
            gain_ap = self.hgain[:, j * 8 + h: j * 8 + h + 1]
            self.head_phase23(es, tmp, qT, kT, kt_all, v_all, v3, S_all, Gall, 64, (1 if gla else 4), wfm[:, 1, :, :], gain_ap, h)
            self.check("e_h%d" % h)


def prep_weights(inp):
    w = {}
    g_all = np.concatenate([inp["mix_norm_g"], inp["ffn_norm_g"], inp["final_norm_g"][None]], 0)
    w["gains"] = np.ascontiguousarray(g_all.reshape(9, KT, 128).transpose(2, 0, 1).reshape(128, 9 * KT))
    wu = inp["ffn_w_up"]
    L = wu.shape[0]
    wu5 = wu.reshape(L, KT, 128, 2, FT, 128)
    w["w_up_t"] = np.ascontiguousarray(wu5.transpose(0, 4, 3, 2, 1, 5).reshape(L, 2 * FT, 128, KT * 128))
    wd = inp["ffn_w_down"]
    wd6 = wd.reshape(L, 2, 11, 128, KT, 128)
    w["w_dn_t"] = np.ascontiguousarray(wd6.transpose(0, 1, 4, 3, 2, 5).reshape(L, 2, KT, 128, 11 * 128))
    cw = inp["ffn_conv_w"]
    cb = inp["ffn_conv_b"]
    c4 = np.concatenate([cw, cb[:, None, :]], 1)
    c4 = c4.reshape(L, 4, 2, FT, 128)
    w["conv_p"] = np.ascontiguousarray(c4.transpose(4, 0, 3, 2, 1).reshape(128, L * 2 * FT * 4))

    wo = np.stack([inp["w_out_even"][0], inp["w_out_odd"][0], inp["w_out_even"][1], inp["w_out_odd"][1]], 0)
    w["w_out_t"] = np.ascontiguousarray(wo.reshape(L, KT, 128, KT, 128).transpose(0, 3, 2, 1, 4).reshape(L, KT, 128, KT * 128))
    wie = inp["w_in_even"]
    o = np.cumsum([0, 256, 256, 512, 512, 16, 16, 256, 256, 256, 512, 512])
    gq, gk, gv, gr, glf, glb, hq, hzf, hzb, hi, hg = (wie[:, :, o[i]:o[i + 1]] for i in range(11))
    tm = np.zeros((2, 8, 1024, 320), np.float32)
    for h in range(4):
        tm[:, h, :, 0:64] = gq[:, :, h * 64:(h + 1) * 64]
        tm[:, h, :, 64:128] = gk[:, :, h * 64:(h + 1) * 64]
        tm[:, h, :, 128:256] = gv[:, :, h * 128:(h + 1) * 128]
        tm[:, 4 + h, :, 0:64] = hq[:, :, h * 64:(h + 1) * 64]
        tm[:, 4 + h, :, 64:128] = hzf[:, :, h * 64:(h + 1) * 64]
        tm[:, 4 + h, :, 128:192] = hzb[:, :, h * 64:(h + 1) * 64]
        tm[:, 4 + h, :, 192:320] = hi[:, :, h * 128:(h + 1) * 128]
    w["w_tm_e"] = np.ascontiguousarray(tm.reshape(2, 8, KT, 128, 320).transpose(0, 1, 3, 2, 4).reshape(2, 8, 128, KT * 320))
    fm = np.zeros((2, 9, 1024, 128), np.float32)
    for h in range(4):
        fm[:, h] = gr[:, :, h * 128:(h + 1) * 128]
        fm[:, 4 + h] = hg[:, :, h * 128:(h + 1) * 128]
    fm[:, 8, :, 0:16] = glf
    fm[:, 8, :, 16:32] = glb
    w["w_fm_e"] = np.ascontiguousarray(fm.reshape(2, 9, KT, 128, 128).transpose(0, 1, 3, 2, 4).reshape(2, 9, 128, KT * 128))
    wa2 = inp["gla_wa2"]
    ba = inp["gla_ba"]
    wa2p = np.zeros((2, 4, 32, 128), np.float32)
    bah = np.zeros((2, 4, 1, 128), np.float32)
    for h in range(4):
        wa2p[:, h, 0:16, 0:64] = wa2[:, 0, :, h * 64:(h + 1) * 64]
        wa2p[:, h, 16:32, 64:128] = wa2[:, 1, :, h * 64:(h + 1) * 64]
        bah[:, h, 0, 0:64] = ba[:, 0, h * 64:(h + 1) * 64]
        bah[:, h, 0, 64:128] = ba[:, 1, h * 64:(h + 1) * 64]
    w["wa2p"] = wa2p
    w["bah"] = bah
    w["lbl"] = np.ascontiguousarray(inp["hgrn_lb_logits"].reshape(-1))
    hg_ = np.zeros((128, 16), np.float32)
    for j in range(2):
        for h in range(4):
            hg_[:, j * 8 + h] = inp["gla_norm_g"][j, h * 128:(h + 1) * 128]
            hg_[:, j * 8 + 4 + h] = inp["hgrn_norm_g"][j, h * 128:(h + 1) * 128]
    w["hgain"] = hg_
    ii = np.arange(128)
    triL = (ii[:, None] <= ii[None, :]).astype(np.float32)
    triU = (ii[:, None] >= ii[None, :]).astype(np.float32)
    blk32 = (ii[:, None] // 32 == ii[None, :] // 32).astype(np.float32)
    w["tri6"] = np.ascontiguousarray(np.stack([triL * (-1.0 / 16), triU * (-1.0 / 16), triL * blk32, triU * blk32,
                                               triL, triU], 1).reshape(128, 768))
    sc = np.zeros((128, 8), np.float32)
    sc[:, 0:4] = -1.0 / 16
    for q_ in range(4):
        sc[32 * q_:32 * (q_ + 1), 4 + q_] = 1.0
    w["sumcols"] = sc
    w["ident"] = np.eye(128, dtype=np.float32)

    wio = inp["w_in_odd"]
    oo = np.cumsum([0, 512, 512, 768, 768, 256])
    rq, rk, rv, rg, su = (wio[:, :, oo[i]:oo[i + 1]] for i in range(5))
    tmo = np.zeros((2, 4, 1024, 448), np.float32)
    for h in range(4):
        tmo[:, h, :, 0:128] = rq[:, :, h * 128:(h + 1) * 128]
        tmo[:, h, :, 128:256] = rk[:, :, h * 128:(h + 1) * 128]
        tmo[:, h, :, 256:448] = rv[:, :, h * 192:(h + 1) * 192]
    w["w_tm_o"] = np.ascontiguousarray(tmo.reshape(2, 4, KT, 128, 448).transpose(0, 1, 3, 2, 4).reshape(2, 4, 128, KT * 448))
    fmo = np.zeros((2, 8, 1024, 128), np.float32)
    hgo = np.zeros((128, 16), np.float32)
    for h in range(4):
        for pi, (f0, nf, etile, p0) in enumerate(Prog.RET_PIECES[h]):
            fmo[:, 2 * h + pi, :, 0:nf] = rg[:, :, h * 192 + f0: h * 192 + f0 + nf]
            for j in range(2):
                hgo[p0:p0 + nf, j * 8 + 2 * h + pi] = inp["ret_norm_g"][j, h * 192 + f0: h * 192 + f0 + nf]
    w["w_fm_o"] = np.ascontiguousarray(fmo.reshape(2, 8, KT, 128, 128).transpose(0, 1, 3, 2, 4).reshape(2, 8, 128, KT * 128))
    w["hgain_o"] = hgo
    w["w_u_o"] = np.ascontiguousarray(su.reshape(2, KT, 128, 256).transpose(0, 2, 1, 3).reshape(2, 128, KT * 256))
    half = 64
    inv = (10000.0 ** (-np.arange(half, dtype=np.float32) / half)).astype(np.float32)
    ang = (np.arange(T, dtype=np.float32)[:, None] * inv[None, :]).astype(np.float32)
    cs = np.stack([np.cos(ang), np.sin(ang)], 0).reshape(2, NCH, 128, half)
    w["rope"] = np.ascontiguousarray(cs.transpose(2, 0, 1, 3).reshape(128, 2 * NCH * half)).astype(np.float32)
    ti = np.arange(128, dtype=np.float64)
    rdec = np.zeros((128, 16), np.float64)
    dm = np.zeros((128, 4, 128), np.float64)
    for h in range(4):
        gf = 1.0 - 2.0 ** (-5.0 - h)
        gb = 1.0 - 2.0 ** (-5.5 - h)
        rdec[:, 4 * h + 0] = gf ** (ti + 1)
        rdec[:, 4 * h + 1] = gb ** (128 - ti)
        rdec[:, 4 * h + 2] = gf ** (127 - ti) * 128.0 ** -0.5
        rdec[:, 4 * h + 3] = gb ** ti * 128.0 ** -0.5
        dji = ti[None, :] - ti[:, None]
        dm[:, h, :] = np.where(dji >= 0, gf ** np.abs(dji), 0.0) + np.where(dji <= 0, gb ** np.abs(dji), 0.0)
    w["rdec"] = rdec.astype(np.float32)
    w["dmask"] = np.ascontiguousarray(dm.reshape(128, 512)).astype(np.float32)

    def qp(a):
        sh = a.shape
        a = a.reshape(sh[:-3] + (8, 2, 64, sh[-1]))
        nd = a.ndim
        perm = (nd - 3, nd - 2) + tuple(range(nd - 4)) + (nd - 4, nd - 1)
        a = a.transpose(perm)
        return a.reshape((128,) + a.shape[2:])
    prm = np.zeros((2, 128, 3, 2, 8), np.float32)
    bc = np.zeros((2, 128, 4, 8, 16), np.float32)
    for j in range(2):
        prm[j, :, 0] = qp(inp["s5_lam_re"][j][..., None])[..., 0]
        prm[j, :, 1] = qp(inp["s5_lam_im"][j][..., None])[..., 0]
        ldt = np.broadcast_to(inp["s5_log_dt"][j][:, :, None, None], (2, 16, 64, 1))
        prm[j, :, 2] = qp(np.ascontiguousarray(ldt))[..., 0]
        bc[j, :, 0] = qp(inp["s5_b_re"][j])
        bc[j, :, 1] = qp(inp["s5_b_im"][j])
        bc[j, :, 2] = qp(np.ascontiguousarray(inp["s5_c_re"][j].transpose(0, 2, 1)))
        bc[j, :, 3] = qp(np.ascontiguousarray(inp["s5_c_im"][j].transpose(0, 2, 1)))
    w["s5_prm"] = prm.reshape(2, 128, 48)
    w["s5_bc"] = bc.reshape(2, 128, 512)
    dsk = inp["s5_d"].reshape(2, 16, 16)
    w["s5_dsk"] = np.ascontiguousarray(np.broadcast_to(dsk.transpose(0, 2, 1)[:, None, :, :], (2, 8, 16, 16)).reshape(2, 128, 16))
    w["s5_glb"] = np.ascontiguousarray(inp["s5_glu_b"].reshape(2, 2, 128).transpose(0, 2, 1))
    w["s5_gw"] = np.ascontiguousarray(inp["s5_glu_w"].reshape(2, 2, 128, 256).transpose(0, 2, 1, 3).reshape(2, 128, 512))
    jj_ = np.arange(128) // 16
    bm = np.stack([(jj_[:, None] <= jj_[None, :]), (jj_[:, None] >= jj_[None, :])], 1).astype(np.float32)
    w["s5_bm"] = np.ascontiguousarray(bm.reshape(128, 256))
    return w


_CFG = {}


def kernel(**inputs):
    inp = {k: np.asarray(v) for k, v in inputs.items()}
    cfg = dict(_CFG)
    x = inp["x"]
    w = prep_weights(inp)
    import time as _t
    _t0 = _t.time()
    prog = Prog(cfg)
    nc = prog.build()
    print("[kernel] build %.1fs, instr counts %s" % (_t.time() - _t0, dict(prog.b.cnt)), flush=True)
    in_maps = []
    ncores = cfg.get("ncores", NCORES)
    for c in range(ncores):
        xs = x[c * NSEQ:(c + 1) * NSEQ]
        xl = np.ascontiguousarray(xs.reshape(NSEQ, T, KT, 128).transpose(0, 3, 2, 1))
        m = {"x_in": xl}
        m.update(w)
        in_maps.append(m)
    _t0 = _t.time()
    res = run_bass_kernel_spmd(nc, in_maps, core_ids=list(range(ncores)))
    print("[kernel] run %.1fs" % (_t.time() - _t0), flush=True)
    outs = []
    for c in range(ncores):
        y = res.results[c]["y_out"]
        outs.append(np.ascontiguousarray(y.transpose(0, 3, 2, 1)).reshape(NSEQ, T, D))
    return np.concatenate(outs, 0).astype(np.float32)
```

```python
import contextlib
import math
import numpy as np
import concourse.bass as bass
import concourse.mybir as mybir
from concourse.bass_utils import run_bass_kernel_spmd

F32 = mybir.dt.float32
BF16 = mybir.dt.bfloat16
AF = mybir.ActivationFunctionType
ALU = mybir.AluOpType

D = 1024
T = 2048
KT = D // 128
NBLK = T // 512
NCH = T // 128
DEPTH = 4
FF = 2816
FT = FF // 128
NSEQ = 2
NCORES = 8
EPS = 1e-6
SAME_ENG_SYNC = True


class Builder:
    def __init__(self):
        self.nc = bass.Bass("TRN2", target_bir_lowering=False, dynamic_dma_scratch_size=4096)
        nc = self.nc
        self.es = contextlib.ExitStack()
        self.eng = dict(pe=nc.tensor, act=nc.scalar, dve=nc.vector, pool=nc.gpsimd, sp=nc.sync)
        self.sem = {e: self.es.enter_context(nc.semaphore("s_" + e)) for e in self.eng}
        self.cnt = {e: 0 for e in self.eng}
        self.seen = {e: {} for e in self.eng}
        self.parts = {}
        self.streams = {}
        self.uid = 0
        self.muted = False

    def sb(self, name, shape, dt, es=None):
        self.uid += 1
        return (es or self.es).enter_context(self.nc.sbuf_tensor(f"{name}_{self.uid}", list(shape), dt))

    def dram_in(self, name, shape, dt=F32):
        return self.nc.dram_tensor(name, list(shape), dt, kind="ExternalInput").ap()

    def dram_out(self, name, shape, dt=F32):
        return self.nc.dram_tensor(name, list(shape), dt, kind="ExternalOutput").ap()

    def _part(self, k):
        p = self.parts.get(k)
        if p is None:
            p = [[], []]
            self.parts[k] = p
        return p

    def _wait(self, e, tickets):
        need = {}
        for (key, h, v, src) in tickets:
            if src == e and (e == "pe" or not SAME_ENG_SYNC):
                continue
            if src is None:
                v = 16 * self.streams[key[2:]][1]
            if self.seen[e].get(key, 0) >= v:
                continue
            if key not in need or need[key][1] < v:
                need[key] = (h, v)
        for key, (h, v) in need.items():
            self.eng[e].wait_ge(h, v)
            self.seen[e][key] = v

    def _deps(self, r, w):
        deps = []
        for k in r:
            deps += self._part(k)[0]
        for k in w:
            p = self._part(k)
            deps += p[0] + p[1]
        return deps

    def _record(self, t, r, w):
        for k in r:
            p = self._part(k)
            p[1] = [x for x in p[1] if x[0] != t[0]] + [t]
        for k in w:
            p = self._part(k)
            p[0] = [t]
            p[1] = []

    def op(self, e, fn, r=(), w=()):
        if self.muted:
            return None
        pr = [k for k in r if isinstance(k, tuple) and k[0] in ("ps", "bank")]
        if pr:
            r = [k for k in r if k not in pr]
            w = list(w) + [k for k in pr if k not in w]
        self._wait(e, self._deps(r, w))
        ins = fn()
        self.cnt[e] += 1
        ins.then_inc(self.sem[e], 1)
        self._record(("e_" + e, self.sem[e], self.cnt[e], e), r, w)
        return ins

    def dma(self, q, out, in_, r=(), w=(), stream="d0"):
        if self.muted:
            return
        st = self.streams.get(stream)
        if st is None:
            st = [self.es.enter_context(self.nc.semaphore("d_" + stream)), 0]
            self.streams[stream] = st
        self._wait(q, self._deps(r, w))
        ins = self.eng[q].dma_start(out=out, in_=in_)
        st[1] += 1
        ins.then_inc(st[0], 16)
        self._record(("d_" + stream, st[0], 16 * st[1], None), r, w)

    def barrier(self):
        ts = [("e_" + e, self.sem[e], self.cnt[e], e) for e in self.eng if self.cnt[e] > 0]
        ts += [("d_" + s, st[0], 16 * st[1], None) for s, st in self.streams.items() if st[1] > 0]
        for e in self.eng:
            self._wait(e, [t for t in ts if t[3] != e])
        self.parts = {}

    def finish(self):
        self.barrier()
        self.es.close()


class StopBuild(Exception):
    pass


class Prog:
    def check(self, tag):
        if self.cfg.get("stop") == tag:
            self.b.muted = True

    def __init__(self, cfg):
        self.cfg = cfg
        self.b = Builder()
        b = self.b
        nc = b.nc
        self.nc = nc
        self.x_in = b.dram_in("x_in", [NSEQ, 128, KT, T])
        self.y_out = b.dram_out("y_out", [NSEQ, 128, KT, T])
        self.gains_d = b.dram_in("gains", [128, 9 * KT])
        self.w_up_d = b.dram_in("w_up_t", [DEPTH, 2 * FT, 128, KT * 128])
        self.w_dn_d = b.dram_in("w_dn_t", [DEPTH, 2, KT, 128, 11 * 128])
        self.conv_d = b.dram_in("conv_p", [128, DEPTH * 2 * FT * 4])
        self.w_out_d = b.dram_in("w_out_t", [DEPTH, KT, 128, KT * 128])
        self.w_tm_e_d = b.dram_in("w_tm_e", [2, 8, 128, KT * 320])
        self.w_fm_e_d = b.dram_in("w_fm_e", [2, 9, 128, KT * 128])
        self.wa2p_d = b.dram_in("wa2p", [2, 4, 32, 128])
        self.bah_d = b.dram_in("bah", [2, 4, 1, 128])
        self.lbl_d = b.dram_in("lbl", [2 * 2 * 256])
        self.hgain_d = b.dram_in("hgain", [128, 16])
        self.tri6_d = b.dram_in("tri6", [128, 6 * 128])
        self.w_tm_o_d = b.dram_in("w_tm_o", [2, 4, 128, KT * 448])
        self.w_fm_o_d = b.dram_in("w_fm_o", [2, 8, 128, KT * 128])
        self.w_u_o_d = b.dram_in("w_u_o", [2, 128, KT * 256])
        self.hgain_o_d = b.dram_in("hgain_o", [128, 16])
        self.rope_d = b.dram_in("rope", [128, 2 * NCH * 64])
        self.rdec_d = b.dram_in("rdec", [128, 16])
        self.dmask_d = b.dram_in("dmask", [128, 4 * 128])
        self.s5_gw_d = b.dram_in("s5_gw", [2, 128, 2 * 256])
        self.s5_prm_d = b.dram_in("s5_prm", [2, 128, 3 * 2 * 8])
        self.s5_bc_d = b.dram_in("s5_bc", [2, 128, 4 * 8 * 16])
        self.s5_dsk_d = b.dram_in("s5_dsk", [2, 128, 16])
        self.s5_glb_d = b.dram_in("s5_glb", [2, 128, 2])
        self.s5_bm_d = b.dram_in("s5_bm", [128, 2 * 128])
        self.sumcols_d = b.dram_in("sumcols", [128, 8])
        self.ident_d = b.dram_in("ident", [128, 128])
        self.xT = b.sb("xT", [128, KT, T], F32)
        self.hT = b.sb("hT", [128, KT, T], BF16)
        self.gains = b.sb("gains", [128, 9 * KT], F32)
        self.ones_bf = b.sb("ones", [128, 128], BF16)
        self.hgain = b.sb("hgain", [128, 16], F32)
        self.hgain_o = b.sb("hgain_o", [128, 16], F32)
        self.tri6 = b.sb("tri6", [128, 6, 128], F32)
        self.sumcols = b.sb("sumcols", [128, 8], F32)
        self.ident = b.sb("ident", [128, 128], BF16)
        self.one_t = b.sb("one", [128, 1], F32)
        self.ps = b.es.enter_context(nc.psum_tensor("ps_all", [128, 4096], F32))
        self.bankrr = 0

    def bank(self, i):
        return self.ps[:, i * 512:(i + 1) * 512]


    def pk(self, bank, c0, c1):
        return [("ps", bank)]

    def pv(self, bank, c0, c1):
        return self.ps[:, bank * 512 + c0: bank * 512 + c1]

    def next_bank(self):
        i = self.bankrr
        self.bankrr = (self.bankrr + 1) % 8
        return i

    def setup(self):
        b = self.b
        nc = self.nc
        b.dma("sp", self.gains[:], self.gains_d[:, :], w=["gains"], stream="c0")
        b.op("pool", lambda: nc.gpsimd.memset(self.ones_bf[:], 1.0), w=["ones"])
        b.op("pool", lambda: nc.gpsimd.memset(self.one_t[:], 1.0), w=["one"])
        b.dma("sp", self.hgain[:], self.hgain_d[:, :], w=["hgain"], stream="c0")
        b.dma("sp", self.hgain_o[:], self.hgain_o_d[:, :], w=["hgain_o"], stream="c0")
        b.dma("sp", self.tri6[:], self.tri6_d[:, :].rearrange("p (a c) -> p a c", a=6), w=["tri6"], stream="c0")
        b.dma("sp", self.sumcols[:], self.sumcols_d[:, :], w=["sumcols"], stream="c0")
        b.dma("pool", self.ident[:], self.ident_d[:, :], w=["ident"], stream="c1")

    def load_x(self, s):
        b = self.b
        for k in range(KT):
            b.dma("sp", self.xT[:, k, :], self.x_in[s, :, k, :], w=[("x", k, blk) for blk in range(NBLK)],
                  stream="xin")

    def store_x(self, s):
        b = self.b
        for k in range(KT):
            b.dma("sp", self.y_out[s, :, k, :], self.xT[:, k, :], r=[("x", k, blk) for blk in range(NBLK)],
                  stream="xout")

    def rmsnorm(self, gidx, out_f32_inplace=False):
        b = self.b
        nc = self.nc
        with contextlib.ExitStack() as es:
            sq = b.sb("sq", [128, 2, KT, 512], BF16, es)
            rs = b.sb("rstd", [128, 2, 512], F32, es)
            for blk in range(NBLK):
                par = blk % 2
                sl = slice(blk * 512, (blk + 1) * 512)
                xk = [("x", k, blk) for k in range(KT)]
                b.op("act", lambda: nc.scalar.activation(out=sq[:, par, :, :], in_=self.xT[:, :, sl], func=AF.Square),
                     r=xk, w=[("sq", par)])
                bi = self.next_bank()
                for k in range(KT):
                    b.op("pe", lambda: nc.tensor.matmul(self.bank(bi), lhsT=self.ones_bf[:], rhs=sq[:, par, k, :],
                                                        start=(k == 0), stop=(k == KT - 1)),
                         r=[("sq", par), "ones"], w=[("bank", bi)])
                b.op("act", lambda: nc.scalar.activation(out=rs[:, par, :], in_=self.bank(bi), func=AF.Sqrt,
                                                         scale=1.0 / D, bias=self.eps_t[:, 0:1]),
                     r=[("bank", bi), "eps"], w=[("rs", par)])
                b.op("dve", lambda: nc.vector.reciprocal(out=rs[:, par, :], in_=rs[:, par, :]),
                     r=[("rs", par)], w=[("rs", par)])
                for k in range(KT):
                    g = self.gains[:, gidx * KT + k: gidx * KT + k + 1]
                    if out_f32_inplace:
                        b.op("dve", lambda: nc.vector.scalar_tensor_tensor(
                            out=self.xT[:, k, sl], in0=self.xT[:, k, sl], scalar=g, in1=rs[:, par, :],
                            op0=ALU.mult, op1=ALU.mult), r=[("x", k, blk), ("rs", par), "gains"], w=[("x", k, blk)])
                    else:
                        b.op("dve", lambda: nc.vector.scalar_tensor_tensor(
                            out=self.hT[:, k, sl], in0=self.xT[:, k, sl], scalar=g, in1=rs[:, par, :],
                            op0=ALU.mult, op1=ALU.mult), r=[("x", k, blk), ("rs", par), "gains"], w=[("h", k, blk)])
            b.barrier()

    def ffn(self, layer):
        b = self.b
        nc = self.nc
        self.rmsnorm(DEPTH + layer)
        hall = [("h", k, blk) for k in range(KT) for blk in range(NBLK)]
        with contextlib.ExitStack() as es:
            g = b.sb("ffg", [128, 11, T], BF16, es)
            wup = b.sb("wup", [128, 3, KT, 128], BF16, es)
            wdn = b.sb("wdn", [128, 2, 11, 128], BF16, es)
            cbuf = b.sb("cbuf", [128, 2, T], F32, es)
            sbuf = b.sb("sbuf", [128, T], BF16, es)
            cp = b.sb("convp", [128, 2 * FT * 4], F32, es)
            b.dma("sp", cp[:], self.conv_d[:, layer * 2 * FT * 4:(layer + 1) * 2 * FT * 4], w=["convp"], stream="c0")
            ucount = 0
            dcount = 0
            for half in range(2):
                for mm in range(11):
                    m = half * 11 + mm
                    for kind in range(2):
                        unit = 2 * m + kind
                        slot = ucount % 3
                        par = ucount % 2
                        ucount += 1
                        b.dma("pool", wup[:, slot, :, :], self.w_up_d[layer, unit, :, :].rearrange("p (k c) -> p k c", k=KT),
                              w=[("wup", slot)], stream=f"wup{slot}")
                        banks = [4 * par + i for i in range(4)]
                        for blk in range(NBLK):
                            for k in range(KT):
                                b.op("pe", lambda: nc.tensor.matmul(
                                    self.bank(banks[blk]), lhsT=wup[:, slot, k, :],
                                    rhs=self.hT[:, k, blk * 512:(blk + 1) * 512],
                                    start=(k == 0), stop=(k == KT - 1)),
                                    r=[("wup", slot), ("h", k, blk)], w=[("bank", banks[blk])])
                        u = self.ps[:, 2048 * par: 2048 * (par + 1)]
                        base = unit * 4
                        w0, w1, w2, cb = (cp[:, base + i: base + i + 1] for i in range(4))
                        c = cbuf[:, par, :]
                        H = T // 2
                        for hf in range(2):
                            t0, t1 = hf * H, (hf + 1) * H
                            bk = [("bank", banks[2 * hf]), ("bank", banks[2 * hf + 1])]
                            ck_ = ("c", par, hf)
                            b.op("act", lambda: nc.scalar.activation(out=c[:, t0:t1], in_=u[:, t0:t1], func=AF.Identity,
                                                                     scale=w1, bias=cb), r=bk + ["convp"], w=[ck_])
                            a0 = max(t0, 1)
                            bkl = bk + ([("bank", banks[1])] if hf == 1 else [])
                            b.op("dve", lambda: nc.vector.scalar_tensor_tensor(
                                out=c[:, a0:t1], in0=u[:, a0 - 1:t1 - 1], scalar=w0, in1=c[:, a0:t1], op0=ALU.mult,
                                op1=ALU.add), r=bkl + ["convp", ck_], w=[ck_])
                            a1 = min(t1, T - 1)
                            bkr = bk + ([("bank", banks[2])] if hf == 0 else [])
                            b.op("dve", lambda: nc.vector.scalar_tensor_tensor(
                                out=c[:, t0:a1], in0=u[:, t0 + 1:a1 + 1], scalar=w2, in1=c[:, t0:a1], op0=ALU.mult,
                                op1=ALU.add), r=bkr + ["convp", ck_], w=[ck_])
                            if kind == 0:
                                b.op("act", lambda: nc.scalar.activation(out=sbuf[:, t0:t1], in_=c[:, t0:t1], func=AF.Silu),
                                     r=[ck_], w=[("s", hf)])
                            else:
                                b.op("pool", lambda: nc.gpsimd.tensor_tensor(out=g[:, mm, t0:t1], in0=sbuf[:, t0:t1],
                                                                               in1=c[:, t0:t1], op=ALU.mult),
                                     r=[ck_, ("s", hf)], w=[("g", mm, hf)])
                for dt in range(KT):
                    slot = dcount % 2
                    dcount += 1
                    b.dma("pool", wdn[:, slot, :, :], self.w_dn_d[layer, half, dt, :, :].rearrange("p (k c) -> p k c", k=11),
                          w=[("wdn", slot)], stream=f"wdn{slot}")
                    for blk in range(NBLK):
                        bi = self.next_bank()
                        for kk in range(11):
                            b.op("pe", lambda: nc.tensor.matmul(
                                self.bank(bi), lhsT=wdn[:, slot, kk, :], rhs=g[:, kk, blk * 512:(blk + 1) * 512],
                                start=(kk == 0), stop=(kk == 10)),
                                r=[("wdn", slot), ("g", kk, blk // 2)], w=[("bank", bi)])
                        sl = slice(blk * 512, (blk + 1) * 512)
                        b.op("dve", lambda: nc.vector.tensor_tensor(out=self.xT[:, dt, sl], in0=self.bank(bi),
                                                                     in1=self.xT[:, dt, sl], op=ALU.add),
                             r=[("bank", bi), ("x", dt, blk)], w=[("x", dt, blk)])
            b.barrier()

    def build(self):
        b = self.b
        nc = self.nc
        cfg = self.cfg
        self.eps_t = b.sb("eps", [128, 1], F32)
        b.op("pool", lambda: nc.gpsimd.memset(self.eps_t[:], EPS), w=["eps"])
        self.setup()
        for s in range(cfg.get("nseq", NSEQ)):
            self.load_x(s)
            try:
                for layer in cfg.get("layers", range(DEPTH)):
                    if "mix" in cfg.get("phases", ("mix", "ffn")):
                        self.mixer(layer)
                    if "ffn" in cfg.get("phases", ("mix", "ffn")):
                        self.ffn(layer)
            except StopBuild:
                pass
            b.muted = False
            b.barrier()
            if cfg.get("final", True):
                self.rmsnorm(2 * DEPTH, out_f32_inplace=True)
            self.store_x(s)
            b.barrier()
        b.finish()
        return nc


    def mixer(self, layer):
        self.rmsnorm(layer)
        with contextlib.ExitStack() as es:
            self.yT = self.b.sb("yT", [128, KT, T], BF16, es)
            if layer % 2 == 0:
                self.even_mixer(layer // 2, es)
            else:
                self.odd_mixer(layer // 2, es)
            if self.cfg.get("dbg_y"):
                for k in self.cfg.get("dbg_tiles", range(KT)):
                    self.b.op("act", lambda: self.nc.scalar.activation(out=self.xT[:, k, :], in_=self.yT[:, k, :], func=AF.Identity),
                              r=[("y", k, blk) for blk in range(NBLK)], w=[("x", k, blk) for blk in range(NBLK)])
            else:
                self.out_proj(layer, es)
            self.b.barrier()

    def out_proj(self, layer, es):
        b = self.b
        nc = self.nc
        wo = b.sb("wo", [128, 2, KT, 128], BF16, es)
        for dt in range(KT):
            slot = dt % 2
            b.dma("pool", wo[:, slot, :, :], self.w_out_d[layer, dt, :, :].rearrange("p (k c) -> p k c", k=KT),
                  w=[("wo", slot)], stream=f"wo{slot}")
            for blk in range(NBLK):
                bi = dt % 2
                for k in range(KT):
                    b.op("pe", lambda: nc.tensor.matmul(self.pv(bi, 0, 512), lhsT=wo[:, slot, k, :],
                                                        rhs=self.yT[:, k, blk * 512:(blk + 1) * 512],
                                                        start=(k == 0), stop=(k == KT - 1)),
                         r=[("wo", slot), ("y", k, blk)], w=self.pk(bi, 0, 512))
                sl = slice(blk * 512, (blk + 1) * 512)
                b.op("dve", lambda: nc.vector.tensor_tensor(out=self.xT[:, dt, sl], in0=self.pv(bi, 0, 512),
                                                             in1=self.xT[:, dt, sl], op=ALU.add),
                     r=self.pk(bi, 0, 512) + [("x", dt, blk)], w=[("x", dt, blk)])

    def head_phase23(self, es, tmp, qT, kT, kt_all, v_all, v3, S_all, Gall, dk, nsub, gate_w, gain_ap, etile,
                     groupnorm=False):
        b = self.b
        nc = self.nc
        Pst = tmp["Pst"]
        sub = 128 // nsub
        nsc = NCH * nsub

        def kv_ops(g):
            c, s_ = divmod(g, nsub)
            if nsub == 1:
                return c, slice(0, 128), v_all, ("v", c)
            if s_ < 3:
                return c, slice(32 * s_, 32 * s_ + 32), v_all, ("v", c)
            return c, slice(64, 128), v3, ("v3", c)

        for step in range(nsc - 1):
            for d, lo in ((0, 0), (1, dk)):
                g = step if d == 0 else nsc - 1 - step
                nxt = 1 if d == 0 else -1
                c, rows, vv, vkey = kv_ops(g)
                bank = (4 if d == 0 else 6) + step % 2
                kv = self.pv(bank, 0, 128)[lo:lo + dk, :]
                kvk = self.pk(bank, 0, 128)
                prow = slice(lo, lo + dk)
                b.op("pe", lambda: nc.tensor.matmul(kv, lhsT=kt_all[rows, c, lo:lo + dk], rhs=vv[rows, c, :],
                                                    start=True, stop=True), r=[("kt", c), vkey], w=kvk)
                pp = step % 2
                pkey = ("Pst", lo, pp)
                pold = ("Pst", lo, 1 - pp)
                if step == 0:
                    b.op("dve", lambda: nc.vector.tensor_copy(out=Pst[prow, pp, :], in_=kv), r=kvk, w=[pkey])
                else:
                    gprev = Gall[prow, g - nxt: g - nxt + 1]
                    b.op("dve", lambda: nc.vector.scalar_tensor_tensor(
                        out=Pst[prow, pp, :], in0=Pst[prow, 1 - pp, :], scalar=gprev, in1=kv, op0=ALU.mult, op1=ALU.add),
                        r=kvk + [pold, "G"], w=[pkey])
                b.op("act", lambda: nc.scalar.activation(out=S_all[prow, g + nxt, :], in_=Pst[prow, pp, :], func=AF.Identity,
                                                         scale=Gall[prow, g: g + 1]),
                     r=[pkey, "G"], w=[("S", g + nxt, lo)])
        self.check("e_p2")
        mF = 4 if nsub == 1 else 2
        maskF = self.tri6[:, mF, :]
        maskB = self.tri6[:, mF + 1, :]
        K2 = 2 * dk
        for blk in range(NBLK):
            sl = slice(blk * 512, (blk + 1) * 512)
            for k in range(KT):
                b.op("pe", lambda: nc.tensor.matmul(self.pv(2, 0, 512), lhsT=gate_w[:, k, :], rhs=self.hT[:, k, sl],
                                                    start=(k == 0), stop=(k == KT - 1)),
                     r=["wfm", ("h", k, blk)], w=self.pk(2, 0, 512))
            for cc in range(4):
                c = blk * 4 + cc
                ch = slice(c * 128, (c + 1) * 128)
                par = c % 2
                sbs = (0, 4) if par == 0 else (3, 6)
                for d, lo in ((0, 0), (1, dk)):
                    b.op("pe", lambda: nc.tensor.matmul(self.pv(sbs[d], 0, 128),
                                                        lhsT=kT[lo:lo + dk, ch], rhs=qT[lo:lo + dk, ch],
                                                        start=True, stop=True),
                         r=[("kT", c), ("qT", c)], w=self.pk(sbs[d], 0, 128))
                for d, lo in ((0, 0), (1, dk)):
                    b.op("dve", lambda: nc.vector.tensor_tensor(
                        out=tmp["sT"][:, par, d, :], in0=self.pv(sbs[d], 0, 128),
                        in1=(maskF if d == 0 else maskB), op=ALU.mult),
                        r=self.pk(sbs[d], 0, 128) + ["tri6"], w=[("sT", par, d)])
                ov = self.pv(1, cc * 128, cc * 128 + 128)
                ok = self.pk(1, cc * 128, cc * 128 + 128)
                b.op("pe", lambda: nc.tensor.matmul(ov, lhsT=v_all[:, c, :], rhs=tmp["sT"][:, par, 0, :],
                                                    start=True, stop=False), r=[("v", c), ("sT", par, 0)], w=ok)
                b.op("pe", lambda: nc.tensor.matmul(ov, lhsT=v_all[:, c, :], rhs=tmp["sT"][:, par, 1, :],
                                                    start=False, stop=False), r=[("v", c), ("sT", par, 1)], w=ok)
                for s_ in range(nsub):
                    g = c * nsub + s_
                    b.op("pe", lambda: nc.tensor.matmul(ov[:, s_ * sub:(s_ + 1) * sub], lhsT=S_all[0:K2, g, :],
                                                        rhs=qT[0:K2, c * 128 + s_ * sub: c * 128 + (s_ + 1) * sub],
                                                        start=False, stop=(s_ == nsub - 1)),
                         r=[("S", g, 0), ("S", g, dk), ("qT", c)], w=ok)
            self.head_norm_gate(tmp, blk, gain_ap, etile, groupnorm, 128)

    def head_norm_gate(self, tmp, blk, gain_ap, etile, groupnorm, dv):
        b = self.b
        nc = self.nc
        sl = slice(blk * 512, (blk + 1) * 512)
        o = self.pv(1, 0, 512)
        ok = self.pk(1, 0, 512)
        osq, rstd, e1, t1 = tmp["osq"], tmp["rstd"], tmp["e1"], tmp["t1"]
        src = o
        srck = ok
        if groupnorm:
            b.op("act", lambda: nc.scalar.activation(out=tmp["o32b"][:dv, :], in_=o[:dv, :], func=AF.Identity),
                 r=ok, w=["o32b"])
            b.op("pe", lambda: nc.tensor.matmul(self.pv(5, 0, 512)[:dv, :], lhsT=self.ones_bf[:dv, :dv],
                                                rhs=tmp["o32b"][:dv, :], start=True, stop=True),
                 r=["o32b", "ones"], w=self.pk(5, 0, 512))
            b.op("dve", lambda: nc.vector.scalar_tensor_tensor(
                out=tmp["oc"][:dv, :], in0=self.pv(5, 0, 512)[:dv, :], scalar=-1.0 / dv, in1=o[:dv, :],
                op0=ALU.mult, op1=ALU.add), r=self.pk(5, 0, 512) + ok, w=["oc"])
            src = tmp["oc"]
            srck = ["oc"]
        b.op("act", lambda: nc.scalar.activation(out=osq[:dv, :], in_=src[:dv, :], func=AF.Square), r=srck, w=["osq"])
        b.op("pe", lambda: nc.tensor.matmul(self.pv(5, 0, 512)[:dv, :], lhsT=self.ones_bf[:dv, :dv], rhs=osq[:dv, :],
                                            start=True, stop=True), r=["osq", "ones"], w=self.pk(5, 0, 512))
        b.op("act", lambda: nc.scalar.activation(out=rstd[:dv, :], in_=self.pv(5, 0, 512)[:dv, :], func=AF.Sqrt,
                                                 scale=1.0 / dv, bias=self.eps_t[:dv, 0:1]),
             r=self.pk(5, 0, 512) + ["eps"], w=["rstd"])
        b.op("dve", lambda: nc.vector.reciprocal(out=rstd[:dv, :], in_=rstd[:dv, :]), r=["rstd"], w=["rstd"])
        gp = self.pv(2, 0, 512)
        gk = self.pk(2, 0, 512)
        b.op("act", lambda: nc.scalar.activation(out=e1[:dv, :], in_=gp[:dv, :], func=AF.Exp, scale=-1.0),
             r=gk, w=["e1"])
        b.op("dve", lambda: nc.vector.tensor_scalar_add(out=e1[:dv, :], in0=e1[:dv, :], scalar1=1.0),
             r=["e1"], w=["e1"])
        b.op("dve", lambda: nc.vector.reciprocal(out=e1[:dv, :], in_=e1[:dv, :]), r=["e1"], w=["e1"])
        b.op("dve", lambda: nc.vector.tensor_tensor(out=e1[:dv, :], in0=gp[:dv, :], in1=e1[:dv, :], op=ALU.mult),
             r=gk + ["e1"], w=["e1"])
        b.op("dve", lambda: nc.vector.scalar_tensor_tensor(out=t1[:dv, :], in0=src[:dv, :], scalar=gain_ap,
                                                           in1=rstd[:dv, :], op0=ALU.mult, op1=ALU.mult),
             r=srck + ["rstd", "hgain"], w=["t1"])
        b.op("dve", lambda: nc.vector.tensor_tensor(out=self.yT[:dv, etile, sl], in0=t1[:dv, :], in1=e1[:dv, :],
                                                     op=ALU.mult), r=["t1", "e1"], w=[("y", etile, blk)])


    RET_PIECES = {0: ((0, 128, 0, 0), (128, 64, 1, 0)), 1: ((0, 64, 1, 64), (64, 128, 2, 0)),
                  2: ((0, 128, 3, 0), (128, 64, 4, 0)), 3: ((0, 64, 4, 64), (64, 128, 5, 0))}

    def odd_mixer(self, j, es):
        with contextlib.ExitStack() as es_r:
            self.retnet(j, es_r)
            self.b.barrier()
        if self.cfg.get("s5", True):
            with contextlib.ExitStack() as es_s:
                self.s5(j, es_s)
                self.b.barrier()

    def retnet(self, j, es):
        b = self.b
        nc = self.nc
        tmp = self.alloc_head_tmps(es, groupnorm=True)
        wtm = b.sb("wtmo", [128, KT, 448], BF16, es)
        wfm = b.sb("wfmo", [128, 2, KT, 128], BF16, es)
        qT3 = b.sb("qT3", [128, 3, T], BF16, es)
        kT = b.sb("kTo", [128, T], BF16, es)
        kt_all = b.sb("kt_allo", [128, NCH, 2, 128], BF16, es)
        v_all = b.sb("v_allo", [128, NCH, 192], BF16, es)
        S_all = b.sb("S_allo", [128, 2, NCH, 192], BF16, es)
        Pst = b.sb("Psto", [128, 2, 2, 192], F32, es)
        rope = b.sb("rope", [128, 2, NCH, 64], BF16, es)
        rdec = b.sb("rdec", [128, 16], F32, es)
        dmask = b.sb("dmask", [128, 4, 128], F32, es)
        AB = b.sb("AB", [128, 2, 2, 4, 64], F32, es)
        rot = b.sb("rot", [128, 2, 4, 64], F32, es)
        qt3 = b.sb("qt3", [128, 2, 4, 128], BF16, es)
        mean_sb = b.sb("mean_sb", [128, 512], F32, es)
        oc2 = b.sb("oc2", [128, 512], F32, es)
        b.dma("pool", rope[:], self.rope_d[:, :].rearrange("p (a c k) -> p a c k", a=2, c=NCH), w=["rope"], stream="c1")
        b.dma("sp", rdec[:], self.rdec_d[:, :], w=["rdec"], stream="c0")
        b.dma("sp", dmask[:], self.dmask_d[:, :].rearrange("p (h i) -> p h i", h=4), w=["dmask"], stream="c0")
        b.op("pool", lambda: nc.gpsimd.memset(S_all[:], 0.0), w=[("S", d, c) for d in range(2) for c in range(NCH)])
        for h in range(4):
            gf128 = float((1.0 - 2.0 ** (-5.0 - h)) ** 128)
            gb128 = float((1.0 - 2.0 ** (-5.5 - h)) ** 128)
            pieces = self.RET_PIECES[h]
            b.dma("pool", wtm[:], self.w_tm_o_d[j, h, :, :].rearrange("p (k c) -> p k c", k=KT), w=["wtm"], stream="wtm0")
            for pi in range(2):
                b.dma("pool", wfm[:, pi, :, :], self.w_fm_o_d[j, 2 * h + pi, :, :].rearrange("p (k c) -> p k c", k=KT),
                      w=["wfm"], stream="wfm")
            for c in range(NCH):
                ch = slice(c * 128, (c + 1) * 128)
                par = c % 2
                P = self.pv(par, 0, 512)
                Pk = self.pk(par, 0, 448)
                for k in range(KT):
                    b.op("pe", lambda: nc.tensor.matmul(P[:, 0:448], lhsT=self.hT[:, k, ch], rhs=wtm[:, k, :],
                                                        start=(k == 0), stop=(k == KT - 1)),
                         r=["wtm", ("h", k, c // 4)], w=Pk)
                P4 = P[:, 0:256].rearrange("p (a k) -> p a k", a=4)
                cosb = rope[:, 0, c, :].unsqueeze(1).to_broadcast([128, 4, 64])
                sinb = rope[:, 1, c, :].unsqueeze(1).to_broadcast([128, 4, 64])
                A = AB[:, par, 0, :, :]
                Bm = AB[:, par, 1, :, :]
                R = rot[:, par, :, :]
                b.op("dve", lambda: nc.vector.tensor_tensor(out=A, in0=P4, in1=cosb, op=ALU.mult), r=Pk + ["rope"],
                     w=[("A", par)])
                b.op("dve", lambda: nc.vector.tensor_tensor(out=Bm, in0=P4, in1=sinb, op=ALU.mult), r=Pk + ["rope"],
                     w=[("B", par)])
                b.op("pool", lambda: nc.gpsimd.tensor_tensor(out=R[:, 0::2, :], in0=A[:, 0::2, :], in1=Bm[:, 1::2, :],
                                                              op=ALU.subtract), r=[("A", par), ("B", par)], w=[("rot", par)])
                b.op("pool", lambda: nc.gpsimd.tensor_tensor(out=R[:, 1::2, :], in0=Bm[:, 0::2, :], in1=A[:, 1::2, :],
                                                              op=ALU.add), r=[("A", par), ("B", par)], w=[("rot", par)])
                rq = rot[:, par, 0:2, :].rearrange("p a k -> p (a k)")
                rk = rot[:, par, 2:4, :].rearrange("p a k -> p (a k)")
                QT = qt3[:, par, :, :]
                sc_k = 128.0 ** -0.5
                b.op("act", lambda: nc.scalar.activation(out=QT[:, 0, :], in_=rq, func=AF.Identity), r=[("rot", par)],
                     w=[("qt3", par)])
                b.op("act", lambda: nc.scalar.activation(out=QT[:, 1, :], in_=rq, func=AF.Identity,
                                                         scale=rdec[:, 4 * h + 0: 4 * h + 1]),
                     r=[("rot", par), "rdec"], w=[("qt3", par)])
                b.op("act", lambda: nc.scalar.activation(out=QT[:, 2, :], in_=rq, func=AF.Identity,
                                                         scale=rdec[:, 4 * h + 1: 4 * h + 2]),
                     r=[("rot", par), "rdec"], w=[("qt3", par)])
                b.op("act", lambda: nc.scalar.activation(out=QT[:, 3, :], in_=rk, func=AF.Identity, scale=sc_k),
                     r=[("rot", par)], w=[("qt3", par)])
                b.op("pool", lambda: nc.gpsimd.tensor_scalar(out=kt_all[:, c, 0, :], in0=rk,
                                                              scalar1=rdec[:, 4 * h + 2: 4 * h + 3], scalar2=None,
                                                              op0=ALU.mult), r=[("rot", par), "rdec"], w=[("kt", c)])
                b.op("pool", lambda: nc.gpsimd.tensor_scalar(out=kt_all[:, c, 1, :], in0=rk,
                                                              scalar1=rdec[:, 4 * h + 3: 4 * h + 4], scalar2=None,
                                                              op0=ALU.mult), r=[("rot", par), "rdec"], w=[("kt", c)])
                b.op("act", lambda: nc.scalar.activation(out=v_all[:, c, :], in_=P[:, 256:448], func=AF.Identity), r=Pk,
                     w=[("v", c)])
                zb = 2 + par
                tp = self.pv(zb, 0, 256).bitcast(BF16)
                tkey = self.pk(zb, 0, 256)
                for a_ in range(4):
                    b.op("pe", lambda: nc.tensor.transpose(tp[:, a_ * 128:(a_ + 1) * 128], QT[:, a_, :], self.ident[:]),
                         r=[("qt3", par), "ident"], w=tkey)
                b.op("dve", lambda: nc.vector.tensor_copy(out=qT3[:, :, ch],
                                                           in_=tp[:, 0:384].rearrange("p (a t) -> p a t", a=3)),
                     r=tkey, w=[("qT", c)])
                b.op("act", lambda: nc.scalar.activation(out=kT[:, ch], in_=tp[:, 384:512], func=AF.Identity), r=tkey,
                     w=[("kT", c)])
            for step in range(NCH - 1):
                for d in range(2):
                    c = step if d == 0 else NCH - 1 - step
                    nxt = 1 if d == 0 else -1
                    gdec = gf128 if d == 0 else gb128
                    bank = (4 if d == 0 else 6) + step % 2
                    kv = self.pv(bank, 0, 192)
                    kvk = self.pk(bank, 0, 192)
                    b.op("pe", lambda: nc.tensor.matmul(kv, lhsT=kt_all[:, c, d, :], rhs=v_all[:, c, :], start=True,
                                                        stop=True), r=[("kt", c), ("v", c)], w=kvk)
                    pp = step % 2
                    pkey = ("Pst", d, pp)
                    pold = ("Pst", d, 1 - pp)
                    if step == 0:
                        b.op("dve", lambda: nc.vector.tensor_copy(out=Pst[:, d, pp, :], in_=kv), r=kvk, w=[pkey])
                    else:
                        b.op("dve", lambda: nc.vector.scalar_tensor_tensor(out=Pst[:, d, pp, :], in0=Pst[:, d, 1 - pp, :],
                                                                           scalar=gdec, in1=kv, op0=ALU.mult, op1=ALU.add),
                             r=kvk + [pold], w=[pkey])
                    b.op("act", lambda: nc.scalar.activation(out=S_all[:, d, c + nxt, :], in_=Pst[:, d, pp, :],
                                                             func=AF.Identity), r=[pkey], w=[("S", d, c + nxt)])
            obank = (1, 6)
            gbank = (2, 7)
            for blk in range(NBLK):
                sl = slice(blk * 512, (blk + 1) * 512)
                for pi, (f0, nf, etile, p0) in enumerate(pieces):
                    for k in range(KT):
                        b.op("pe", lambda: nc.tensor.matmul(self.pv(gbank[pi], 0, 512)[p0:p0 + nf, :],
                                                            lhsT=wfm[:, pi, k, 0:nf], rhs=self.hT[:, k, sl],
                                                            start=(k == 0), stop=(k == KT - 1)),
                             r=["wfm", ("h", k, blk)], w=self.pk(gbank[pi], 0, 512))
                for cc in range(4):
                    c = blk * 4 + cc
                    ch = slice(c * 128, (c + 1) * 128)
                    par = c % 2
                    sb_ = 0 if par == 0 else 3
                    b.op("pe", lambda: nc.tensor.matmul(self.pv(sb_, 0, 128), lhsT=kT[:, ch], rhs=qT3[:, 0, ch],
                                                        start=True, stop=True), r=[("kT", c), ("qT", c)],
                         w=self.pk(sb_, 0, 128))
                    b.op("dve", lambda: nc.vector.tensor_tensor(out=tmp["sT"][:, par, 0, :], in0=self.pv(sb_, 0, 128),
                                                                 in1=dmask[:, h, :], op=ALU.mult),
                         r=self.pk(sb_, 0, 128) + ["dmask"], w=[("sT", par)])
                    for pi, (f0, nf, etile, p0) in enumerate(pieces):
                        ov = self.pv(obank[pi], cc * 128, cc * 128 + 128)[p0:p0 + nf, :]
                        ok = self.pk(obank[pi], 0, 512)
                        b.op("pe", lambda: nc.tensor.matmul(ov, lhsT=v_all[:, c, f0:f0 + nf], rhs=tmp["sT"][:, par, 0, :],
                                                            start=True, stop=False), r=[("v", c), ("sT", par)], w=ok)
                        b.op("pe", lambda: nc.tensor.matmul(ov, lhsT=S_all[:, 0, c, f0:f0 + nf], rhs=qT3[:, 1, ch],
                                                            start=False, stop=False), r=[("S", 0, c), ("qT", c)], w=ok)
                        b.op("pe", lambda: nc.tensor.matmul(ov, lhsT=S_all[:, 1, c, f0:f0 + nf], rhs=qT3[:, 2, ch],
                                                            start=False, stop=True), r=[("S", 1, c), ("qT", c)], w=ok)
                stat = self.pv(5, 0, 512)
                statk = self.pk(5, 0, 512)
                osq, rstd, e1, t1, o32b, oc = (tmp[n] for n in ("osq", "rstd", "e1", "t1", "o32b", "oc"))
                for pi, (f0, nf, etile, p0) in enumerate(pieces):
                    pr = slice(p0, p0 + nf)
                    b.op("act", lambda: nc.scalar.activation(out=o32b[pr, :] if pi == 0 else osq[pr, :],
                                                             in_=self.pv(obank[pi], 0, 512)[pr, :], func=AF.Identity),
                         r=self.pk(obank[pi], 0, 512), w=[("o32b", pi)])
                for pi, (f0, nf, etile, p0) in enumerate(pieces):
                    pr = slice(p0, p0 + nf)
                    src = o32b if pi == 0 else osq
                    b.op("pe", lambda: nc.tensor.matmul(stat, lhsT=self.ones_bf[pr, :], rhs=src[pr, :], start=(pi == 0),
                                                        stop=(pi == 1)), r=[("o32b", pi), "ones"], w=statk)
                b.op("act", lambda: nc.scalar.activation(out=mean_sb[:], in_=stat, func=AF.Identity, scale=-1.0 / 192),
                     r=statk, w=["mean"])
                ocs = (oc, oc2)
                for pi, (f0, nf, etile, p0) in enumerate(pieces):
                    pr = slice(p0, p0 + nf)
                    b.op("dve", lambda: nc.vector.tensor_tensor(out=ocs[pi][pr, :], in0=self.pv(obank[pi], 0, 512)[pr, :],
                                                                 in1=mean_sb[pr, :], op=ALU.add),
                         r=self.pk(obank[pi], 0, 512) + ["mean"], w=[("oc", pi)])
                    b.op("act", lambda: nc.scalar.activation(out=o32b[pr, :] if pi == 0 else osq[pr, :], in_=ocs[pi][pr, :],
                                                             func=AF.Square), r=[("oc", pi)], w=[("o32b", pi)])
                for pi, (f0, nf, etile, p0) in enumerate(pieces):
                    pr = slice(p0, p0 + nf)
                    src = o32b if pi == 0 else osq
                    b.op("pe", lambda: nc.tensor.matmul(stat, lhsT=self.ones_bf[pr, :], rhs=src[pr, :], start=(pi == 0),
                                                        stop=(pi == 1)), r=[("o32b", pi), "ones"], w=statk)
                b.op("act", lambda: nc.scalar.activation(out=rstd[:], in_=stat, func=AF.Sqrt, scale=1.0 / 192,
                                                         bias=self.eps_t[:, 0:1]), r=statk + ["eps"], w=["rstd"])
                b.op("dve", lambda: nc.vector.reciprocal(out=rstd[:], in_=rstd[:]), r=["rstd"], w=["rstd"])
                for pi, (f0, nf, etile, p0) in enumerate(pieces):
                    pr = slice(p0, p0 + nf)
                    gp = self.pv(gbank[pi], 0, 512)[pr, :]
                    gk = self.pk(gbank[pi], 0, 512)
                    gain_ap = self.hgain_o[pr, j * 8 + 2 * h + pi: j * 8 + 2 * h + pi + 1]
                    b.op("act", lambda: nc.scalar.activation(out=e1[pr, :], in_=gp, func=AF.Exp, scale=-1.0), r=gk,
                         w=["e1"])
                    b.op("dve", lambda: nc.vector.tensor_scalar_add(out=e1[pr, :], in0=e1[pr, :], scalar1=1.0),
                         r=["e1"], w=["e1"])
                    b.op("dve", lambda: nc.vector.reciprocal(out=e1[pr, :], in_=e1[pr, :]), r=["e1"], w=["e1"])
                    b.op("dve", lambda: nc.vector.tensor_tensor(out=e1[pr, :], in0=gp, in1=e1[pr, :], op=ALU.mult),
                         r=gk + ["e1"], w=["e1"])
                    b.op("dve", lambda: nc.vector.scalar_tensor_tensor(out=t1[pr, :], in0=ocs[pi][pr, :], scalar=gain_ap,
                                                                       in1=rstd[pr, :], op0=ALU.mult, op1=ALU.mult),
                         r=[("oc", pi), "rstd", "hgain_o"], w=["t1"])
                    b.op("dve", lambda: nc.vector.tensor_tensor(out=self.yT[pr, etile, sl], in0=t1[pr, :], in1=e1[pr, :],
                                                                 op=ALU.mult), r=["t1", "e1"],
                         w=[("y", etile, blk)])


    def cmul(self, out, x, y, t, shape_keys):
        b = self.b
        nc = self.nc
        (o_r, o_i), ko = out
        (xr, xi), kx = x
        (yr, yi), ky = y
        (t1, t2), kt = t
        b.op("dve", lambda: nc.vector.tensor_tensor(out=t1, in0=xr, in1=yr, op=ALU.mult), r=[kx, ky], w=[kt])
        b.op("dve", lambda: nc.vector.tensor_tensor(out=t2, in0=xi, in1=yi, op=ALU.mult), r=[kx, ky], w=[kt])
        b.op("dve", lambda: nc.vector.tensor_tensor(out=o_r, in0=t1, in1=t2, op=ALU.subtract), r=[kt], w=[ko])
        b.op("dve", lambda: nc.vector.tensor_tensor(out=t1, in0=xr, in1=yi, op=ALU.mult), r=[kx, ky, ko], w=[kt])
        b.op("dve", lambda: nc.vector.tensor_tensor(out=t2, in0=xi, in1=yr, op=ALU.mult), r=[kx, ky], w=[kt])
        b.op("dve", lambda: nc.vector.tensor_tensor(out=o_i, in0=t1, in1=t2, op=ALU.add), r=[kt], w=[ko])

    def s5(self, j, es):
        b = self.b
        nc = self.nc
        V = nc.vector
        NC8 = T // 8

        def tt(out, in0, in1, op, r, w):
            b.op("dve", lambda: V.tensor_tensor(out=out, in0=in0, in1=in1, op=op), r=r, w=w)

        def ts(out, in0, s1, s2, op0, op1, r, w):
            b.op("dve", lambda: V.tensor_scalar(out=out, in0=in0, scalar1=s1, scalar2=s2, op0=op0, op1=op1), r=r, w=w)

        Uflat = b.sb("Uflat", [128, 16, NC8], BF16, es)
        Tg = b.sb("Tg", [128, 16, 128], BF16, es)
        KX = b.sb("KX", [128, 2, 2, 8, 128], BF16, es)
        QY = b.sb("QY", [128, 2, 2, 8, 128], BF16, es)
        Hs = b.sb("Hs", [128, 2, 2, 8, NC8], BF16, es)
        gw = b.sb("gw", [128, 2, 256], BF16, es)
        glb = b.sb("s5glb", [128, 2], F32, es)
        A8 = b.sb("A8", [128, 2, 2, 8], F32, es)
        b.dma("pool", gw[:], self.s5_gw_d[j, :, :].rearrange("p (k c) -> p k c", k=2), w=["gw"], stream="wfm")
        b.dma("sp", glb[:], self.s5_glb_d[j, :, :], w=["glb"], stream="c0")

        with contextlib.ExitStack() as e1:
            u2 = b.sb("u2", [128, 2, 16, 8, 16], BF16, e1)
            wu = b.sb("wu", [128, KT, 256], BF16, e1)
            b.dma("pool", wu[:], self.w_u_o_d[j, :, :].rearrange("p (k c) -> p k c", k=KT), w=["wu"], stream="wtm0")
            cnt = 0
            for half in range(2):
                for jj in range(8):
                    bi = cnt % 2
                    cnt += 1
                    for k in range(KT):
                        b.op("pe", lambda: nc.tensor.matmul(
                            self.pv(bi, 0, 256), lhsT=self.hT[:, k, half * 1024 + jj: half * 1024 + 1024: 8],
                            rhs=wu[:, k, :], start=(k == 0), stop=(k == KT - 1)),
                            r=["wu"] + [("h", k, blk) for blk in (2 * half, 2 * half + 1)], w=self.pk(bi, 0, 256))
                    b.op("act", lambda: nc.scalar.activation(
                        out=u2[:, half, :, jj, :], in_=self.pv(bi, 0, 256).rearrange("p (g k) -> p g k", g=16),
                        func=AF.Identity), r=self.pk(bi, 0, 256), w=[("u2", half)])
            cnt = 0
            for half in range(2):
                for g4 in range(4):
                    bi = 2 + cnt % 2
                    cnt += 1
                    tp = self.pv(bi, 0, 256).bitcast(BF16)
                    for gg in range(4):
                        g = g4 * 4 + gg
                        b.op("pe", lambda: nc.tensor.transpose(tp[:, gg * 128:(gg + 1) * 128],
                                                               u2[:, half, g, :, :].rearrange("p a k -> p (a k)"),
                                                               self.ident[:]),
                             r=[("u2", half), "ident"], w=self.pk(bi, 0, 256))
                    b.op("dve", lambda: V.tensor_copy(out=Uflat[:, g4 * 4:(g4 + 1) * 4, half * 128:(half + 1) * 128],
                                                      in_=tp.rearrange("p (a c) -> p a c", a=4)),
                         r=self.pk(bi, 0, 256), w=["Uflat"])
            b.barrier()

        with contextlib.ExitStack() as e2:
            prm = b.sb("s5prm", [128, 3, 2, 8], F32, e2)
            BC = b.sb("s5BC", [128, 4, 8, 16], F32, e2)
            dsk = b.sb("s5dsk", [128, 16], F32, e2)
            bmask = b.sb("s5bm", [128, 2, 128], F32, e2)
            identf = b.sb("identf", [128, 128], F32, e2)
            b.dma("sp", prm[:], self.s5_prm_d[j, :, :].rearrange("p (a d g) -> p a d g", a=3, d=2), w=["prm"], stream="c0")
            b.dma("sp", BC[:], self.s5_bc_d[j, :, :].rearrange("p (a g k) -> p a g k", a=4, g=8), w=["BC"], stream="c0")
            b.dma("sp", dsk[:], self.s5_dsk_d[j, :, :], w=["dsk"], stream="c0")
            b.dma("sp", bmask[:], self.s5_bm_d[:, :].rearrange("p (a c) -> p a c", a=2), w=["bmask"], stream="c0")
            b.dma("sp", identf[:], self.ident_d[:, :], w=["identf"], stream="c0")
            sc = {}
            for nm in ("lr", "dt", "th", "s8", "c8", "m8", "ar", "ai", "t1", "t2", "t3", "den", "cr", "ci", "nar", "nai"):
                sc[nm] = b.sb("s5_" + nm, [128, 2, 8], F32, e2)
            PW = b.sb("s5PW", [128, 2, 2, 2, 8, 9], F32, e2)
            BB = b.sb("s5BB", [128, 2, 2, 8, 16], F32, e2)
            TB = b.sb("s5TB", [128, 4, 8, 8, 16], F32, e2)
            TX = b.sb("s5TX", [128, 2, 8, 8, 16], F32, e2)
            tw = b.sb("s5tw", [128, 2, 8, 8, 16], F32, e2)
            Tacc = b.sb("s5Tacc", [128, 128], F32, e2)
            lr, dt, th = sc["lr"][:], sc["dt"][:], sc["th"][:]
            ts(lr, prm[:, 0, :, :], -1e-4, None, ALU.min, ALU.bypass, ["prm"], ["lr"])
            b.op("act", lambda: nc.scalar.activation(out=dt, in_=prm[:, 2, :, :], func=AF.Exp), r=["prm"], w=["dt"])
            tt(th, prm[:, 1, :, :], dt, ALU.mult, ["prm", "dt"], ["th"])
            b.op("act", lambda: nc.scalar.activation(out=sc["s8"][:], in_=th, func=AF.Sin, scale=1.0 / 8), r=["th"],
                 w=["s8"])
            b.op("act", lambda: nc.scalar.activation(out=sc["t1"][:], in_=th, func=AF.Sin, scale=1.0 / 16), r=["th"],
                 w=["t1"])
            tt(sc["t1"][:], sc["t1"][:], sc["t1"][:], ALU.mult, ["t1"], ["t1"])
            ts(sc["c8"][:], sc["t1"][:], -2.0, 1.0, ALU.mult, ALU.add, ["t1"], ["c8"])
            tt(sc["t2"][:], lr, dt, ALU.mult, ["lr", "dt"], ["t2"])
            b.op("act", lambda: nc.scalar.activation(out=sc["m8"][:], in_=sc["t2"][:], func=AF.Exp, scale=1.0 / 8),
                 r=["t2"], w=["m8"])
            ar, ai = sc["ar"][:], sc["ai"][:]
            tt(ar, sc["m8"][:], sc["c8"][:], ALU.mult, ["m8", "c8"], ["a"])
            tt(ai, sc["m8"][:], sc["s8"][:], ALU.mult, ["m8", "s8"], ["a"])
            for _ in range(3):
                tt(sc["t1"][:], ar, ar, ALU.mult, ["a"], ["t1"])
                tt(sc["t2"][:], ai, ai, ALU.mult, ["a"], ["t2"])
                tt(sc["t3"][:], ar, ai, ALU.mult, ["a"], ["t3"])
                tt(ar, sc["t1"][:], sc["t2"][:], ALU.subtract, ["t1", "t2"], ["a"])
                ts(ai, sc["t3"][:], 2.0, None, ALU.mult, ALU.bypass, ["t3"], ["a"])
            tt(sc["t1"][:], ar, ar, ALU.mult, ["a"], ["t1"])
            tt(sc["t2"][:], ai, ai, ALU.mult, ["a"], ["t2"])
            tt(sc["t1"][:], sc["t1"][:], sc["t2"][:], ALU.add, ["t1", "t2"], ["t1"])
            b.op("dve", lambda: V.reciprocal(out=sc["t1"][:], in_=sc["t1"][:]), r=["t1"], w=["t1"])
            tt(sc["nar"][:], ar, sc["t1"][:], ALU.mult, ["a", "t1"], ["na"])
            tt(sc["nai"][:], ai, sc["t1"][:], ALU.mult, ["a", "t1"], ["na"])
            ts(sc["nai"][:], sc["nai"][:], -1.0, None, ALU.mult, ALU.bypass, ["na"], ["na"])
            b.op("dve", lambda: V.memset(PW[:, :, 0, :, :, 0], 1.0), w=["PW"])
            b.op("dve", lambda: V.memset(PW[:, :, 1, :, :, 0], 0.0), r=["PW"], w=["PW"])
            for pi_, (br_, bi_, kb) in enumerate(((ar, ai, "a"), (sc["nar"][:], sc["nai"][:], "na"))):
                for k in range(8):
                    self.cmul(((PW[:, pi_, 0, :, :, k + 1], PW[:, pi_, 1, :, :, k + 1]), "PW"),
                              ((PW[:, pi_, 0, :, :, k], PW[:, pi_, 1, :, :, k]), "PW"),
                              ((br_, bi_), kb), ((sc["t1"][:], sc["t2"][:]), "t12"), None)
            b.op("dve", lambda: V.tensor_copy(out=A8[:, 0, :, :], in_=PW[:, 0, 0, :, :, 8]), r=["PW"], w=["A8"])
            b.op("dve", lambda: V.tensor_copy(out=A8[:, 1, :, :], in_=PW[:, 0, 1, :, :, 8]), r=["PW"], w=["A8"])
            den, cr, ci = sc["den"][:], sc["cr"][:], sc["ci"][:]
            li = prm[:, 1, :, :]
            tt(sc["t1"][:], lr, lr, ALU.mult, ["lr"], ["t1"])
            tt(sc["t2"][:], li, li, ALU.mult, ["prm"], ["t2"])
            tt(den, sc["t1"][:], sc["t2"][:], ALU.add, ["t1", "t2"], ["den"])
            b.op("dve", lambda: V.reciprocal(out=den, in_=den), r=["den"], w=["den"])
            ts(sc["t3"][:], ar, -1.0, None, ALU.add, ALU.bypass, ["a"], ["t3"])
            tt(sc["t1"][:], sc["t3"][:], lr, ALU.mult, ["t3", "lr"], ["t1"])
            tt(sc["t2"][:], ai, li, ALU.mult, ["a", "prm"], ["t2"])
            tt(cr, sc["t1"][:], sc["t2"][:], ALU.add, ["t1", "t2"], ["c"])
            tt(cr, cr, den, ALU.mult, ["c", "den"], ["c"])
            tt(sc["t1"][:], ai, lr, ALU.mult, ["a", "lr"], ["t1"])
            tt(sc["t2"][:], sc["t3"][:], li, ALU.mult, ["t3", "prm"], ["t2"])
            tt(ci, sc["t1"][:], sc["t2"][:], ALU.subtract, ["t1", "t2"], ["c"])
            tt(ci, ci, den, ALU.mult, ["c", "den"], ["c"])
            sh4 = [128, 2, 8, 16]
            crb = cr.unsqueeze(3).to_broadcast(sh4)
            cib = ci.unsqueeze(3).to_broadcast(sh4)
            brb = BC[:, 0, :, :].unsqueeze(1).to_broadcast(sh4)
            bib = BC[:, 1, :, :].unsqueeze(1).to_broadcast(sh4)
            w4 = tw[:, :, 0:2, 0, :].rearrange("p a d k -> p a d k")
            t4a = tw[:, 0, :, 0:2, :]
            tA = TX[:, 0, 0:2, :, :]
            tB = TX[:, 1, 0:2, :, :]
            self.cmul(((BB[:, 0, :, :, :], BB[:, 1, :, :, :]), "BB"), ((crb, cib), "c"), ((brb, bib), "BC"),
                      ((tA, tB), "TX"), None)
            sh5 = [128, 8, 8, 16]
            for d in range(2):
                kp = 1 if d == 0 else 0
                qp = 0 if d == 0 else 1
                pkr = PW[:, kp, 0, d, :, 0:8].unsqueeze(3).to_broadcast(sh5)
                pki = PW[:, kp, 1, d, :, 0:8].unsqueeze(3).to_broadcast(sh5)
                pqr = PW[:, qp, 0, d, :, 0:8].unsqueeze(3).to_broadcast(sh5)
                pqi = PW[:, qp, 1, d, :, 0:8].unsqueeze(3).to_broadcast(sh5)
                bbr = BB[:, 0, d, :, :].unsqueeze(2).to_broadcast(sh5)
                bbi = BB[:, 1, d, :, :].unsqueeze(2).to_broadcast(sh5)
                ccr = BC[:, 2, :, :].unsqueeze(2).to_broadcast(sh5)
                cci = BC[:, 3, :, :].unsqueeze(2).to_broadcast(sh5)
                self.cmul(((TB[:, 0], TB[:, 1]), "TB"), ((pkr, pki), "PW"), ((bbr, bbi), "BB"), ((tw[:, 0], tw[:, 1]), "tw"),
                          None)
                self.cmul(((TB[:, 2], TB[:, 3]), "TB"), ((pqr, pqi), "PW"), ((ccr, cci), "BC"), ((tw[:, 0], tw[:, 1]), "tw"),
                          None)
                ts(TB[:, 3], TB[:, 3], -1.0, None, ALU.mult, ALU.bypass, ["TB"], ["TB"])
                for g in range(16):
                    pair, g2 = divmod(g, 2)
                    rows = slice(g2 * 64, g2 * 64 + 64)
                    bi = g % 2
                    tps = self.pv(bi, 0, 128)
                    fl = lambda a_: TB[rows, a_, pair, :, :].rearrange("p a k -> p (a k)")
                    b.op("pe", lambda: nc.tensor.matmul(tps, lhsT=fl(0), rhs=fl(2), start=True, stop=False), r=["TB"],
                         w=self.pk(bi, 0, 128))
                    b.op("pe", lambda: nc.tensor.matmul(tps, lhsT=fl(1), rhs=fl(3), start=False, stop=True), r=["TB"],
                         w=self.pk(bi, 0, 128))
                    if d == 0:
                        tt(Tacc[:], tps, bmask[:, 0, :], ALU.mult, self.pk(bi, 0, 128) + ["bmask"], ["Tacc"])
                        b.op("dve", lambda: V.scalar_tensor_tensor(out=Tacc[:], in0=identf[:], scalar=dsk[:, g:g + 1],
                                                                   in1=Tacc[:], op0=ALU.mult, op1=ALU.add),
                             r=["identf", "dsk", "Tacc"], w=["Tacc"])
                        b.op("dve", lambda: V.tensor_copy(out=Tg[:, g, :], in_=Tacc[:]), r=["Tacc"], w=[("Tg", g)])
                    else:
                        tt(Tacc[:], tps, bmask[:, 1, :], ALU.mult, self.pk(bi, 0, 128) + ["bmask"], ["Tacc"])
                        tt(Tg[:, g, :], Tacc[:], Tg[:, g, :], ALU.add, ["Tacc", ("Tg", g)], [("Tg", g)])
                if d == 0:
                    p7r = PW[:, 0, 0, d, :, 7].unsqueeze(2).unsqueeze(3).to_broadcast(sh5)
                    p7i = PW[:, 0, 1, d, :, 7].unsqueeze(2).unsqueeze(3).to_broadcast(sh5)
                    self.cmul(((TX[:, 0], TX[:, 1]), "TX"), ((TB[:, 0], TB[:, 1]), "TB"), ((p7r, p7i), "PW"),
                              ((tw[:, 0], tw[:, 1]), "tw"), None)
                    kxr, kxi, kxk = TX[:, 0], TX[:, 1], "TX"
                else:
                    kxr, kxi, kxk = TB[:, 0], TB[:, 1], "TB"
                for ri, src in enumerate((kxr, kxi)):
                    for p4 in range(2):
                        bi = 2 + (ri * 2 + p4) % 2
                        for pp in range(4):
                            pair = p4 * 4 + pp
                            b.op("pe", lambda: nc.tensor.transpose(self.pv(bi, pp * 128, pp * 128 + 128),
                                                                   src[:, pair, :, :].rearrange("p a k -> p (a k)"),
                                                                   identf[:]), r=[kxk, "identf"], w=self.pk(bi, 0, 512))
                        b.op("act", lambda: nc.scalar.activation(
                            out=KX[:, d, ri, p4 * 4:(p4 + 1) * 4, :],
                            in_=self.pv(bi, 0, 512).rearrange("p (a q) -> p a q", a=4), func=AF.Identity),
                            r=self.pk(bi, 0, 512), w=["KX"])
                pw_idx = 1 if d == 0 else 8
                pyr = PW[:, 0, 0, d, :, pw_idx].unsqueeze(2).unsqueeze(3).to_broadcast(sh5)
                pyi = PW[:, 0, 1, d, :, pw_idx].unsqueeze(2).unsqueeze(3).to_broadcast(sh5)
                tt(tw[:, 0], TB[:, 2], pyr, ALU.mult, ["TB", "PW"], ["tw"])
                tt(tw[:, 1], TB[:, 3], pyi, ALU.mult, ["TB", "PW"], ["tw"])
                tt(TX[:, 0], tw[:, 0], tw[:, 1], ALU.add, ["tw"], ["TX"])
                b.op("act", lambda: nc.scalar.activation(out=QY[:, d, 0, :, :], in_=TX[:, 0].rearrange("p g a k -> p g (a k)"),
                                                         func=AF.Identity), r=["TX"], w=["QY"])
                tt(tw[:, 0], TB[:, 3], pyr, ALU.mult, ["TB", "PW", "TX"], ["tw"])
                tt(tw[:, 1], TB[:, 2], pyi, ALU.mult, ["TB", "PW"], ["tw"])
                tt(TX[:, 1], tw[:, 0], tw[:, 1], ALU.subtract, ["tw"], ["TX"])
                b.op("act", lambda: nc.scalar.activation(out=QY[:, d, 1, :, :], in_=TX[:, 1].rearrange("p g a k -> p g (a k)"),
                                                         func=AF.Identity), r=["TX"], w=["QY"])
            b.barrier()

        with contextlib.ExitStack() as e3:
            RC = b.sb("s5RC", [128, 2, 8, NC8], F32, e3)
            XH = b.sb("s5XH", [128, 2, 8, NC8], F32, e3)
            W1 = b.sb("s5W1", [128, 8, 128], F32, e3)
            W2 = b.sb("s5W2", [128, 8, 128], F32, e3)
            mm = b.sb("s5mm", [128, 3, 8], F32, e3)
            for d in range(2):
                a8r, a8i = A8[:, 0, d, :], A8[:, 1, d, :]
                tt(mm[:, 1, :], a8r, a8r, ALU.mult, ["A8"], ["mm1"])
                tt(mm[:, 2, :], a8i, a8i, ALU.mult, ["A8"], ["mm2"])
                tt(mm[:, 0, :], mm[:, 1, :], mm[:, 2, :], ALU.add, ["mm1", "mm2"], ["mm0"])
                b.op("act", lambda: nc.scalar.activation(out=mm[:, 0, :], in_=mm[:, 0, :], func=AF.Sqrt), r=["mm0"],
                     w=["mm0"])
                b.op("dve", lambda: V.reciprocal(out=mm[:, 1, :], in_=mm[:, 0, :]), r=["mm0", "mm1"], w=["mm1"])
                tt(mm[:, 2, :], a8i, mm[:, 1, :], ALU.mult, ["A8", "mm1", "mm2"], ["mm2"])
                tt(mm[:, 1, :], a8r, mm[:, 1, :], ALU.mult, ["A8", "mm1"], ["mm1"])
                b.op("dve", lambda: V.tensor_copy(out=RC[:, 0, :, 0], in_=mm[:, 1, :]), r=["mm1"], w=["RC"])
                b.op("dve", lambda: V.tensor_copy(out=RC[:, 1, :, 0], in_=mm[:, 2, :]), r=["mm2"], w=["RC"])
                wdt = 1
                while wdt < NC8:
                    shw = [128, 8, wdt]
                    sr = RC[:, 0, :, wdt - 1].unsqueeze(2).to_broadcast(shw)
                    si = RC[:, 1, :, wdt - 1].unsqueeze(2).to_broadcast(shw)
                    self.cmul(((RC[:, 0, :, wdt:2 * wdt], RC[:, 1, :, wdt:2 * wdt]), "RC"),
                              ((RC[:, 0, :, 0:wdt], RC[:, 1, :, 0:wdt]), "RC"), ((sr, si), "RC"),
                              ((W1[:, :, 0:wdt], W2[:, :, 0:wdt]), "W12"), None)
                    wdt *= 2
                for g in range(16):
                    pair, g2 = divmod(g, 2)
                    rows = slice(g2 * 64, g2 * 64 + 64)
                    for ri in range(2):
                        bi = ri * 4 + pair // 2
                        c0 = (pair % 2) * 256
                        b.op("pe", lambda: nc.tensor.matmul(self.pv(bi, c0, c0 + 256)[rows, :], lhsT=KX[:, d, ri, pair, rows],
                                                            rhs=Uflat[:, g, :], start=True, stop=True),
                             r=["KX", "Uflat"], w=self.pk(bi, 0, 512))
                XR = self.ps[:, 0:2048].rearrange("p (g c) -> p g c", g=8)
                XI = self.ps[:, 2048:4096].rearrange("p (g c) -> p g c", g=8)
                kR = [("ps", i) for i in range(4)]
                kI = [("ps", i) for i in range(4, 8)]
                if d == 0:
                    rcr, rci = RC[:, 0, :, :], RC[:, 1, :, :]
                else:
                    rcr, rci = RC[:, 0, :, ::-1], RC[:, 1, :, ::-1]
                for hc in range(2):
                    cs = slice(hc * 128, (hc + 1) * 128)
                    tt(W1[:], XR[:, :, cs], rcr[:, :, cs], ALU.mult, kR + ["RC"], ["W1"])
                    tt(W2[:], XI[:, :, cs], rci[:, :, cs], ALU.mult, kI + ["RC"], ["W2"])
                    tt(XH[:, 0, :, cs], W1[:], W2[:], ALU.add, ["W1", "W2"], ["XHr"])
                    tt(W1[:], XI[:, :, cs], rcr[:, :, cs], ALU.mult, kI + ["RC", "XHr"], ["W1"])
                    tt(W2[:], XR[:, :, cs], rci[:, :, cs], ALU.mult, kR + ["RC", "XHr"], ["W2"])
                    tt(XH[:, 1, :, cs], W1[:], W2[:], ALU.subtract, ["W1", "W2"], ["XHi"])
                for ri, kk in ((0, "XHr"), (1, "XHi")):
                    for pair in range(8):
                        mb = mm[:, 0, pair:pair + 1].to_broadcast([128, NC8])
                        if d == 0:
                            dat = XH[:, ri, pair, :]
                        else:
                            dat = XH[:, ri, pair, ::-1]
                        b.op("dve", lambda: V.tensor_tensor_scan(out=dat, data0=mb, data1=dat, initial=0.0, op0=ALU.mult,
                                                                 op1=ALU.add), r=[kk, "mm0"], w=[kk])
                sh_ = 1 if d == 0 else -1
                zsl = slice(0, 1) if d == 0 else slice(NC8 - 1, NC8)
                for hc in range(2):
                    c0, c1 = hc * 128, (hc + 1) * 128
                    s0, s1 = c0, c1
                    if d == 0 and hc == 1:
                        s1 = NC8 - 1
                    if d == 1 and hc == 0:
                        s0 = 1
                    cs = slice(c0, c1)
                    wsl = slice(s0 - c0, s1 - c0)
                    dsl = slice(s0 + sh_, s1 + sh_)
                    tt(W1[:], XH[:, 0, :, cs], rcr[:, :, cs], ALU.mult, ["XHr", "RC", ("Hs", d)], ["W1"])
                    tt(W2[:], XH[:, 1, :, cs], rci[:, :, cs], ALU.mult, ["XHi", "RC", ("Hs", d)], ["W2"])
                    tt(Hs[:, d, 0, :, dsl], W1[:, :, wsl], W2[:, :, wsl], ALU.subtract, ["W1", "W2"], [("Hs", d)])
                    tt(W1[:], XH[:, 0, :, cs], rci[:, :, cs], ALU.mult, ["XHr", "RC", ("Hs", d)], ["W1"])
                    tt(W2[:], XH[:, 1, :, cs], rcr[:, :, cs], ALU.mult, ["XHi", "RC", ("Hs", d)], ["W2"])
                    tt(Hs[:, d, 1, :, dsl], W1[:, :, wsl], W2[:, :, wsl], ALU.add, ["W1", "W2"], [("Hs", d)])
                b.op("dve", lambda: V.memset(Hs[:, d, :, :, zsl], 0.0), r=[("Hs", d)], w=[("Hs", d)])
            b.barrier()

        with contextlib.ExitStack() as e4:
            yf = b.sb("s5yf", [128, 2, T], F32, e4)
            nglb = b.sb("s5nglb", [128, 2], F32, e4)
            e4a = contextlib.ExitStack()
            Ysb = b.sb("s5Ysb", [128, 16, NC8], BF16, e4a)
            y2 = b.sb("s5y2", [128, 2, 8, 16, 16], BF16, e4a)
            for g in range(16):
                pair, g2 = divmod(g, 2)
                rows = slice(g2 * 64, g2 * 64 + 64)
                bi = g % 4
                yp = self.pv(bi, 0, 256)
                yk = self.pk(bi, 0, 256)
                b.op("pe", lambda: nc.tensor.matmul(yp, lhsT=Tg[:, g, :], rhs=Uflat[:, g, :], start=True, stop=False),
                     r=[("Tg", g), "Uflat"], w=yk)
                for d in range(2):
                    for ri in range(2):
                        b.op("pe", lambda: nc.tensor.matmul(yp, lhsT=QY[rows, d, ri, pair, :], rhs=Hs[rows, d, ri, pair, :],
                                                            start=False, stop=(d == 1 and ri == 1)),
                             r=["QY", ("Hs", d)], w=yk)
                b.op("act", lambda: nc.scalar.activation(out=Ysb[:, g, :], in_=yp, func=AF.Identity), r=yk, w=[("Ysb", g)])
            cnt = 0
            for half in range(2):
                for g4 in range(4):
                    bi = 4 + cnt % 2
                    cnt += 1
                    tp = self.pv(bi, 0, 256).bitcast(BF16)
                    for gg in range(4):
                        g = g4 * 4 + gg
                        b.op("pe", lambda: nc.tensor.transpose(tp[:, gg * 128:(gg + 1) * 128],
                                                               Ysb[:, g, half * 128:(half + 1) * 128], self.ident[:]),
                             r=[("Ysb", g), "ident"], w=self.pk(bi, 0, 256))
                    b.op("dve", lambda: V.tensor_copy(
                        out=y2[:, half, :, g4 * 4:(g4 + 1) * 4, :].rearrange("p i g k -> p g i k"),
                        in_=tp.rearrange("p (g i k) -> p g i k", g=4, i=8)), r=self.pk(bi, 0, 256), w=[("y2", half)])
            cnt = 0
            for half in range(2):
                for i2 in range(2):
                    bi = 6 + cnt % 2
                    cnt += 1
                    tp = self.pv(bi, 0, 512).bitcast(BF16)
                    for i4 in range(4):
                        ii_ = i2 * 4 + i4
                        for kk in range(2):
                            b.op("pe", lambda: nc.tensor.transpose(
                                tp[:, (i4 * 2 + kk) * 128:(i4 * 2 + kk + 1) * 128],
                                y2[:, half, ii_, kk * 8:(kk + 1) * 8, :].rearrange("p g k -> p (g k)"), self.ident[:]),
                                r=[("y2", half), "ident"], w=self.pk(bi, 0, 512))
                    for kk in range(2):
                        src = tp.rearrange("p (i k c) -> p i k c", i=4, k=2)[:, :, kk, :]
                        base = half * 1024 + i2 * 4
                        dst = yf[:, kk, half * 1024:(half + 1) * 1024].rearrange("p (c i) -> p i c", i=8)[:, i2 * 4:(i2 + 1) * 4, :]
                        b.op("act", lambda: nc.scalar.activation(out=dst, in_=src, func=AF.Identity), r=self.pk(bi, 0, 512),
                             w=["yf"])
            b.barrier()
            e4a.close()
            gt = b.sb("s5gt", [128, 2, T], F32, e4)
            geb = b.sb("s5geb", [128, 2, T], BF16, e4)
            c2 = 2.0 * math.sqrt(2.0 / math.pi)
            tt(gt[:], yf[:], yf[:], ALU.mult, ["yf"], ["gt"])
            ts(gt[:], gt[:], 0.044715, 1.0, ALU.mult, ALU.add, ["gt"], ["gt"])
            tt(gt[:], gt[:], yf[:], ALU.mult, ["gt", "yf"], ["gt"])
            ts(gt[:], gt[:], -30.0, None, ALU.max, ALU.bypass, ["gt"], ["gt"])
            b.op("act", lambda: nc.scalar.activation(out=gt[:], in_=gt[:], func=AF.Exp, scale=-c2), r=["gt"], w=["gt"])
            ts(gt[:], gt[:], 1.0, None, ALU.add, ALU.bypass, ["gt"], ["gt"])
            b.op("dve", lambda: V.reciprocal(out=gt[:], in_=gt[:]), r=["gt"], w=["gt"])
            tt(yf[:], yf[:], gt[:], ALU.mult, ["gt", "yf"], ["yf"])
            b.op("act", lambda: nc.scalar.activation(out=geb[:], in_=yf[:], func=AF.Identity), r=["yf"], w=["geb"])
            ts(nglb[:], glb[:], -1.0, None, ALU.mult, ALU.bypass, ["glb"], ["nglb"])
            for et in range(2):
                for blk in range(NBLK):
                    sl = slice(blk * 512, (blk + 1) * 512)
                    bi = (et * NBLK + blk) % 4
                    zp = self.pv(bi, 0, 512)
                    zk = self.pk(bi, 0, 512)
                    for kk in range(2):
                        b.op("pe", lambda: nc.tensor.matmul(zp, lhsT=gw[:, kk, et * 128:(et + 1) * 128], rhs=geb[:, kk, sl],
                                                            start=(kk == 0), stop=(kk == 1)), r=["gw", "geb"], w=zk)
                    g1 = gt[:, et, sl]
                    ts(g1, zp, glb[:, et:et + 1], None, ALU.add, ALU.bypass, zk + ["glb"], [("g1", et, blk)])
                    b.op("act", lambda: nc.scalar.activation(out=g1, in_=g1, func=AF.Exp, scale=-1.0),
                         r=[("g1", et, blk)], w=[("g1", et, blk)])
                    ts(g1, g1, 1.0, None, ALU.add, ALU.bypass, [("g1", et, blk)], [("g1", et, blk)])
                    b.op("dve", lambda: V.reciprocal(out=g1, in_=g1), r=[("g1", et, blk)], w=[("g1", et, blk)])
                    tt(self.yT[:, 6 + et, sl], yf[:, et, sl], g1, ALU.mult, ["yf", ("g1", et, blk)], [("y", 6 + et, blk)])
            b.barrier()

    def alloc_head_tmps(self, es, groupnorm=False):
        b = self.b
        tmp = {}
        lst = [("Pst", [128, 2, 128], F32), ("sT", [128, 2, 2, 128], BF16), ("osq", [128, 512], BF16),
               ("rstd", [128, 512], F32), ("e1", [128, 512], F32), ("t1", [128, 512], F32)]
        if groupnorm:
            lst += [("o32b", [128, 512], BF16), ("oc", [128, 512], F32)]
        for nm, shape, dt in lst:
            tmp[nm] = b.sb(nm, shape, dt, es)
        return tmp

    def even_mixer(self, j, es):
        b = self.b
        nc = self.nc
        tmp = self.alloc_head_tmps(es)
        wtm = b.sb("wtm", [128, 1, KT, 320], BF16, es)
        wfm = b.sb("wfm", [128, 2, KT, 128], BF16, es)
        wa2p = b.sb("wa2p", [32, 128], BF16, es)
        bah = b.sb("bah", [1, 128], BF16, es)
        lrT = b.sb("lrT", [32, T], BF16, es)
        qT = b.sb("qT", [128, T], BF16, es)
        kT = b.sb("kT", [128, T], BF16, es)
        kt_all = b.sb("kt_all", [128, NCH, 128], BF16, es)
        v_all = b.sb("v_all", [128, NCH, 128], BF16, es)
        v3 = b.sb("v3", [128, NCH, 128], BF16, es)
        S_all = b.sb("S_all", [128, NCH * 4, 128], BF16, es)
        Gall = b.sb("Gall", [128, NCH * 4], F32, es)
        LB = b.sb("LB", [128, 2, 256], F32, es)
        OML = b.sb("OML", [128, 2, 256], F32, es)
        es_lg = contextlib.ExitStack()
        lg = b.sb("lg", [128, 2, 2, 256], F32, es_lg)
        b.op("pool", lambda: nc.gpsimd.memset(S_all[:], 0.0), w=[("S", c, lo) for c in range(NCH * 4) for lo in (0, 64)])
        b.op("pool", lambda: nc.gpsimd.memset(v3[:], 0.0), w=[("v3", c) for c in range(NCH)])
        b.dma("sp", lg[:], self.lbl_d.partition_broadcast(128).rearrange("p (d l k) -> p d l k", d=2, l=2), w=["lg"], stream="c0")
        if j == 0:
            b.op("pool", lambda: nc.gpsimd.memset(LB[:], 0.0), w=["LB"])
            b.op("pool", lambda: nc.gpsimd.memset(OML[:], 1.0), w=["OML"])
        else:
            b.op("dve", lambda: nc.vector.tensor_tensor(out=OML[:], in0=lg[:, :, 0, :], in1=lg[:, :, 1, :], op=ALU.max),
                 r=["lg"], w=["OML"])
            for li in range(2):
                b.op("dve", lambda: nc.vector.tensor_tensor(out=lg[:, :, li, :], in0=lg[:, :, li, :], in1=OML[:],
                                                             op=ALU.subtract), r=["lg", "OML"], w=["lg"])
            b.op("act", lambda: nc.scalar.activation(out=lg[:], in_=lg[:], func=AF.Exp), r=["lg"], w=["lg"])
            b.op("dve", lambda: nc.vector.tensor_tensor(out=OML[:], in0=lg[:, :, 0, :], in1=lg[:, :, 1, :], op=ALU.add),
                 r=["lg"], w=["OML"])
            b.op("dve", lambda: nc.vector.reciprocal(out=OML[:], in_=OML[:]), r=["OML"], w=["OML"])
            b.op("dve", lambda: nc.vector.tensor_tensor(out=LB[:], in0=lg[:, :, 1, :], in1=OML[:], op=ALU.mult),
                 r=["lg", "OML"], w=["LB"])
            b.op("dve", lambda: nc.vector.tensor_scalar(out=OML[:], in0=LB[:], scalar1=-1.0, scalar2=1.0,
                                                        op0=ALU.mult, op1=ALU.add), r=["LB"], w=["OML"])
        b.barrier()
        es_lg.close()
        self.check("e_setup")
        ck = {}
        for nm, shape, dt in (("e", [128, 2, 128], F32), ("l", [128, 2, 128], F32), ("E", [128, 2, 128], F32),
                              ("Ei", [128, 2, 128], F32), ("qt", [128, 2, 128], BF16), ("key", [128, 2, 128], F32),
                              ("qs", [128, 2, 64], F32)):
            ck[nm] = b.sb(nm, shape, dt, es)
        b.dma("pool", wfm[:, 0, :, :], self.w_fm_e_d[j, 8, :, :].rearrange("p (k c) -> p k c", k=KT), w=["wfm"],
              stream="wfm")
        for blk in range(NBLK):
            sl = slice(blk * 512, (blk + 1) * 512)
            for k in range(KT):
                b.op("pe", lambda: nc.tensor.matmul(self.pv(0, 0, 512), lhsT=wfm[:, 0, k, :], rhs=self.hT[:, k, sl],
                                                    start=(k == 0), stop=(k == KT - 1)),
                     r=["wfm", ("h", k, blk)], w=self.pk(0, 0, 512))
            b.op("act", lambda: nc.scalar.activation(out=lrT[:, sl], in_=self.pv(0, 0, 512)[0:32, :], func=AF.Identity),
                 r=self.pk(0, 0, 512), w=["lrT"])
        self.check("e_lrT")
        for h in range(8):
            gla = h < 4
            hh = h % 4
            ncols = 256 if gla else 320
            slot = 0
            b.dma("pool", wtm[:, slot, :, :], self.w_tm_e_d[j, h, :, :].rearrange("p (k c) -> p k c", k=KT),
                  w=[("wtm", slot)], stream=f"wtm{slot}")
            b.dma("pool", wfm[:, 1, :, :], self.w_fm_e_d[j, h, :, :].rearrange("p (k c) -> p k c", k=KT), w=["wfm"],
                  stream="wfm")
            if gla:
                b.dma("pool", wa2p[:], self.wa2p_d[j, hh, :, :], w=["wa2p"], stream="wa2")
                b.dma("pool", bah[:], self.bah_d[j, hh, :, :], w=["bah"], stream="wa2")
            for c in range(NCH):
                ch = slice(c * 128, (c + 1) * 128)
                par = c % 2
                P = self.pv(par, 0, 512)
                Pk = self.pk(par, 0, ncols)
                for k in range(KT):
                    b.op("pe", lambda: nc.tensor.matmul(P[:, 0:ncols], lhsT=self.hT[:, k, ch], rhs=wtm[:, slot, k, 0:ncols],
                                                        start=(k == 0), stop=(k == KT - 1)),
                         r=[("wtm", slot), ("h", k, c // 4)], w=Pk)
                self.check("e_inproj")
                zb = 2 + par
                e, l, E, Ei, qt = (ck[n][:, par, :] for n in ("e", "l", "E", "Ei", "qt"))
                if gla:
                    zv = self.pv(zb, 0, 128)
                    zk = self.pk(zb, 0, 128)
                    b.op("pe", lambda: nc.tensor.matmul(zv, lhsT=lrT[0:32, ch], rhs=wa2p[0:32, :], start=True, stop=False),
                         r=["lrT", "wa2p"], w=zk)
                    self.check("e_z1")
                    b.op("pe", lambda: nc.tensor.matmul(zv, lhsT=self.ones_bf[0:1, :], rhs=bah[0:1, :], start=False,
                                                        stop=True), r=["ones", "bah"], w=zk)
                    self.check("e_z2")
                    b.op("act", lambda: nc.scalar.activation(out=e, in_=zv, func=AF.Exp, scale=-1.0), r=zk, w=[("e", par)])
                    self.check("e_z3")
                    b.op("dve", lambda: nc.vector.tensor_scalar_add(out=e, in0=e, scalar1=1.0), r=[("e", par)], w=[("e", par)])
                    b.op("act", lambda: nc.scalar.activation(out=l, in_=e, func=AF.Ln), r=[("e", par)], w=[("l", par)])
                    mi, s0, ns = 0, 0, 1
                    q_src, q_keys = P[:, 0:64], Pk
                    v_src = P[:, 128:256]
                else:
                    key = ck["key"][:, par, :]
                    qs = ck["qs"][:, par, :]
                    zz = P[:, 64:192]
                    b.op("act", lambda: nc.scalar.activation(out=e, in_=zz, func=AF.Exp, scale=-1.0), r=Pk, w=[("e", par)])
                    b.op("dve", lambda: nc.vector.tensor_scalar_add(out=e, in0=e, scalar1=1.0), r=[("e", par)], w=[("e", par)])
                    b.op("dve", lambda: nc.vector.reciprocal(out=e, in_=e), r=[("e", par)], w=[("e", par)])
                    e3 = e.rearrange("p (d k) -> p d k", d=2)
                    lbv = LB[:, :, hh * 64:(hh + 1) * 64]
                    omv = OML[:, :, hh * 64:(hh + 1) * 64]
                    b.op("dve", lambda: nc.vector.tensor_tensor(out=e3, in0=e3, in1=omv, op=ALU.mult),
                         r=[("e", par), "OML"], w=[("e", par)])
                    b.op("dve", lambda: nc.vector.tensor_tensor(out=e3, in0=e3, in1=lbv, op=ALU.add),
                         r=[("e", par), "LB"], w=[("e", par)])
                    b.op("dve", lambda: nc.vector.tensor_scalar(out=key, in0=e, scalar1=-1.0, scalar2=1.0, op0=ALU.mult,
                                                                op1=ALU.add), r=[("e", par)], w=[("key", par)])
                    b.op("dve", lambda: nc.vector.tensor_scalar_max(out=e, in0=e, scalar1=1e-20), r=[("e", par)],
                         w=[("e", par)])
                    b.op("act", lambda: nc.scalar.activation(out=l, in_=e, func=AF.Ln), r=[("e", par)], w=[("l", par)])
                    b.op("act", lambda: nc.scalar.activation(out=qs, in_=P[:, 0:64], func=AF.Exp, scale=-1.0), r=Pk,
                         w=[("qs", par)])
                    b.op("dve", lambda: nc.vector.tensor_scalar_add(out=qs, in0=qs, scalar1=1.0), r=[("qs", par)],
                         w=[("qs", par)])
                    b.op("dve", lambda: nc.vector.reciprocal(out=qs, in_=qs), r=[("qs", par)], w=[("qs", par)])
                    b.op("dve", lambda: nc.vector.tensor_tensor(out=qs, in0=P[:, 0:64], in1=qs, op=ALU.mult),
                         r=Pk + [("qs", par)], w=[("qs", par)])
                    mi, s0, ns = 2, 4, 4
                    q_src, q_keys = qs, [("qs", par)]
                    v_src = P[:, 192:320]
                self.check("e_z")
                for d in range(2):
                    b.op("pe", lambda: nc.tensor.matmul(self.pv(zb, 128 + d * 64, 192 + d * 64), lhsT=self.tri6[:, mi + d, :],
                                                        rhs=l[:, d * 64:(d + 1) * 64], start=True, stop=True),
                         r=["tri6", ("l", par)], w=self.pk(zb, 128, 256))
                b.op("pe", lambda: nc.tensor.matmul(self.pv(zb, 256, 260), lhsT=l, rhs=self.sumcols[:, s0:s0 + 4],
                                                    start=True, stop=True), r=["sumcols", ("l", par)], w=self.pk(zb, 256, 260))
                b.op("act", lambda: nc.scalar.activation(out=Gall[:, c * ns:(c + 1) * ns], in_=self.pv(zb, 256, 256 + ns),
                                                         func=AF.Exp), r=self.pk(zb, 256, 260), w=["G"])
                self.check("e_cum")
                Cv = self.pv(zb, 128, 256)
                Ck = self.pk(zb, 128, 256)
                b.op("act", lambda: nc.scalar.activation(out=E, in_=Cv, func=AF.Exp), r=Ck, w=[("E", par)])
                self.check("e_E1")
                b.op("act", lambda: nc.scalar.activation(out=Ei, in_=Cv, func=AF.Exp, scale=-1.0), r=Ck, w=[("Ei", par)])
                self.check("e_E2")
                for d in range(2):
                    cs = slice(d * 64, (d + 1) * 64)
                    b.op("dve", lambda: nc.vector.tensor_tensor(out=qt[:, cs], in0=q_src, in1=E[:, cs], op=ALU.mult),
                         r=q_keys + [("E", par)], w=[("qt", par)])
                    self.check("e_E3")
                    if gla:
                        b.op("dve", lambda: nc.vector.scalar_tensor_tensor(
                            out=kt_all[:, c, cs], in0=P[:, 64:128], scalar=0.125, in1=Ei[:, cs], op0=ALU.mult, op1=ALU.mult),
                            r=Pk + [("Ei", par)], w=[("kt", c)])
                if not gla:
                    b.op("dve", lambda: nc.vector.tensor_tensor(out=kt_all[:, c, :], in0=ck["key"][:, par, :], in1=Ei,
                                                                 op=ALU.mult), r=[("key", par), ("Ei", par)], w=[("kt", c)])
                self.check("e_E4")
                b.op("act", lambda: nc.scalar.activation(out=v_all[:, c, :], in_=v_src, func=AF.Identity), r=Pk, w=[("v", c)])
                if not gla:
                    b.op("act", lambda: nc.scalar.activation(out=v3[96:128, c, :], in_=v_src[96:128, :], func=AF.Identity),
                         r=Pk, w=[("v3", c)])
                self.check("e_E")
                tq = self.pv(zb, 384, 448).bitcast(BF16)
                tk = self.pv(zb, 448, 512).bitcast(BF16)
                tkey = self.pk(zb, 384, 512)
                b.op("pe", lambda: nc.tensor.transpose(tq, qt, self.ident[:]), r=[("qt", par), "ident"], w=tkey)
                b.op("pe", lambda: nc.tensor.transpose(tk, kt_all[:, c, :], self.ident[:]), r=[("kt", c), "ident"], w=tkey)
                b.op("act", lambda: nc.scalar.activation(out=qT[:, ch], in_=tq, func=AF.Identity), r=tkey, w=[("qT", c)])
                b.op("dve", lambda: nc.vector.tensor_copy(out=kT[:, ch], in_=tk), r=tkey, w=[("kT", c)])
            self.check("e_p1")
            gain_ap = self.hgain[:, j * 8 + h: j * 8 + h + 1]
            self.head_phase23(es, tmp, qT, kT, kt_all, v_all, v3, S_all, Gall, 64, (1 if gla else 4), wfm[:, 1, :, :], gain_ap, h)
            self.check("e_h%d" % h)


def prep_weights(inp):
    w = {}
    g_all = np.concatenate([inp["mix_norm_g"], inp["ffn_norm_g"], inp["final_norm_g"][None]], 0)
    w["gains"] = np.ascontiguousarray(g_all.reshape(9, KT, 128).transpose(2, 0, 1).reshape(128, 9 * KT))
    wu = inp["ffn_w_up"]
    L = wu.shape[0]
    wu5 = wu.reshape(L, KT, 128, 2, FT, 128)
    w["w_up_t"] = np.ascontiguousarray(wu5.transpose(0, 4, 3, 2, 1, 5).reshape(L, 2 * FT, 128, KT * 128))
    wd = inp["ffn_w_down"]
    wd6 = wd.reshape(L, 2, 11, 128, KT, 128)
    w["w_dn_t"] = np.ascontiguousarray(wd6.transpose(0, 1, 4, 3, 2, 5).reshape(L, 2, KT, 128, 11 * 128))
    cw = inp["ffn_conv_w"]
    cb = inp["ffn_conv_b"]
    c4 = np.concatenate([cw, cb[:, None, :]], 1)
    c4 = c4.reshape(L, 4, 2, FT, 128)
    w["conv_p"] = np.ascontiguousarray(c4.transpose(4, 0, 3, 2, 1).reshape(128, L * 2 * FT * 4))

    wo = np.stack([inp["w_out_even"][0], inp["w_out_odd"][0], inp["w_out_even"][1], inp["w_out_odd"][1]], 0)
    w["w_out_t"] = np.ascontiguousarray(wo.reshape(L, KT, 128, KT, 128).transpose(0, 3, 2, 1, 4).reshape(L, KT, 128, KT * 128))
    wie = inp["w_in_even"]
    o = np.cumsum([0, 256, 256, 512, 512, 16, 16, 256, 256, 256, 512, 512])
    gq, gk, gv, gr, glf, glb, hq, hzf, hzb, hi, hg = (wie[:, :, o[i]:o[i + 1]] for i in range(11))
    tm = np.zeros((2, 8, 1024, 320), np.float32)
    for h in range(4):
        tm[:, h, :, 0:64] = gq[:, :, h * 64:(h + 1) * 64]
        tm[:, h, :, 64:128] = gk[:, :, h * 64:(h + 1) * 64]
        tm[:, h, :, 128:256] = gv[:, :, h * 128:(h + 1) * 128]
        tm[:, 4 + h, :, 0:64] = hq[:, :, h * 64:(h + 1) * 64]
        tm[:, 4 + h, :, 64:128] = hzf[:, :, h * 64:(h + 1) * 64]
        tm[:, 4 + h, :, 128:192] = hzb[:, :, h * 64:(h + 1) * 64]
        tm[:, 4 + h, :, 192:320] = hi[:, :, h * 128:(h + 1) * 128]
    w["w_tm_e"] = np.ascontiguousarray(tm.reshape(2, 8, KT, 128, 320).transpose(0, 1, 3, 2, 4).reshape(2, 8, 128, KT * 320))
    fm = np.zeros((2, 9, 1024, 128), np.float32)
    for h in range(4):
        fm[:, h] = gr[:, :, h * 128:(h + 1) * 128]
        fm[:, 4 + h] = hg[:, :, h * 128:(h + 1) * 128]
    fm[:, 8, :, 0:16] = glf
    fm[:, 8, :, 16:32] = glb
    w["w_fm_e"] = np.ascontiguousarray(fm.reshape(2, 9, KT, 128, 128).transpose(0, 1, 3, 2, 4).reshape(2, 9, 128, KT * 128))
    wa2 = inp["gla_wa2"]
    ba = inp["gla_ba"]
    wa2p = np.zeros((2, 4, 32, 128), np.float32)
    bah = np.zeros((2, 4, 1, 128), np.float32)
    for h in range(4):
        wa2p[:, h, 0:16, 0:64] = wa2[:, 0, :, h * 64:(h + 1) * 64]
        wa2p[:, h, 16:32, 64:128] = wa2[:, 1, :, h * 64:(h + 1) * 64]
        bah[:, h, 0, 0:64] = ba[:, 0, h * 64:(h + 1) * 64]
        bah[:, h, 0, 64:128] = ba[:, 1, h * 64:(h + 1) * 64]
    w["wa2p"] = wa2p
    w["bah"] = bah
    w["lbl"] = np.ascontiguousarray(inp["hgrn_lb_logits"].reshape(-1))
    hg_ = np.zeros((128, 16), np.float32)
    for j in range(2):
        for h in range(4):
            hg_[:, j * 8 + h] = inp["gla_norm_g"][j, h * 128:(h + 1) * 128]
            hg_[:, j * 8 + 4 + h] = inp["hgrn_norm_g"][j, h * 128:(h + 1) * 128]
    w["hgain"] = hg_
    ii = np.arange(128)
    triL = (ii[:, None] <= ii[None, :]).astype(np.float32)
    triU = (ii[:, None] >= ii[None, :]).astype(np.float32)
    blk32 = (ii[:, None] // 32 == ii[None, :] // 32).astype(np.float32)
    w["tri6"] = np.ascontiguousarray(np.stack([triL * (-1.0 / 16), triU * (-1.0 / 16), triL * blk32, triU * blk32,
                                               triL, triU], 1).reshape(128, 768))
    sc = np.zeros((128, 8), np.float32)
    sc[:, 0:4] = -1.0 / 16
    for q_ in range(4):
        sc[32 * q_:32 * (q_ + 1), 4 + q_] = 1.0
    w["sumcols"] = sc
    w["ident"] = np.eye(128, dtype=np.float32)

    wio = inp["w_in_odd"]
    oo = np.cumsum([0, 512, 512, 768, 768, 256])
    rq, rk, rv, rg, su = (wio[:, :, oo[i]:oo[i + 1]] for i in range(5))
    tmo = np.zeros((2, 4, 1024, 448), np.float32)
    for h in range(4):
        tmo[:, h, :, 0:128] = rq[:, :, h * 128:(h + 1) * 128]
        tmo[:, h, :, 128:256] = rk[:, :, h * 128:(h + 1) * 128]
        tmo[:, h, :, 256:448] = rv[:, :, h * 192:(h + 1) * 192]
    w["w_tm_o"] = np.ascontiguousarray(tmo.reshape(2, 4, KT, 128, 448).transpose(0, 1, 3, 2, 4).reshape(2, 4, 128, KT * 448))
    fmo = np.zeros((2, 8, 1024, 128), np.float32)
    hgo = np.zeros((128, 16), np.float32)
    for h in range(4):
        for pi, (f0, nf, etile, p0) in enumerate(Prog.RET_PIECES[h]):
            fmo[:, 2 * h + pi, :, 0:nf] = rg[:, :, h * 192 + f0: h * 192 + f0 + nf]
            for j in range(2):
                hgo[p0:p0 + nf, j * 8 + 2 * h + pi] = inp["ret_norm_g"][j, h * 192 + f0: h * 192 + f0 + nf]
    w["w_fm_o"] = np.ascontiguousarray(fmo.reshape(2, 8, KT, 128, 128).transpose(0, 1, 3, 2, 4).reshape(2, 8, 128, KT * 128))
    w["hgain_o"] = hgo
    w["w_u_o"] = np.ascontiguousarray(su.reshape(2, KT, 128, 256).transpose(0, 2, 1, 3).reshape(2, 128, KT * 256))
    half = 64
    inv = (10000.0 ** (-np.arange(half, dtype=np.float32) / half)).astype(np.float32)
    ang = (np.arange(T, dtype=np.float32)[:, None] * inv[None, :]).astype(np.float32)
    cs = np.stack([np.cos(ang), np.sin(ang)], 0).reshape(2, NCH, 128, half)
    w["rope"] = np.ascontiguousarray(cs.transpose(2, 0, 1, 3).reshape(128, 2 * NCH * half)).astype(np.float32)
    ti = np.arange(128, dtype=np.float64)
    rdec = np.zeros((128, 16), np.float64)
    dm = np.zeros((128, 4, 128), np.float64)
    for h in range(4):
        gf = 1.0 - 2.0 ** (-5.0 - h)
        gb = 1.0 - 2.0 ** (-5.5 - h)
        rdec[:, 4 * h + 0] = gf ** (ti + 1)
        rdec[:, 4 * h + 1] = gb ** (128 - ti)
        rdec[:, 4 * h + 2] = gf ** (127 - ti) * 128.0 ** -0.5
        rdec[:, 4 * h + 3] = gb ** ti * 128.0 ** -0.5
        dji = ti[None, :] - ti[:, None]
        dm[:, h, :] = np.where(dji >= 0, gf ** np.abs(dji), 0.0) + np.where(dji <= 0, gb ** np.abs(dji), 0.0)
    w["rdec"] = rdec.astype(np.float32)
    w["dmask"] = np.ascontiguousarray(dm.reshape(128, 512)).astype(np.float32)

    def qp(a):
        sh = a.shape
        a = a.reshape(sh[:-3] + (8, 2, 64, sh[-1]))
        nd = a.ndim
        perm = (nd - 3, nd - 2) + tuple(range(nd - 4)) + (nd - 4, nd - 1)
        a = a.transpose(perm)
        return a.reshape((128,) + a.shape[2:])
    prm = np.zeros((2, 128, 3, 2, 8), np.float32)
    bc = np.zeros((2, 128, 4, 8, 16), np.float32)
    for j in range(2):
        prm[j, :, 0] = qp(inp["s5_lam_re"][j][..., None])[..., 0]
        prm[j, :, 1] = qp(inp["s5_lam_im"][j][..., None])[..., 0]
        ldt = np.broadcast_to(inp["s5_log_dt"][j][:, :, None, None], (2, 16, 64, 1))
        prm[j, :, 2] = qp(np.ascontiguousarray(ldt))[..., 0]
        bc[j, :, 0] = qp(inp["s5_b_re"][j])
        bc[j, :, 1] = qp(inp["s5_b_im"][j])
        bc[j, :, 2] = qp(np.ascontiguousarray(inp["s5_c_re"][j].transpose(0, 2, 1)))
        bc[j, :, 3] = qp(np.ascontiguousarray(inp["s5_c_im"][j].transpose(0, 2, 1)))
    w["s5_prm"] = prm.reshape(2, 128, 48)
    w["s5_bc"] = bc.reshape(2, 128, 512)
    dsk = inp["s5_d"].reshape(2, 16, 16)
    w["s5_dsk"] = np.ascontiguousarray(np.broadcast_to(dsk.transpose(0, 2, 1)[:, None, :, :], (2, 8, 16, 16)).reshape(2, 128, 16))
    w["s5_glb"] = np.ascontiguousarray(inp["s5_glu_b"].reshape(2, 2, 128).transpose(0, 2, 1))
    w["s5_gw"] = np.ascontiguousarray(inp["s5_glu_w"].reshape(2, 2, 128, 256).transpose(0, 2, 1, 3).reshape(2, 128, 512))
    jj_ = np.arange(128) // 16
    bm = np.stack([(jj_[:, None] <= jj_[None, :]), (jj_[:, None] >= jj_[None, :])], 1).astype(np.float32)
    w["s5_bm"] = np.ascontiguousarray(bm.reshape(128, 256))
    return w


_CFG = {}


def kernel(**inputs):
    inp = {k: np.asarray(v) for k, v in inputs.items()}
    cfg = dict(_CFG)
    x = inp["x"]
    w = prep_weights(inp)
    import time as _t
    _t0 = _t.time()
    prog = Prog(cfg)
    nc = prog.build()
    print("[kernel] build %.1fs, instr counts %s" % (_t.time() - _t0, dict(prog.b.cnt)), flush=True)
    in_maps = []
    ncores = cfg.get("ncores", NCORES)
    for c in range(ncores):
        xs = x[c * NSEQ:(c + 1) * NSEQ]
        xl = np.ascontiguousarray(xs.reshape(NSEQ, T, KT, 128).transpose(0, 3, 2, 1))
        m = {"x_in": xl}
        m.update(w)
        in_maps.append(m)
    _t0 = _t.time()
    res = run_bass_kernel_spmd(nc, in_maps, core_ids=list(range(ncores)))
    print("[kernel] run %.1fs" % (_t.time() - _t0), flush=True)
    outs = []
    for c in range(ncores):
        y = res.results[c]["y_out"]
        outs.append(np.ascontiguousarray(y.transpose(0, 3, 2, 1)).reshape(NSEQ, T, D))
    return np.concatenate(outs, 0).astype(np.float32)
```

```python
import contextlib
import math
import numpy as np
import concourse.bass as bass
import concourse.mybir as mybir
from concourse.bass_utils import run_bass_kernel_spmd

F32 = mybir.dt.float32
BF16 = mybir.dt.bfloat16
AF = mybir.ActivationFunctionType
ALU = mybir.AluOpType

D = 1024
T = 2048
KT = D // 128
NBLK = T // 512
NCH = T // 128
DEPTH = 4
FF = 2816
FT = FF // 128
NSEQ = 2
NCORES = 8
EPS = 1e-6
SAME_ENG_SYNC = True


class Builder:
    def __init__(self):
        self.nc = bass.Bass("TRN2", target_bir_lowering=False, dynamic_dma_scratch_size=4096)
        nc = self.nc
        self.es = contextlib.ExitStack()
        self.eng = dict(pe=nc.tensor, act=nc.scalar, dve=nc.vector, pool=nc.gpsimd, sp=nc.sync)
        self.sem = {e: self.es.enter_context(nc.semaphore("s_" + e)) for e in self.eng}
        self.cnt = {e: 0 for e in self.eng}
        self.seen = {e: {} for e in self.eng}
        self.parts = {}
        self.streams = {}
        self.uid = 0
        self.muted = False

    def sb(self, name, shape, dt, es=None):
        self.uid += 1
        return (es or self.es).enter_context(self.nc.sbuf_tensor(f"{name}_{self.uid}", list(shape), dt))

    def dram_in(self, name, shape, dt=F32):
        return self.nc.dram_tensor(name, list(shape), dt, kind="ExternalInput").ap()

    def dram_out(self, name, shape, dt=F32):
        return self.nc.dram_tensor(name, list(shape), dt, kind="ExternalOutput").ap()

    def _part(self, k):
        p = self.parts.get(k)
        if p is None:
            p = [[], []]
            self.parts[k] = p
        return p

    def _wait(self, e, tickets):
        need = {}
        for (key, h, v, src) in tickets:
            if src == e and (e == "pe" or not SAME_ENG_SYNC):
                continue
            if src is None:
                v = 16 * self.streams[key[2:]][1]
            if self.seen[e].get(key, 0) >= v:
                continue
            if key not in need or need[key][1] < v:
                need[key] = (h, v)
        for key, (h, v) in need.items():
            self.eng[e].wait_ge(h, v)
            self.seen[e][key] = v

    def _deps(self, r, w):
        deps = []
        for k in r:
            deps += self._part(k)[0]
        for k in w:
            p = self._part(k)
            deps += p[0] + p[1]
        return deps

    def _record(self, t, r, w):
        for k in r:
            p = self._part(k)
            p[1] = [x for x in p[1] if x[0] != t[0]] + [t]
        for k in w:
            p = self._part(k)
            p[0] = [t]
            p[1] = []

    def op(self, e, fn, r=(), w=()):
        if self.muted:
            return None
        pr = [k for k in r if isinstance(k, tuple) and k[0] in ("ps", "bank")]
        if pr:
            r = [k for k in r if k not in pr]
            w = list(w) + [k for k in pr if k not in w]
        self._wait(e, self._deps(r, w))
        ins = fn()
        self.cnt[e] += 1
        ins.then_inc(self.sem[e], 1)
        self._record(("e_" + e, self.sem[e], self.cnt[e], e), r, w)
        return ins

    def dma(self, q, out, in_, r=(), w=(), stream="d0"):
        if self.muted:
            return
        st = self.streams.get(stream)
        if st is None:
            st = [self.es.enter_context(self.nc.semaphore("d_" + stream)), 0]
            self.streams[stream] = st
        self._wait(q, self._deps(r, w))
        ins = self.eng[q].dma_start(out=out, in_=in_)
        st[1] += 1
        ins.then_inc(st[0], 16)
        self._record(("d_" + stream, st[0], 16 * st[1], None), r, w)

    def barrier(self):
        ts = [("e_" + e, self.sem[e], self.cnt[e], e) for e in self.eng if self.cnt[e] > 0]
        ts += [("d_" + s, st[0], 16 * st[1], None) for s, st in self.streams.items() if st[1] > 0]
        for e in self.eng:
            self._wait(e, [t for t in ts if t[3] != e])
        self.parts = {}

    def finish(self):
        self.barrier()
        self.es.close()


class StopBuild(Exception):
    pass


class Prog:
    def check(self, tag):
        if self.cfg.get("stop") == tag:
            self.b.muted = True

    def __init__(self, cfg):
        self.cfg = cfg
        self.b = Builder()
        b = self.b
        nc = b.nc
        self.nc = nc
        self.x_in = b.dram_in("x_in", [NSEQ, 128, KT, T])
        self.y_out = b.dram_out("y_out", [NSEQ, 128, KT, T])
        self.gains_d = b.dram_in("gains", [128, 9 * KT])
        self.w_up_d = b.dram_in("w_up_t", [DEPTH, 2 * FT, 128, KT * 128])
        self.w_dn_d = b.dram_in("w_dn_t", [DEPTH, 2, KT, 128, 11 * 128])
        self.conv_d = b.dram_in("conv_p", [128, DEPTH * 2 * FT * 4])
        self.w_out_d = b.dram_in("w_out_t", [DEPTH, KT, 128, KT * 128])
        self.w_tm_e_d = b.dram_in("w_tm_e", [2, 8, 128, KT * 320])
        self.w_fm_e_d = b.dram_in("w_fm_e", [2, 9, 128, KT * 128])
        self.wa2p_d = b.dram_in("wa2p", [2, 4, 32, 128])
        self.bah_d = b.dram_in("bah", [2, 4, 1, 128])
        self.lbl_d = b.dram_in("lbl", [2 * 2 * 256])
        self.hgain_d = b.dram_in("hgain", [128, 16])
        self.tri6_d = b.dram_in("tri6", [128, 6 * 128])
        self.w_tm_o_d = b.dram_in("w_tm_o", [2, 4, 128, KT * 448])
        self.w_fm_o_d = b.dram_in("w_fm_o", [2, 8, 128, KT * 128])
        self.w_u_o_d = b.dram_in("w_u_o", [2, 128, KT * 256])
        self.hgain_o_d = b.dram_in("hgain_o", [128, 16])
        self.rope_d = b.dram_in("rope", [128, 2 * NCH * 64])
        self.rdec_d = b.dram_in("rdec", [128, 16])
        self.dmask_d = b.dram_in("dmask", [128, 4 * 128])
        self.s5_gw_d = b.dram_in("s5_gw", [2, 128, 2 * 256])
        self.s5_prm_d = b.dram_in("s5_prm", [2, 128, 3 * 2 * 8])
        self.s5_bc_d = b.dram_in("s5_bc", [2, 128, 4 * 8 * 16])
        self.s5_dsk_d = b.dram_in("s5_dsk", [2, 128, 16])
        self.s5_glb_d = b.dram_in("s5_glb", [2, 128, 2])
        self.s5_bm_d = b.dram_in("s5_bm", [128, 2 * 128])
        self.sumcols_d = b.dram_in("sumcols", [128, 8])
        self.ident_d = b.dram_in("ident", [128, 128])
        self.xT = b.sb("xT", [128, KT, T], F32)
        self.hT = b.sb("hT", [128, KT, T], BF16)
        self.gains = b.sb("gains", [128, 9 * KT], F32)
        self.ones_bf = b.sb("ones", [128, 128], BF16)
        self.hgain = b.sb("hgain", [128, 16], F32)
        self.hgain_o = b.sb("hgain_o", [128, 16], F32)
        self.tri6 = b.sb("tri6", [128, 6, 128], F32)
        self.sumcols = b.sb("sumcols", [128, 8], F32)
        self.ident = b.sb("ident", [128, 128], BF16)
        self.one_t = b.sb("one", [128, 1], F32)
        self.ps = b.es.enter_context(nc.psum_tensor("ps_all", [128, 4096], F32))
        self.bankrr = 0

    def bank(self, i):
        return self.ps[:, i * 512:(i + 1) * 512]


    def pk(self, bank, c0, c1):
        return [("ps", bank)]

    def pv(self, bank, c0, c1):
        return self.ps[:, bank * 512 + c0: bank * 512 + c1]

    def next_bank(self):
        i = self.bankrr
        self.bankrr = (self.bankrr + 1) % 8
        return i

    def setup(self):
        b = self.b
        nc = self.nc
        b.dma("sp", self.gains[:], self.gains_d[:, :], w=["gains"], stream="c0")
        b.op("pool", lambda: nc.gpsimd.memset(self.ones_bf[:], 1.0), w=["ones"])
        b.op("pool", lambda: nc.gpsimd.memset(self.one_t[:], 1.0), w=["one"])
        b.dma("sp", self.hgain[:], self.hgain_d[:, :], w=["hgain"], stream="c0")
        b.dma("sp", self.hgain_o[:], self.hgain_o_d[:, :], w=["hgain_o"], stream="c0")
        b.dma("sp", self.tri6[:], self.tri6_d[:, :].rearrange("p (a c) -> p a c", a=6), w=["tri6"], stream="c0")
        b.dma("sp", self.sumcols[:], self.sumcols_d[:, :], w=["sumcols"], stream="c0")
        b.dma("pool", self.ident[:], self.ident_d[:, :], w=["ident"], stream="c1")

    def load_x(self, s):
        b = self.b
        for k in range(KT):
            b.dma("sp", self.xT[:, k, :], self.x_in[s, :, k, :], w=[("x", k, blk) for blk in range(NBLK)],
                  stream="xin")

    def store_x(self, s):
        b = self.b
        for k in range(KT):
            b.dma("sp", self.y_out[s, :, k, :], self.xT[:, k, :], r=[("x", k, blk) for blk in range(NBLK)],
                  stream="xout")

    def rmsnorm(self, gidx, out_f32_inplace=False):
        b = self.b
        nc = self.nc
        with contextlib.ExitStack() as es:
            sq = b.sb("sq", [128, 2, KT, 512], BF16, es)
            rs = b.sb("rstd", [128, 2, 512], F32, es)
            for blk in range(NBLK):
                par = blk % 2
                sl = slice(blk * 512, (blk + 1) * 512)
                xk = [("x", k, blk) for k in range(KT)]
                b.op("act", lambda: nc.scalar.activation(out=sq[:, par, :, :], in_=self.xT[:, :, sl], func=AF.Square),
                     r=xk, w=[("sq", par)])
                bi = self.next_bank()
                for k in range(KT):
                    b.op("pe", lambda: nc.tensor.matmul(self.bank(bi), lhsT=self.ones_bf[:], rhs=sq[:, par, k, :],
                                                        start=(k == 0), stop=(k == KT - 1)),
                         r=[("sq", par), "ones"], w=[("bank", bi)])
                b.op("act", lambda: nc.scalar.activation(out=rs[:, par, :], in_=self.bank(bi), func=AF.Sqrt,
                                                         scale=1.0 / D, bias=self.eps_t[:, 0:1]),
                     r=[("bank", bi), "eps"], w=[("rs", par)])
                b.op("dve", lambda: nc.vector.reciprocal(out=rs[:, par, :], in_=rs[:, par, :]),
                     r=[("rs", par)], w=[("rs", par)])
                for k in range(KT):
                    g = self.gains[:, gidx * KT + k: gidx * KT + k + 1]
                    if out_f32_inplace:
                        b.op("dve", lambda: nc.vector.scalar_tensor_tensor(
                            out=self.xT[:, k, sl], in0=self.xT[:, k, sl], scalar=g, in1=rs[:, par, :],
                            op0=ALU.mult, op1=ALU.mult), r=[("x", k, blk), ("rs", par), "gains"], w=[("x", k, blk)])
                    else:
                        b.op("dve", lambda: nc.vector.scalar_tensor_tensor(
                            out=self.hT[:, k, sl], in0=self.xT[:, k, sl], scalar=g, in1=rs[:, par, :],
                            op0=ALU.mult, op1=ALU.mult), r=[("x", k, blk), ("rs", par), "gains"], w=[("h", k, blk)])
            b.barrier()

    def ffn(self, layer):
        b = self.b
        nc = self.nc
        self.rmsnorm(DEPTH + layer)
        hall = [("h", k, blk) for k in range(KT) for blk in range(NBLK)]
        with contextlib.ExitStack() as es:
            g = b.sb("ffg", [128, 11, T], BF16, es)
            wup = b.sb("wup", [128, 3, KT, 128], BF16, es)
            wdn = b.sb("wdn", [128, 2, 11, 128], BF16, es)
            cbuf = b.sb("cbuf", [128, 2, T], F32, es)
            sbuf = b.sb("sbuf", [128, T], BF16, es)
            cp = b.sb("convp", [128, 2 * FT * 4], F32, es)
            b.dma("sp", cp[:], self.conv_d[:, layer * 2 * FT * 4:(layer + 1) * 2 * FT * 4], w=["convp"], stream="c0")
            ucount = 0
            dcount = 0
            for half in range(2):
                for mm in range(11):
                    m = half * 11 + mm
                    for kind in range(2):
                        unit = 2 * m + kind
                        slot = ucount % 3
                        par = ucount % 2
                        ucount += 1
                        b.dma("pool", wup[:, slot, :, :], self.w_up_d[layer, unit, :, :].rearrange("p (k c) -> p k c", k=KT),
                              w=[("wup", slot)], stream=f"wup{slot}")
                        banks = [4 * par + i for i in range(4)]
                        for blk in range(NBLK):
                            for k in range(KT):
                                b.op("pe", lambda: nc.tensor.matmul(
                                    self.bank(banks[blk]), lhsT=wup[:, slot, k, :],
                                    rhs=self.hT[:, k, blk * 512:(blk + 1) * 512],
                                    start=(k == 0), stop=(k == KT - 1)),
                                    r=[("wup", slot), ("h", k, blk)], w=[("bank", banks[blk])])
                        u = self.ps[:, 2048 * par: 2048 * (par + 1)]
                        base = unit * 4
                        w0, w1, w2, cb = (cp[:, base + i: base + i + 1] for i in range(4))
                        c = cbuf[:, par, :]
                        H = T // 2
                        for hf in range(2):
                            t0, t1 = hf * H, (hf + 1) * H
                            bk = [("bank", banks[2 * hf]), ("bank", banks[2 * hf + 1])]
                            ck_ = ("c", par, hf)
                            b.op("act", lambda: nc.scalar.activation(out=c[:, t0:t1], in_=u[:, t0:t1], func=AF.Identity,
                                                                     scale=w1, bias=cb), r=bk + ["convp"], w=[ck_])
                            a0 = max(t0, 1)
                            bkl = bk + ([("bank", banks[1])] if hf == 1 else [])
                            b.op("dve", lambda: nc.vector.scalar_tensor_tensor(
                                out=c[:, a0:t1], in0=u[:, a0 - 1:t1 - 1], scalar=w0, in1=c[:, a0:t1], op0=ALU.mult,
                                op1=ALU.add), r=bkl + ["convp", ck_], w=[ck_])
                            a1 = min(t1, T - 1)
                            bkr = bk + ([("bank", banks[2])] if hf == 0 else [])
                            b.op("dve", lambda: nc.vector.scalar_tensor_tensor(
                                out=c[:, t0:a1], in0=u[:, t0 + 1:a1 + 1], scalar=w2, in1=c[:, t0:a1], op0=ALU.mult,
                                op1=ALU.add), r=bkr + ["convp", ck_], w=[ck_])
                            if kind == 0:
                                b.op("act", lambda: nc.scalar.activation(out=sbuf[:, t0:t1], in_=c[:, t0:t1], func=AF.Silu),
                                     r=[ck_], w=[("s", hf)])
                            else:
                                b.op("pool", lambda: nc.gpsimd.tensor_tensor(out=g[:, mm, t0:t1], in0=sbuf[:, t0:t1],
                                                                               in1=c[:, t0:t1], op=ALU.mult),
                                     r=[ck_, ("s", hf)], w=[("g", mm, hf)])
                for dt in range(KT):
                    slot = dcount % 2
                    dcount += 1
                    b.dma("pool", wdn[:, slot, :, :], self.w_dn_d[layer, half, dt, :, :].rearrange("p (k c) -> p k c", k=11),
                          w=[("wdn", slot)], stream=f"wdn{slot}")
                    for blk in range(NBLK):
                        bi = self.next_bank()
                        for kk in range(11):
                            b.op("pe", lambda: nc.tensor.matmul(
                                self.bank(bi), lhsT=wdn[:, slot, kk, :], rhs=g[:, kk, blk * 512:(blk + 1) * 512],
                                start=(kk == 0), stop=(kk == 10)),
                                r=[("wdn", slot), ("g", kk, blk // 2)], w=[("bank", bi)])
                        sl = slice(blk * 512, (blk + 1) * 512)
                        b.op("dve", lambda: nc.vector.tensor_tensor(out=self.xT[:, dt, sl], in0=self.bank(bi),
                                                                     in1=self.xT[:, dt, sl], op=ALU.add),
                             r=[("bank", bi), ("x", dt, blk)], w=[("x", dt, blk)])
            b.barrier()

    def build(self):
        b = self.b
        nc = self.nc
        cfg = self.cfg
        self.eps_t = b.sb("eps", [128, 1], F32)
        b.op("pool", lambda: nc.gpsimd.memset(self.eps_t[:], EPS), w=["eps"])
        self.setup()
        for s in range(cfg.get("nseq", NSEQ)):
            self.load_x(s)
            try:
                for layer in cfg.get("layers", range(DEPTH)):
                    if "mix" in cfg.get("phases", ("mix", "ffn")):
                        self.mixer(layer)
                    if "ffn" in cfg.get("phases", ("mix", "ffn")):
                        self.ffn(layer)
            except StopBuild:
                pass
            b.muted = False
            b.barrier()
            if cfg.get("final", True):
                self.rmsnorm(2 * DEPTH, out_f32_inplace=True)
            self.store_x(s)
            b.barrier()
        b.finish()
        return nc


    def mixer(self, layer):
        self.rmsnorm(layer)
        with contextlib.ExitStack() as es:
            self.yT = self.b.sb("yT", [128, KT, T], BF16, es)
            if layer % 2 == 0:
                self.even_mixer(layer // 2, es)
            else:
                self.odd_mixer(layer // 2, es)
            if self.cfg.get("dbg_y"):
                for k in self.cfg.get("dbg_tiles", range(KT)):
                    self.b.op("act", lambda: self.nc.scalar.activation(out=self.xT[:, k, :], in_=self.yT[:, k, :], func=AF.Identity),
                              r=[("y", k, blk) for blk in range(NBLK)], w=[("x", k, blk) for blk in range(NBLK)])
            else:
                self.out_proj(layer, es)
            self.b.barrier()

    def out_proj(self, layer, es):
        b = self.b
        nc = self.nc
        wo = b.sb("wo", [128, 2, KT, 128], BF16, es)
        for dt in range(KT):
            slot = dt % 2
            b.dma("pool", wo[:, slot, :, :], self.w_out_d[layer, dt, :, :].rearrange("p (k c) -> p k c", k=KT),
                  w=[("wo", slot)], stream=f"wo{slot}")
            for blk in range(NBLK):
                bi = dt % 2
                for k in range(KT):
                    b.op("pe", lambda: nc.tensor.matmul(self.pv(bi, 0, 512), lhsT=wo[:, slot, k, :],
                                                        rhs=self.yT[:, k, blk * 512:(blk + 1) * 512],
                                                        start=(k == 0), stop=(k == KT - 1)),
                         r=[("wo", slot), ("y", k, blk)], w=self.pk(bi, 0, 512))
                sl = slice(blk * 512, (blk + 1) * 512)
                b.op("dve", lambda: nc.vector.tensor_tensor(out=self.xT[:, dt, sl], in0=self.pv(bi, 0, 512),
                                                             in1=self.xT[:, dt, sl], op=ALU.add),
                     r=self.pk(bi, 0, 512) + [("x", dt, blk)], w=[("x", dt, blk)])

    def head_phase23(self, es, tmp, qT, kT, kt_all, v_all, v3, S_all, Gall, dk, nsub, gate_w, gain_ap, etile,
                     groupnorm=False):
        b = self.b
        nc = self.nc
        Pst = tmp["Pst"]
        sub = 128 // nsub
        nsc = NCH * nsub

        def kv_ops(g):
            c, s_ = divmod(g, nsub)
            if nsub == 1:
                return c, slice(0, 128), v_all, ("v", c)
            if s_ < 3:
                return c, slice(32 * s_, 32 * s_ + 32), v_all, ("v", c)
            return c, slice(64, 128), v3, ("v3", c)

        for step in range(nsc - 1):
            for d, lo in ((0, 0), (1, dk)):
                g = step if d == 0 else nsc - 1 - step
                nxt = 1 if d == 0 else -1
                c, rows, vv, vkey = kv_ops(g)
                bank = (4 if d == 0 else 6) + step % 2
                kv = self.pv(bank, 0, 128)[lo:lo + dk, :]
                kvk = self.pk(bank, 0, 128)
                prow = slice(lo, lo + dk)
                b.op("pe", lambda: nc.tensor.matmul(kv, lhsT=kt_all[rows, c, lo:lo + dk], rhs=vv[rows, c, :],
                                                    start=True, stop=True), r=[("kt", c), vkey], w=kvk)
                pp = step % 2
                pkey = ("Pst", lo, pp)
                pold = ("Pst", lo, 1 - pp)
                if step == 0:
                    b.op("dve", lambda: nc.vector.tensor_copy(out=Pst[prow, pp, :], in_=kv), r=kvk, w=[pkey])
                else:
                    gprev = Gall[prow, g - nxt: g - nxt + 1]
                    b.op("dve", lambda: nc.vector.scalar_tensor_tensor(
                        out=Pst[prow, pp, :], in0=Pst[prow, 1 - pp, :], scalar=gprev, in1=kv, op0=ALU.mult, op1=ALU.add),
                        r=kvk + [pold, "G"], w=[pkey])
                b.op("act", lambda: nc.scalar.activation(out=S_all[prow, g + nxt, :], in_=Pst[prow, pp, :], func=AF.Identity,
                                                         scale=Gall[prow, g: g + 1]),
                     r=[pkey, "G"], w=[("S", g + nxt, lo)])
        self.check("e_p2")
        mF = 4 if nsub == 1 else 2
        maskF = self.tri6[:, mF, :]
        maskB = self.tri6[:, mF + 1, :]
        K2 = 2 * dk
        for blk in range(NBLK):
            sl = slice(blk * 512, (blk + 1) * 512)
            for k in range(KT):
                b.op("pe", lambda: nc.tensor.matmul(self.pv(2, 0, 512), lhsT=gate_w[:, k, :], rhs=self.hT[:, k, sl],
                                                    start=(k == 0), stop=(k == KT - 1)),
                     r=["wfm", ("h", k, blk)], w=self.pk(2, 0, 512))
            for cc in range(4):
                c = blk * 4 + cc
                ch = slice(c * 128, (c + 1) * 128)
                par = c % 2
                sbs = (0, 4) if par == 0 else (3, 6)
                for d, lo in ((0, 0), (1, dk)):
                    b.op("pe", lambda: nc.tensor.matmul(self.pv(sbs[d], 0, 128),
                                                        lhsT=kT[lo:lo + dk, ch], rhs=qT[lo:lo + dk, ch],
                                                        start=True, stop=True),
                         r=[("kT", c), ("qT", c)], w=self.pk(sbs[d], 0, 128))
                for d, lo in ((0, 0), (1, dk)):
                    b.op("dve", lambda: nc.vector.tensor_tensor(
                        out=tmp["sT"][:, par, d, :], in0=self.pv(sbs[d], 0, 128),
                        in1=(maskF if d == 0 else maskB), op=ALU.mult),
                        r=self.pk(sbs[d], 0, 128) + ["tri6"], w=[("sT", par, d)])
                ov = self.pv(1, cc * 128, cc * 128 + 128)
                ok = self.pk(1, cc * 128, cc * 128 + 128)
                b.op("pe", lambda: nc.tensor.matmul(ov, lhsT=v_all[:, c, :], rhs=tmp["sT"][:, par, 0, :],
                                                    start=True, stop=False), r=[("v", c), ("sT", par, 0)], w=ok)
                b.op("pe", lambda: nc.tensor.matmul(ov, lhsT=v_all[:, c, :], rhs=tmp["sT"][:, par, 1, :],
                                                    start=False, stop=False), r=[("v", c), ("sT", par, 1)], w=ok)
                for s_ in range(nsub):
                    g = c * nsub + s_
                    b.op("pe", lambda: nc.tensor.matmul(ov[:, s_ * sub:(s_ + 1) * sub], lhsT=S_all[0:K2, g, :],
                                                        rhs=qT[0:K2, c * 128 + s_ * sub: c * 128 + (s_ + 1) * sub],
                                                        start=False, stop=(s_ == nsub - 1)),
                         r=[("S", g, 0), ("S", g, dk), ("qT", c)], w=ok)
            self.head_norm_gate(tmp, blk, gain_ap, etile, groupnorm, 128)

    def head_norm_gate(self, tmp, blk, gain_ap, etile, groupnorm, dv):
        b = self.b
        nc = self.nc
        sl = slice(blk * 512, (blk + 1) * 512)
        o = self.pv(1, 0, 512)
        ok = self.pk(1, 0, 512)
        osq, rstd, e1, t1 = tmp["osq"], tmp["rstd"], tmp["e1"], tmp["t1"]
        src = o
        srck = ok
        if groupnorm:
            b.op("act", lambda: nc.scalar.activation(out=tmp["o32b"][:dv, :], in_=o[:dv, :], func=AF.Identity),
                 r=ok, w=["o32b"])
            b.op("pe", lambda: nc.tensor.matmul(self.pv(5, 0, 512)[:dv, :], lhsT=self.ones_bf[:dv, :dv],
                                                rhs=tmp["o32b"][:dv, :], start=True, stop=True),
                 r=["o32b", "ones"], w=self.pk(5, 0, 512))
            b.op("dve", lambda: nc.vector.scalar_tensor_tensor(
                out=tmp["oc"][:dv, :], in0=self.pv(5, 0, 512)[:dv, :], scalar=-1.0 / dv, in1=o[:dv, :],
                op0=ALU.mult, op1=ALU.add), r=self.pk(5, 0, 512) + ok, w=["oc"])
            src = tmp["oc"]
            srck = ["oc"]
        b.op("act", lambda: nc.scalar.activation(out=osq[:dv, :], in_=src[:dv, :], func=AF.Square), r=srck, w=["osq"])
        b.op("pe", lambda: nc.tensor.matmul(self.pv(5, 0, 512)[:dv, :], lhsT=self.ones_bf[:dv, :dv], rhs=osq[:dv, :],
                                            start=True, stop=True), r=["osq", "ones"], w=self.pk(5, 0, 512))
        b.op("act", lambda: nc.scalar.activation(out=rstd[:dv, :], in_=self.pv(5, 0, 512)[:dv, :], func=AF.Sqrt,
                                                 scale=1.0 / dv, bias=self.eps_t[:dv, 0:1]),
             r=self.pk(5, 0, 512) + ["eps"], w=["rstd"])
        b.op("dve", lambda: nc.vector.reciprocal(out=rstd[:dv, :], in_=rstd[:dv, :]), r=["rstd"], w=["rstd"])
        gp = self.pv(2, 0, 512)
        gk = self.pk(2, 0, 512)
        b.op("act", lambda: nc.scalar.activation(out=e1[:dv, :], in_=gp[:dv, :], func=AF.Exp, scale=-1.0),
             r=gk, w=["e1"])
        b.op("dve", lambda: nc.vector.tensor_scalar_add(out=e1[:dv, :], in0=e1[:dv, :], scalar1=1.0),
             r=["e1"], w=["e1"])
        b.op("dve", lambda: nc.vector.reciprocal(out=e1[:dv, :], in_=e1[:dv, :]), r=["e1"], w=["e1"])
        b.op("dve", lambda: nc.vector.tensor_tensor(out=e1[:dv, :], in0=gp[:dv, :], in1=e1[:dv, :], op=ALU.mult),
             r=gk + ["e1"], w=["e1"])
        b.op("dve", lambda: nc.vector.scalar_tensor_tensor(out=t1[:dv, :], in0=src[:dv, :], scalar=gain_ap,
                                                           in1=rstd[:dv, :], op0=ALU.mult, op1=ALU.mult),
             r=srck + ["rstd", "hgain"], w=["t1"])
        b.op("dve", lambda: nc.vector.tensor_tensor(out=self.yT[:dv, etile, sl], in0=t1[:dv, :], in1=e1[:dv, :],
                                                     op=ALU.mult), r=["t1", "e1"], w=[("y", etile, blk)])


    RET_PIECES = {0: ((0, 128, 0, 0), (128, 64, 1, 0)), 1: ((0, 64, 1, 64), (64, 128, 2, 0)),
                  2: ((0, 128, 3, 0), (128, 64, 4, 0)), 3: ((0, 64, 4, 64), (64, 128, 5, 0))}

    def odd_mixer(self, j, es):
        with contextlib.ExitStack() as es_r:
            self.retnet(j, es_r)
            self.b.barrier()
        if self.cfg.get("s5", True):
            with contextlib.ExitStack() as es_s:
                self.s5(j, es_s)
                self.b.barrier()

    def retnet(self, j, es):
        b = self.b
        nc = self.nc
        tmp = self.alloc_head_tmps(es, groupnorm=True)
        wtm = b.sb("wtmo", [128, KT, 448], BF16, es)
        wfm = b.sb("wfmo", [128, 2, KT, 128], BF16, es)
        qT3 = b.sb("qT3", [128, 3, T], BF16, es)
        kT = b.sb("kTo", [128, T], BF16, es)
        kt_all = b.sb("kt_allo", [128, NCH, 2, 128], BF16, es)
        v_all = b.sb("v_allo", [128, NCH, 192], BF16, es)
        S_all = b.sb("S_allo", [128, 2, NCH, 192], BF16, es)
        Pst = b.sb("Psto", [128, 2, 2, 192], F32, es)
        rope = b.sb("rope", [128, 2, NCH, 64], BF16, es)
        rdec = b.sb("rdec", [128, 16], F32, es)
        dmask = b.sb("dmask", [128, 4, 128], F32, es)
        AB = b.sb("AB", [128, 2, 2, 4, 64], F32, es)
        rot = b.sb("rot", [128, 2, 4, 64], F32, es)
        qt3 = b.sb("qt3", [128, 2, 4, 128], BF16, es)
        mean_sb = b.sb("mean_sb", [128, 512], F32, es)
        oc2 = b.sb("oc2", [128, 512], F32, es)
        b.dma("pool", rope[:], self.rope_d[:, :].rearrange("p (a c k) -> p a c k", a=2, c=NCH), w=["rope"], stream="c1")
        b.dma("sp", rdec[:], self.rdec_d[:, :], w=["rdec"], stream="c0")
        b.dma("sp", dmask[:], self.dmask_d[:, :].rearrange("p (h i) -> p h i", h=4), w=["dmask"], stream="c0")
        b.op("pool", lambda: nc.gpsimd.memset(S_all[:], 0.0), w=[("S", d, c) for d in range(2) for c in range(NCH)])
        for h in range(4):
            gf128 = float((1.0 - 2.0 ** (-5.0 - h)) ** 128)
            gb128 = float((1.0 - 2.0 ** (-5.5 - h)) ** 128)
            pieces = self.RET_PIECES[h]
            b.dma("pool", wtm[:], self.w_tm_o_d[j, h, :, :].rearrange("p (k c) -> p k c", k=KT), w=["wtm"], stream="wtm0")
            for pi in range(2):
                b.dma("pool", wfm[:, pi, :, :], self.w_fm_o_d[j, 2 * h + pi, :, :].rearrange("p (k c) -> p k c", k=KT),
                      w=["wfm"], stream="wfm")
            for c in range(NCH):
                ch = slice(c * 128, (c + 1) * 128)
                par = c % 2
                P = self.pv(par, 0, 512)
                Pk = self.pk(par, 0, 448)
                for k in range(KT):
                    b.op("pe", lambda: nc.tensor.matmul(P[:, 0:448], lhsT=self.hT[:, k, ch], rhs=wtm[:, k, :],
                                                        start=(k == 0), stop=(k == KT - 1)),
                         r=["wtm", ("h", k, c // 4)], w=Pk)
                P4 = P[:, 0:256].rearrange("p (a k) -> p a k", a=4)
                cosb = rope[:, 0, c, :].unsqueeze(1).to_broadcast([128, 4, 64])
                sinb = rope[:, 1, c, :].unsqueeze(1).to_broadcast([128, 4, 64])
                A = AB[:, par, 0, :, :]
                Bm = AB[:, par, 1, :, :]
                R = rot[:, par, :, :]
                b.op("dve", lambda: nc.vector.tensor_tensor(out=A, in0=P4, in1=cosb, op=ALU.mult), r=Pk + ["rope"],
                     w=[("A", par)])
                b.op("dve", lambda: nc.vector.tensor_tensor(out=Bm, in0=P4, in1=sinb, op=ALU.mult), r=Pk + ["rope"],
                     w=[("B", par)])
                b.op("pool", lambda: nc.gpsimd.tensor_tensor(out=R[:, 0::2, :], in0=A[:, 0::2, :], in1=Bm[:, 1::2, :],
                                                              op=ALU.subtract), r=[("A", par), ("B", par)], w=[("rot", par)])
                b.op("pool", lambda: nc.gpsimd.tensor_tensor(out=R[:, 1::2, :], in0=Bm[:, 0::2, :], in1=A[:, 1::2, :],
                                                              op=ALU.add), r=[("A", par), ("B", par)], w=[("rot", par)])
                rq = rot[:, par, 0:2, :].rearrange("p a k -> p (a k)")
                rk = rot[:, par, 2:4, :].rearrange("p a k -> p (a k)")
                QT = qt3[:, par, :, :]
                sc_k = 128.0 ** -0.5
                b.op("act", lambda: nc.scalar.activation(out=QT[:, 0, :], in_=rq, func=AF.Identity), r=[("rot", par)],
                     w=[("qt3", par)])
                b.op("act", lambda: nc.scalar.activation(out=QT[:, 1, :], in_=rq, func=AF.Identity,
                                                         scale=rdec[:, 4 * h + 0: 4 * h + 1]),
                     r=[("rot", par), "rdec"], w=[("qt3", par)])
                b.op("act", lambda: nc.scalar.activation(out=QT[:, 2, :], in_=rq, func=AF.Identity,
                                                         scale=rdec[:, 4 * h + 1: 4 * h + 2]),
                     r=[("rot", par), "rdec"], w=[("qt3", par)])
                b.op("act", lambda: nc.scalar.activation(out=QT[:, 3, :], in_=rk, func=AF.Identity, scale=sc_k),
                     r=[("rot", par)], w=[("qt3", par)])
                b.op("pool", lambda: nc.gpsimd.tensor_scalar(out=kt_all[:, c, 0, :], in0=rk,
                                                              scalar1=rdec[:, 4 * h + 2: 4 * h + 3], scalar2=None,
                                                              op0=ALU.mult), r=[("rot", par), "rdec"], w=[("kt", c)])
                b.op("pool", lambda: nc.gpsimd.tensor_scalar(out=kt_all[:, c, 1, :], in0=rk,
                                                              scalar1=rdec[:, 4 * h + 3: 4 * h + 4], scalar2=None,
                                                              op0=ALU.mult), r=[("rot", par), "rdec"], w=[("kt", c)])
                b.op("act", lambda: nc.scalar.activation(out=v_all[:, c, :], in_=P[:, 256:448], func=AF.Identity), r=Pk,
                     w=[("v", c)])
                zb = 2 + par
                tp = self.pv(zb, 0, 256).bitcast(BF16)
                tkey = self.pk(zb, 0, 256)
                for a_ in range(4):
                    b.op("pe", lambda: nc.tensor.transpose(tp[:, a_ * 128:(a_ + 1) * 128], QT[:, a_, :], self.ident[:]),
                         r=[("qt3", par), "ident"], w=tkey)
                b.op("dve", lambda: nc.vector.tensor_copy(out=qT3[:, :, ch],
                                                           in_=tp[:, 0:384].rearrange("p (a t) -> p a t", a=3)),
                     r=tkey, w=[("qT", c)])
                b.op("act", lambda: nc.scalar.activation(out=kT[:, ch], in_=tp[:, 384:512], func=AF.Identity), r=tkey,
                     w=[("kT", c)])
            for step in range(NCH - 1):
                for d in range(2):
                    c = step if d == 0 else NCH - 1 - step
                    nxt = 1 if d == 0 else -1
                    gdec = gf128 if d == 0 else gb128
                    bank = (4 if d == 0 else 6) + step % 2
                    kv = self.pv(bank, 0, 192)
                    kvk = self.pk(bank, 0, 192)
                    b.op("pe", lambda: nc.tensor.matmul(kv, lhsT=kt_all[:, c, d, :], rhs=v_all[:, c, :], start=True,
                                                        stop=True), r=[("kt", c), ("v", c)], w=kvk)
                    pp = step % 2
                    pkey = ("Pst", d, pp)
                    pold = ("Pst", d, 1 - pp)
                    if step == 0:
                        b.op("dve", lambda: nc.vector.tensor_copy(out=Pst[:, d, pp, :], in_=kv), r=kvk, w=[pkey])
                    else:
                        b.op("dve", lambda: nc.vector.scalar_tensor_tensor(out=Pst[:, d, pp, :], in0=Pst[:, d, 1 - pp, :],
                                                                           scalar=gdec, in1=kv, op0=ALU.mult, op1=ALU.add),
                             r=kvk + [pold], w=[pkey])
                    b.op("act", lambda: nc.scalar.activation(out=S_all[:, d, c + nxt, :], in_=Pst[:, d, pp, :],
                                                             func=AF.Identity), r=[pkey], w=[("S", d, c + nxt)])
            obank = (1, 6)
            gbank = (2, 7)
            for blk in range(NBLK):
                sl = slice(blk * 512, (blk + 1) * 512)
                for pi, (f0, nf, etile, p0) in enumerate(pieces):
                    for k in range(KT):
                        b.op("pe", lambda: nc.tensor.matmul(self.pv(gbank[pi], 0, 512)[p0:p0 + nf, :],
                                                            lhsT=wfm[:, pi, k, 0:nf], rhs=self.hT[:, k, sl],
                                                            start=(k == 0), stop=(k == KT - 1)),
                             r=["wfm", ("h", k, blk)], w=self.pk(gbank[pi], 0, 512))
                for cc in range(4):
                    c = blk * 4 + cc
                    ch = slice(c * 128, (c + 1) * 128)
                    par = c % 2
                    sb_ = 0 if par == 0 else 3
                    b.op("pe", lambda: nc.tensor.matmul(self.pv(sb_, 0, 128), lhsT=kT[:, ch], rhs=qT3[:, 0, ch],
                                                        start=True, stop=True), r=[("kT", c), ("qT", c)],
                         w=self.pk(sb_, 0, 128))
                    b.op("dve", lambda: nc.vector.tensor_tensor(out=tmp["sT"][:, par, 0, :], in0=self.pv(sb_, 0, 128),
                                                                 in1=dmask[:, h, :], op=ALU.mult),
                         r=self.pk(sb_, 0, 128) + ["dmask"], w=[("sT", par)])
                    for pi, (f0, nf, etile, p0) in enumerate(pieces):
                        ov = self.pv(obank[pi], cc * 128, cc * 128 + 128)[p0:p0 + nf, :]
                        ok = self.pk(obank[pi], 0, 512)
                        b.op("pe", lambda: nc.tensor.matmul(ov, lhsT=v_all[:, c, f0:f0 + nf], rhs=tmp["sT"][:, par, 0, :],
                                                            start=True, stop=False), r=[("v", c), ("sT", par)], w=ok)
                        b.op("pe", lambda: nc.tensor.matmul(ov, lhsT=S_all[:, 0, c, f0:f0 + nf], rhs=qT3[:, 1, ch],
                                                            start=False, stop=False), r=[("S", 0, c), ("qT", c)], w=ok)
                        b.op("pe", lambda: nc.tensor.matmul(ov, lhsT=S_all[:, 1, c, f0:f0 + nf], rhs=qT3[:, 2, ch],
                                                            start=False, stop=True), r=[("S", 1, c), ("qT", c)], w=ok)
                stat = self.pv(5, 0, 512)
                statk = self.pk(5, 0, 512)
                osq, rstd, e1, t1, o32b, oc = (tmp[n] for n in ("osq", "rstd", "e1", "t1", "o32b", "oc"))
                for pi, (f0, nf, etile, p0) in enumerate(pieces):
                    pr = slice(p0, p0 + nf)
                    b.op("act", lambda: nc.scalar.activation(out=o32b[pr, :] if pi == 0 else osq[pr, :],
                                                             in_=self.pv(obank[pi], 0, 512)[pr, :], func=AF.Identity),
                         r=self.pk(obank[pi], 0, 512), w=[("o32b", pi)])
                for pi, (f0, nf, etile, p0) in enumerate(pieces):
                    pr = slice(p0, p0 + nf)
                    src = o32b if pi == 0 else osq
                    b.op("pe", lambda: nc.tensor.matmul(stat, lhsT=self.ones_bf[pr, :], rhs=src[pr, :], start=(pi == 0),
                                                        stop=(pi == 1)), r=[("o32b", pi), "ones"], w=statk)
                b.op("act", lambda: nc.scalar.activation(out=mean_sb[:], in_=stat, func=AF.Identity, scale=-1.0 / 192),
                     r=statk, w=["mean"])
                ocs = (oc, oc2)
                for pi, (f0, nf, etile, p0) in enumerate(pieces):
                    pr = slice(p0, p0 + nf)
                    b.op("dve", lambda: nc.vector.tensor_tensor(out=ocs[pi][pr, :], in0=self.pv(obank[pi], 0, 512)[pr, :],
                                                                 in1=mean_sb[pr, :], op=ALU.add),
                         r=self.pk(obank[pi], 0, 512) + ["mean"], w=[("oc", pi)])
                    b.op("act", lambda: nc.scalar.activation(out=o32b[pr, :] if pi == 0 else osq[pr, :], in_=ocs[pi][pr, :],
                                                             func=AF.Square), r=[("oc", pi)], w=[("o32b", pi)])
                for pi, (f0, nf, etile, p0) in enumerate(pieces):
                    pr = slice(p0, p0 + nf)
                    src = o32b if pi == 0 else osq
                    b.op("pe", lambda: nc.tensor.matmul(stat, lhsT=self.ones_bf[pr, :], rhs=src[pr, :], start=(pi == 0),
                                                        stop=(pi == 1)), r=[("o32b", pi), "ones"], w=statk)
                b.op("act", lambda: nc.scalar.activation(out=rstd[:], in_=stat, func=AF.Sqrt, scale=1.0 / 192,
                                                         bias=self.eps_t[:, 0:1]), r=statk + ["eps"], w=["rstd"])
                b.op("dve", lambda: nc.vector.reciprocal(out=rstd[:], in_=rstd[:]), r=["rstd"], w=["rstd"])
                for pi, (f0, nf, etile, p0) in enumerate(pieces):
                    pr = slice(p0, p0 + nf)
                    gp = self.pv(gbank[pi], 0, 512)[pr, :]
                    gk = self.pk(gbank[pi], 0, 512)
                    gain_ap = self.hgain_o[pr, j * 8 + 2 * h + pi: j * 8 + 2 * h + pi + 1]
                    b.op("act", lambda: nc.scalar.activation(out=e1[pr, :], in_=gp, func=AF.Exp, scale=-1.0), r=gk,
                         w=["e1"])
                    b.op("dve", lambda: nc.vector.tensor_scalar_add(out=e1[pr, :], in0=e1[pr, :], scalar1=1.0),
                         r=["e1"], w=["e1"])
                    b.op("dve", lambda: nc.vector.reciprocal(out=e1[pr, :], in_=e1[pr, :]), r=["e1"], w=["e1"])
                    b.op("dve", lambda: nc.vector.tensor_tensor(out=e1[pr, :], in0=gp, in1=e1[pr, :], op=ALU.mult),
                         r=gk + ["e1"], w=["e1"])
                    b.op("dve", lambda: nc.vector.scalar_tensor_tensor(out=t1[pr, :], in0=ocs[pi][pr, :], scalar=gain_ap,
                                                                       in1=rstd[pr, :], op0=ALU.mult, op1=ALU.mult),
                         r=[("oc", pi), "rstd", "hgain_o"], w=["t1"])
                    b.op("dve", lambda: nc.vector.tensor_tensor(out=self.yT[pr, etile, sl], in0=t1[pr, :], in1=e1[pr, :],
                                                                 op=ALU.mult), r=["t1", "e1"],
                         w=[("y", etile, blk)])


    def cmul(self, out, x, y, t, shape_keys):
        b = self.b
        nc = self.nc
        (o_r, o_i), ko = out
        (xr, xi), kx = x
        (yr, yi), ky = y
        (t1, t2), kt = t
        b.op("dve", lambda: nc.vector.tensor_tensor(out=t1, in0=xr, in1=yr, op=ALU.mult), r=[kx, ky], w=[kt])
        b.op("dve", lambda: nc.vector.tensor_tensor(out=t2, in0=xi, in1=yi, op=ALU.mult), r=[kx, ky], w=[kt])
        b.op("dve", lambda: nc.vector.tensor_tensor(out=o_r, in0=t1, in1=t2, op=ALU.subtract), r=[kt], w=[ko])
        b.op("dve", lambda: nc.vector.tensor_tensor(out=t1, in0=xr, in1=yi, op=ALU.mult), r=[kx, ky, ko], w=[kt])
        b.op("dve", lambda: nc.vector.tensor_tensor(out=t2, in0=xi, in1=yr, op=ALU.mult), r=[kx, ky], w=[kt])
        b.op("dve", lambda: nc.vector.tensor_tensor(out=o_i, in0=t1, in1=t2, op=ALU.add), r=[kt], w=[ko])

    def s5(self, j, es):
        b = self.b
        nc = self.nc
        V = nc.vector
        NC8 = T // 8

        def tt(out, in0, in1, op, r, w):
            b.op("dve", lambda: V.tensor_tensor(out=out, in0=in0, in1=in1, op=op), r=r, w=w)

        def ts(out, in0, s1, s2, op0, op1, r, w):
            b.op("dve", lambda: V.tensor_scalar(out=out, in0=in0, scalar1=s1, scalar2=s2, op0=op0, op1=op1), r=r, w=w)

        Uflat = b.sb("Uflat", [128, 16, NC8], BF16, es)
        Tg = b.sb("Tg", [128, 16, 128], BF16, es)
        KX = b.sb("KX", [128, 2, 2, 8, 128], BF16, es)
        QY = b.sb("QY", [128, 2, 2, 8, 128], BF16, es)
        Hs = b.sb("Hs", [128, 2, 2, 8, NC8], BF16, es)
        gw = b.sb("gw", [128, 2, 256], BF16, es)
        glb = b.sb("s5glb", [128, 2], F32, es)
        A8 = b.sb("A8", [128, 2, 2, 8], F32, es)
        b.dma("pool", gw[:], self.s5_gw_d[j, :, :].rearrange("p (k c) -> p k c", k=2), w=["gw"], stream="wfm")
        b.dma("sp", glb[:], self.s5_glb_d[j, :, :], w=["glb"], stream="c0")

        with contextlib.ExitStack() as e1:
            u2 = b.sb("u2", [128, 2, 16, 8, 16], BF16, e1)
            wu = b.sb("wu", [128, KT, 256], BF16, e1)
            b.dma("pool", wu[:], self.w_u_o_d[j, :, :].rearrange("p (k c) -> p k c", k=KT), w=["wu"], stream="wtm0")
            cnt = 0
            for half in range(2):
                for jj in range(8):
                    bi = cnt % 2
                    cnt += 1
                    for k in range(KT):
                        b.op("pe", lambda: nc.tensor.matmul(
                            self.pv(bi, 0, 256), lhsT=self.hT[:, k, half * 1024 + jj: half * 1024 + 1024: 8],
                            rhs=wu[:, k, :], start=(k == 0), stop=(k == KT - 1)),
                            r=["wu"] + [("h", k, blk) for blk in (2 * half, 2 * half + 1)], w=self.pk(bi, 0, 256))
                    b.op("act", lambda: nc.scalar.activation(
                        out=u2[:, half, :, jj, :], in_=self.pv(bi, 0, 256).rearrange("p (g k) -> p g k", g=16),
                        func=AF.Identity), r=self.pk(bi, 0, 256), w=[("u2", half)])
            cnt = 0
            for half in range(2):
                for g4 in range(4):
                    bi = 2 + cnt % 2
                    cnt += 1
                    tp = self.pv(bi, 0, 256).bitcast(BF16)
                    for gg in range(4):
                        g = g4 * 4 + gg
                        b.op("pe", lambda: nc.tensor.transpose(tp[:, gg * 128:(gg + 1) * 128],
                                                               u2[:, half, g, :, :].rearrange("p a k -> p (a k)"),
                                                               self.ident[:]),
                             r=[("u2", half), "ident"], w=self.pk(bi, 0, 256))
                    b.op("dve", lambda: V.tensor_copy(out=Uflat[:, g4 * 4:(g4 + 1) * 4, half * 128:(half + 1) * 128],
                                                      in_=tp.rearrange("p (a c) -> p a c", a=4)),
                         r=self.pk(bi, 0, 256), w=["Uflat"])
            b.barrier()

        with contextlib.ExitStack() as e2:
            prm = b.sb("s5prm", [128, 3, 2, 8], F32, e2)
            BC = b.sb("s5BC", [128, 4, 8, 16], F32, e2)
            dsk = b.sb("s5dsk", [128, 16], F32, e2)
            bmask = b.sb("s5bm", [128, 2, 128], F32, e2)
            identf = b.sb("identf", [128, 128], F32, e2)
            b.dma("sp", prm[:], self.s5_prm_d[j, :, :].rearrange("p (a d g) -> p a d g", a=3, d=2), w=["prm"], stream="c0")
            b.dma("sp", BC[:], self.s5_bc_d[j, :, :].rearrange("p (a g k) -> p a g k", a=4, g=8), w=["BC"], stream="c0")
            b.dma("sp", dsk[:], self.s5_dsk_d[j, :, :], w=["dsk"], stream="c0")
            b.dma("sp", bmask[:], self.s5_bm_d[:, :].rearrange("p (a c) -> p a c", a=2), w=["bmask"], stream="c0")
            b.dma("sp", identf[:], self.ident_d[:, :], w=["identf"], stream="c0")
            sc = {}
            for nm in ("lr", "dt", "th", "s8", "c8", "m8", "ar", "ai", "t1", "t2", "t3", "den", "cr", "ci", "nar", "nai"):
                sc[nm] = b.sb("s5_" + nm, [128, 2, 8], F32, e2)
            PW = b.sb("s5PW", [128, 2, 2, 2, 8, 9], F32, e2)
            BB = b.sb("s5BB", [128, 2, 2, 8, 16], F32, e2)
            TB = b.sb("s5TB", [128, 4, 8, 8, 16], F32, e2)
            TX = b.sb("s5TX", [128, 2, 8, 8, 16], F32, e2)
            tw = b.sb("s5tw", [128, 2, 8, 8, 16], F32, e2)
            Tacc = b.sb("s5Tacc", [128, 128], F32, e2)
            lr, dt, th = sc["lr"][:], sc["dt"][:], sc["th"][:]
            ts(lr, prm[:, 0, :, :], -1e-4, None, ALU.min, ALU.bypass, ["prm"], ["lr"])
            b.op("act", lambda: nc.scalar.activation(out=dt, in_=prm[:, 2, :, :], func=AF.Exp), r=["prm"], w=["dt"])
            tt(th, prm[:, 1, :, :], dt, ALU.mult, ["prm", "dt"], ["th"])
            b.op("act", lambda: nc.scalar.activation(out=sc["s8"][:], in_=th, func=AF.Sin, scale=1.0 / 8), r=["th"],
                 w=["s8"])
            b.op("act", lambda: nc.scalar.activation(out=sc["t1"][:], in_=th, func=AF.Sin, scale=1.0 / 16), r=["th"],
                 w=["t1"])
            tt(sc["t1"][:], sc["t1"][:], sc["t1"][:], ALU.mult, ["t1"], ["t1"])
            ts(sc["c8"][:], sc["t1"][:], -2.0, 1.0, ALU.mult, ALU.add, ["t1"], ["c8"])
            tt(sc["t2"][:], lr, dt, ALU.mult, ["lr", "dt"], ["t2"])
            b.op("act", lambda: nc.scalar.activation(out=sc["m8"][:], in_=sc["t2"][:], func=AF.Exp, scale=1.0 / 8),
                 r=["t2"], w=["m8"])
            ar, ai = sc["ar"][:], sc["ai"][:]
            tt(ar, sc["m8"][:], sc["c8"][:], ALU.mult, ["m8", "c8"], ["a"])
            tt(ai, sc["m8"][:], sc["s8"][:], ALU.mult, ["m8", "s8"], ["a"])
            for _ in range(3):
                tt(sc["t1"][:], ar, ar, ALU.mult, ["a"], ["t1"])
                tt(sc["t2"][:], ai, ai, ALU.mult, ["a"], ["t2"])
                tt(sc["t3"][:], ar, ai, ALU.mult, ["a"], ["t3"])
                tt(ar, sc["t1"][:], sc["t2"][:], ALU.subtract, ["t1", "t2"], ["a"])
                ts(ai, sc["t3"][:], 2.0, None, ALU.mult, ALU.bypass, ["t3"], ["a"])
            tt(sc["t1"][:], ar, ar, ALU.mult, ["a"], ["t1"])
            tt(sc["t2"][:], ai, ai, ALU.mult, ["a"], ["t2"])
            tt(sc["t1"][:], sc["t1"][:], sc["t2"][:], ALU.add, ["t1", "t2"], ["t1"])
            b.op("dve", lambda: V.reciprocal(out=sc["t1"][:], in_=sc["t1"][:]), r=["t1"], w=["t1"])
            tt(sc["nar"][:], ar, sc["t1"][:], ALU.mult, ["a", "t1"], ["na"])
            tt(sc["nai"][:], ai, sc["t1"][:], ALU.mult, ["a", "t1"], ["na"])
            ts(sc["nai"][:], sc["nai"][:], -1.0, None, ALU.mult, ALU.bypass, ["na"], ["na"])
            b.op("dve", lambda: V.memset(PW[:, :, 0, :, :, 0], 1.0), w=["PW"])
            b.op("dve", lambda: V.memset(PW[:, :, 1, :, :, 0], 0.0), r=["PW"], w=["PW"])
            for pi_, (br_, bi_, kb) in enumerate(((ar, ai, "a"), (sc["nar"][:], sc["nai"][:], "na"))):
                for k in range(8):
                    self.cmul(((PW[:, pi_, 0, :, :, k + 1], PW[:, pi_, 1, :, :, k + 1]), "PW"),
                              ((PW[:, pi_, 0, :, :, k], PW[:, pi_, 1, :, :, k]), "PW"),
                              ((br_, bi_), kb), ((sc["t1"][:], sc["t2"][:]), "t12"), None)
            b.op("dve", lambda: V.tensor_copy(out=A8[:, 0, :, :], in_=PW[:, 0, 0, :, :, 8]), r=["PW"], w=["A8"])
            b.op("dve", lambda: V.tensor_copy(out=A8[:, 1, :, :], in_=PW[:, 0, 1, :, :, 8]), r=["PW"], w=["A8"])
            den, cr, ci = sc["den"][:], sc["cr"][:], sc["ci"][:]
            li = prm[:, 1, :, :]
            tt(sc["t1"][:], lr, lr, ALU.mult, ["lr"], ["t1"])
            tt(sc["t2"][:], li, li, ALU.mult, ["prm"], ["t2"])
            tt(den, sc["t1"][:], sc["t2"][:], ALU.add, ["t1", "t2"], ["den"])
            b.op("dve", lambda: V.reciprocal(out=den, in_=den), r=["den"], w=["den"])
            ts(sc["t3"][:], ar, -1.0, None, ALU.add, ALU.bypass, ["a"], ["t3"])
            tt(sc["t1"][:], sc["t3"][:], lr, ALU.mult, ["t3", "lr"], ["t1"])
            tt(sc["t2"][:], ai, li, ALU.mult, ["a", "prm"], ["t2"])
            tt(cr, sc["t1"][:], sc["t2"][:], ALU.add, ["t1", "t2"], ["c"])
            tt(cr, cr, den, ALU.mult, ["c", "den"], ["c"])
            tt(sc["t1"][:], ai, lr, ALU.mult, ["a", "lr"], ["t1"])
            tt(sc["t2"][:], sc["t3"][:], li, ALU.mult, ["t3", "prm"], ["t2"])
            tt(ci, sc["t1"][:], sc["t2"][:], ALU.subtract, ["t1", "t2"], ["c"])
            tt(ci, ci, den, ALU.mult, ["c", "den"], ["c"])
            sh4 = [128, 2, 8, 16]
            crb = cr.unsqueeze(3).to_broadcast(sh4)
            cib = ci.unsqueeze(3).to_broadcast(sh4)
            brb = BC[:, 0, :, :].unsqueeze(1).to_broadcast(sh4)
            bib = BC[:, 1, :, :].unsqueeze(1).to_broadcast(sh4)
            w4 = tw[:, :, 0:2, 0, :].rearrange("p a d k -> p a d k")
            t4a = tw[:, 0, :, 0:2, :]
            tA = TX[:, 0, 0:2, :, :]
            tB = TX[:, 1, 0:2, :, :]
            self.cmul(((BB[:, 0, :, :, :], BB[:, 1, :, :, :]), "BB"), ((crb, cib), "c"), ((brb, bib), "BC"),
                      ((tA, tB), "TX"), None)
            sh5 = [128, 8, 8, 16]
            for d in range(2):
                kp = 1 if d == 0 else 0
                qp = 0 if d == 0 else 1
                pkr = PW[:, kp, 0, d, :, 0:8].unsqueeze(3).to_broadcast(sh5)
                pki = PW[:, kp, 1, d, :, 0:8].unsqueeze(3).to_broadcast(sh5)
                pqr = PW[:, qp, 0, d, :, 0:8].unsqueeze(3).to_broadcast(sh5)
                pqi = PW[:, qp, 1, d, :, 0:8].unsqueeze(3).to_broadcast(sh5)
                bbr = BB[:, 0, d, :, :].unsqueeze(2).to_broadcast(sh5)
                bbi = BB[:, 1, d, :, :].unsqueeze(2).to_broadcast(sh5)
                ccr = BC[:, 2, :, :].unsqueeze(2).to_broadcast(sh5)
                cci = BC[:, 3, :, :].unsqueeze(2).to_broadcast(sh5)
                self.cmul(((TB[:, 0], TB[:, 1]), "TB"), ((pkr, pki), "PW"), ((bbr, bbi), "BB"), ((tw[:, 0], tw[:, 1]), "tw"),
                          None)
                self.cmul(((TB[:, 2], TB[:, 3]), "TB"), ((pqr, pqi), "PW"), ((ccr, cci), "BC"), ((tw[:, 0], tw[:, 1]), "tw"),
                          None)
                ts(TB[:, 3], TB[:, 3], -1.0, None, ALU.mult, ALU.bypass, ["TB"], ["TB"])
                for g in range(16):
                    pair, g2 = divmod(g, 2)
                    rows = slice(g2 * 64, g2 * 64 + 64)
                    bi = g % 2
                    tps = self.pv(bi, 0, 128)
                    fl = lambda a_: TB[rows, a_, pair, :, :].rearrange("p a k -> p (a k)")
                    b.op("pe", lambda: nc.tensor.matmul(tps, lhsT=fl(0), rhs=fl(2), start=True, stop=False), r=["TB"],
                         w=self.pk(bi, 0, 128))
                    b.op("pe", lambda: nc.tensor.matmul(tps, lhsT=fl(1), rhs=fl(3), start=False, stop=True), r=["TB"],
                         w=self.pk(bi, 0, 128))
                    if d == 0:
                        tt(Tacc[:], tps, bmask[:, 0, :], ALU.mult, self.pk(bi, 0, 128) + ["bmask"], ["Tacc"])
                        b.op("dve", lambda: V.scalar_tensor_tensor(out=Tacc[:], in0=identf[:], scalar=dsk[:, g:g + 1],
                                                                   in1=Tacc[:], op0=ALU.mult, op1=ALU.add),
                             r=["identf", "dsk", "Tacc"], w=["Tacc"])
                        b.op("dve", lambda: V.tensor_copy(out=Tg[:, g, :], in_=Tacc[:]), r=["Tacc"], w=[("Tg", g)])
                    else:
                        tt(Tacc[:], tps, bmask[:, 1, :], ALU.mult, self.pk(bi, 0, 128) + ["bmask"], ["Tacc"])
                        tt(Tg[:, g, :], Tacc[:], Tg[:, g, :], ALU.add, ["Tacc", ("Tg", g)], [("Tg", g)])
                if d == 0:
                    p7r = PW[:, 0, 0, d, :, 7].unsqueeze(2).unsqueeze(3).to_broadcast(sh5)
                    p7i = PW[:, 0, 1, d, :, 7].unsqueeze(2).unsqueeze(3).to_broadcast(sh5)
                    self.cmul(((TX[:, 0], TX[:, 1]), "TX"), ((TB[:, 0], TB[:, 1]), "TB"), ((p7r, p7i), "PW"),
                              ((tw[:, 0], tw[:, 1]), "tw"), None)
                    kxr, kxi, kxk = TX[:, 0], TX[:, 1], "TX"
                else:
                    kxr, kxi, kxk = TB[:, 0], TB[:, 1], "TB"
                for ri, src in enumerate((kxr, kxi)):
                    for p4 in range(2):
                        bi = 2 + (ri * 2 + p4) % 2
                        for pp in range(4):
                            pair = p4 * 4 + pp
                            b.op("pe", lambda: nc.tensor.transpose(self.pv(bi, pp * 128, pp * 128 + 128),
                                                                   src[:, pair, :, :].rearrange("p a k -> p (a k)"),
                                                                   identf[:]), r=[kxk, "identf"], w=self.pk(bi, 0, 512))
                        b.op("act", lambda: nc.scalar.activation(
                            out=KX[:, d, ri, p4 * 4:(p4 + 1) * 4, :],
                            in_=self.pv(bi, 0, 512).rearrange("p (a q) -> p a q", a=4), func=AF.Identity),
                            r=self.pk(bi, 0, 512), w=["KX"])
                pw_idx = 1 if d == 0 else 8
                pyr = PW[:, 0, 0, d, :, pw_idx].unsqueeze(2).unsqueeze(3).to_broadcast(sh5)
                pyi = PW[:, 0, 1, d, :, pw_idx].unsqueeze(2).unsqueeze(3).to_broadcast(sh5)
                tt(tw[:, 0], TB[:, 2], pyr, ALU.mult, ["TB", "PW"], ["tw"])
                tt(tw[:, 1], TB[:, 3], pyi, ALU.mult, ["TB", "PW"], ["tw"])
                tt(TX[:, 0], tw[:, 0], tw[:, 1], ALU.add, ["tw"], ["TX"])
                b.op("act", lambda: nc.scalar.activation(out=QY[:, d, 0, :, :], in_=TX[:, 0].rearrange("p g a k -> p g (a k)"),
                                                         func=AF.Identity), r=["TX"], w=["QY"])
                tt(tw[:, 0], TB[:, 3], pyr, ALU.mult, ["TB", "PW", "TX"], ["tw"])
                tt(tw[:, 1], TB[:, 2], pyi, ALU.mult, ["TB", "PW"], ["tw"])
                tt(TX[:, 1], tw[:, 0], tw[:, 1], ALU.subtract, ["tw"], ["TX"])
                b.op("act", lambda: nc.scalar.activation(out=QY[:, d, 1, :, :], in_=TX[:, 1].rearrange("p g a k -> p g (a k)"),
                                                         func=AF.Identity), r=["TX"], w=["QY"])
            b.barrier()

        with contextlib.ExitStack() as e3:
            RC = b.sb("s5RC", [128, 2, 8, NC8], F32, e3)
            XH = b.sb("s5XH", [128, 2, 8, NC8], F32, e3)
            W1 = b.sb("s5W1", [128, 8, 128], F32, e3)
            W2 = b.sb("s5W2", [128, 8, 128], F32, e3)
            mm = b.sb("s5mm", [128, 3, 8], F32, e3)
            for d in range(2):
                a8r, a8i = A8[:, 0, d, :], A8[:, 1, d, :]
                tt(mm[:, 1, :], a8r, a8r, ALU.mult, ["A8"], ["mm1"])
                tt(mm[:, 2, :], a8i, a8i, ALU.mult, ["A8"], ["mm2"])
                tt(mm[:, 0, :], mm[:, 1, :], mm[:, 2, :], ALU.add, ["mm1", "mm2"], ["mm0"])
                b.op("act", lambda: nc.scalar.activation(out=mm[:, 0, :], in_=mm[:, 0, :], func=AF.Sqrt), r=["mm0"],
                     w=["mm0"])
                b.op("dve", lambda: V.reciprocal(out=mm[:, 1, :], in_=mm[:, 0, :]), r=["mm0", "mm1"], w=["mm1"])
                tt(mm[:, 2, :], a8i, mm[:, 1, :], ALU.mult, ["A8", "mm1", "mm2"], ["mm2"])
                tt(mm[:, 1, :], a8r, mm[:, 1, :], ALU.mult, ["A8", "mm1"], ["mm1"])
                b.op("dve", lambda: V.tensor_copy(out=RC[:, 0, :, 0], in_=mm[:, 1, :]), r=["mm1"], w=["RC"])
                b.op("dve", lambda: V.tensor_copy(out=RC[:, 1, :, 0], in_=mm[:, 2, :]), r=["mm2"], w=["RC"])
                wdt = 1
                while wdt < NC8:
                    shw = [128, 8, wdt]
                    sr = RC[:, 0, :, wdt - 1].unsqueeze(2).to_broadcast(shw)
                    si = RC[:, 1, :, wdt - 1].unsqueeze(2).to_broadcast(shw)
                    self.cmul(((RC[:, 0, :, wdt:2 * wdt], RC[:, 1, :, wdt:2 * wdt]), "RC"),
                              ((RC[:, 0, :, 0:wdt], RC[:, 1, :, 0:wdt]), "RC"), ((sr, si), "RC"),
                              ((W1[:, :, 0:wdt], W2[:, :, 0:wdt]), "W12"), None)
                    wdt *= 2
                for g in range(16):
                    pair, g2 = divmod(g, 2)
                    rows = slice(g2 * 64, g2 * 64 + 64)
                    for ri in range(2):
                        bi = ri * 4 + pair // 2
                        c0 = (pair % 2) * 256
                        b.op("pe", lambda: nc.tensor.matmul(self.pv(bi, c0, c0 + 256)[rows, :], lhsT=KX[:, d, ri, pair, rows],
                                                            rhs=Uflat[:, g, :], start=True, stop=True),
                             r=["KX", "Uflat"], w=self.pk(bi, 0, 512))
                XR = self.ps[:, 0:2048].rearrange("p (g c) -> p g c", g=8)
                XI = self.ps[:, 2048:4096].rearrange("p (g c) -> p g c", g=8)
                kR = [("ps", i) for i in range(4)]
                kI = [("ps", i) for i in range(4, 8)]
                if d == 0:
                    rcr, rci = RC[:, 0, :, :], RC[:, 1, :, :]
                else:
                    rcr, rci = RC[:, 0, :, ::-1], RC[:, 1, :, ::-1]
                for hc in range(2):
                    cs = slice(hc * 128, (hc + 1) * 128)
                    tt(W1[:], XR[:, :, cs], rcr[:, :, cs], ALU.mult, kR + ["RC"], ["W1"])
                    tt(W2[:], XI[:, :, cs], rci[:, :, cs], ALU.mult, kI + ["RC"], ["W2"])
                    tt(XH[:, 0, :, cs], W1[:], W2[:], ALU.add, ["W1", "W2"], ["XHr"])
                    tt(W1[:], XI[:, :, cs], rcr[:, :, cs], ALU.mult, kI + ["RC", "XHr"], ["W1"])
                    tt(W2[:], XR[:, :, cs], rci[:, :, cs], ALU.mult, kR + ["RC", "XHr"], ["W2"])
                    tt(XH[:, 1, :, cs], W1[:], W2[:], ALU.subtract, ["W1", "W2"], ["XHi"])
                for ri, kk in ((0, "XHr"), (1, "XHi")):
                    for pair in range(8):
                        mb = mm[:, 0, pair:pair + 1].to_broadcast([128, NC8])
                        if d == 0:
                            dat = XH[:, ri, pair, :]
                        else:
                            dat = XH[:, ri, pair, ::-1]
                        b.op("dve", lambda: V.tensor_tensor_scan(out=dat, data0=mb, data1=dat, initial=0.0, op0=ALU.mult,
                                                                 op1=ALU.add), r=[kk, "mm0"], w=[kk])
                sh_ = 1 if d == 0 else -1
                zsl = slice(0, 1) if d == 0 else slice(NC8 - 1, NC8)
                for hc in range(2):
                    c0, c1 = hc * 128, (hc + 1) * 128
                    s0, s1 = c0, c1
                    if d == 0 and hc == 1:
                        s1 = NC8 - 1
                    if d == 1 and hc == 0:
                        s0 = 1
                    cs = slice(c0, c1)
                    wsl = slice(s0 - c0, s1 - c0)
                    dsl = slice(s0 + sh_, s1 + sh_)
                    tt(W1[:], XH[:, 0, :, cs], rcr[:, :, cs], ALU.mult, ["XHr", "RC", ("Hs", d)], ["W1"])
                    tt(W2[:], XH[:, 1, :, cs], rci[:, :, cs], ALU.mult, ["XHi", "RC", ("Hs", d)], ["W2"])
                    tt(Hs[:, d, 0, :, dsl], W1[:, :, wsl], W2[:, :, wsl], ALU.subtract, ["W1", "W2"], [("Hs", d)])
                    tt(W1[:], XH[:, 0, :, cs], rci[:, :, cs], ALU.mult, ["XHr", "RC", ("Hs", d)], ["W1"])
                    tt(W2[:], XH[:, 1, :, cs], rcr[:, :, cs], ALU.mult, ["XHi", "RC", ("Hs", d)], ["W2"])
                    tt(Hs[:, d, 1, :, dsl], W1[:, :, wsl], W2[:, :, wsl], ALU.add, ["W1", "W2"], [("Hs", d)])
                b.op("dve", lambda: V.memset(Hs[:, d, :, :, zsl], 0.0), r=[("Hs", d)], w=[("Hs", d)])
            b.barrier()

        with contextlib.ExitStack() as e4:
            yf = b.sb("s5yf", [128, 2, T], F32, e4)
            nglb = b.sb("s5nglb", [128, 2], F32, e4)
            e4a = contextlib.ExitStack()
            Ysb = b.sb("s5Ysb", [128, 16, NC8], BF16, e4a)
            y2 = b.sb("s5y2", [128, 2, 8, 16, 16], BF16, e4a)
            for g in range(16):
                pair, g2 = divmod(g, 2)
                rows = slice(g2 * 64, g2 * 64 + 64)
                bi = g % 4
                yp = self.pv(bi, 0, 256)
                yk = self.pk(bi, 0, 256)
                b.op("pe", lambda: nc.tensor.matmul(yp, lhsT=Tg[:, g, :], rhs=Uflat[:, g, :], start=True, stop=False),
                     r=[("Tg", g), "Uflat"], w=yk)
                for d in range(2):
                    for ri in range(2):
                        b.op("pe", lambda: nc.tensor.matmul(yp, lhsT=QY[rows, d, ri, pair, :], rhs=Hs[rows, d, ri, pair, :],
                                                            start=False, stop=(d == 1 and ri == 1)),
                             r=["QY", ("Hs", d)], w=yk)
                b.op("act", lambda: nc.scalar.activation(out=Ysb[:, g, :], in_=yp, func=AF.Identity), r=yk, w=[("Ysb", g)])
            cnt = 0
            for half in range(2):
                for g4 in range(4):
                    bi = 4 + cnt % 2
                    cnt += 1
                    tp = self.pv(bi, 0, 256).bitcast(BF16)
                    for gg in range(4):
                        g = g4 * 4 + gg
                        b.op("pe", lambda: nc.tensor.transpose(tp[:, gg * 128:(gg + 1) * 128],
                                                               Ysb[:, g, half * 128:(half + 1) * 128], self.ident[:]),
                             r=[("Ysb", g), "ident"], w=self.pk(bi, 0, 256))
                    b.op("dve", lambda: V.tensor_copy(
                        out=y2[:, half, :, g4 * 4:(g4 + 1) * 4, :].rearrange("p i g k -> p g i k"),
                        in_=tp.rearrange("p (g i k) -> p g i k", g=4, i=8)), r=self.pk(bi, 0, 256), w=[("y2", half)])
            cnt = 0
            for half in range(2):
                for i2 in range(2):
                    bi = 6 + cnt % 2
                    cnt += 1
                    tp = self.pv(bi, 0, 512).bitcast(BF16)
                    for i4 in range(4):
                        ii_ = i2 * 4 + i4
                        for kk in range(2):
                            b.op("pe", lambda: nc.tensor.transpose(
                                tp[:, (i4 * 2 + kk) * 128:(i4 * 2 + kk + 1) * 128],
                                y2[:, half, ii_, kk * 8:(kk + 1) * 8, :].rearrange("p g k -> p (g k)"), self.ident[:]),
                                r=[("y2", half), "ident"], w=self.pk(bi, 0, 512))
                    for kk in range(2):
                        src = tp.rearrange("p (i k c) -> p i k c", i=4, k=2)[:, :, kk, :]
                        base = half * 1024 + i2 * 4
                        dst = yf[:, kk, half * 1024:(half + 1) * 1024].rearrange("p (c i) -> p i c", i=8)[:, i2 * 4:(i2 + 1) * 4, :]
                        b.op("act", lambda: nc.scalar.activation(out=dst, in_=src, func=AF.Identity), r=self.pk(bi, 0, 512),
                             w=["yf"])
            b.barrier()
            e4a.close()
            gt = b.sb("s5gt", [128, 2, T], F32, e4)
            geb = b.sb("s5geb", [128, 2, T], BF16, e4)
            c2 = 2.0 * math.sqrt(2.0 / math.pi)
            tt(gt[:], yf[:], yf[:], ALU.mult, ["yf"], ["gt"])
            ts(gt[:], gt[:], 0.044715, 1.0, ALU.mult, ALU.add, ["gt"], ["gt"])
            tt(gt[:], gt[:], yf[:], ALU.mult, ["gt", "yf"], ["gt"])
            ts(gt[:], gt[:], -30.0, None, ALU.max, ALU.bypass, ["gt"], ["gt"])
            b.op("act", lambda: nc.scalar.activation(out=gt[:], in_=gt[:], func=AF.Exp, scale=-c2), r=["gt"], w=["gt"])
            ts(gt[:], gt[:], 1.0, None, ALU.add, ALU.bypass, ["gt"], ["gt"])
            b.op("dve", lambda: V.reciprocal(out=gt[:], in_=gt[:]), r=["gt"], w=["gt"])
            tt(yf[:], yf[:], gt[:], ALU.mult, ["gt", "yf"], ["yf"])
            b.op("act", lambda: nc.scalar.activation(out=geb[:], in_=yf[:], func=AF.Identity), r=["yf"], w=["geb"])
            ts(nglb[:], glb[:], -1.0, None, ALU.mult, ALU.bypass, ["glb"], ["nglb"])
            for et in range(2):
                for blk in range(NBLK):
                    sl = slice(blk * 512, (blk + 1) * 512)
                    bi = (et * NBLK + blk) % 4
                    zp = self.pv(bi, 0, 512)
                    zk = self.pk(bi, 0, 512)
                    for kk in range(2):
                        b.op("pe", lambda: nc.tensor.matmul(zp, lhsT=gw[:, kk, et * 128:(et + 1) * 128], rhs=geb[:, kk, sl],
                                                            start=(kk == 0), stop=(kk == 1)), r=["gw", "geb"], w=zk)
                    g1 = gt[:, et, sl]
                    ts(g1, zp, glb[:, et:et + 1], None, ALU.add, ALU.bypass, zk + ["glb"], [("g1", et, blk)])
                    b.op("act", lambda: nc.scalar.activation(out=g1, in_=g1, func=AF.Exp, scale=-1.0),
                         r=[("g1", et, blk)], w=[("g1", et, blk)])
                    ts(g1, g1, 1.0, None, ALU.add, ALU.bypass, [("g1", et, blk)], [("g1", et, blk)])
                    b.op("dve", lambda: V.reciprocal(out=g1, in_=g1), r=[("g1", et, blk)], w=[("g1", et, blk)])
                    tt(self.yT[:, 6 + et, sl], yf[:, et, sl], g1, ALU.mult, ["yf", ("g1", et, blk)], [("y", 6 + et, blk)])
            b.barrier()

    def alloc_head_tmps(self, es, groupnorm=False):
        b = self.b
        tmp = {}
        lst = [("Pst", [128, 2, 128], F32), ("sT", [128, 2, 2, 128], BF16), ("osq", [128, 512], BF16),
               ("rstd", [128, 512], F32), ("e1", [128, 512], F32), ("t1", [128, 512], F32)]
        if groupnorm:
            lst += [("o32b", [128, 512], BF16), ("oc", [128, 512], F32)]
        for nm, shape, dt in lst:
            tmp[nm] = b.sb(nm, shape, dt, es)
        return tmp

    def even_mixer(self, j, es):
        b = self.b
        nc = self.nc
        tmp = self.alloc_head_tmps(es)
        wtm = b.sb("wtm", [128, 1, KT, 320], BF16, es)
        wfm = b.sb("wfm", [128, 2, KT, 128], BF16, es)
        wa2p = b.sb("wa2p", [32, 128], BF16, es)
        bah = b.sb("bah", [1, 128], BF16, es)
        lrT = b.sb("lrT", [32, T], BF16, es)
        qT = b.sb("qT", [128, T], BF16, es)
        kT = b.sb("kT", [128, T], BF16, es)
        kt_all = b.sb("kt_all", [128, NCH, 128], BF16, es)
        v_all = b.sb("v_all", [128, NCH, 128], BF16, es)
        v3 = b.sb("v3", [128, NCH, 128], BF16, es)
        S_all = b.sb("S_all", [128, NCH * 4, 128], BF16, es)
        Gall = b.sb("Gall", [128, NCH * 4], F32, es)
        LB = b.sb("LB", [128, 2, 256], F32, es)
        OML = b.sb("OML", [128, 2, 256], F32, es)
        es_lg = contextlib.ExitStack()
        lg = b.sb("lg", [128, 2, 2, 256], F32, es_lg)
        b.op("pool", lambda: nc.gpsimd.memset(S_all[:], 0.0), w=[("S", c, lo) for c in range(NCH * 4) for lo in (0, 64)])
        b.op("pool", lambda: nc.gpsimd.memset(v3[:], 0.0), w=[("v3", c) for c in range(NCH)])
        b.dma("sp", lg[:], self.lbl_d.partition_broadcast(128).rearrange("p (d l k) -> p d l k", d=2, l=2), w=["lg"], stream="c0")
        if j == 0:
            b.op("pool", lambda: nc.gpsimd.memset(LB[:], 0.0), w=["LB"])
            b.op("pool", lambda: nc.gpsimd.memset(OML[:], 1.0), w=["OML"])
        else:
            b.op("dve", lambda: nc.vector.tensor_tensor(out=OML[:], in0=lg[:, :, 0, :], in1=lg[:, :, 1, :], op=ALU.max),
                 r=["lg"], w=["OML"])
            for li in range(2):
                b.op("dve", lambda: nc.vector.tensor_tensor(out=lg[:, :, li, :], in0=lg[:, :, li, :], in1=OML[:],
                                                             op=ALU.subtract), r=["lg", "OML"], w=["lg"])
            b.op("act", lambda: nc.scalar.activation(out=lg[:], in_=lg[:], func=AF.Exp), r=["lg"], w=["lg"])
            b.op("dve", lambda: nc.vector.tensor_tensor(out=OML[:], in0=lg[:, :, 0, :], in1=lg[:, :, 1, :], op=ALU.add),
                 r=["lg"], w=["OML"])
            b.op("dve", lambda: nc.vector.reciprocal(out=OML[:], in_=OML[:]), r=["OML"], w=["OML"])
            b.op("dve", lambda: nc.vector.tensor_tensor(out=LB[:], in0=lg[:, :, 1, :], in1=OML[:], op=ALU.mult),
                 r=["lg", "OML"], w=["LB"])
            b.op("dve", lambda: nc.vector.tensor_scalar(out=OML[:], in0=LB[:], scalar1=-1.0, scalar2=1.0,
                                                        op0=ALU.mult, op1=ALU.add), r=["LB"], w=["OML"])
        b.barrier()
        es_lg.close()
        self.check("e_setup")
        ck = {}
        for nm, shape, dt in (("e", [128, 4, 128], F32), ("l", [128, 4, 128], F32), ("E", [128, 4, 128], F32),
                              ("Ei", [128, 4, 128], F32), ("qt", [128, 4, 128], BF16), ("key", [128, 4, 128], F32),
                              ("qs", [128, 4, 64], F32)):
            ck[nm] = b.sb(nm, shape, dt, es)
        b.dma("pool", wfm[:, 0, :, :], self.w_fm_e_d[j, 8, :, :].rearrange("p (k c) -> p k c", k=KT), w=["wfm"],
              stream="wfm")
        for blk in range(NBLK):
            sl = slice(blk * 512, (blk + 1) * 512)
            for k in range(KT):
                b.op("pe", lambda: nc.tensor.matmul(self.pv(0, 0, 512), lhsT=wfm[:, 0, k, :], rhs=self.hT[:, k, sl],
                                                    start=(k == 0), stop=(k == KT - 1)),
                     r=["wfm", ("h", k, blk)], w=self.pk(0, 0, 512))
            b.op("act", lambda: nc.scalar.activation(out=lrT[:, sl], in_=self.pv(0, 0, 512)[0:32, :], func=AF.Identity),
                 r=self.pk(0, 0, 512), w=["lrT"])
        self.check("e_lrT")
        for h in range(8):
            gla = h < 4
            hh = h % 4
            ncols = 256 if gla else 320
            slot = 0
            b.dma("pool", wtm[:, slot, :, :], self.w_tm_e_d[j, h, :, :].rearrange("p (k c) -> p k c", k=KT),
                  w=[("wtm", slot)], stream=f"wtm{slot}")
            b.dma("pool", wfm[:, 1, :, :], self.w_fm_e_d[j, h, :, :].rearrange("p (k c) -> p k c", k=KT), w=["wfm"],
                  stream="wfm")
            if gla:
                b.dma("pool", wa2p[:], self.wa2p_d[j, hh, :, :], w=["wa2p"], stream="wa2")
                b.dma("pool", bah[:], self.bah_d[j, hh, :, :], w=["bah"], stream="wa2")
            for c in range(NCH):
                ch = slice(c * 128, (c + 1) * 128)
                par = c % 4
                P = self.pv(par, 0, 512)
                Pk = self.pk(par, 0, ncols)
                for k in range(KT):
                    b.op("pe", lambda: nc.tensor.matmul(P[:, 0:ncols], lhsT=self.hT[:, k, ch], rhs=wtm[:, slot, k, 0:ncols],
                                                        start=(k == 0), stop=(k == KT - 1)),
                         r=[("wtm", slot), ("h", k, c // 4)], w=Pk)
                self.check("e_inproj")
                zb = 4 + par
                e, l, E, Ei, qt = (ck[n][:, par, :] for n in ("e", "l", "E", "Ei", "qt"))
                if gla:
                    zv = self.pv(zb, 0, 128)
                    zk = self.pk(zb, 0, 128)
                    b.op("pe", lambda: nc.tensor.matmul(zv, lhsT=lrT[0:32, ch], rhs=wa2p[0:32, :], start=True, stop=False),
                         r=["lrT", "wa2p"], w=zk)
                    self.check("e_z1")
                    b.op("pe", lambda: nc.tensor.matmul(zv, lhsT=self.ones_bf[0:1, :], rhs=bah[0:1, :], start=False,
                                                        stop=True), r=["ones", "bah"], w=zk)
                    self.check("e_z2")
                    b.op("act", lambda: nc.scalar.activation(out=e, in_=zv, func=AF.Exp, scale=-1.0), r=zk, w=[("e", par)])
                    self.check("e_z3")
                    b.op("dve", lambda: nc.vector.tensor_scalar_add(out=e, in0=e, scalar1=1.0), r=[("e", par)], w=[("e", par)])
                    b.op("act", lambda: nc.scalar.activation(out=l, in_=e, func=AF.Ln), r=[("e", par)], w=[("l", par)])
                    mi, s0, ns = 0, 0, 1
                    q_src, q_keys = P[:, 0:64], Pk
                    v_src = P[:, 128:256]
                else:
                    key = ck["key"][:, par, :]
                    qs = ck["qs"][:, par, :]
                    zz = P[:, 64:192]
                    b.op("act", lambda: nc.scalar.activation(out=e, in_=zz, func=AF.Exp, scale=-1.0), r=Pk, w=[("e", par)])
                    b.op("dve", lambda: nc.vector.tensor_scalar_add(out=e, in0=e, scalar1=1.0), r=[("e", par)], w=[("e", par)])
                    b.op("dve", lambda: nc.vector.reciprocal(out=e, in_=e), r=[("e", par)], w=[("e", par)])
                    e3 = e.rearrange("p (d k) -> p d k", d=2)
                    lbv = LB[:, :, hh * 64:(hh + 1) * 64]
                    omv = OML[:, :, hh * 64:(hh + 1) * 64]
                    b.op("dve", lambda: nc.vector.tensor_tensor(out=e3, in0=e3, in1=omv, op=ALU.mult),
                         r=[("e", par), "OML"], w=[("e", par)])
                    b.op("dve", lambda: nc.vector.tensor_tensor(out=e3, in0=e3, in1=lbv, op=ALU.add),
                         r=[("e", par), "LB"], w=[("e", par)])
                    b.op("dve", lambda: nc.vector.tensor_scalar(out=key, in0=e, scalar1=-1.0, scalar2=1.0, op0=ALU.mult,
                                                                op1=ALU.add), r=[("e", par)], w=[("key", par)])
                    b.op("dve", lambda: nc.vector.tensor_scalar_max(out=e, in0=e, scalar1=1e-20), r=[("e", par)],
                         w=[("e", par)])
                    b.op("act", lambda: nc.scalar.activation(out=l, in_=e, func=AF.Ln), r=[("e", par)], w=[("l", par)])
                    b.op("act", lambda: nc.scalar.activation(out=qs, in_=P[:, 0:64], func=AF.Exp, scale=-1.0), r=Pk,
                         w=[("qs", par)])
                    b.op("dve", lambda: nc.vector.tensor_scalar_add(out=qs, in0=qs, scalar1=1.0), r=[("qs", par)],
                         w=[("qs", par)])
                    b.op("dve", lambda: nc.vector.reciprocal(out=qs, in_=qs), r=[("qs", par)], w=[("qs", par)])
                    b.op("dve", lambda: nc.vector.tensor_tensor(out=qs, in0=P[:, 0:64], in1=qs, op=ALU.mult),
                         r=Pk + [("qs", par)], w=[("qs", par)])
                    mi, s0, ns = 2, 4, 4
                    q_src, q_keys = qs, [("qs", par)]
                    v_src = P[:, 192:320]
                self.check("e_z")
                for d in range(2):
                    b.op("pe", lambda: nc.tensor.matmul(self.pv(zb, 128 + d * 64, 192 + d * 64), lhsT=self.tri6[:, mi + d, :],
                                                        rhs=l[:, d * 64:(d + 1) * 64], start=True, stop=True),
                         r=["tri6", ("l", par)], w=self.pk(zb, 128, 256))
                b.op("pe", lambda: nc.tensor.matmul(self.pv(zb, 256, 260), lhsT=l, rhs=self.sumcols[:, s0:s0 + 4],
                                                    start=True, stop=True), r=["sumcols", ("l", par)], w=self.pk(zb, 256, 260))
                b.op("act", lambda: nc.scalar.activation(out=Gall[:, c * ns:(c + 1) * ns], in_=self.pv(zb, 256, 256 + ns),
                                                         func=AF.Exp), r=self.pk(zb, 256, 260), w=["G"])
                self.check("e_cum")
                Cv = self.pv(zb, 128, 256)
                Ck = self.pk(zb, 128, 256)
                b.op("act", lambda: nc.scalar.activation(out=E, in_=Cv, func=AF.Exp), r=Ck, w=[("E", par)])
                self.check("e_E1")
                b.op("act", lambda: nc.scalar.activation(out=Ei, in_=Cv, func=AF.Exp, scale=-1.0), r=Ck, w=[("Ei", par)])
                self.check("e_E2")
                for d in range(2):
                    cs = slice(d * 64, (d + 1) * 64)
                    b.op("dve", lambda: nc.vector.tensor_tensor(out=qt[:, cs], in0=q_src, in1=E[:, cs], op=ALU.mult),
                         r=q_keys + [("E", par)], w=[("qt", par)])
                    self.check("e_E3")
                    if gla:
                        b.op("dve", lambda: nc.vector.scalar_tensor_tensor(
                            out=kt_all[:, c, cs], in0=P[:, 64:128], scalar=0.125, in1=Ei[:, cs], op0=ALU.mult, op1=ALU.mult),
                            r=Pk + [("Ei", par)], w=[("kt", c)])
                if not gla:
                    b.op("dve", lambda: nc.vector.tensor_tensor(out=kt_all[:, c, :], in0=ck["key"][:, par, :], in1=Ei,
                                                                 op=ALU.mult), r=[("key", par), ("Ei", par)], w=[("kt", c)])
                self.check("e_E4")
                b.op("act", lambda: nc.scalar.activation(out=v_all[:, c, :], in_=v_src, func=AF.Identity), r=Pk, w=[("v", c)])
                if not gla:
                    b.op("act", lambda: nc.scalar.activation(out=v3[96:128, c, :], in_=v_src[96:128, :], func=AF.Identity),
                         r=Pk, w=[("v3", c)])
                self.check("e_E")
                tq = self.pv(zb, 384, 448).bitcast(BF16)
                tk = self.pv(zb, 448, 512).bitcast(BF16)
                tkey = self.pk(zb, 384, 512)
                b.op("pe", lambda: nc.tensor.transpose(tq, qt, self.ident[:]), r=[("qt", par), "ident"], w=tkey)
                b.op("pe", lambda: nc.tensor.transpose(tk, kt_all[:, c, :], self.ident[:]), r=[("kt", c), "ident"], w=tkey)
                b.op("act", lambda: nc.scalar.activation(out=qT[:, ch], in_=tq, func=AF.Identity), r=tkey, w=[("qT", c)])
                b.op("dve", lambda: nc.vector.tensor_copy(out=kT[:, ch], in_=tk), r=tkey, w=[("kT", c)])
            self.check("e_p1")
            gain_ap = self.hgain[:, j * 8 + h: j * 8 + h + 1]
            self.head_phase23(es, tmp, qT, kT, kt_all, v_all, v3, S_all, Gall, 64, (1 if gla else 4), wfm[:, 1, :, :], gain_ap, h)
            self.check("e_h%d" % h)


def prep_weights(inp):
    w = {}
    g_all = np.concatenate([inp["mix_norm_g"], inp["ffn_norm_g"], inp["final_norm_g"][None]], 0)
    w["gains"] = np.ascontiguousarray(g_all.reshape(9, KT, 128).transpose(2, 0, 1).reshape(128, 9 * KT))
    wu = inp["ffn_w_up"]
    L = wu.shape[0]
    wu5 = wu.reshape(L, KT, 128, 2, FT, 128)
    w["w_up_t"] = np.ascontiguousarray(wu5.transpose(0, 4, 3, 2, 1, 5).reshape(L, 2 * FT, 128, KT * 128))
    wd = inp["ffn_w_down"]
    wd6 = wd.reshape(L, 2, 11, 128, KT, 128)
    w["w_dn_t"] = np.ascontiguousarray(wd6.transpose(0, 1, 4, 3, 2, 5).reshape(L, 2, KT, 128, 11 * 128))
    cw = inp["ffn_conv_w"]
    cb = inp["ffn_conv_b"]
    c4 = np.concatenate([cw, cb[:, None, :]], 1)
    c4 = c4.reshape(L, 4, 2, FT, 128)
    w["conv_p"] = np.ascontiguousarray(c4.transpose(4, 0, 3, 2, 1).reshape(128, L * 2 * FT * 4))

    wo = np.stack([inp["w_out_even"][0], inp["w_out_odd"][0], inp["w_out_even"][1], inp["w_out_odd"][1]], 0)
    w["w_out_t"] = np.ascontiguousarray(wo.reshape(L, KT, 128, KT, 128).transpose(0, 3, 2, 1, 4).reshape(L, KT, 128, KT * 128))
    wie = inp["w_in_even"]
    o = np.cumsum([0, 256, 256, 512, 512, 16, 16, 256, 256, 256, 512, 512])
    gq, gk, gv, gr, glf, glb, hq, hzf, hzb, hi, hg = (wie[:, :, o[i]:o[i + 1]] for i in range(11))
    tm = np.zeros((2, 8, 1024, 320), np.float32)
    for h in range(4):
        tm[:, h, :, 0:64] = gq[:, :, h * 64:(h + 1) * 64]
        tm[:, h, :, 64:128] = gk[:, :, h * 64:(h + 1) * 64]
        tm[:, h, :, 128:256] = gv[:, :, h * 128:(h + 1) * 128]
        tm[:, 4 + h, :, 0:64] = hq[:, :, h * 64:(h + 1) * 64]
        tm[:, 4 + h, :, 64:128] = hzf[:, :, h * 64:(h + 1) * 64]
        tm[:, 4 + h, :, 128:192] = hzb[:, :, h * 64:(h + 1) * 64]
        tm[:, 4 + h, :, 192:320] = hi[:, :, h * 128:(h + 1) * 128]
    w["w_tm_e"] = np.ascontiguousarray(tm.reshape(2, 8, KT, 128, 320).transpose(0, 1, 3, 2, 4).reshape(2, 8, 128, KT * 320))
    fm = np.zeros((2, 9, 1024, 128), np.float32)
    for h in range(4):
        fm[:, h] = gr[:, :, h * 128:(h + 1) * 128]
        fm[:, 4 + h] = hg[:, :, h * 128:(h + 1) * 128]
    fm[:, 8, :, 0:16] = glf
    fm[:, 8, :, 16:32] = glb
    w["w_fm_e"] = np.ascontiguousarray(fm.reshape(2, 9, KT, 128, 128).transpose(0, 1, 3, 2, 4).reshape(2, 9, 128, KT * 128))
    wa2 = inp["gla_wa2"]
    ba = inp["gla_ba"]
    wa2p = np.zeros((2, 4, 32, 128), np.float32)
    bah = np.zeros((2, 4, 1, 128), np.float32)
    for h in range(4):
        wa2p[:, h, 0:16, 0:64] = wa2[:, 0, :, h * 64:(h + 1) * 64]
        wa2p[:, h, 16:32, 64:128] = wa2[:, 1, :, h * 64:(h + 1) * 64]
        bah[:, h, 0, 0:64] = ba[:, 0, h * 64:(h + 1) * 64]
        bah[:, h, 0, 64:128] = ba[:, 1, h * 64:(h + 1) * 64]
    w["wa2p"] = wa2p
    w["bah"] = bah
    w["lbl"] = np.ascontiguousarray(inp["hgrn_lb_logits"].reshape(-1))
    hg_ = np.zeros((128, 16), np.float32)
    for j in range(2):
        for h in range(4):
            hg_[:, j * 8 + h] = inp["gla_norm_g"][j, h * 128:(h + 1) * 128]
            hg_[:, j * 8 + 4 + h] = inp["hgrn_norm_g"][j, h * 128:(h + 1) * 128]
    w["hgain"] = hg_
    ii = np.arange(128)
    triL = (ii[:, None] <= ii[None, :]).astype(np.float32)
    triU = (ii[:, None] >= ii[None, :]).astype(np.float32)
    blk32 = (ii[:, None] // 32 == ii[None, :] // 32).astype(np.float32)
    w["tri6"] = np.ascontiguousarray(np.stack([triL * (-1.0 / 16), triU * (-1.0 / 16), triL * blk32, triU * blk32,
                                               triL, triU], 1).reshape(128, 768))
    sc = np.zeros((128, 8), np.float32)
    sc[:, 0:4] = -1.0 / 16
    for q_ in range(4):
        sc[32 * q_:32 * (q_ + 1), 4 + q_] = 1.0
    w["sumcols"] = sc
    w["ident"] = np.eye(128, dtype=np.float32)

    wio = inp["w_in_odd"]
    oo = np.cumsum([0, 512, 512, 768, 768, 256])
    rq, rk, rv, rg, su = (wio[:, :, oo[i]:oo[i + 1]] for i in range(5))
    tmo = np.zeros((2, 4, 1024, 448), np.float32)
    for h in range(4):
        tmo[:, h, :, 0:128] = rq[:, :, h * 128:(h + 1) * 128]
        tmo[:, h, :, 128:256] = rk[:, :, h * 128:(h + 1) * 128]
        tmo[:, h, :, 256:448] = rv[:, :, h * 192:(h + 1) * 192]
    w["w_tm_o"] = np.ascontiguousarray(tmo.reshape(2, 4, KT, 128, 448).transpose(0, 1, 3, 2, 4).reshape(2, 4, 128, KT * 448))
    fmo = np.zeros((2, 8, 1024, 128), np.float32)
    hgo = np.zeros((128, 16), np.float32)
    for h in range(4):
        for pi, (f0, nf, etile, p0) in enumerate(Prog.RET_PIECES[h]):
            fmo[:, 2 * h + pi, :, 0:nf] = rg[:, :, h * 192 + f0: h * 192 + f0 + nf]
            for j in range(2):
                hgo[p0:p0 + nf, j * 8 + 2 * h + pi] = inp["ret_norm_g"][j, h * 192 + f0: h * 192 + f0 + nf]
    w["w_fm_o"] = np.ascontiguousarray(fmo.reshape(2, 8, KT, 128, 128).transpose(0, 1, 3, 2, 4).reshape(2, 8, 128, KT * 128))
    w["hgain_o"] = hgo
    w["w_u_o"] = np.ascontiguousarray(su.reshape(2, KT, 128, 256).transpose(0, 2, 1, 3).reshape(2, 128, KT * 256))
    half = 64
    inv = (10000.0 ** (-np.arange(half, dtype=np.float32) / half)).astype(np.float32)
    ang = (np.arange(T, dtype=np.float32)[:, None] * inv[None, :]).astype(np.float32)
    cs = np.stack([np.cos(ang), np.sin(ang)], 0).reshape(2, NCH, 128, half)
    w["rope"] = np.ascontiguousarray(cs.transpose(2, 0, 1, 3).reshape(128, 2 * NCH * half)).astype(np.float32)
    ti = np.arange(128, dtype=np.float64)
    rdec = np.zeros((128, 16), np.float64)
    dm = np.zeros((128, 4, 128), np.float64)
    for h in range(4):
        gf = 1.0 - 2.0 ** (-5.0 - h)
        gb = 1.0 - 2.0 ** (-5.5 - h)
        rdec[:, 4 * h + 0] = gf ** (ti + 1)
        rdec[:, 4 * h + 1] = gb ** (128 - ti)
        rdec[:, 4 * h + 2] = gf ** (127 - ti) * 128.0 ** -0.5
        rdec[:, 4 * h + 3] = gb ** ti * 128.0 ** -0.5
        dji = ti[None, :] - ti[:, None]
        dm[:, h, :] = np.where(dji >= 0, gf ** np.abs(dji), 0.0) + np.where(dji <= 0, gb ** np.abs(dji), 0.0)
    w["rdec"] = rdec.astype(np.float32)
    w["dmask"] = np.ascontiguousarray(dm.reshape(128, 512)).astype(np.float32)

    def qp(a):
        sh = a.shape
        a = a.reshape(sh[:-3] + (8, 2, 64, sh[-1]))
        nd = a.ndim
        perm = (nd - 3, nd - 2) + tuple(range(nd - 4)) + (nd - 4, nd - 1)
        a = a.transpose(perm)
        return a.reshape((128,) + a.shape[2:])
    prm = np.zeros((2, 128, 3, 2, 8), np.float32)
    bc = np.zeros((2, 128, 4, 8, 16), np.float32)
    for j in range(2):
        prm[j, :, 0] = qp(inp["s5_lam_re"][j][..., None])[..., 0]
        prm[j, :, 1] = qp(inp["s5_lam_im"][j][..., None])[..., 0]
        ldt = np.broadcast_to(inp["s5_log_dt"][j][:, :, None, None], (2, 16, 64, 1))
        prm[j, :, 2] = qp(np.ascontiguousarray(ldt))[..., 0]
        bc[j, :, 0] = qp(inp["s5_b_re"][j])
        bc[j, :, 1] = qp(inp["s5_b_im"][j])
        bc[j, :, 2] = qp(np.ascontiguousarray(inp["s5_c_re"][j].transpose(0, 2, 1)))
        bc[j, :, 3] = qp(np.ascontiguousarray(inp["s5_c_im"][j].transpose(0, 2, 1)))
    w["s5_prm"] = prm.reshape(2, 128, 48)
    w["s5_bc"] = bc.reshape(2, 128, 512)
    dsk = inp["s5_d"].reshape(2, 16, 16)
    w["s5_dsk"] = np.ascontiguousarray(np.broadcast_to(dsk.transpose(0, 2, 1)[:, None, :, :], (2, 8, 16, 16)).reshape(2, 128, 16))
    w["s5_glb"] = np.ascontiguousarray(inp["s5_glu_b"].reshape(2, 2, 128).transpose(0, 2, 1))
    w["s5_gw"] = np.ascontiguousarray(inp["s5_glu_w"].reshape(2, 2, 128, 256).transpose(0, 2, 1, 3).reshape(2, 128, 512))
    jj_ = np.arange(128) // 16
    bm = np.stack([(jj_[:, None] <= jj_[None, :]), (jj_[:, None] >= jj_[None, :])], 1).astype(np.float32)
    w["s5_bm"] = np.ascontiguousarray(bm.reshape(128, 256))
    return w


_CFG = {}


def kernel(**inputs):
    inp = {k: np.asarray(v) for k, v in inputs.items()}
    cfg = dict(_CFG)
    x = inp["x"]
    w = prep_weights(inp)
    import time as _t
    _t0 = _t.time()
    prog = Prog(cfg)
    nc = prog.build()
    print("[kernel] build %.1fs, instr counts %s" % (_t.time() - _t0, dict(prog.b.cnt)), flush=True)
    in_maps = []
    ncores = cfg.get("ncores", NCORES)
    for c in range(ncores):
        xs = x[c * NSEQ:(c + 1) * NSEQ]
        xl = np.ascontiguousarray(xs.reshape(NSEQ, T, KT, 128).transpose(0, 3, 2, 1))
        m = {"x_in": xl}
        m.update(w)
        in_maps.append(m)
    _t0 = _t.time()
    res = run_bass_kernel_spmd(nc, in_maps, core_ids=list(range(ncores)))
    print("[kernel] run %.1fs" % (_t.time() - _t0), flush=True)
    outs = []
    for c in range(ncores):
        y = res.results[c]["y_out"]
        outs.append(np.ascontiguousarray(y.transpose(0, 3, 2, 1)).reshape(NSEQ, T, D))
    return np.concatenate(outs, 0).astype(np.float32)
```

```python
import contextlib
import math
import numpy as np
import concourse.bass as bass
import concourse.mybir as mybir
from concourse.bass_utils import run_bass_kernel_spmd

F32 = mybir.dt.float32
BF16 = mybir.dt.bfloat16
AF = mybir.ActivationFunctionType
ALU = mybir.AluOpType

D = 1024
T = 2048
KT = D // 128
NBLK = T // 512
NCH = T // 128
DEPTH = 4
FF = 2816
FT = FF // 128
NSEQ = 2
NCORES = 8
EPS = 1e-6
SAME_ENG_SYNC = True


class Builder:
    def __init__(self):
        self.nc = bass.Bass("TRN2", target_bir_lowering=False, dynamic_dma_scratch_size=4096)
        nc = self.nc
        self.es = contextlib.ExitStack()
        self.eng = dict(pe=nc.tensor, act=nc.scalar, dve=nc.vector, pool=nc.gpsimd, sp=nc.sync)
        self.sem = {e: self.es.enter_context(nc.semaphore("s_" + e)) for e in self.eng}
        self.cnt = {e: 0 for e in self.eng}
        self.seen = {e: {} for e in self.eng}
        self.parts = {}
        self.streams = {}
        self.uid = 0
        self.muted = False

    def sb(self, name, shape, dt, es=None):
        self.uid += 1
        return (es or self.es).enter_context(self.nc.sbuf_tensor(f"{name}_{self.uid}", list(shape), dt))

    def dram_in(self, name, shape, dt=F32):
        return self.nc.dram_tensor(name, list(shape), dt, kind="ExternalInput").ap()

    def dram_out(self, name, shape, dt=F32):
        return self.nc.dram_tensor(name, list(shape), dt, kind="ExternalOutput").ap()

    def _part(self, k):
        p = self.parts.get(k)
        if p is None:
            p = [[], []]
            self.parts[k] = p
        return p

    def _wait(self, e, tickets):
        need = {}
        for (key, h, v, src) in tickets:
            if src == e and (e == "pe" or not SAME_ENG_SYNC):
                continue
            if src is None:
                v = 16 * self.streams[key[2:]][1]
            if self.seen[e].get(key, 0) >= v:
                continue
            if key not in need or need[key][1] < v:
                need[key] = (h, v)
        for key, (h, v) in need.items():
            self.eng[e].wait_ge(h, v)
            self.seen[e][key] = v

    def _deps(self, r, w):
        deps = []
        for k in r:
            deps += self._part(k)[0]
        for k in w:
            p = self._part(k)
            deps += p[0] + p[1]
        return deps

    def _record(self, t, r, w):
        for k in r:
            p = self._part(k)
            p[1] = [x for x in p[1] if x[0] != t[0]] + [t]
        for k in w:
            p = self._part(k)
            p[0] = [t]
            p[1] = []

    def op(self, e, fn, r=(), w=()):
        if self.muted:
            return None
        pr = [k for k in r if isinstance(k, tuple) and k[0] in ("ps", "bank")]
        if pr:
            r = [k for k in r if k not in pr]
            w = list(w) + [k for k in pr if k not in w]
        self._wait(e, self._deps(r, w))
        ins = fn()
        self.cnt[e] += 1
        ins.then_inc(self.sem[e], 1)
        self._record(("e_" + e, self.sem[e], self.cnt[e], e), r, w)
        return ins

    def dma(self, q, out, in_, r=(), w=(), stream="d0"):
        if self.muted:
            return
        st = self.streams.get(stream)
        if st is None:
            st = [self.es.enter_context(self.nc.semaphore("d_" + stream)), 0]
            self.streams[stream] = st
        self._wait(q, self._deps(r, w))
        ins = self.eng[q].dma_start(out=out, in_=in_)
        st[1] += 1
        ins.then_inc(st[0], 16)
        self._record(("d_" + stream, st[0], 16 * st[1], None), r, w)

    def barrier(self):
        ts = [("e_" + e, self.sem[e], self.cnt[e], e) for e in self.eng if self.cnt[e] > 0]
        ts += [("d_" + s, st[0], 16 * st[1], None) for s, st in self.streams.items() if st[1] > 0]
        for e in self.eng:
            self._wait(e, [t for t in ts if t[3] != e])
        self.parts = {}

    def finish(self):
        self.barrier()
        self.es.close()


class StopBuild(Exception):
    pass


class Prog:
    def check(self, tag):
        if self.cfg.get("stop") == tag:
            self.b.muted = True

    def __init__(self, cfg):
        self.cfg = cfg
        self.b = Builder()
        b = self.b
        nc = b.nc
        self.nc = nc
        self.x_in = b.dram_in("x_in", [NSEQ, 128, KT, T])
        self.y_out = b.dram_out("y_out", [NSEQ, 128, KT, T])
        self.gains_d = b.dram_in("gains", [128, 9 * KT])
        self.w_up_d = b.dram_in("w_up_t", [DEPTH, 2 * FT, 128, KT * 128])
        self.w_dn_d = b.dram_in("w_dn_t", [DEPTH, 2, KT, 128, 11 * 128])
        self.conv_d = b.dram_in("conv_p", [128, DEPTH * 2 * FT * 4])
        self.w_out_d = b.dram_in("w_out_t", [DEPTH, KT, 128, KT * 128])
        self.w_tm_e_d = b.dram_in("w_tm_e", [2, 8, 128, KT * 320])
        self.w_fm_e_d = b.dram_in("w_fm_e", [2, 9, 128, KT * 128])
        self.wa2p_d = b.dram_in("wa2p", [2, 4, 32, 128])
        self.bah_d = b.dram_in("bah", [2, 4, 1, 128])
        self.lbl_d = b.dram_in("lbl", [2 * 2 * 256])
        self.hgain_d = b.dram_in("hgain", [128, 16])
        self.tri6_d = b.dram_in("tri6", [128, 6 * 128])
        self.w_tm_o_d = b.dram_in("w_tm_o", [2, 4, 128, KT * 448])
        self.w_fm_o_d = b.dram_in("w_fm_o", [2, 8, 128, KT * 128])
        self.w_u_o_d = b.dram_in("w_u_o", [2, 128, KT * 256])
        self.hgain_o_d = b.dram_in("hgain_o", [128, 16])
        self.rope_d = b.dram_in("rope", [128, 2 * NCH * 64])
        self.rdec_d = b.dram_in("rdec", [128, 16])
        self.dmask_d = b.dram_in("dmask", [128, 4 * 128])
        self.s5_gw_d = b.dram_in("s5_gw", [2, 128, 2 * 256])
        self.s5_prm_d = b.dram_in("s5_prm", [2, 128, 3 * 2 * 8])
        self.s5_bc_d = b.dram_in("s5_bc", [2, 128, 4 * 8 * 16])
        self.s5_dsk_d = b.dram_in("s5_dsk", [2, 128, 16])
        self.s5_glb_d = b.dram_in("s5_glb", [2, 128, 2])
        self.s5_bm_d = b.dram_in("s5_bm", [128, 2 * 128])
        self.sumcols_d = b.dram_in("sumcols", [128, 8])
        self.ident_d = b.dram_in("ident", [128, 128])
        self.xT = b.sb("xT", [128, KT, T], F32)
        self.hT = b.sb("hT", [128, KT, T], BF16)
        self.gains = b.sb("gains", [128, 9 * KT], F32)
        self.ones_bf = b.sb("ones", [128, 128], BF16)
        self.hgain = b.sb("hgain", [128, 16], F32)
        self.hgain_o = b.sb("hgain_o", [128, 16], F32)
        self.tri6 = b.sb("tri6", [128, 6, 128], F32)
        self.sumcols = b.sb("sumcols", [128, 8], F32)
        self.ident = b.sb("ident", [128, 128], BF16)
        self.one_t = b.sb("one", [128, 1], F32)
        self.ps = b.es.enter_context(nc.psum_tensor("ps_all", [128, 4096], F32))
        self.bankrr = 0

    def bank(self, i):
        return self.ps[:, i * 512:(i + 1) * 512]


    def pk(self, bank, c0, c1):
        return [("ps", bank)]

    def pv(self, bank, c0, c1):
        return self.ps[:, bank * 512 + c0: bank * 512 + c1]

    def next_bank(self):
        i = self.bankrr
        self.bankrr = (self.bankrr + 1) % 8
        return i

    def setup(self):
        b = self.b
        nc = self.nc
        b.dma("sp", self.gains[:], self.gains_d[:, :], w=["gains"], stream="c0")
        b.op("pool", lambda: nc.gpsimd.memset(self.ones_bf[:], 1.0), w=["ones"])
        b.op("pool", lambda: nc.gpsimd.memset(self.one_t[:], 1.0), w=["one"])
        b.dma("sp", self.hgain[:], self.hgain_d[:, :], w=["hgain"], stream="c0")
        b.dma("sp", self.hgain_o[:], self.hgain_o_d[:, :], w=["hgain_o"], stream="c0")
        b.dma("sp", self.tri6[:], self.tri6_d[:, :].rearrange("p (a c) -> p a c", a=6), w=["tri6"], stream="c0")
        b.dma("sp", self.sumcols[:], self.sumcols_d[:, :], w=["sumcols"], stream="c0")
        b.dma("pool", self.ident[:], self.ident_d[:, :], w=["ident"], stream="c1")

    def load_x(self, s):
        b = self.b
        for k in range(KT):
            b.dma("sp", self.xT[:, k, :], self.x_in[s, :, k, :], w=[("x", k, blk) for blk in range(NBLK)],
                  stream="xin")

    def store_x(self, s):
        b = self.b
        for k in range(KT):
            b.dma("sp", self.y_out[s, :, k, :], self.xT[:, k, :], r=[("x", k, blk) for blk in range(NBLK)],
                  stream="xout")

    def rmsnorm(self, gidx, out_f32_inplace=False):
        b = self.b
        nc = self.nc
        with contextlib.ExitStack() as es:
            sq = b.sb("sq", [128, 2, KT, 512], BF16, es)
            rs = b.sb("rstd", [128, 2, 512], F32, es)
            for blk in range(NBLK):
                par = blk % 2
                sl = slice(blk * 512, (blk + 1) * 512)
                xk = [("x", k, blk) for k in range(KT)]
                b.op("act", lambda: nc.scalar.activation(out=sq[:, par, :, :], in_=self.xT[:, :, sl], func=AF.Square),
                     r=xk, w=[("sq", par)])
                bi = self.next_bank()
                for k in range(KT):
                    b.op("pe", lambda: nc.tensor.matmul(self.bank(bi), lhsT=self.ones_bf[:], rhs=sq[:, par, k, :],
                                                        start=(k == 0), stop=(k == KT - 1)),
                         r=[("sq", par), "ones"], w=[("bank", bi)])
                b.op("act", lambda: nc.scalar.activation(out=rs[:, par, :], in_=self.bank(bi), func=AF.Sqrt,
                                                         scale=1.0 / D, bias=self.eps_t[:, 0:1]),
                     r=[("bank", bi), "eps"], w=[("rs", par)])
                b.op("dve", lambda: nc.vector.reciprocal(out=rs[:, par, :], in_=rs[:, par, :]),
                     r=[("rs", par)], w=[("rs", par)])
                for k in range(KT):
                    g = self.gains[:, gidx * KT + k: gidx * KT + k + 1]
                    if out_f32_inplace:
                        b.op("dve", lambda: nc.vector.scalar_tensor_tensor(
                            out=self.xT[:, k, sl], in0=self.xT[:, k, sl], scalar=g, in1=rs[:, par, :],
                            op0=ALU.mult, op1=ALU.mult), r=[("x", k, blk), ("rs", par), "gains"], w=[("x", k, blk)])
                    else:
                        b.op("dve", lambda: nc.vector.scalar_tensor_tensor(
                            out=self.hT[:, k, sl], in0=self.xT[:, k, sl], scalar=g, in1=rs[:, par, :],
                            op0=ALU.mult, op1=ALU.mult), r=[("x", k, blk), ("rs", par), "gains"], w=[("h", k, blk)])
            b.barrier()

    def ffn(self, layer):
        b = self.b
        nc = self.nc
        self.rmsnorm(DEPTH + layer)
        hall = [("h", k, blk) for k in range(KT) for blk in range(NBLK)]
        with contextlib.ExitStack() as es:
            g = b.sb("ffg", [128, 11, T], BF16, es)
            wup = b.sb("wup", [128, 4, KT, 128], BF16, es)
            wdn = b.sb("wdn", [128, 2, 11, 128], BF16, es)
            cbuf = b.sb("cbuf", [128, 2, T], F32, es)
            sbuf = b.sb("sbuf", [128, T], BF16, es)
            cp = b.sb("convp", [128, 2 * FT * 4], F32, es)
            b.dma("sp", cp[:], self.conv_d[:, layer * 2 * FT * 4:(layer + 1) * 2 * FT * 4], w=["convp"], stream="c0")
            ucount = 0
            dcount = 0
            NSL = 4
            issued = [0]

            def issue_up(upto):
                while issued[0] <= min(upto, 2 * FT - 1):
                    u_ = issued[0]
                    sl_ = u_ % NSL
                    b.dma("pool", wup[:, sl_, :, :], self.w_up_d[layer, u_, :, :].rearrange("p (k c) -> p k c", k=KT),
                          w=[("wup", sl_)], stream=f"wup{sl_}")
                    issued[0] += 1

            for half in range(2):
                for dt0 in range(2):
                    b.dma("pool", wdn[:, dt0, :, :], self.w_dn_d[layer, half, dt0, :, :].rearrange("p (k c) -> p k c", k=11),
                          w=[("wdn", dt0)], stream=f"wdn{dt0}")
                for mm in range(11):
                    m = half * 11 + mm
                    for kind in range(2):
                        unit = 2 * m + kind
                        slot = unit % NSL
                        par = ucount % 2
                        ucount += 1
                        issue_up(unit + 2)
                        banks = [4 * par + i for i in range(4)]
                        for blk in range(NBLK):
                            for k in range(KT):
                                b.op("pe", lambda: nc.tensor.matmul(
                                    self.bank(banks[blk]), lhsT=wup[:, slot, k, :],
                                    rhs=self.hT[:, k, blk * 512:(blk + 1) * 512],
                                    start=(k == 0), stop=(k == KT - 1)),
                                    r=[("wup", slot), ("h", k, blk)], w=[("bank", banks[blk])])
                        u = self.ps[:, 2048 * par: 2048 * (par + 1)]
                        base = unit * 4
                        w0, w1, w2, cb = (cp[:, base + i: base + i + 1] for i in range(4))
                        c = cbuf[:, par, :]
                        H = T // 2
                        for hf in range(2):
                            t0, t1 = hf * H, (hf + 1) * H
                            bk = [("bank", banks[2 * hf]), ("bank", banks[2 * hf + 1])]
                            ck_ = ("c", par, hf)
                            b.op("act", lambda: nc.scalar.activation(out=c[:, t0:t1], in_=u[:, t0:t1], func=AF.Identity,
                                                                     scale=w1, bias=cb), r=bk + ["convp"], w=[ck_])
                            a0 = max(t0, 1)
                            bkl = bk + ([("bank", banks[1])] if hf == 1 else [])
                            b.op("dve", lambda: nc.vector.scalar_tensor_tensor(
                                out=c[:, a0:t1], in0=u[:, a0 - 1:t1 - 1], scalar=w0, in1=c[:, a0:t1], op0=ALU.mult,
                                op1=ALU.add), r=bkl + ["convp", ck_], w=[ck_])
                            a1 = min(t1, T - 1)
                            bkr = bk + ([("bank", banks[2])] if hf == 0 else [])
                            b.op("dve", lambda: nc.vector.scalar_tensor_tensor(
                                out=c[:, t0:a1], in0=u[:, t0 + 1:a1 + 1], scalar=w2, in1=c[:, t0:a1], op0=ALU.mult,
                                op1=ALU.add), r=bkr + ["convp", ck_], w=[ck_])
                            if kind == 0:
                                b.op("act", lambda: nc.scalar.activation(out=sbuf[:, t0:t1], in_=c[:, t0:t1], func=AF.Silu),
                                     r=[ck_], w=[("s", hf)])
                            else:
                                b.op("pool", lambda: nc.gpsimd.tensor_tensor(out=g[:, mm, t0:t1], in0=sbuf[:, t0:t1],
                                                                               in1=c[:, t0:t1], op=ALU.mult),
                                     r=[ck_, ("s", hf)], w=[("g", mm, hf)])
                for dt in range(KT):
                    slot = dt % 2
                    for blk in range(NBLK):
                        bi = self.next_bank()
                        for kk in range(11):
                            b.op("pe", lambda: nc.tensor.matmul(
                                self.bank(bi), lhsT=wdn[:, slot, kk, :], rhs=g[:, kk, blk * 512:(blk + 1) * 512],
                                start=(kk == 0), stop=(kk == 10)),
                                r=[("wdn", slot), ("g", kk, blk // 2)], w=[("bank", bi)])
                        sl = slice(blk * 512, (blk + 1) * 512)
                        b.op("dve", lambda: nc.vector.tensor_tensor(out=self.xT[:, dt, sl], in0=self.bank(bi),
                                                                     in1=self.xT[:, dt, sl], op=ALU.add),
                             r=[("bank", bi), ("x", dt, blk)], w=[("x", dt, blk)])
                    if dt + 2 < KT:
                        b.dma("pool", wdn[:, slot, :, :],
                              self.w_dn_d[layer, half, dt + 2, :, :].rearrange("p (k c) -> p k c", k=11),
                              w=[("wdn", slot)], stream=f"wdn{slot}")
            b.barrier()

    def build(self):
        b = self.b
        nc = self.nc
        cfg = self.cfg
        self.eps_t = b.sb("eps", [128, 1], F32)
        b.op("pool", lambda: nc.gpsimd.memset(self.eps_t[:], EPS), w=["eps"])
        self.setup()
        for s in range(cfg.get("nseq", NSEQ)):
            self.load_x(s)
            try:
                for layer in cfg.get("layers", range(DEPTH)):
                    if "mix" in cfg.get("phases", ("mix", "ffn")):
                        self.mixer(layer)
                    if "ffn" in cfg.get("phases", ("mix", "ffn")):
                        self.ffn(layer)
            except StopBuild:
                pass
            b.muted = False
            b.barrier()
            if cfg.get("final", True):
                self.rmsnorm(2 * DEPTH, out_f32_inplace=True)
            self.store_x(s)
            b.barrier()
        b.finish()
        return nc


    def mixer(self, layer):
        self.rmsnorm(layer)
        with contextlib.ExitStack() as es:
            self.yT = self.b.sb("yT", [128, KT, T], BF16, es)
            if layer % 2 == 0:
                self.even_mixer(layer // 2, es)
            else:
                self.odd_mixer(layer // 2, es)
            if self.cfg.get("dbg_y"):
                for k in self.cfg.get("dbg_tiles", range(KT)):
                    self.b.op("act", lambda: self.nc.scalar.activation(out=self.xT[:, k, :], in_=self.yT[:, k, :], func=AF.Identity),
                              r=[("y", k, blk) for blk in range(NBLK)], w=[("x", k, blk) for blk in range(NBLK)])
            else:
                self.out_proj(layer, es)
            self.b.barrier()

    def out_proj(self, layer, es):
        b = self.b
        nc = self.nc
        wo = b.sb("wo", [128, 2, KT, 128], BF16, es)
        for dt in range(KT):
            slot = dt % 2
            b.dma("pool", wo[:, slot, :, :], self.w_out_d[layer, dt, :, :].rearrange("p (k c) -> p k c", k=KT),
                  w=[("wo", slot)], stream=f"wo{slot}")
            for blk in range(NBLK):
                bi = dt % 2
                for k in range(KT):
                    b.op("pe", lambda: nc.tensor.matmul(self.pv(bi, 0, 512), lhsT=wo[:, slot, k, :],
                                                        rhs=self.yT[:, k, blk * 512:(blk + 1) * 512],
                                                        start=(k == 0), stop=(k == KT - 1)),
                         r=[("wo", slot), ("y", k, blk)], w=self.pk(bi, 0, 512))
                sl = slice(blk * 512, (blk + 1) * 512)
                b.op("dve", lambda: nc.vector.tensor_tensor(out=self.xT[:, dt, sl], in0=self.pv(bi, 0, 512),
                                                             in1=self.xT[:, dt, sl], op=ALU.add),
                     r=self.pk(bi, 0, 512) + [("x", dt, blk)], w=[("x", dt, blk)])

    def head_phase23(self, es, tmp, qT, kT, kt_all, v_all, v3, S_all, Gall, dk, nsub, gate_w, gain_ap, etile,
                     groupnorm=False):
        b = self.b
        nc = self.nc
        Pst = tmp["Pst"]
        sub = 128 // nsub
        nsc = NCH * nsub

        def kv_ops(g):
            c, s_ = divmod(g, nsub)
            if nsub == 1:
                return c, slice(0, 128), v_all, ("v", c)
            if s_ < 3:
                return c, slice(32 * s_, 32 * s_ + 32), v_all, ("v", c)
            return c, slice(64, 128), v3, ("v3", c)

        for step in range(nsc - 1):
            for d, lo in ((0, 0), (1, dk)):
                g = step if d == 0 else nsc - 1 - step
                nxt = 1 if d == 0 else -1
                c, rows, vv, vkey = kv_ops(g)
                bank = (4 if d == 0 else 6) + step % 2
                kv = self.pv(bank, 0, 128)[lo:lo + dk, :]
                kvk = self.pk(bank, 0, 128)
                prow = slice(lo, lo + dk)
                b.op("pe", lambda: nc.tensor.matmul(kv, lhsT=kt_all[rows, c, lo:lo + dk], rhs=vv[rows, c, :],
                                                    start=True, stop=True), r=[("kt", c), vkey], w=kvk)
                pp = step % 2
                pkey = ("Pst", lo, pp)
                pold = ("Pst", lo, 1 - pp)
                if step == 0:
                    b.op("dve", lambda: nc.vector.tensor_copy(out=Pst[prow, pp, :], in_=kv), r=kvk, w=[pkey])
                else:
                    gprev = Gall[prow, g - nxt: g - nxt + 1]
                    b.op("dve", lambda: nc.vector.scalar_tensor_tensor(
                        out=Pst[prow, pp, :], in0=Pst[prow, 1 - pp, :], scalar=gprev, in1=kv, op0=ALU.mult, op1=ALU.add),
                        r=kvk + [pold, "G"], w=[pkey])
                b.op("act", lambda: nc.scalar.activation(out=S_all[prow, g + nxt, :], in_=Pst[prow, pp, :], func=AF.Identity,
                                                         scale=Gall[prow, g: g + 1]),
                     r=[pkey, "G"], w=[("S", g + nxt, lo)])
        self.check("e_p2")
        mF = 4 if nsub == 1 else 2
        maskF = self.tri6[:, mF, :]
        maskB = self.tri6[:, mF + 1, :]
        K2 = 2 * dk
        for blk in range(NBLK):
            sl = slice(blk * 512, (blk + 1) * 512)
            for k in range(KT):
                b.op("pe", lambda: nc.tensor.matmul(self.pv(2, 0, 512), lhsT=gate_w[:, k, :], rhs=self.hT[:, k, sl],
                                                    start=(k == 0), stop=(k == KT - 1)),
                     r=["wfm", ("h", k, blk)], w=self.pk(2, 0, 512))
            for cc in range(4):
                c = blk * 4 + cc
                ch = slice(c * 128, (c + 1) * 128)
                par = c % 2
                sbs = (0, 4) if par == 0 else (3, 6)
                for d, lo in ((0, 0), (1, dk)):
                    b.op("pe", lambda: nc.tensor.matmul(self.pv(sbs[d], 0, 128),
                                                        lhsT=kT[lo:lo + dk, ch], rhs=qT[lo:lo + dk, ch],
                                                        start=True, stop=True),
                         r=[("kT", c), ("qT", c)], w=self.pk(sbs[d], 0, 128))
                for d, lo in ((0, 0), (1, dk)):
                    b.op("dve", lambda: nc.vector.tensor_tensor(
                        out=tmp["sT"][:, par, d, :], in0=self.pv(sbs[d], 0, 128),
                        in1=(maskF if d == 0 else maskB), op=ALU.mult),
                        r=self.pk(sbs[d], 0, 128) + ["tri6"], w=[("sT", par, d)])
                ov = self.pv(1, cc * 128, cc * 128 + 128)
                ok = self.pk(1, cc * 128, cc * 128 + 128)
                b.op("pe", lambda: nc.tensor.matmul(ov, lhsT=v_all[:, c, :], rhs=tmp["sT"][:, par, 0, :],
                                                    start=True, stop=False), r=[("v", c), ("sT", par, 0)], w=ok)
                b.op("pe", lambda: nc.tensor.matmul(ov, lhsT=v_all[:, c, :], rhs=tmp["sT"][:, par, 1, :],
                                                    start=False, stop=False), r=[("v", c), ("sT", par, 1)], w=ok)
                for s_ in range(nsub):
                    g = c * nsub + s_
                    b.op("pe", lambda: nc.tensor.matmul(ov[:, s_ * sub:(s_ + 1) * sub], lhsT=S_all[0:K2, g, :],
                                                        rhs=qT[0:K2, c * 128 + s_ * sub: c * 128 + (s_ + 1) * sub],
                                                        start=False, stop=(s_ == nsub - 1)),
                         r=[("S", g, 0), ("S", g, dk), ("qT", c)], w=ok)
            self.head_norm_gate(tmp, blk, gain_ap, etile, groupnorm, 128)

    def head_norm_gate(self, tmp, blk, gain_ap, etile, groupnorm, dv):
        b = self.b
        nc = self.nc
        sl = slice(blk * 512, (blk + 1) * 512)
        o = self.pv(1, 0, 512)
        ok = self.pk(1, 0, 512)
        osq, rstd, e1, t1 = tmp["osq"], tmp["rstd"], tmp["e1"], tmp["t1"]
        src = o
        srck = ok
        if groupnorm:
            b.op("act", lambda: nc.scalar.activation(out=tmp["o32b"][:dv, :], in_=o[:dv, :], func=AF.Identity),
                 r=ok, w=["o32b"])
            b.op("pe", lambda: nc.tensor.matmul(self.pv(5, 0, 512)[:dv, :], lhsT=self.ones_bf[:dv, :dv],
                                                rhs=tmp["o32b"][:dv, :], start=True, stop=True),
                 r=["o32b", "ones"], w=self.pk(5, 0, 512))
            b.op("dve", lambda: nc.vector.scalar_tensor_tensor(
                out=tmp["oc"][:dv, :], in0=self.pv(5, 0, 512)[:dv, :], scalar=-1.0 / dv, in1=o[:dv, :],
                op0=ALU.mult, op1=ALU.add), r=self.pk(5, 0, 512) + ok, w=["oc"])
            src = tmp["oc"]
            srck = ["oc"]
        b.op("act", lambda: nc.scalar.activation(out=osq[:dv, :], in_=src[:dv, :], func=AF.Square), r=srck, w=["osq"])
        b.op("pe", lambda: nc.tensor.matmul(self.pv(5, 0, 512)[:dv, :], lhsT=self.ones_bf[:dv, :dv], rhs=osq[:dv, :],
                                            start=True, stop=True), r=["osq", "ones"], w=self.pk(5, 0, 512))
        b.op("act", lambda: nc.scalar.activation(out=rstd[:dv, :], in_=self.pv(5, 0, 512)[:dv, :], func=AF.Sqrt,
                                                 scale=1.0 / dv, bias=self.eps_t[:dv, 0:1]),
             r=self.pk(5, 0, 512) + ["eps"], w=["rstd"])
        b.op("dve", lambda: nc.vector.reciprocal(out=rstd[:dv, :], in_=rstd[:dv, :]), r=["rstd"], w=["rstd"])
        gp = self.pv(2, 0, 512)
        gk = self.pk(2, 0, 512)
        b.op("act", lambda: nc.scalar.activation(out=e1[:dv, :], in_=gp[:dv, :], func=AF.Exp, scale=-1.0),
             r=gk, w=["e1"])
        b.op("dve", lambda: nc.vector.tensor_scalar_add(out=e1[:dv, :], in0=e1[:dv, :], scalar1=1.0),
             r=["e1"], w=["e1"])
        b.op("dve", lambda: nc.vector.reciprocal(out=e1[:dv, :], in_=e1[:dv, :]), r=["e1"], w=["e1"])
        b.op("dve", lambda: nc.vector.tensor_tensor(out=e1[:dv, :], in0=gp[:dv, :], in1=e1[:dv, :], op=ALU.mult),
             r=gk + ["e1"], w=["e1"])
        b.op("dve", lambda: nc.vector.scalar_tensor_tensor(out=t1[:dv, :], in0=src[:dv, :], scalar=gain_ap,
                                                           in1=rstd[:dv, :], op0=ALU.mult, op1=ALU.mult),
             r=srck + ["rstd", "hgain"], w=["t1"])
        b.op("dve", lambda: nc.vector.tensor_tensor(out=self.yT[:dv, etile, sl], in0=t1[:dv, :], in1=e1[:dv, :],
                                                     op=ALU.mult), r=["t1", "e1"], w=[("y", etile, blk)])


    RET_PIECES = {0: ((0, 128, 0, 0), (128, 64, 1, 0)), 1: ((0, 64, 1, 64), (64, 128, 2, 0)),
                  2: ((0, 128, 3, 0), (128, 64, 4, 0)), 3: ((0, 64, 4, 64), (64, 128, 5, 0))}

    def odd_mixer(self, j, es):
        with contextlib.ExitStack() as es_r:
            self.retnet(j, es_r)
            self.b.barrier()
        if self.cfg.get("s5", True):
            with contextlib.ExitStack() as es_s:
                self.s5(j, es_s)
                self.b.barrier()

    def retnet(self, j, es):
        b = self.b
        nc = self.nc
        tmp = self.alloc_head_tmps(es, groupnorm=True)
        wtm = b.sb("wtmo", [128, KT, 448], BF16, es)
        wfm = b.sb("wfmo", [128, 2, KT, 128], BF16, es)
        qT3 = b.sb("qT3", [128, 3, T], BF16, es)
        kT = b.sb("kTo", [128, T], BF16, es)
        kt_all = b.sb("kt_allo", [128, NCH, 2, 128], BF16, es)
        v_all = b.sb("v_allo", [128, NCH, 192], BF16, es)
        S_all = b.sb("S_allo", [128, 2, NCH, 192], BF16, es)
        Pst = b.sb("Psto", [128, 2, 2, 192], F32, es)
        rope = b.sb("rope", [128, 2, NCH, 64], BF16, es)
        rdec = b.sb("rdec", [128, 16], F32, es)
        dmask = b.sb("dmask", [128, 4, 128], F32, es)
        AB = b.sb("AB", [128, 2, 2, 4, 64], F32, es)
        rot = b.sb("rot", [128, 2, 4, 64], F32, es)
        qt3 = b.sb("qt3", [128, 2, 4, 128], BF16, es)
        mean_sb = b.sb("mean_sb", [128, 512], F32, es)
        oc2 = b.sb("oc2", [128, 512], F32, es)
        b.dma("pool", rope[:], self.rope_d[:, :].rearrange("p (a c k) -> p a c k", a=2, c=NCH), w=["rope"], stream="c1")
        b.dma("sp", rdec[:], self.rdec_d[:, :], w=["rdec"], stream="c0")
        b.dma("sp", dmask[:], self.dmask_d[:, :].rearrange("p (h i) -> p h i", h=4), w=["dmask"], stream="c0")
        b.op("pool", lambda: nc.gpsimd.memset(S_all[:], 0.0), w=[("S", d, c) for d in range(2) for c in range(NCH)])
        for h in range(4):
            gf128 = float((1.0 - 2.0 ** (-5.0 - h)) ** 128)
            gb128 = float((1.0 - 2.0 ** (-5.5 - h)) ** 128)
            pieces = self.RET_PIECES[h]
            b.dma("pool", wtm[:], self.w_tm_o_d[j, h, :, :].rearrange("p (k c) -> p k c", k=KT), w=["wtm"], stream="wtm0")
            for pi in range(2):
                b.dma("pool", wfm[:, pi, :, :], self.w_fm_o_d[j, 2 * h + pi, :, :].rearrange("p (k c) -> p k c", k=KT),
                      w=["wfm"], stream="wfm")
            for c in range(NCH):
                ch = slice(c * 128, (c + 1) * 128)
                par = c % 2
                P = self.pv(par, 0, 512)
                Pk = self.pk(par, 0, 448)
                for k in range(KT):
                    b.op("pe", lambda: nc.tensor.matmul(P[:, 0:448], lhsT=self.hT[:, k, ch], rhs=wtm[:, k, :],
                                                        start=(k == 0), stop=(k == KT - 1)),
                         r=["wtm", ("h", k, c // 4)], w=Pk)
                P4 = P[:, 0:256].rearrange("p (a k) -> p a k", a=4)
                cosb = rope[:, 0, c, :].unsqueeze(1).to_broadcast([128, 4, 64])
                sinb = rope[:, 1, c, :].unsqueeze(1).to_broadcast([128, 4, 64])
                A = AB[:, par, 0, :, :]
                Bm = AB[:, par, 1, :, :]
                R = rot[:, par, :, :]
                b.op("dve", lambda: nc.vector.tensor_tensor(out=A, in0=P4, in1=cosb, op=ALU.mult), r=Pk + ["rope"],
                     w=[("A", par)])
                b.op("dve", lambda: nc.vector.tensor_tensor(out=Bm, in0=P4, in1=sinb, op=ALU.mult), r=Pk + ["rope"],
                     w=[("B", par)])
                b.op("pool", lambda: nc.gpsimd.tensor_tensor(out=R[:, 0::2, :], in0=A[:, 0::2, :], in1=Bm[:, 1::2, :],
                                                              op=ALU.subtract), r=[("A", par), ("B", par)], w=[("rot", par)])
                b.op("pool", lambda: nc.gpsimd.tensor_tensor(out=R[:, 1::2, :], in0=Bm[:, 0::2, :], in1=A[:, 1::2, :],
                                                              op=ALU.add), r=[("A", par), ("B", par)], w=[("rot", par)])
                rq = rot[:, par, 0:2, :].rearrange("p a k -> p (a k)")
                rk = rot[:, par, 2:4, :].rearrange("p a k -> p (a k)")
                QT = qt3[:, par, :, :]
                sc_k = 128.0 ** -0.5
                b.op("act", lambda: nc.scalar.activation(out=QT[:, 0, :], in_=rq, func=AF.Identity), r=[("rot", par)],
                     w=[("qt3", par)])
                b.op("act", lambda: nc.scalar.activation(out=QT[:, 1, :], in_=rq, func=AF.Identity,
                                                         scale=rdec[:, 4 * h + 0: 4 * h + 1]),
                     r=[("rot", par), "rdec"], w=[("qt3", par)])
                b.op("act", lambda: nc.scalar.activation(out=QT[:, 2, :], in_=rq, func=AF.Identity,
                                                         scale=rdec[:, 4 * h + 1: 4 * h + 2]),
                     r=[("rot", par), "rdec"], w=[("qt3", par)])
                b.op("act", lambda: nc.scalar.activation(out=QT[:, 3, :], in_=rk, func=AF.Identity, scale=sc_k),
                     r=[("rot", par)], w=[("qt3", par)])
                b.op("pool", lambda: nc.gpsimd.tensor_scalar(out=kt_all[:, c, 0, :], in0=rk,
                                                              scalar1=rdec[:, 4 * h + 2: 4 * h + 3], scalar2=None,
                                                              op0=ALU.mult), r=[("rot", par), "rdec"], w=[("kt", c)])
                b.op("pool", lambda: nc.gpsimd.tensor_scalar(out=kt_all[:, c, 1, :], in0=rk,
                                                              scalar1=rdec[:, 4 * h + 3: 4 * h + 4], scalar2=None,
                                                              op0=ALU.mult), r=[("rot", par), "rdec"], w=[("kt", c)])
                b.op("act", lambda: nc.scalar.activation(out=v_all[:, c, :], in_=P[:, 256:448], func=AF.Identity), r=Pk,
                     w=[("v", c)])
                zb = 2 + par
                tp = self.pv(zb, 0, 256).bitcast(BF16)
                tkey = self.pk(zb, 0, 256)
                for a_ in range(4):
                    b.op("pe", lambda: nc.tensor.transpose(tp[:, a_ * 128:(a_ + 1) * 128], QT[:, a_, :], self.ident[:]),
                         r=[("qt3", par), "ident"], w=tkey)
                b.op("dve", lambda: nc.vector.tensor_copy(out=qT3[:, :, ch],
                                                           in_=tp[:, 0:384].rearrange("p (a t) -> p a t", a=3)),
                     r=tkey, w=[("qT", c)])
                b.op("act", lambda: nc.scalar.activation(out=kT[:, ch], in_=tp[:, 384:512], func=AF.Identity), r=tkey,
                     w=[("kT", c)])
            for step in range(NCH - 1):
                for d in range(2):
                    c = step if d == 0 else NCH - 1 - step
                    nxt = 1 if d == 0 else -1
                    gdec = gf128 if d == 0 else gb128
                    bank = (4 if d == 0 else 6) + step % 2
                    kv = self.pv(bank, 0, 192)
                    kvk = self.pk(bank, 0, 192)
                    b.op("pe", lambda: nc.tensor.matmul(kv, lhsT=kt_all[:, c, d, :], rhs=v_all[:, c, :], start=True,
                                                        stop=True), r=[("kt", c), ("v", c)], w=kvk)
                    pp = step % 2
                    pkey = ("Pst", d, pp)
                    pold = ("Pst", d, 1 - pp)
                    if step == 0:
                        b.op("dve", lambda: nc.vector.tensor_copy(out=Pst[:, d, pp, :], in_=kv), r=kvk, w=[pkey])
                    else:
                        b.op("dve", lambda: nc.vector.scalar_tensor_tensor(out=Pst[:, d, pp, :], in0=Pst[:, d, 1 - pp, :],
                                                                           scalar=gdec, in1=kv, op0=ALU.mult, op1=ALU.add),
                             r=kvk + [pold], w=[pkey])
                    b.op("act", lambda: nc.scalar.activation(out=S_all[:, d, c + nxt, :], in_=Pst[:, d, pp, :],
                                                             func=AF.Identity), r=[pkey], w=[("S", d, c + nxt)])
            obank = (1, 6)
            gbank = (2, 7)
            for blk in range(NBLK):
                sl = slice(blk * 512, (blk + 1) * 512)
                for pi, (f0, nf, etile, p0) in enumerate(pieces):
                    for k in range(KT):
                        b.op("pe", lambda: nc.tensor.matmul(self.pv(gbank[pi], 0, 512)[p0:p0 + nf, :],
                                                            lhsT=wfm[:, pi, k, 0:nf], rhs=self.hT[:, k, sl],
                                                            start=(k == 0), stop=(k == KT - 1)),
                             r=["wfm", ("h", k, blk)], w=self.pk(gbank[pi], 0, 512))
                for cc in range(4):
                    c = blk * 4 + cc
                    ch = slice(c * 128, (c + 1) * 128)
                    par = c % 2
                    sb_ = 0 if par == 0 else 3
                    b.op("pe", lambda: nc.tensor.matmul(self.pv(sb_, 0, 128), lhsT=kT[:, ch], rhs=qT3[:, 0, ch],
                                                        start=True, stop=True), r=[("kT", c), ("qT", c)],
                         w=self.pk(sb_, 0, 128))
                    b.op("dve", lambda: nc.vector.tensor_tensor(out=tmp["sT"][:, par, 0, :], in0=self.pv(sb_, 0, 128),
                                                                 in1=dmask[:, h, :], op=ALU.mult),
                         r=self.pk(sb_, 0, 128) + ["dmask"], w=[("sT", par)])
                    for pi, (f0, nf, etile, p0) in enumerate(pieces):
                        ov = self.pv(obank[pi], cc * 128, cc * 128 + 128)[p0:p0 + nf, :]
                        ok = self.pk(obank[pi], 0, 512)
                        b.op("pe", lambda: nc.tensor.matmul(ov, lhsT=v_all[:, c, f0:f0 + nf], rhs=tmp["sT"][:, par, 0, :],
                                                            start=True, stop=False), r=[("v", c), ("sT", par)], w=ok)
                        b.op("pe", lambda: nc.tensor.matmul(ov, lhsT=S_all[:, 0, c, f0:f0 + nf], rhs=qT3[:, 1, ch],
                                                            start=False, stop=False), r=[("S", 0, c), ("qT", c)], w=ok)
                        b.op("pe", lambda: nc.tensor.matmul(ov, lhsT=S_all[:, 1, c, f0:f0 + nf], rhs=qT3[:, 2, ch],
                                                            start=False, stop=True), r=[("S", 1, c), ("qT", c)], w=ok)
                stat = self.pv(5, 0, 512)
                statk = self.pk(5, 0, 512)
                osq, rstd, e1, t1, o32b, oc = (tmp[n] for n in ("osq", "rstd", "e1", "t1", "o32b", "oc"))
                for pi, (f0, nf, etile, p0) in enumerate(pieces):
                    pr = slice(p0, p0 + nf)
                    b.op("act", lambda: nc.scalar.activation(out=o32b[pr, :] if pi == 0 else osq[pr, :],
                                                             in_=self.pv(obank[pi], 0, 512)[pr, :], func=AF.Identity),
                         r=self.pk(obank[pi], 0, 512), w=[("o32b", pi)])
                for pi, (f0, nf, etile, p0) in enumerate(pieces):
                    pr = slice(p0, p0 + nf)
                    src = o32b if pi == 0 else osq
                    b.op("pe", lambda: nc.tensor.matmul(stat, lhsT=self.ones_bf[pr, :], rhs=src[pr, :], start=(pi == 0),
                                                        stop=(pi == 1)), r=[("o32b", pi), "ones"], w=statk)
                b.op("act", lambda: nc.scalar.activation(out=mean_sb[:], in_=stat, func=AF.Identity, scale=-1.0 / 192),
                     r=statk, w=["mean"])
                ocs = (oc, oc2)
                for pi, (f0, nf, etile, p0) in enumerate(pieces):
                    pr = slice(p0, p0 + nf)
                    b.op("dve", lambda: nc.vector.tensor_tensor(out=ocs[pi][pr, :], in0=self.pv(obank[pi], 0, 512)[pr, :],
                                                                 in1=mean_sb[pr, :], op=ALU.add),
                         r=self.pk(obank[pi], 0, 512) + ["mean"], w=[("oc", pi)])
                    b.op("act", lambda: nc.scalar.activation(out=o32b[pr, :] if pi == 0 else osq[pr, :], in_=ocs[pi][pr, :],
                                                             func=AF.Square), r=[("oc", pi)], w=[("o32b", pi)])
                for pi, (f0, nf, etile, p0) in enumerate(pieces):
                    pr = slice(p0, p0 + nf)
                    src = o32b if pi == 0 else osq
                    b.op("pe", lambda: nc.tensor.matmul(stat, lhsT=self.ones_bf[pr, :], rhs=src[pr, :], start=(pi == 0),
                                                        stop=(pi == 1)), r=[("o32b", pi), "ones"], w=statk)
                b.op("act", lambda: nc.scalar.activation(out=rstd[:], in_=stat, func=AF.Sqrt, scale=1.0 / 192,
                                                         bias=self.eps_t[:, 0:1]), r=statk + ["eps"], w=["rstd"])
                b.op("dve", lambda: nc.vector.reciprocal(out=rstd[:], in_=rstd[:]), r=["rstd"], w=["rstd"])
                for pi, (f0, nf, etile, p0) in enumerate(pieces):
                    pr = slice(p0, p0 + nf)
                    gp = self.pv(gbank[pi], 0, 512)[pr, :]
                    gk = self.pk(gbank[pi], 0, 512)
                    gain_ap = self.hgain_o[pr, j * 8 + 2 * h + pi: j * 8 + 2 * h + pi + 1]
                    b.op("act", lambda: nc.scalar.activation(out=e1[pr, :], in_=gp, func=AF.Exp, scale=-1.0), r=gk,
                         w=["e1"])
                    b.op("dve", lambda: nc.vector.tensor_scalar_add(out=e1[pr, :], in0=e1[pr, :], scalar1=1.0),
                         r=["e1"], w=["e1"])
                    b.op("dve", lambda: nc.vector.reciprocal(out=e1[pr, :], in_=e1[pr, :]), r=["e1"], w=["e1"])
                    b.op("dve", lambda: nc.vector.tensor_tensor(out=e1[pr, :], in0=gp, in1=e1[pr, :], op=ALU.mult),
                         r=gk + ["e1"], w=["e1"])
                    b.op("dve", lambda: nc.vector.scalar_tensor_tensor(out=t1[pr, :], in0=ocs[pi][pr, :], scalar=gain_ap,
                                                                       in1=rstd[pr, :], op0=ALU.mult, op1=ALU.mult),
                         r=[("oc", pi), "rstd", "hgain_o"], w=["t1"])
                    b.op("dve", lambda: nc.vector.tensor_tensor(out=self.yT[pr, etile, sl], in0=t1[pr, :], in1=e1[pr, :],
                                                                 op=ALU.mult), r=["t1", "e1"],
                         w=[("y", etile, blk)])


    def cmul(self, out, x, y, t, shape_keys):
        b = self.b
        nc = self.nc
        (o_r, o_i), ko = out
        (xr, xi), kx = x
        (yr, yi), ky = y
        (t1, t2), kt = t
        b.op("dve", lambda: nc.vector.tensor_tensor(out=t1, in0=xr, in1=yr, op=ALU.mult), r=[kx, ky], w=[kt])
        b.op("dve", lambda: nc.vector.tensor_tensor(out=t2, in0=xi, in1=yi, op=ALU.mult), r=[kx, ky], w=[kt])
        b.op("dve", lambda: nc.vector.tensor_tensor(out=o_r, in0=t1, in1=t2, op=ALU.subtract), r=[kt], w=[ko])
        b.op("dve", lambda: nc.vector.tensor_tensor(out=t1, in0=xr, in1=yi, op=ALU.mult), r=[kx, ky, ko], w=[kt])
        b.op("dve", lambda: nc.vector.tensor_tensor(out=t2, in0=xi, in1=yr, op=ALU.mult), r=[kx, ky], w=[kt])
        b.op("dve", lambda: nc.vector.tensor_tensor(out=o_i, in0=t1, in1=t2, op=ALU.add), r=[kt], w=[ko])

    def s5(self, j, es):
        b = self.b
        nc = self.nc
        V = nc.vector
        NC8 = T // 8

        def tt(out, in0, in1, op, r, w):
            b.op("dve", lambda: V.tensor_tensor(out=out, in0=in0, in1=in1, op=op), r=r, w=w)

        def ts(out, in0, s1, s2, op0, op1, r, w):
            b.op("dve", lambda: V.tensor_scalar(out=out, in0=in0, scalar1=s1, scalar2=s2, op0=op0, op1=op1), r=r, w=w)

        Uflat = b.sb("Uflat", [128, 16, NC8], BF16, es)
        Tg = b.sb("Tg", [128, 16, 128], BF16, es)
        KX = b.sb("KX", [128, 2, 2, 8, 128], BF16, es)
        QY = b.sb("QY", [128, 2, 2, 8, 128], BF16, es)
        Hs = b.sb("Hs", [128, 2, 2, 8, NC8], BF16, es)
        gw = b.sb("gw", [128, 2, 256], BF16, es)
        glb = b.sb("s5glb", [128, 2], F32, es)
        A8 = b.sb("A8", [128, 2, 2, 8], F32, es)
        b.dma("pool", gw[:], self.s5_gw_d[j, :, :].rearrange("p (k c) -> p k c", k=2), w=["gw"], stream="wfm")
        b.dma("sp", glb[:], self.s5_glb_d[j, :, :], w=["glb"], stream="c0")

        with contextlib.ExitStack() as e1:
            u2 = b.sb("u2", [128, 2, 16, 8, 16], BF16, e1)
            wu = b.sb("wu", [128, KT, 256], BF16, e1)
            b.dma("pool", wu[:], self.w_u_o_d[j, :, :].rearrange("p (k c) -> p k c", k=KT), w=["wu"], stream="wtm0")
            cnt = 0
            for half in range(2):
                for jj in range(8):
                    bi = cnt % 2
                    cnt += 1
                    for k in range(KT):
                        b.op("pe", lambda: nc.tensor.matmul(
                            self.pv(bi, 0, 256), lhsT=self.hT[:, k, half * 1024 + jj: half * 1024 + 1024: 8],
                            rhs=wu[:, k, :], start=(k == 0), stop=(k == KT - 1)),
                            r=["wu"] + [("h", k, blk) for blk in (2 * half, 2 * half + 1)], w=self.pk(bi, 0, 256))
                    b.op("act", lambda: nc.scalar.activation(
                        out=u2[:, half, :, jj, :], in_=self.pv(bi, 0, 256).rearrange("p (g k) -> p g k", g=16),
                        func=AF.Identity), r=self.pk(bi, 0, 256), w=[("u2", half)])
            cnt = 0
            for half in range(2):
                for g4 in range(4):
                    bi = 2 + cnt % 2
                    cnt += 1
                    tp = self.pv(bi, 0, 256).bitcast(BF16)
                    for gg in range(4):
                        g = g4 * 4 + gg
                        b.op("pe", lambda: nc.tensor.transpose(tp[:, gg * 128:(gg + 1) * 128],
                                                               u2[:, half, g, :, :].rearrange("p a k -> p (a k)"),
                                                               self.ident[:]),
                             r=[("u2", half), "ident"], w=self.pk(bi, 0, 256))
                    b.op("dve", lambda: V.tensor_copy(out=Uflat[:, g4 * 4:(g4 + 1) * 4, half * 128:(half + 1) * 128],
                                                      in_=tp.rearrange("p (a c) -> p a c", a=4)),
                         r=self.pk(bi, 0, 256), w=["Uflat"])
            b.barrier()

        with contextlib.ExitStack() as e2:
            prm = b.sb("s5prm", [128, 3, 2, 8], F32, e2)
            BC = b.sb("s5BC", [128, 4, 8, 16], F32, e2)
            dsk = b.sb("s5dsk", [128, 16], F32, e2)
            bmask = b.sb("s5bm", [128, 2, 128], F32, e2)
            identf = b.sb("identf", [128, 128], F32, e2)
            b.dma("sp", prm[:], self.s5_prm_d[j, :, :].rearrange("p (a d g) -> p a d g", a=3, d=2), w=["prm"], stream="c0")
            b.dma("sp", BC[:], self.s5_bc_d[j, :, :].rearrange("p (a g k) -> p a g k", a=4, g=8), w=["BC"], stream="c0")
            b.dma("sp", dsk[:], self.s5_dsk_d[j, :, :], w=["dsk"], stream="c0")
            b.dma("sp", bmask[:], self.s5_bm_d[:, :].rearrange("p (a c) -> p a c", a=2), w=["bmask"], stream="c0")
            b.dma("sp", identf[:], self.ident_d[:, :], w=["identf"], stream="c0")
            sc = {}
            for nm in ("lr", "dt", "th", "s8", "c8", "m8", "ar", "ai", "t1", "t2", "t3", "den", "cr", "ci", "nar", "nai"):
                sc[nm] = b.sb("s5_" + nm, [128, 2, 8], F32, e2)
            PW = b.sb("s5PW", [128, 2, 2, 2, 8, 9], F32, e2)
            BB = b.sb("s5BB", [128, 2, 2, 8, 16], F32, e2)
            TB = b.sb("s5TB", [128, 4, 8, 8, 16], F32, e2)
            TX = b.sb("s5TX", [128, 2, 8, 8, 16], F32, e2)
            tw = b.sb("s5tw", [128, 2, 8, 8, 16], F32, e2)
            Tacc = b.sb("s5Tacc", [128, 128], F32, e2)
            lr, dt, th = sc["lr"][:], sc["dt"][:], sc["th"][:]
            ts(lr, prm[:, 0, :, :], -1e-4, None, ALU.min, ALU.bypass, ["prm"], ["lr"])
            b.op("act", lambda: nc.scalar.activation(out=dt, in_=prm[:, 2, :, :], func=AF.Exp), r=["prm"], w=["dt"])
            tt(th, prm[:, 1, :, :], dt, ALU.mult, ["prm", "dt"], ["th"])
            b.op("act", lambda: nc.scalar.activation(out=sc["s8"][:], in_=th, func=AF.Sin, scale=1.0 / 8), r=["th"],
                 w=["s8"])
            b.op("act", lambda: nc.scalar.activation(out=sc["t1"][:], in_=th, func=AF.Sin, scale=1.0 / 16), r=["th"],
                 w=["t1"])
            tt(sc["t1"][:], sc["t1"][:], sc["t1"][:], ALU.mult, ["t1"], ["t1"])
            ts(sc["c8"][:], sc["t1"][:], -2.0, 1.0, ALU.mult, ALU.add, ["t1"], ["c8"])
            tt(sc["t2"][:], lr, dt, ALU.mult, ["lr", "dt"], ["t2"])
            b.op("act", lambda: nc.scalar.activation(out=sc["m8"][:], in_=sc["t2"][:], func=AF.Exp, scale=1.0 / 8),
                 r=["t2"], w=["m8"])
            ar, ai = sc["ar"][:], sc["ai"][:]
            tt(ar, sc["m8"][:], sc["c8"][:], ALU.mult, ["m8", "c8"], ["a"])
            tt(ai, sc["m8"][:], sc["s8"][:], ALU.mult, ["m8", "s8"], ["a"])
            for _ in range(3):
                tt(sc["t1"][:], ar, ar, ALU.mult, ["a"], ["t1"])
                tt(sc["t2"][:], ai, ai, ALU.mult, ["a"], ["t2"])
                tt(sc["t3"][:], ar, ai, ALU.mult, ["a"], ["t3"])
                tt(ar, sc["t1"][:], sc["t2"][:], ALU.subtract, ["t1", "t2"], ["a"])
                ts(ai, sc["t3"][:], 2.0, None, ALU.mult, ALU.bypass, ["t3"], ["a"])
            tt(sc["t1"][:], ar, ar, ALU.mult, ["a"], ["t1"])
            tt(sc["t2"][:], ai, ai, ALU.mult, ["a"], ["t2"])
            tt(sc["t1"][:], sc["t1"][:], sc["t2"][:], ALU.add, ["t1", "t2"], ["t1"])
            b.op("dve", lambda: V.reciprocal(out=sc["t1"][:], in_=sc["t1"][:]), r=["t1"], w=["t1"])
            tt(sc["nar"][:], ar, sc["t1"][:], ALU.mult, ["a", "t1"], ["na"])
            tt(sc["nai"][:], ai, sc["t1"][:], ALU.mult, ["a", "t1"], ["na"])
            ts(sc["nai"][:], sc["nai"][:], -1.0, None, ALU.mult, ALU.bypass, ["na"], ["na"])
            b.op("dve", lambda: V.memset(PW[:, :, 0, :, :, 0], 1.0), w=["PW"])
            b.op("dve", lambda: V.memset(PW[:, :, 1, :, :, 0], 0.0), r=["PW"], w=["PW"])
            for pi_, (br_, bi_, kb) in enumerate(((ar, ai, "a"), (sc["nar"][:], sc["nai"][:], "na"))):
                for k in range(8):
                    self.cmul(((PW[:, pi_, 0, :, :, k + 1], PW[:, pi_, 1, :, :, k + 1]), "PW"),
                              ((PW[:, pi_, 0, :, :, k], PW[:, pi_, 1, :, :, k]), "PW"),
                              ((br_, bi_), kb), ((sc["t1"][:], sc["t2"][:]), "t12"), None)
            b.op("dve", lambda: V.tensor_copy(out=A8[:, 0, :, :], in_=PW[:, 0, 0, :, :, 8]), r=["PW"], w=["A8"])
            b.op("dve", lambda: V.tensor_copy(out=A8[:, 1, :, :], in_=PW[:, 0, 1, :, :, 8]), r=["PW"], w=["A8"])
            den, cr, ci = sc["den"][:], sc["cr"][:], sc["ci"][:]
            li = prm[:, 1, :, :]
            tt(sc["t1"][:], lr, lr, ALU.mult, ["lr"], ["t1"])
            tt(sc["t2"][:], li, li, ALU.mult, ["prm"], ["t2"])
            tt(den, sc["t1"][:], sc["t2"][:], ALU.add, ["t1", "t2"], ["den"])
            b.op("dve", lambda: V.reciprocal(out=den, in_=den), r=["den"], w=["den"])
            ts(sc["t3"][:], ar, -1.0, None, ALU.add, ALU.bypass, ["a"], ["t3"])
            tt(sc["t1"][:], sc["t3"][:], lr, ALU.mult, ["t3", "lr"], ["t1"])
            tt(sc["t2"][:], ai, li, ALU.mult, ["a", "prm"], ["t2"])
            tt(cr, sc["t1"][:], sc["t2"][:], ALU.add, ["t1", "t2"], ["c"])
            tt(cr, cr, den, ALU.mult, ["c", "den"], ["c"])
            tt(sc["t1"][:], ai, lr, ALU.mult, ["a", "lr"], ["t1"])
            tt(sc["t2"][:], sc["t3"][:], li, ALU.mult, ["t3", "prm"], ["t2"])
            tt(ci, sc["t1"][:], sc["t2"][:], ALU.subtract, ["t1", "t2"], ["c"])
            tt(ci, ci, den, ALU.mult, ["c", "den"], ["c"])
            sh4 = [128, 2, 8, 16]
            crb = cr.unsqueeze(3).to_broadcast(sh4)
            cib = ci.unsqueeze(3).to_broadcast(sh4)
            brb = BC[:, 0, :, :].unsqueeze(1).to_broadcast(sh4)
            bib = BC[:, 1, :, :].unsqueeze(1).to_broadcast(sh4)
            w4 = tw[:, :, 0:2, 0, :].rearrange("p a d k -> p a d k")
            t4a = tw[:, 0, :, 0:2, :]
            tA = TX[:, 0, 0:2, :, :]
            tB = TX[:, 1, 0:2, :, :]
            self.cmul(((BB[:, 0, :, :, :], BB[:, 1, :, :, :]), "BB"), ((crb, cib), "c"), ((brb, bib), "BC"),
                      ((tA, tB), "TX"), None)
            sh5 = [128, 8, 8, 16]
            for d in range(2):
                kp = 1 if d == 0 else 0
                qp = 0 if d == 0 else 1
                pkr = PW[:, kp, 0, d, :, 0:8].unsqueeze(3).to_broadcast(sh5)
                pki = PW[:, kp, 1, d, :, 0:8].unsqueeze(3).to_broadcast(sh5)
                pqr = PW[:, qp, 0, d, :, 0:8].unsqueeze(3).to_broadcast(sh5)
                pqi = PW[:, qp, 1, d, :, 0:8].unsqueeze(3).to_broadcast(sh5)
                bbr = BB[:, 0, d, :, :].unsqueeze(2).to_broadcast(sh5)
                bbi = BB[:, 1, d, :, :].unsqueeze(2).to_broadcast(sh5)
                ccr = BC[:, 2, :, :].unsqueeze(2).to_broadcast(sh5)
                cci = BC[:, 3, :, :].unsqueeze(2).to_broadcast(sh5)
                self.cmul(((TB[:, 0], TB[:, 1]), "TB"), ((pkr, pki), "PW"), ((bbr, bbi), "BB"), ((tw[:, 0], tw[:, 1]), "tw"),
                          None)
                self.cmul(((TB[:, 2], TB[:, 3]), "TB"), ((pqr, pqi), "PW"), ((ccr, cci), "BC"), ((tw[:, 0], tw[:, 1]), "tw"),
                          None)
                ts(TB[:, 3], TB[:, 3], -1.0, None, ALU.mult, ALU.bypass, ["TB"], ["TB"])
                for g in range(16):
                    pair, g2 = divmod(g, 2)
                    rows = slice(g2 * 64, g2 * 64 + 64)
                    bi = g % 2
                    tps = self.pv(bi, 0, 128)
                    fl = lambda a_: TB[rows, a_, pair, :, :].rearrange("p a k -> p (a k)")
                    b.op("pe", lambda: nc.tensor.matmul(tps, lhsT=fl(0), rhs=fl(2), start=True, stop=False), r=["TB"],
                         w=self.pk(bi, 0, 128))
                    b.op("pe", lambda: nc.tensor.matmul(tps, lhsT=fl(1), rhs=fl(3), start=False, stop=True), r=["TB"],
                         w=self.pk(bi, 0, 128))
                    if d == 0:
                        tt(Tacc[:], tps, bmask[:, 0, :], ALU.mult, self.pk(bi, 0, 128) + ["bmask"], ["Tacc"])
                        b.op("dve", lambda: V.scalar_tensor_tensor(out=Tacc[:], in0=identf[:], scalar=dsk[:, g:g + 1],
                                                                   in1=Tacc[:], op0=ALU.mult, op1=ALU.add),
                             r=["identf", "dsk", "Tacc"], w=["Tacc"])
                        b.op("dve", lambda: V.tensor_copy(out=Tg[:, g, :], in_=Tacc[:]), r=["Tacc"], w=[("Tg", g)])
                    else:
                        tt(Tacc[:], tps, bmask[:, 1, :], ALU.mult, self.pk(bi, 0, 128) + ["bmask"], ["Tacc"])
                        tt(Tg[:, g, :], Tacc[:], Tg[:, g, :], ALU.add, ["Tacc", ("Tg", g)], [("Tg", g)])
                if d == 0:
                    p7r = PW[:, 0, 0, d, :, 7].unsqueeze(2).unsqueeze(3).to_broadcast(sh5)
                    p7i = PW[:, 0, 1, d, :, 7].unsqueeze(2).unsqueeze(3).to_broadcast(sh5)
                    self.cmul(((TX[:, 0], TX[:, 1]), "TX"), ((TB[:, 0], TB[:, 1]), "TB"), ((p7r, p7i), "PW"),
                              ((tw[:, 0], tw[:, 1]), "tw"), None)
                    kxr, kxi, kxk = TX[:, 0], TX[:, 1], "TX"
                else:
                    kxr, kxi, kxk = TB[:, 0], TB[:, 1], "TB"
                for ri, src in enumerate((kxr, kxi)):
                    for p4 in range(2):
                        bi = 2 + (ri * 2 + p4) % 2
                        for pp in range(4):
                            pair = p4 * 4 + pp
                            b.op("pe", lambda: nc.tensor.transpose(self.pv(bi, pp * 128, pp * 128 + 128),
                                                                   src[:, pair, :, :].rearrange("p a k -> p (a k)"),
                                                                   identf[:]), r=[kxk, "identf"], w=self.pk(bi, 0, 512))
                        b.op("act", lambda: nc.scalar.activation(
                            out=KX[:, d, ri, p4 * 4:(p4 + 1) * 4, :],
                            in_=self.pv(bi, 0, 512).rearrange("p (a q) -> p a q", a=4), func=AF.Identity),
                            r=self.pk(bi, 0, 512), w=["KX"])
                pw_idx = 1 if d == 0 else 8
                pyr = PW[:, 0, 0, d, :, pw_idx].unsqueeze(2).unsqueeze(3).to_broadcast(sh5)
                pyi = PW[:, 0, 1, d, :, pw_idx].unsqueeze(2).unsqueeze(3).to_broadcast(sh5)
                tt(tw[:, 0], TB[:, 2], pyr, ALU.mult, ["TB", "PW"], ["tw"])
                tt(tw[:, 1], TB[:, 3], pyi, ALU.mult, ["TB", "PW"], ["tw"])
                tt(TX[:, 0], tw[:, 0], tw[:, 1], ALU.add, ["tw"], ["TX"])
                b.op("act", lambda: nc.scalar.activation(out=QY[:, d, 0, :, :], in_=TX[:, 0].rearrange("p g a k -> p g (a k)"),
                                                         func=AF.Identity), r=["TX"], w=["QY"])
                tt(tw[:, 0], TB[:, 3], pyr, ALU.mult, ["TB", "PW", "TX"], ["tw"])
                tt(tw[:, 1], TB[:, 2], pyi, ALU.mult, ["TB", "PW"], ["tw"])
                tt(TX[:, 1], tw[:, 0], tw[:, 1], ALU.subtract, ["tw"], ["TX"])
                b.op("act", lambda: nc.scalar.activation(out=QY[:, d, 1, :, :], in_=TX[:, 1].rearrange("p g a k -> p g (a k)"),
                                                         func=AF.Identity), r=["TX"], w=["QY"])
            b.barrier()

        with contextlib.ExitStack() as e3:
            RC = b.sb("s5RC", [128, 2, 8, NC8], F32, e3)
            XH = b.sb("s5XH", [128, 2, 8, NC8], F32, e3)
            W1 = b.sb("s5W1", [128, 8, 128], F32, e3)
            W2 = b.sb("s5W2", [128, 8, 128], F32, e3)
            mm = b.sb("s5mm", [128, 3, 8], F32, e3)
            for d in range(2):
                a8r, a8i = A8[:, 0, d, :], A8[:, 1, d, :]
                tt(mm[:, 1, :], a8r, a8r, ALU.mult, ["A8"], ["mm1"])
                tt(mm[:, 2, :], a8i, a8i, ALU.mult, ["A8"], ["mm2"])
                tt(mm[:, 0, :], mm[:, 1, :], mm[:, 2, :], ALU.add, ["mm1", "mm2"], ["mm0"])
                b.op("act", lambda: nc.scalar.activation(out=mm[:, 0, :], in_=mm[:, 0, :], func=AF.Sqrt), r=["mm0"],
                     w=["mm0"])
                b.op("dve", lambda: V.reciprocal(out=mm[:, 1, :], in_=mm[:, 0, :]), r=["mm0", "mm1"], w=["mm1"])
                tt(mm[:, 2, :], a8i, mm[:, 1, :], ALU.mult, ["A8", "mm1", "mm2"], ["mm2"])
                tt(mm[:, 1, :], a8r, mm[:, 1, :], ALU.mult, ["A8", "mm1"], ["mm1"])
                b.op("dve", lambda: V.tensor_copy(out=RC[:, 0, :, 0], in_=mm[:, 1, :]), r=["mm1"], w=["RC"])
                b.op("dve", lambda: V.tensor_copy(out=RC[:, 1, :, 0], in_=mm[:, 2, :]), r=["mm2"], w=["RC"])
                wdt = 1
                while wdt < NC8:
                    shw = [128, 8, wdt]
                    sr = RC[:, 0, :, wdt - 1].unsqueeze(2).to_broadcast(shw)
                    si = RC[:, 1, :, wdt - 1].unsqueeze(2).to_broadcast(shw)
                    self.cmul(((RC[:, 0, :, wdt:2 * wdt], RC[:, 1, :, wdt:2 * wdt]), "RC"),
                              ((RC[:, 0, :, 0:wdt], RC[:, 1, :, 0:wdt]), "RC"), ((sr, si), "RC"),
                              ((W1[:, :, 0:wdt], W2[:, :, 0:wdt]), "W12"), None)
                    wdt *= 2
                for g in range(16):
                    pair, g2 = divmod(g, 2)
                    rows = slice(g2 * 64, g2 * 64 + 64)
                    for ri in range(2):
                        bi = ri * 4 + pair // 2
                        c0 = (pair % 2) * 256
                        b.op("pe", lambda: nc.tensor.matmul(self.pv(bi, c0, c0 + 256)[rows, :], lhsT=KX[:, d, ri, pair, rows],
                                                            rhs=Uflat[:, g, :], start=True, stop=True),
                             r=["KX", "Uflat"], w=self.pk(bi, 0, 512))
                XR = self.ps[:, 0:2048].rearrange("p (g c) -> p g c", g=8)
                XI = self.ps[:, 2048:4096].rearrange("p (g c) -> p g c", g=8)
                kR = [("ps", i) for i in range(4)]
                kI = [("ps", i) for i in range(4, 8)]
                if d == 0:
                    rcr, rci = RC[:, 0, :, :], RC[:, 1, :, :]
                else:
                    rcr, rci = RC[:, 0, :, ::-1], RC[:, 1, :, ::-1]
                for hc in range(2):
                    cs = slice(hc * 128, (hc + 1) * 128)
                    tt(W1[:], XR[:, :, cs], rcr[:, :, cs], ALU.mult, kR + ["RC"], ["W1"])
                    tt(W2[:], XI[:, :, cs], rci[:, :, cs], ALU.mult, kI + ["RC"], ["W2"])
                    tt(XH[:, 0, :, cs], W1[:], W2[:], ALU.add, ["W1", "W2"], ["XHr"])
                    tt(W1[:], XI[:, :, cs], rcr[:, :, cs], ALU.mult, kI + ["RC", "XHr"], ["W1"])
                    tt(W2[:], XR[:, :, cs], rci[:, :, cs], ALU.mult, kR + ["RC", "XHr"], ["W2"])
                    tt(XH[:, 1, :, cs], W1[:], W2[:], ALU.subtract, ["W1", "W2"], ["XHi"])
                for ri, kk in ((0, "XHr"), (1, "XHi")):
                    for pair in range(8):
                        mb = mm[:, 0, pair:pair + 1].to_broadcast([128, NC8])
                        if d == 0:
                            dat = XH[:, ri, pair, :]
                        else:
                            dat = XH[:, ri, pair, ::-1]
                        b.op("dve", lambda: V.tensor_tensor_scan(out=dat, data0=mb, data1=dat, initial=0.0, op0=ALU.mult,
                                                                 op1=ALU.add), r=[kk, "mm0"], w=[kk])
                sh_ = 1 if d == 0 else -1
                zsl = slice(0, 1) if d == 0 else slice(NC8 - 1, NC8)
                for hc in range(2):
                    c0, c1 = hc * 128, (hc + 1) * 128
                    s0, s1 = c0, c1
                    if d == 0 and hc == 1:
                        s1 = NC8 - 1
                    if d == 1 and hc == 0:
                        s0 = 1
                    cs = slice(c0, c1)
                    wsl = slice(s0 - c0, s1 - c0)
                    dsl = slice(s0 + sh_, s1 + sh_)
                    tt(W1[:], XH[:, 0, :, cs], rcr[:, :, cs], ALU.mult, ["XHr", "RC", ("Hs", d)], ["W1"])
                    tt(W2[:], XH[:, 1, :, cs], rci[:, :, cs], ALU.mult, ["XHi", "RC", ("Hs", d)], ["W2"])
                    tt(Hs[:, d, 0, :, dsl], W1[:, :, wsl], W2[:, :, wsl], ALU.subtract, ["W1", "W2"], [("Hs", d)])
                    tt(W1[:], XH[:, 0, :, cs], rci[:, :, cs], ALU.mult, ["XHr", "RC", ("Hs", d)], ["W1"])
                    tt(W2[:], XH[:, 1, :, cs], rcr[:, :, cs], ALU.mult, ["XHi", "RC", ("Hs", d)], ["W2"])
                    tt(Hs[:, d, 1, :, dsl], W1[:, :, wsl], W2[:, :, wsl], ALU.add, ["W1", "W2"], [("Hs", d)])
                b.op("dve", lambda: V.memset(Hs[:, d, :, :, zsl], 0.0), r=[("Hs", d)], w=[("Hs", d)])
            b.barrier()

        with contextlib.ExitStack() as e4:
            yf = b.sb("s5yf", [128, 2, T], F32, e4)
            nglb = b.sb("s5nglb", [128, 2], F32, e4)
            e4a = contextlib.ExitStack()
            Ysb = b.sb("s5Ysb", [128, 16, NC8], BF16, e4a)
            y2 = b.sb("s5y2", [128, 2, 8, 16, 16], BF16, e4a)
            for g in range(16):
                pair, g2 = divmod(g, 2)
                rows = slice(g2 * 64, g2 * 64 + 64)
                bi = g % 4
                yp = self.pv(bi, 0, 256)
                yk = self.pk(bi, 0, 256)
                b.op("pe", lambda: nc.tensor.matmul(yp, lhsT=Tg[:, g, :], rhs=Uflat[:, g, :], start=True, stop=False),
                     r=[("Tg", g), "Uflat"], w=yk)
                for d in range(2):
                    for ri in range(2):
                        b.op("pe", lambda: nc.tensor.matmul(yp, lhsT=QY[rows, d, ri, pair, :], rhs=Hs[rows, d, ri, pair, :],
                                                            start=False, stop=(d == 1 and ri == 1)),
                             r=["QY", ("Hs", d)], w=yk)
                b.op("act", lambda: nc.scalar.activation(out=Ysb[:, g, :], in_=yp, func=AF.Identity), r=yk, w=[("Ysb", g)])
            cnt = 0
            for half in range(2):
                for g4 in range(4):
                    bi = 4 + cnt % 2
                    cnt += 1
                    tp = self.pv(bi, 0, 256).bitcast(BF16)
                    for gg in range(4):
                        g = g4 * 4 + gg
                        b.op("pe", lambda: nc.tensor.transpose(tp[:, gg * 128:(gg + 1) * 128],
                                                               Ysb[:, g, half * 128:(half + 1) * 128], self.ident[:]),
                             r=[("Ysb", g), "ident"], w=self.pk(bi, 0, 256))
                    b.op("dve", lambda: V.tensor_copy(
                        out=y2[:, half, :, g4 * 4:(g4 + 1) * 4, :].rearrange("p i g k -> p g i k"),
                        in_=tp.rearrange("p (g i k) -> p g i k", g=4, i=8)), r=self.pk(bi, 0, 256), w=[("y2", half)])
            cnt = 0
            for half in range(2):
                for i2 in range(2):
                    bi = 6 + cnt % 2
                    cnt += 1
                    tp = self.pv(bi, 0, 512).bitcast(BF16)
                    for i4 in range(4):
                        ii_ = i2 * 4 + i4
                        for kk in range(2):
                            b.op("pe", lambda: nc.tensor.transpose(
                                tp[:, (i4 * 2 + kk) * 128:(i4 * 2 + kk + 1) * 128],
                                y2[:, half, ii_, kk * 8:(kk + 1) * 8, :].rearrange("p g k -> p (g k)"), self.ident[:]),
                                r=[("y2", half), "ident"], w=self.pk(bi, 0, 512))
                    for kk in range(2):
                        src = tp.rearrange("p (i k c) -> p i k c", i=4, k=2)[:, :, kk, :]
                        base = half * 1024 + i2 * 4
                        dst = yf[:, kk, half * 1024:(half + 1) * 1024].rearrange("p (c i) -> p i c", i=8)[:, i2 * 4:(i2 + 1) * 4, :]
                        b.op("act", lambda: nc.scalar.activation(out=dst, in_=src, func=AF.Identity), r=self.pk(bi, 0, 512),
                             w=["yf"])
            b.barrier()
            e4a.close()
            gt = b.sb("s5gt", [128, 2, T], F32, e4)
            geb = b.sb("s5geb", [128, 2, T], BF16, e4)
            c2 = 2.0 * math.sqrt(2.0 / math.pi)
            tt(gt[:], yf[:], yf[:], ALU.mult, ["yf"], ["gt"])
            ts(gt[:], gt[:], 0.044715, 1.0, ALU.mult, ALU.add, ["gt"], ["gt"])
            tt(gt[:], gt[:], yf[:], ALU.mult, ["gt", "yf"], ["gt"])
            ts(gt[:], gt[:], -30.0, None, ALU.max, ALU.bypass, ["gt"], ["gt"])
            b.op("act", lambda: nc.scalar.activation(out=gt[:], in_=gt[:], func=AF.Exp, scale=-c2), r=["gt"], w=["gt"])
            ts(gt[:], gt[:], 1.0, None, ALU.add, ALU.bypass, ["gt"], ["gt"])
            b.op("dve", lambda: V.reciprocal(out=gt[:], in_=gt[:]), r=["gt"], w=["gt"])
            tt(yf[:], yf[:], gt[:], ALU.mult, ["gt", "yf"], ["yf"])
            b.op("act", lambda: nc.scalar.activation(out=geb[:], in_=yf[:], func=AF.Identity), r=["yf"], w=["geb"])
            ts(nglb[:], glb[:], -1.0, None, ALU.mult, ALU.bypass, ["glb"], ["nglb"])
            for et in range(2):
                for blk in range(NBLK):
                    sl = slice(blk * 512, (blk + 1) * 512)
                    bi = (et * NBLK + blk) % 4
                    zp = self.pv(bi, 0, 512)
                    zk = self.pk(bi, 0, 512)
                    for kk in range(2):
                        b.op("pe", lambda: nc.tensor.matmul(zp, lhsT=gw[:, kk, et * 128:(et + 1) * 128], rhs=geb[:, kk, sl],
                                                            start=(kk == 0), stop=(kk == 1)), r=["gw", "geb"], w=zk)
                    g1 = gt[:, et, sl]
                    ts(g1, zp, glb[:, et:et + 1], None, ALU.add, ALU.bypass, zk + ["glb"], [("g1", et, blk)])
                    b.op("act", lambda: nc.scalar.activation(out=g1, in_=g1, func=AF.Exp, scale=-1.0),
                         r=[("g1", et, blk)], w=[("g1", et, blk)])
                    ts(g1, g1, 1.0, None, ALU.add, ALU.bypass, [("g1", et, blk)], [("g1", et, blk)])
                    b.op("dve", lambda: V.reciprocal(out=g1, in_=g1), r=[("g1", et, blk)], w=[("g1", et, blk)])
                    tt(self.yT[:, 6 + et, sl], yf[:, et, sl], g1, ALU.mult, ["yf", ("g1", et, blk)], [("y", 6 + et, blk)])
            b.barrier()

    def alloc_head_tmps(self, es, groupnorm=False):
        b = self.b
        tmp = {}
        lst = [("Pst", [128, 2, 128], F32), ("sT", [128, 2, 2, 128], BF16), ("osq", [128, 512], BF16),
               ("rstd", [128, 512], F32), ("e1", [128, 512], F32), ("t1", [128, 512], F32)]
        if groupnorm:
            lst += [("o32b", [128, 512], BF16), ("oc", [128, 512], F32)]
        for nm, shape, dt in lst:
            tmp[nm] = b.sb(nm, shape, dt, es)
        return tmp

    def even_mixer(self, j, es):
        b = self.b
        nc = self.nc
        tmp = self.alloc_head_tmps(es)
        wtm = b.sb("wtm", [128, 1, KT, 320], BF16, es)
        wfm = b.sb("wfm", [128, 2, KT, 128], BF16, es)
        wa2p = b.sb("wa2p", [32, 128], BF16, es)
        bah = b.sb("bah", [1, 128], BF16, es)
        lrT = b.sb("lrT", [32, T], BF16, es)
        qT = b.sb("qT", [128, T], BF16, es)
        kT = b.sb("kT", [128, T], BF16, es)
        kt_all = b.sb("kt_all", [128, NCH, 128], BF16, es)
        v_all = b.sb("v_all", [128, NCH, 128], BF16, es)
        v3 = b.sb("v3", [128, NCH, 128], BF16, es)
        S_all = b.sb("S_all", [128, NCH * 4, 128], BF16, es)
        Gall = b.sb("Gall", [128, NCH * 4], F32, es)
        LB = b.sb("LB", [128, 2, 256], F32, es)
        OML = b.sb("OML", [128, 2, 256], F32, es)
        es_lg = contextlib.ExitStack()
        lg = b.sb("lg", [128, 2, 2, 256], F32, es_lg)
        b.op("pool", lambda: nc.gpsimd.memset(S_all[:], 0.0), w=[("S", c, lo) for c in range(NCH * 4) for lo in (0, 64)])
        b.op("pool", lambda: nc.gpsimd.memset(v3[:], 0.0), w=[("v3", c) for c in range(NCH)])
        b.dma("sp", lg[:], self.lbl_d.partition_broadcast(128).rearrange("p (d l k) -> p d l k", d=2, l=2), w=["lg"], stream="c0")
        if j == 0:
            b.op("pool", lambda: nc.gpsimd.memset(LB[:], 0.0), w=["LB"])
            b.op("pool", lambda: nc.gpsimd.memset(OML[:], 1.0), w=["OML"])
        else:
            b.op("dve", lambda: nc.vector.tensor_tensor(out=OML[:], in0=lg[:, :, 0, :], in1=lg[:, :, 1, :], op=ALU.max),
                 r=["lg"], w=["OML"])
            for li in range(2):
                b.op("dve", lambda: nc.vector.tensor_tensor(out=lg[:, :, li, :], in0=lg[:, :, li, :], in1=OML[:],
                                                             op=ALU.subtract), r=["lg", "OML"], w=["lg"])
            b.op("act", lambda: nc.scalar.activation(out=lg[:], in_=lg[:], func=AF.Exp), r=["lg"], w=["lg"])
            b.op("dve", lambda: nc.vector.tensor_tensor(out=OML[:], in0=lg[:, :, 0, :], in1=lg[:, :, 1, :], op=ALU.add),
                 r=["lg"], w=["OML"])
            b.op("dve", lambda: nc.vector.reciprocal(out=OML[:], in_=OML[:]), r=["OML"], w=["OML"])
            b.op("dve", lambda: nc.vector.tensor_tensor(out=LB[:], in0=lg[:, :, 1, :], in1=OML[:], op=ALU.mult),
                 r=["lg", "OML"], w=["LB"])
            b.op("dve", lambda: nc.vector.tensor_scalar(out=OML[:], in0=LB[:], scalar1=-1.0, scalar2=1.0,
                                                        op0=ALU.mult, op1=ALU.add), r=["LB"], w=["OML"])
        b.barrier()
        es_lg.close()
        self.check("e_setup")
        ck = {}
        for nm, shape, dt in (("e", [128, 4, 128], F32), ("l", [128, 4, 128], F32), ("E", [128, 4, 128], F32),
                              ("Ei", [128, 4, 128], F32), ("qt", [128, 4, 128], BF16), ("key", [128, 4, 128], F32),
                              ("qs", [128, 4, 64], F32)):
            ck[nm] = b.sb(nm, shape, dt, es)
        b.dma("pool", wfm[:, 0, :, :], self.w_fm_e_d[j, 8, :, :].rearrange("p (k c) -> p k c", k=KT), w=["wfm"],
              stream="wfm")
        for blk in range(NBLK):
            sl = slice(blk * 512, (blk + 1) * 512)
            for k in range(KT):
                b.op("pe", lambda: nc.tensor.matmul(self.pv(0, 0, 512), lhsT=wfm[:, 0, k, :], rhs=self.hT[:, k, sl],
                                                    start=(k == 0), stop=(k == KT - 1)),
                     r=["wfm", ("h", k, blk)], w=self.pk(0, 0, 512))
            b.op("act", lambda: nc.scalar.activation(out=lrT[:, sl], in_=self.pv(0, 0, 512)[0:32, :], func=AF.Identity),
                 r=self.pk(0, 0, 512), w=["lrT"])
        self.check("e_lrT")
        for h in range(8):
            gla = h < 4
            hh = h % 4
            ncols = 256 if gla else 320
            slot = 0
            b.dma("pool", wtm[:, slot, :, :], self.w_tm_e_d[j, h, :, :].rearrange("p (k c) -> p k c", k=KT),
                  w=[("wtm", slot)], stream=f"wtm{slot}")
            b.dma("pool", wfm[:, 1, :, :], self.w_fm_e_d[j, h, :, :].rearrange("p (k c) -> p k c", k=KT), w=["wfm"],
                  stream="wfm")
            if gla:
                b.dma("pool", wa2p[:], self.wa2p_d[j, hh, :, :], w=["wa2p"], stream="wa2")
                b.dma("pool", bah[:], self.bah_d[j, hh, :, :], w=["bah"], stream="wa2")
            for c in range(NCH):
                ch = slice(c * 128, (c + 1) * 128)
                par = c % 4
                P = self.pv(par, 0, 512)
                Pk = self.pk(par, 0, ncols)
                for k in range(KT):
                    b.op("pe", lambda: nc.tensor.matmul(P[:, 0:ncols], lhsT=self.hT[:, k, ch], rhs=wtm[:, slot, k, 0:ncols],
                                                        start=(k == 0), stop=(k == KT - 1)),
                         r=[("wtm", slot), ("h", k, c // 4)], w=Pk)
                self.check("e_inproj")
                zb = 4 + par
                e, l, E, Ei, qt = (ck[n][:, par, :] for n in ("e", "l", "E", "Ei", "qt"))
                if gla:
                    zv = self.pv(zb, 0, 128)
                    zk = self.pk(zb, 0, 128)
                    b.op("pe", lambda: nc.tensor.matmul(zv, lhsT=lrT[0:32, ch], rhs=wa2p[0:32, :], start=True, stop=False),
                         r=["lrT", "wa2p"], w=zk)
                    self.check("e_z1")
                    b.op("pe", lambda: nc.tensor.matmul(zv, lhsT=self.ones_bf[0:1, :], rhs=bah[0:1, :], start=False,
                                                        stop=True), r=["ones", "bah"], w=zk)
                    self.check("e_z2")
                    b.op("act", lambda: nc.scalar.activation(out=e, in_=zv, func=AF.Exp, scale=-1.0), r=zk, w=[("e", par)])
                    self.check("e_z3")
                    b.op("dve", lambda: nc.vector.tensor_scalar_add(out=e, in0=e, scalar1=1.0), r=[("e", par)], w=[("e", par)])
                    b.op("act", lambda: nc.scalar.activation(out=l, in_=e, func=AF.Ln), r=[("e", par)], w=[("l", par)])
                    mi, s0, ns = 0, 0, 1
                    q_src, q_keys = P[:, 0:64], Pk
                    v_src = P[:, 128:256]
                else:
                    key = ck["key"][:, par, :]
                    qs = ck["qs"][:, par, :]
                    zz = P[:, 64:192]
                    b.op("act", lambda: nc.scalar.activation(out=e, in_=zz, func=AF.Exp, scale=-1.0), r=Pk, w=[("e", par)])
                    b.op("dve", lambda: nc.vector.tensor_scalar_add(out=e, in0=e, scalar1=1.0), r=[("e", par)], w=[("e", par)])
                    b.op("dve", lambda: nc.vector.reciprocal(out=e, in_=e), r=[("e", par)], w=[("e", par)])
                    e3 = e.rearrange("p (d k) -> p d k", d=2)
                    lbv = LB[:, :, hh * 64:(hh + 1) * 64]
                    omv = OML[:, :, hh * 64:(hh + 1) * 64]
                    b.op("dve", lambda: nc.vector.tensor_tensor(out=e3, in0=e3, in1=omv, op=ALU.mult),
                         r=[("e", par), "OML"], w=[("e", par)])
                    b.op("dve", lambda: nc.vector.tensor_tensor(out=e3, in0=e3, in1=lbv, op=ALU.add),
                         r=[("e", par), "LB"], w=[("e", par)])
                    b.op("dve", lambda: nc.vector.tensor_scalar(out=key, in0=e, scalar1=-1.0, scalar2=1.0, op0=ALU.mult,
                                                                op1=ALU.add), r=[("e", par)], w=[("key", par)])
                    b.op("dve", lambda: nc.vector.tensor_scalar_max(out=e, in0=e, scalar1=1e-20), r=[("e", par)],
                         w=[("e", par)])
                    b.op("act", lambda: nc.scalar.activation(out=l, in_=e, func=AF.Ln), r=[("e", par)], w=[("l", par)])
                    b.op("act", lambda: nc.scalar.activation(out=qs, in_=P[:, 0:64], func=AF.Exp, scale=-1.0), r=Pk,
                         w=[("qs", par)])
                    b.op("dve", lambda: nc.vector.tensor_scalar_add(out=qs, in0=qs, scalar1=1.0), r=[("qs", par)],
                         w=[("qs", par)])
                    b.op("dve", lambda: nc.vector.reciprocal(out=qs, in_=qs), r=[("qs", par)], w=[("qs", par)])
                    b.op("dve", lambda: nc.vector.tensor_tensor(out=qs, in0=P[:, 0:64], in1=qs, op=ALU.mult),
                         r=Pk + [("qs", par)], w=[("qs", par)])
                    mi, s0, ns = 2, 4, 4
                    q_src, q_keys = qs, [("qs", par)]
                    v_src = P[:, 192:320]
                self.check("e_z")
                for d in range(2):
                    b.op("pe", lambda: nc.tensor.matmul(self.pv(zb, 128 + d * 64, 192 + d * 64), lhsT=self.tri6[:, mi + d, :],
                                                        rhs=l[:, d * 64:(d + 1) * 64], start=True, stop=True),
                         r=["tri6", ("l", par)], w=self.pk(zb, 128, 256))
                b.op("pe", lambda: nc.tensor.matmul(self.pv(zb, 256, 260), lhsT=l, rhs=self.sumcols[:, s0:s0 + 4],
                                                    start=True, stop=True), r=["sumcols", ("l", par)], w=self.pk(zb, 256, 260))
                b.op("act", lambda: nc.scalar.activation(out=Gall[:, c * ns:(c + 1) * ns], in_=self.pv(zb, 256, 256 + ns),
                                                         func=AF.Exp), r=self.pk(zb, 256, 260), w=["G"])
                self.check("e_cum")
                Cv = self.pv(zb, 128, 256)
                Ck = self.pk(zb, 128, 256)
                b.op("act", lambda: nc.scalar.activation(out=E, in_=Cv, func=AF.Exp), r=Ck, w=[("E", par)])
                self.check("e_E1")
                b.op("act", lambda: nc.scalar.activation(out=Ei, in_=Cv, func=AF.Exp, scale=-1.0), r=Ck, w=[("Ei", par)])
                self.check("e_E2")
                for d in range(2):
                    cs = slice(d * 64, (d + 1) * 64)
                    b.op("dve", lambda: nc.vector.tensor_tensor(out=qt[:, cs], in0=q_src, in1=E[:, cs], op=ALU.mult),
                         r=q_keys + [("E", par)], w=[("qt", par)])
                    self.check("e_E3")
                    if gla:
                        b.op("dve", lambda: nc.vector.scalar_tensor_tensor(
                            out=kt_all[:, c, cs], in0=P[:, 64:128], scalar=0.125, in1=Ei[:, cs], op0=ALU.mult, op1=ALU.mult),
                            r=Pk + [("Ei", par)], w=[("kt", c)])
                if not gla:
                    b.op("dve", lambda: nc.vector.tensor_tensor(out=kt_all[:, c, :], in0=ck["key"][:, par, :], in1=Ei,
                                                                 op=ALU.mult), r=[("key", par), ("Ei", par)], w=[("kt", c)])
                self.check("e_E4")
                b.op("act", lambda: nc.scalar.activation(out=v_all[:, c, :], in_=v_src, func=AF.Identity), r=Pk, w=[("v", c)])
                if not gla:
                    b.op("act", lambda: nc.scalar.activation(out=v3[96:128, c, :], in_=v_src[96:128, :], func=AF.Identity),
                         r=Pk, w=[("v3", c)])
                self.check("e_E")
                tq = self.pv(zb, 384, 448).bitcast(BF16)
                tk = self.pv(zb, 448, 512).bitcast(BF16)
                tkey = self.pk(zb, 384, 512)
                b.op("pe", lambda: nc.tensor.transpose(tq, qt, self.ident[:]), r=[("qt", par), "ident"], w=tkey)
                b.op("pe", lambda: nc.tensor.transpose(tk, kt_all[:, c, :], self.ident[:]), r=[("kt", c), "ident"], w=tkey)
                b.op("act", lambda: nc.scalar.activation(out=qT[:, ch], in_=tq, func=AF.Identity), r=tkey, w=[("qT", c)])
                b.op("dve", lambda: nc.vector.tensor_copy(out=kT[:, ch], in_=tk), r=tkey, w=[("kT", c)])
            self.check("e_p1")
            gain_ap = self.hgain[:, j * 8 + h: j * 8 + h + 1]
            self.head_phase23(es, tmp, qT, kT, kt_all, v_all, v3, S_all, Gall, 64, (1 if gla else 4), wfm[:, 1, :, :], gain_ap, h)
            self.check("e_h%d" % h)


def prep_weights(inp):
    w = {}
    g_all = np.concatenate([inp["mix_norm_g"], inp["ffn_norm_g"], inp["final_norm_g"][None]], 0)
    w["gains"] = np.ascontiguousarray(g_all.reshape(9, KT, 128).transpose(2, 0, 1).reshape(128, 9 * KT))
    wu = inp["ffn_w_up"]
    L = wu.shape[0]
    wu5 = wu.reshape(L, KT, 128, 2, FT, 128)
    w["w_up_t"] = np.ascontiguousarray(wu5.transpose(0, 4, 3, 2, 1, 5).reshape(L, 2 * FT, 128, KT * 128))
    wd = inp["ffn_w_down"]
    wd6 = wd.reshape(L, 2, 11, 128, KT, 128)
    w["w_dn_t"] = np.ascontiguousarray(wd6.transpose(0, 1, 4, 3, 2, 5).reshape(L, 2, KT, 128, 11 * 128))
    cw = inp["ffn_conv_w"]
    cb = inp["ffn_conv_b"]
    c4 = np.concatenate([cw, cb[:, None, :]], 1)
    c4 = c4.reshape(L, 4, 2, FT, 128)
    w["conv_p"] = np.ascontiguousarray(c4.transpose(4, 0, 3, 2, 1).reshape(128, L * 2 * FT * 4))

    wo = np.stack([inp["w_out_even"][0], inp["w_out_odd"][0], inp["w_out_even"][1], inp["w_out_odd"][1]], 0)
    w["w_out_t"] = np.ascontiguousarray(wo.reshape(L, KT, 128, KT, 128).transpose(0, 3, 2, 1, 4).reshape(L, KT, 128, KT * 128))
    wie = inp["w_in_even"]
    o = np.cumsum([0, 256, 256, 512, 512, 16, 16, 256, 256, 256, 512, 512])
    gq, gk, gv, gr, glf, glb, hq, hzf, hzb, hi, hg = (wie[:, :, o[i]:o[i + 1]] for i in range(11))
    tm = np.zeros((2, 8, 1024, 320), np.float32)
    for h in range(4):
        tm[:, h, :, 0:64] = gq[:, :, h * 64:(h + 1) * 64]
        tm[:, h, :, 64:128] = gk[:, :, h * 64:(h + 1) * 64]
        tm[:, h, :, 128:256] = gv[:, :, h * 128:(h + 1) * 128]
        tm[:, 4 + h, :, 0:64] = hq[:, :, h * 64:(h + 1) * 64]
        tm[:, 4 + h, :, 64:128] = hzf[:, :, h * 64:(h + 1) * 64]
        tm[:, 4 + h, :, 128:192] = hzb[:, :, h * 64:(h + 1) * 64]
        tm[:, 4 + h, :, 192:320] = hi[:, :, h * 128:(h + 1) * 128]
    w["w_tm_e"] = np.ascontiguousarray(tm.reshape(2, 8, KT, 128, 320).transpose(0, 1, 3, 2, 4).reshape(2, 8, 128, KT * 320))
    fm = np.zeros((2, 9, 1024, 128), np.float32)
    for h in range(4):
        fm[:, h] = gr[:, :, h * 128:(h + 1) * 128]
        fm[:, 4 + h] = hg[:, :, h * 128:(h + 1) * 128]
    fm[:, 8, :, 0:16] = glf
    fm[:, 8, :, 16:32] = glb
    w["w_fm_e"] = np.ascontiguousarray(fm.reshape(2, 9, KT, 128, 128).transpose(0, 1, 3, 2, 4).reshape(2, 9, 128, KT * 128))
    wa2 = inp["gla_wa2"]
    ba = inp["gla_ba"]
    wa2p = np.zeros((2, 4, 32, 128), np.float32)
    bah = np.zeros((2, 4, 1, 128), np.float32)
    for h in range(4):
        wa2p[:, h, 0:16, 0:64] = wa2[:, 0, :, h * 64:(h + 1) * 64]
        wa2p[:, h, 16:32, 64:128] = wa2[:, 1, :, h * 64:(h + 1) * 64]
        bah[:, h, 0, 0:64] = ba[:, 0, h * 64:(h + 1) * 64]
        bah[:, h, 0, 64:128] = ba[:, 1, h * 64:(h + 1) * 64]
    w["wa2p"] = wa2p
    w["bah"] = bah
    w["lbl"] = np.ascontiguousarray(inp["hgrn_lb_logits"].reshape(-1))
    hg_ = np.zeros((128, 16), np.float32)
    for j in range(2):
        for h in range(4):
            hg_[:, j * 8 + h] = inp["gla_norm_g"][j, h * 128:(h + 1) * 128]
            hg_[:, j * 8 + 4 + h] = inp["hgrn_norm_g"][j, h * 128:(h + 1) * 128]
    w["hgain"] = hg_
    ii = np.arange(128)
    triL = (ii[:, None] <= ii[None, :]).astype(np.float32)
    triU = (ii[:, None] >= ii[None, :]).astype(np.float32)
    blk32 = (ii[:, None] // 32 == ii[None, :] // 32).astype(np.float32)
    w["tri6"] = np.ascontiguousarray(np.stack([triL * (-1.0 / 16), triU * (-1.0 / 16), triL * blk32, triU * blk32,
                                               triL, triU], 1).reshape(128, 768))
    sc = np.zeros((128, 8), np.float32)
    sc[:, 0:4] = -1.0 / 16
    for q_ in range(4):
        sc[32 * q_:32 * (q_ + 1), 4 + q_] = 1.0
    w["sumcols"] = sc
    w["ident"] = np.eye(128, dtype=np.float32)

    wio = inp["w_in_odd"]
    oo = np.cumsum([0, 512, 512, 768, 768, 256])
    rq, rk, rv, rg, su = (wio[:, :, oo[i]:oo[i + 1]] for i in range(5))
    tmo = np.zeros((2, 4, 1024, 448), np.float32)
    for h in range(4):
        tmo[:, h, :, 0:128] = rq[:, :, h * 128:(h + 1) * 128]
        tmo[:, h, :, 128:256] = rk[:, :, h * 128:(h + 1) * 128]
        tmo[:, h, :, 256:448] = rv[:, :, h * 192:(h + 1) * 192]
    w["w_tm_o"] = np.ascontiguousarray(tmo.reshape(2, 4, KT, 128, 448).transpose(0, 1, 3, 2, 4).reshape(2, 4, 128, KT * 448))
    fmo = np.zeros((2, 8, 1024, 128), np.float32)
    hgo = np.zeros((128, 16), np.float32)
    for h in range(4):
        for pi, (f0, nf, etile, p0) in enumerate(Prog.RET_PIECES[h]):
            fmo[:, 2 * h + pi, :, 0:nf] = rg[:, :, h * 192 + f0: h * 192 + f0 + nf]
            for j in range(2):
                hgo[p0:p0 + nf, j * 8 + 2 * h + pi] = inp["ret_norm_g"][j, h * 192 + f0: h * 192 + f0 + nf]
    w["w_fm_o"] = np.ascontiguousarray(fmo.reshape(2, 8, KT, 128, 128).transpose(0, 1, 3, 2, 4).reshape(2, 8, 128, KT * 128))
    w["hgain_o"] = hgo
    w["w_u_o"] = np.ascontiguousarray(su.reshape(2, KT, 128, 256).transpose(0, 2, 1, 3).reshape(2, 128, KT * 256))
    half = 64
    inv = (10000.0 ** (-np.arange(half, dtype=np.float32) / half)).astype(np.float32)
    ang = (np.arange(T, dtype=np.float32)[:, None] * inv[None, :]).astype(np.float32)
    cs = np.stack([np.cos(ang), np.sin(ang)], 0).reshape(2, NCH, 128, half)
    w["rope"] = np.ascontiguousarray(cs.transpose(2, 0, 1, 3).reshape(128, 2 * NCH * half)).astype(np.float32)
    ti = np.arange(128, dtype=np.float64)
    rdec = np.zeros((128, 16), np.float64)
    dm = np.zeros((128, 4, 128), np.float64)
    for h in range(4):
        gf = 1.0 - 2.0 ** (-5.0 - h)
        gb = 1.0 - 2.0 ** (-5.5 - h)
        rdec[:, 4 * h + 0] = gf ** (ti + 1)
        rdec[:, 4 * h + 1] = gb ** (128 - ti)
        rdec[:, 4 * h + 2] = gf ** (127 - ti) * 128.0 ** -0.5
        rdec[:, 4 * h + 3] = gb ** ti * 128.0 ** -0.5
        dji = ti[None, :] - ti[:, None]
        dm[:, h, :] = np.where(dji >= 0, gf ** np.abs(dji), 0.0) + np.where(dji <= 0, gb ** np.abs(dji), 0.0)
    w["rdec"] = rdec.astype(np.float32)
    w["dmask"] = np.ascontiguousarray(dm.reshape(128, 512)).astype(np.float32)

    def qp(a):
        sh = a.shape
        a = a.reshape(sh[:-3] + (8, 2, 64, sh[-1]))
        nd = a.ndim
        perm = (nd - 3, nd - 2) + tuple(range(nd - 4)) + (nd - 4, nd - 1)
        a = a.transpose(perm)
        return a.reshape((128,) + a.shape[2:])
    prm = np.zeros((2, 128, 3, 2, 8), np.float32)
    bc = np.zeros((2, 128, 4, 8, 16), np.float32)
    for j in range(2):
        prm[j, :, 0] = qp(inp["s5_lam_re"][j][..., None])[..., 0]
        prm[j, :, 1] = qp(inp["s5_lam_im"][j][..., None])[..., 0]
        ldt = np.broadcast_to(inp["s5_log_dt"][j][:, :, None, None], (2, 16, 64, 1))
        prm[j, :, 2] = qp(np.ascontiguousarray(ldt))[..., 0]
        bc[j, :, 0] = qp(inp["s5_b_re"][j])
        bc[j, :, 1] = qp(inp["s5_b_im"][j])
        bc[j, :, 2] = qp(np.ascontiguousarray(inp["s5_c_re"][j].transpose(0, 2, 1)))
        bc[j, :, 3] = qp(np.ascontiguousarray(inp["s5_c_im"][j].transpose(0, 2, 1)))
    w["s5_prm"] = prm.reshape(2, 128, 48)
    w["s5_bc"] = bc.reshape(2, 128, 512)
    dsk = inp["s5_d"].reshape(2, 16, 16)
    w["s5_dsk"] = np.ascontiguousarray(np.broadcast_to(dsk.transpose(0, 2, 1)[:, None, :, :], (2, 8, 16, 16)).reshape(2, 128, 16))
    w["s5_glb"] = np.ascontiguousarray(inp["s5_glu_b"].reshape(2, 2, 128).transpose(0, 2, 1))
    w["s5_gw"] = np.ascontiguousarray(inp["s5_glu_w"].reshape(2, 2, 128, 256).transpose(0, 2, 1, 3).reshape(2, 128, 512))
    jj_ = np.arange(128) // 16
    bm = np.stack([(jj_[:, None] <= jj_[None, :]), (jj_[:, None] >= jj_[None, :])], 1).astype(np.float32)
    w["s5_bm"] = np.ascontiguousarray(bm.reshape(128, 256))
    return w


_CFG = {}


def kernel(**inputs):
    inp = {k: np.asarray(v) for k, v in inputs.items()}
    cfg = dict(_CFG)
    x = inp["x"]
    w = prep_weights(inp)
    import time as _t
    _t0 = _t.time()
    prog = Prog(cfg)
    nc = prog.build()
    print("[kernel] build %.1fs, instr counts %s" % (_t.time() - _t0, dict(prog.b.cnt)), flush=True)
    in_maps = []
    ncores = cfg.get("ncores", NCORES)
    for c in range(ncores):
        xs = x[c * NSEQ:(c + 1) * NSEQ]
        xl = np.ascontiguousarray(xs.reshape(NSEQ, T, KT, 128).transpose(0, 3, 2, 1))
        m = {"x_in": xl}
        m.update(w)
        in_maps.append(m)
    _t0 = _t.time()
    res = run_bass_kernel_spmd(nc, in_maps, core_ids=list(range(ncores)))
    print("[kernel] run %.1fs" % (_t.time() - _t0), flush=True)
    outs = []
    for c in range(ncores):
        y = res.results[c]["y_out"]
        outs.append(np.ascontiguousarray(y.transpose(0, 3, 2, 1)).reshape(NSEQ, T, D))
    return np.concatenate(outs, 0).astype(np.float32)
```
